# Optimizing a Trainium2 kernel written in Bass

```python
import jax, jax.numpy as jnp
from jax import lax
import numpy as np

D_MODEL = 1024
BATCH = 4
SEQ = 8192
DEPTH = 1

HGRN_HEADS = 8
HGRN_HEAD_DIM = 128
D_A = HGRN_HEADS * HGRN_HEAD_DIM
CHUNK = 64
D_B = D_MODEL
CONV_WIDTH = 31
D_FF = 4 * D_MODEL
EPS = 1e-6

SPLITS = (D_A, 2 * D_A, 3 * D_A, 4 * D_A, 4 * D_A + 2 * D_B, 4 * D_A + 2 * D_B + D_MODEL)
D_IN = 4 * D_A + 2 * D_B + 2 * D_MODEL

kernel_name = "hgrn2_conformer_gated_hybrid"


def rms_norm(x, g):
    xf = x.astype(jnp.float32)
    y = xf * lax.rsqrt(jnp.mean(xf * xf, axis=-1, keepdims=True) + EPS)
    return (y * g.astype(jnp.float32)).astype(x.dtype)


def layer_norm(x, g, b):
    xf = x.astype(jnp.float32)
    mu = jnp.mean(xf, axis=-1, keepdims=True)
    xc = xf - mu
    var = jnp.mean(xc * xc, axis=-1, keepdims=True)
    y = xc * lax.rsqrt(var + EPS) * g.astype(jnp.float32) + b.astype(jnp.float32)
    return y.astype(x.dtype)


def hgrn2_chunked(q, k, v, log_f):
    B, S, H, dk = q.shape
    dv = v.shape[-1]
    n = S // CHUNK

    def to_chunks(t):
        return t.reshape(B, n, CHUNK, H, t.shape[-1]).transpose(1, 0, 3, 2, 4)

    causal = jnp.tril(jnp.ones((CHUNK, CHUNK), dtype=bool))

    def step(state, inp):
        qc, kc, vc, lf = inp
        b = jnp.cumsum(lf, axis=2)
        o_inter = jnp.einsum('bhtd,bhdv->bhtv', qc * jnp.exp(b), state)
        diff = b[:, :, :, None, :] - b[:, :, None, :, :]
        decay = jnp.exp(jnp.where(causal[None, None, :, :, None], diff, -jnp.inf))
        scores = jnp.einsum('bhtd,bhsd,bhtsd->bhts', qc, kc, decay)
        o_intra = jnp.einsum('bhts,bhsv->bhtv', scores, vc)
        b_last = b[:, :, -1:, :]
        k_dec = kc * jnp.exp(b_last - b)
        new_state = (jnp.exp(b_last[:, :, 0, :])[..., None] * state
                     + jnp.einsum('bhsd,bhsv->bhdv', k_dec, vc))
        return new_state, o_inter + o_intra

    s0 = jnp.zeros((B, H, dk, dv), jnp.float32)
    _, o = lax.scan(step, s0, (to_chunks(q), to_chunks(k), to_chunks(v), to_chunks(log_f)))
    return o.transpose(1, 0, 3, 2, 4).reshape(B, S, H, dv)


def causal_depthwise_conv(u, w, b):
    y = lax.conv_general_dilated(
        u, w[:, None, :].astype(u.dtype), window_strides=(1,),
        padding=[(CONV_WIDTH - 1, 0)],
        dimension_numbers=('NWC', 'WIO', 'NWC'),
        feature_group_count=u.shape[-1])
    return y + b.astype(u.dtype)


def setup_inputs(seed: int = 0) -> dict:
    key = jax.random.key(seed)
    ks = jax.random.split(key, 16)
    f32 = jnp.float32

    def nrm(k, shape, scale):
        return jax.random.normal(k, shape, f32) * scale

    return {
        "x": nrm(ks[0], (BATCH, SEQ, D_MODEL), 1.0),
        "norm_mix_g": 1.0 + nrm(ks[1], (DEPTH, D_MODEL), 0.02),
        "w_in": nrm(ks[2], (DEPTH, D_MODEL, D_IN), D_MODEL ** -0.5),
        "lb_param": nrm(ks[3], (DEPTH + 1, D_A), 0.5),
        "hgrn_norm_g": 1.0 + nrm(ks[4], (DEPTH, HGRN_HEAD_DIM), 0.02),
        "w_a_out": nrm(ks[5], (DEPTH, D_A, D_MODEL), D_A ** -0.5),
        "conv_w": nrm(ks[6], (DEPTH, CONV_WIDTH, D_B), CONV_WIDTH ** -0.5),
        "conv_b": nrm(ks[7], (DEPTH, D_B), 0.02),
        "conv_ln_g": 1.0 + nrm(ks[8], (DEPTH, D_B), 0.02),
        "conv_ln_b": nrm(ks[9], (DEPTH, D_B), 0.02),
        "w_b_out": nrm(ks[10], (DEPTH, D_B, D_MODEL), D_B ** -0.5),
        "w_out": nrm(ks[11], (DEPTH, D_MODEL, D_MODEL), D_MODEL ** -0.5),
        "norm_mlp_g": 1.0 + nrm(ks[12], (DEPTH, D_MODEL), 0.02),
        "w_mlp_in": nrm(ks[13], (DEPTH, D_MODEL, D_FF), D_MODEL ** -0.5),
        "w_mlp_out": nrm(ks[14], (DEPTH, D_FF, D_MODEL), D_FF ** -0.5),
        "norm_final_g": 1.0 + nrm(ks[15], (D_MODEL,), 0.02),
    }


def reference(x, norm_mix_g, w_in, lb_param, hgrn_norm_g, w_a_out, conv_w, conv_b,
              conv_ln_g, conv_ln_b, w_b_out, w_out, norm_mlp_g, w_mlp_in, w_mlp_out,
              norm_final_g):
    B, S, _ = x.shape
    dt = x.dtype
    lb_all = jnp.cumsum(jax.nn.softmax(lb_param.astype(jnp.float32), axis=0), axis=0)

    for l in range(DEPTH):
        h = rms_norm(x, norm_mix_g[l])
        proj = jnp.einsum('bsd,de->bse', h, w_in[l])
        q, f_logit, i_in, g_out, glu_in, gate_a, gate_b = jnp.split(proj, SPLITS, axis=-1)

        lb = lb_all[l]
        zf = f_logit.astype(jnp.float32)
        log_f = jnp.log(lb + (1.0 - lb) * jax.nn.sigmoid(zf))
        k_in = (1.0 - lb) * jax.nn.sigmoid(-zf)
        hd = (B, S, HGRN_HEADS, HGRN_HEAD_DIM)
        o_a = hgrn2_chunked(jax.nn.silu(q.astype(jnp.float32)).reshape(hd),
                            k_in.reshape(hd), i_in.astype(jnp.float32).reshape(hd),
                            log_f.reshape(hd))
        o_a = rms_norm(o_a, hgrn_norm_g[l]).reshape(B, S, D_A).astype(dt)
        o_a = o_a * jax.nn.silu(g_out)
        branch_a = jnp.einsum('bse,ed->bsd', o_a, w_a_out[l])

        glu_a, glu_b = jnp.split(glu_in, 2, axis=-1)
        u = glu_a * jax.nn.sigmoid(glu_b)
        u = causal_depthwise_conv(u, conv_w[l], conv_b[l])
        u = jax.nn.silu(layer_norm(u, conv_ln_g[l], conv_ln_b[l]))
        branch_b = jnp.einsum('bsc,cd->bsd', u, w_b_out[l])

        merged = jax.nn.sigmoid(gate_a) * branch_a + jax.nn.sigmoid(gate_b) * branch_b
        x = x + jnp.einsum('bsd,de->bse', merged, w_out[l])

        h2 = rms_norm(x, norm_mlp_g[l])
        z = jax.nn.relu(jnp.einsum('bsd,df->bsf', h2, w_mlp_in[l]))
        x = x + jnp.einsum('bsf,fd->bsd', z * z, w_mlp_out[l])

    return rms_norm(x, norm_final_g)
```

```python
import contextlib
import numpy as np
import concourse.bass as bass
import concourse.mybir as mybir
from concourse.bass_utils import run_bass_kernel_spmd

F32 = mybir.dt.float32
BF16 = mybir.dt.bfloat16
AF = mybir.ActivationFunctionType
ALU = mybir.AluOpType

P = 128
D = 1024
H = 8
T = 512
EPS = 1e-6
CW = 31
UW = T + CW - 1
NBLK = 38
HB0, GLU0, GA0, GB0, WA0, WB0, WO0, M10, M20 = 0, 8, 12, 14, 16, 18, 20, 22, 30
PC_GM, PC_GML, PC_P0, PC_P1, PC_HG, PC_CB, PC_LG, PC_LB, PC_CW = 0, 8, 16, 24, 32, 33, 41, 49, 57
NPAR = 57 + 8 * CW
CC_ID, CC_MASK, CC_SCAN = 0, 128, 256
NCST = 256 + T

ENGS = ['pe', 'act', 'dve', 'pool', 'sp']


class V:
    def __init__(s, t, W, name, off, n, esz, ap=None, whole=None):
        s.t, s.W, s.name, s.off, s.n, s.esz = t, W, name, off, n, esz
        s._ap = ap
        s.whole = whole

    def res(s):
        if s.whole is not None:
            return (s.name, 0, s.whole)
        return (s.name, s.off * s.esz, (s.off + s.n) * s.esz)

    def sub(s, a, n):
        assert a + n <= s.n
        return V(s.t, s.W, s.name, s.off + a, n, s.esz)

    def ap(s, dims=None, off=0, p0=0, np_=P):
        if s._ap is not None:
            return s._ap
        if dims is None:
            dims = [[1, s.n - off]]
        return bass.AP(s.t, p0 * s.W + s.off + off, [[s.W, np_]] + [list(d) for d in dims])


class Sched:
    def __init__(s):
        s.ops = {e: [] for e in ENGS}
        s.res = {}
        s.seen = {e: {} for e in ENGS}
        s.dma_cnt = {}
        s.nbank = 0
        s.held = set()

    def bank(s):
        while True:
            b = s.nbank % 8
            s.nbank += 1
            if b not in s.held:
                return b

    def op(s, eng, iname, args, kw=None, r=(), w=(), dma=None):
        idx = len(s.ops[eng])
        deps = {}

        def add(d):
            if deps.get(d[0], -1) < d[1]:
                deps[d[0]] = d[1]

        rres = [v.res() for v in r]
        wres = [v.res() for v in w]
        for (name, lo, hi) in rres:
            R = s.res.setdefault(name, {'w': [], 'r': []})
            for (l, h, d) in R['w']:
                if l < hi and lo < h:
                    add(d)
        for (name, lo, hi) in wres:
            R = s.res.setdefault(name, {'w': [], 'r': []})
            for (l, h, d) in R['w']:
                if l < hi and lo < h:
                    add(d)
            for (l, h, d) in R['r']:
                if l < hi and lo < h:
                    add(d)
        if dma is not None:
            s.dma_cnt[dma] = s.dma_cnt.get(dma, 0) + 1
            tok = (('dma', dma), s.dma_cnt[dma] * 16)
        else:
            tok = (eng, idx)
        for (name, lo, hi) in rres:
            R = s.res[name]
            R['r'] = [x for x in R['r'] if not (x[2][0] == tok[0] and lo <= x[0] and x[1] <= hi)]
            R['r'].append((lo, hi, tok))
        for (name, lo, hi) in wres:
            R = s.res[name]
            R['w'] = [x for x in R['w'] if not (lo <= x[0] and x[1] <= hi)]
            R['r'] = [x for x in R['r'] if not (lo <= x[0] and x[1] <= hi)]
            R['w'].append((lo, hi, tok))
        waits = []
        for src, i in deps.items():
            if src == 'pe' and eng == 'pe':
                continue
            if s.seen[eng].get(src, -1) >= i:
                continue
            s.seen[eng][src] = i
            waits.append((src, i))
        s.ops[eng].append({'name': iname, 'args': args, 'kw': kw or {}, 'waits': waits, 'dma': dma})
        return tok

    def emit(s, nc, es, final_waits):
        sig = {e: set() for e in ENGS}
        for e in ENGS:
            for o in s.ops[e]:
                for (src, i) in o['waits']:
                    if isinstance(src, str):
                        sig[src].add(i)
        cnt = {}
        for e in ENGS:
            c = 0
            m = {}
            for i in range(len(s.ops[e])):
                if i in sig[e]:
                    c += 1
                    m[i] = c
            cnt[e] = m
        sems = {}
        for e in ENGS:
            if cnt[e]:
                sems[e] = es.enter_context(nc.semaphore('s_' + e))
        for k in s.dma_cnt:
            sems[('dma', k)] = es.enter_context(nc.semaphore('d_' + k))
        block = es.enter_context(nc.Block())

        def run(ename, eobj):
            for i, o in enumerate(s.ops[ename]):
                for (src, v) in o['waits']:
                    if isinstance(src, str):
                        eobj.wait_ge(sems[src], cnt[src][v])
                    else:
                        eobj.wait_ge(sems[src], v)
                ins = getattr(eobj, o['name'])(*o['args'], **o['kw'])
                if o['dma'] is not None:
                    ins.then_inc(sems[('dma', o['dma'])], 16)
                elif i in cnt[ename]:
                    ins.then_inc(sems[ename], 1)
            if ename == 'sp':
                for (src, v) in final_waits:
                    eobj.wait_ge(sems[src], v)

        @block.tensor
        def _(e):
            run('pe', e)

        @block.scalar
        def _(e):
            run('act', e)

        @block.vector
        def _(e):
            run('dve', e)

        @block.gpsimd
        def _(e):
            run('pool', e)

        @block.sync
        def _(e):
            run('sp', e)


def build(NT):
    import os as _os
    nc = bass.Bass("TRN2", target_bir_lowering=False)
    x_d = nc.dram_tensor("x", [NT * 4, P, D], F32, kind="ExternalInput").ap()
    xp_d = nc.dram_tensor("xprev", [NT * 4, P, D], F32, kind="ExternalInput").ap()
    w_d = nc.dram_tensor("wall", [NBLK, P, 4096], F32, kind="ExternalInput").ap()
    par_d = nc.dram_tensor("par", [P, NPAR], F32, kind="ExternalInput").ap()
    cst_d = nc.dram_tensor("cst", [P, NCST], F32, kind="ExternalInput").ap()
    gfin_d = nc.dram_tensor("gfin", [P, D], F32, kind="ExternalInput").ap()
    y_d = nc.dram_tensor("y", [NT * 4, P, D], F32, kind="ExternalOutput").ap()
    wbf_d = nc.dram_tensor("wbf", [NBLK, P, 4096], BF16).ap()

    es = contextlib.ExitStack()
    S = Sched()

    def sb(name, n, dt):
        t = es.enter_context(nc.sbuf_tensor("sb_" + name, [P, n], dt))
        return V(t, n, name, 0, n, 4 if dt == F32 else 2)

    par = sb("par", NPAR, F32)
    lbv = sb("lbv", 32, F32)
    cst = sb("cst", NCST, F32)
    identb = sb("identb", 128, BF16)
    onesb = sb("onesb", 128, BF16)
    gfin = sb("gfin", D, F32)
    diag = sb("diag", 2 * CW * 128, BF16)
    S32 = sb("S32", H * 128, F32)
    S32b = sb("S32b", H * 128, F32)
    Sc = [sb("Sc0", H * 128, BF16), sb("Sc1", H * 128, BF16)]
    u = sb("u", 8 * UW, BF16)
    x = sb("x", 4 * D, F32)
    hT = sb("hT", 8 * T, BF16)
    wbuf = [sb("wb%d" % i, 4096, BF16) for i in range(3)]
    stat = sb("stat", 16, F32)
    ARENA = 114 * 1024
    ar_t = es.enter_context(nc.sbuf_tensor("arena", [P, ARENA // 2], BF16))
    ar_f = ar_t.bitcast(F32)

    def A(byte_off, n, dt):
        byte_off = int(byte_off)
        if dt == F32:
            assert byte_off % 4 == 0 and byte_off + 4 * n <= ARENA
            return V(ar_f, ARENA // 4, "arena", byte_off // 4, n, 4)
        assert byte_off % 2 == 0 and byte_off + 2 * n <= ARENA
        return V(ar_t, ARENA // 2, "arena", byte_off // 2, n, 2)

    K = 1024
    o_a = A(0, 8 * T, BF16)
    mergedA = A(8 * K, 8 * T, F32)

    def slot_full(g):
        b = 24 * K + g * 19 * K
        return dict(ta=A(b, T, F32), tb=A(b + 2 * K, T, F32), tc=A(b + 4 * K, T, F32), td=A(b + 6 * K, T, F32),
                    qs=A(b + 8 * K, T, F32), sg=A(b + 10 * K, T, F32), qb=A(b + 12 * K, T, BF16),
                    kb=A(b + 13 * K, T, BF16), kdT=A(b + 14 * K, T, BF16), kdtm=A(b + 15 * K, T, BF16),
                    vh=A(b + 16 * K, T, BF16), Sall=A(b + 17 * K, 7 * 128, BF16))

    def slot_pre(g):
        b = g * 8 * K
        return dict(ta=A(b, T, F32), tb=A(b + 2 * K, T, F32), tc=A(b + 4 * K, T, F32),
                    kdT=A(b + 6 * K, T, BF16), kdtm=A(b + 6 * K, T, BF16), vh=A(b + 7 * K, T, BF16))

    SLF4 = [slot_full(g) for g in range(4)]
    wpre = A(76 * K, 8 * 2048, BF16)
    SLP = [slot_pre(g) for g in range(8)]
    osq = [A(100 * K, T, BF16), A(101 * K, T, BF16)]
    scm = [A(102 * K, T, BF16), A(103 * K, T, BF16)]
    rsv = [A(104 * K, T, F32), A(106 * K, T, F32)]
    xn = [A(108 * K, D, BF16), A(110 * K, D, BF16)]
    ynorm = A(70 * K, 8 * T, BF16)
    y32 = A(24 * K, 8 * T, F32)
    ybf = [A(40 * K, T, BF16), A(41 * K, T, BF16)]
    ysq = [A(42 * K, T, BF16), A(43 * K, T, BF16)]
    meanv = A(44 * K, T, F32)
    msqv = A(46 * K, T, F32)
    rs2v = A(48 * K, T, F32)
    tcv = [A(50 * K, T, F32), A(52 * K, T, F32)]
    sgate = [A(54 * K, T, F32), A(56 * K, T, F32)]
    tbv = [A(58 * K, T, F32), A(60 * K, T, F32)]
    merged = A(62 * K, 8 * T, BF16)
    z2T = A(0, 32 * T, BF16)
    r32 = [A(32 * K, T, F32), A(34 * K, T, F32)]
    st32 = [A((64 + 4 * i) * K, 1024, F32) for i in range(2)]
    stbf = [A((72 + 2 * i) * K, 1024, BF16) for i in range(2)]

    ps_t = [es.enter_context(nc.psum_tensor("ps%d" % i, [P, 512], F32)) for i in range(8)]
    ps_b = [t.bitcast(BF16) for t in ps_t]
    free_banks = list(range(8))

    def alloc(n=1):
        assert len(free_banks) >= n, "out of PSUM banks"
        out = [free_banks.pop(0) for _ in range(n)]
        return out if n > 1 else out[0]

    def rel(*bs):
        for b in bs:
            assert b not in free_banks
            free_banks.append(b)

    def PS(b, off=0, n=512):
        return V(ps_t[b], 512, "ps%d" % b, off, n, 4, whole=2048)

    def PSB(b, off=0, n=1024):
        return V(ps_b[b], 1024, "ps%d" % b, off, n, 2, whole=2048)

    def dram(apx, name, lo, hi):
        return V(None, 0, name, lo, hi - lo, 1, ap=apx)

    def mm(out, lhsT, rhs, start, stop, r, w):
        S.op('pe', 'matmul', (out, lhsT, rhs), dict(start=start, stop=stop), r=r, w=w)

    def act(out_ap, in_ap, func, r, w, bias=None, scale=None, accum=None):
        kw = {}
        if bias is not None:
            kw['bias'] = bias
        if scale is not None:
            kw['scale'] = scale
        if accum is not None:
            kw['accum_out'] = accum
        S.op('act', 'activation', (out_ap, in_ap, func), kw, r=r, w=w)

    def tt(eng, out, in0, in1, op, r, w):
        S.op(eng, 'tensor_tensor', (out, in0, in1, op), r=r, w=w)

    def dma(out_v, in_v, key, eng='sp'):
        return S.op(eng, 'dma_start', (), dict(out=out_v.ap(), in_=in_v.ap()), r=[in_v], w=[out_v], dma=key)

    wslot = [0]

    preloaded = {}

    def preload(b):
        preloaded[b] = load_block(b)

    def load_block(b):
        if b in preloaded:
            return preloaded.pop(b)
        k = wslot[0] % 3
        wslot[0] += 1
        dma(wbuf[k], dram(wbf_d[b], "wbf", b, b + 1), "w%d" % k)
        return wbuf[k]

    def col(v, c):
        return v.sub(c, 1)

    epsc = col(stat, 12)

    dma(par, dram(par_d, "par_d", 0, 1), "c0")
    dma(cst, dram(cst_d, "cst_d", 0, 1), "c1")
    dma(gfin, dram(gfin_d, "gfin_d", 0, 1), "c2")
    S.op('dve', 'memset', (epsc.ap(), EPS), w=[epsc])
    dv = lbv.sub(24, 8)
    tt('dve', dv.ap(), par.sub(PC_P0, 8).ap(), par.sub(PC_P1, 8).ap(), ALU.subtract, r=[par], w=[dv])
    act(lbv.sub(0, 8).ap(), dv.ap(), AF.Sigmoid, r=[dv], w=[lbv.sub(0, 8)])
    act(lbv.sub(8, 8).ap(), dv.ap(), AF.Sigmoid, r=[dv], w=[lbv.sub(8, 8)], scale=-1.0)
    S.op('dve', 'tensor_scalar', (lbv.sub(16, 8).ap(), lbv.sub(8, 8).ap(), -1.0, None, ALU.mult),
         r=[lbv.sub(8, 8)], w=[lbv.sub(16, 8)])
    S.op('dve', 'tensor_copy', (identb.ap(), cst.sub(CC_ID, 128).ap()), r=[cst], w=[identb])
    S.op('dve', 'memset', (onesb.ap(), 1.0), w=[onesb])
    S.op('dve', 'memset', (S32.ap(), 0.0), w=[S32])
    S.op('pool', 'memset', (u.ap(), 0.0), w=[u])
    identf = cst.sub(CC_ID, 128)

    def build_diag(m):
        for j in range(CW):
            dst = diag.sub(((m % 2) * CW + j) * 128, 128)
            sc = col(par, PC_CW + m * CW + j)
            e3 = j % 3
            if e3 == 0:
                S.op('pool', 'tensor_scalar', (dst.ap(), identf.ap(), sc.ap(), 0.0, ALU.mult, ALU.add), r=[identf, sc], w=[dst])
            elif e3 == 1:
                act(dst.ap(), identf.ap(), AF.Copy, r=[identf, sc], w=[dst], scale=sc.ap())
            else:
                S.op('dve', 'tensor_scalar', (dst.ap(), identf.ap(), sc.ap(), 0.0, ALU.mult, ALU.add), r=[identf, sc], w=[dst])

    cast_i = [0]
    LA = 1

    def cast_blocks(blocks):
        items = [(b, qd) for b in blocks for qd in range(4)]
        n = len(items)
        base = cast_i[0]
        cast_i[0] += n

        def ld(j):
            b, qd = items[j]
            k = (base + j) % 2
            src_ = dram(w_d[b][:, qd * 1024:(qd + 1) * 1024], "w_d", b * 4 + qd, b * 4 + qd + 1)
            dma(st32[k], src_, "st%d" % k)

        def cs(j):
            b, qd = items[j]
            i = base + j
            k = i % 2
            gain = None
            if b < 16:
                gain = PC_GM
            elif M10 <= b < M20:
                gain = PC_GML
            if gain is not None:
                eng = 'dve' if i % 2 == 0 else 'pool'
                gv = par.sub(gain + 2 * qd, 2)
                tt(eng, stbf[k].ap([[512, 2], [1, 512]]), st32[k].ap([[512, 2], [1, 512]]),
                   gv.ap([[1, 2], [0, 512]]), ALU.mult, r=[st32[k], gv], w=[stbf[k]])
            else:
                if i % 2 == 0:
                    act(stbf[k].ap(), st32[k].ap(), AF.Copy, r=[st32[k]], w=[stbf[k]])
                else:
                    S.op('dve', 'tensor_copy', (stbf[k].ap(), st32[k].ap()), r=[st32[k]], w=[stbf[k]])
            dst = dram(wbf_d[b][:, qd * 1024:(qd + 1) * 1024], "wbf", b + qd * 0.25, b + (qd + 1) * 0.25)
            dma(dst, stbf[k], "so%d" % k)

        for j in range(n + LA):
            if j < n:
                ld(j)
            if j >= LA:
                cs(j - LA)

    def load_wpre():
        for h in range(8):
            srcap = wbf_d[HB0 + h].rearrange("p (k c) -> p k c", c=512)[:, :, 0:256]
            dstv = wpre.sub(h * 2048, 2048)
            S.op('sp', 'dma_start', (), dict(out=dstv.ap([[256, 8], [1, 256]]), in_=srcap),
                 r=[dram(srcap, "wbf", HB0 + h, HB0 + h + 1)], w=[dstv], dma="wp%d" % h)

    def rms_stats(xs, s, junkv):
        ss, ln_, rs = col(stat, s), col(stat, 4 + s), col(stat, 8 + s)
        act(junkv.ap(), xs.ap(), AF.Square, r=[xs], w=[junkv, ss], accum=ss.ap())
        act(ln_.ap(), ss.ap(), AF.Ln, r=[ss, epsc], w=[ln_], bias=epsc.ap(), scale=1.0 / D)
        act(rs.ap(), ln_.ap(), AF.Exp, r=[ln_], w=[rs], scale=-0.5)
        return rs

    def rmsnorm_T():
        for s in range(4):
            xs = x.sub(s * D, D)
            b = alloc()
            xb = xn[s % 2]
            rs = rms_stats(xs, s, xb)
            S.op('dve', 'tensor_scalar', (xb.ap(), xs.ap(), rs.ap(), None, ALU.mult), r=[xs, rs], w=[xb])
            for kc in range(8):
                o = PSB(b, kc * 128, 128)
                i_ = xb.sub(kc * 128, 128)
                S.op('pe', 'transpose', (o.ap(), i_.ap(), identb.ap()), r=[i_, identb], w=[o])
            pv = PSB(b)
            dst_ap = hT.ap([[512, 8], [1, 128]], off=s * 128)
            src_ap = pv.ap([[128, 8], [1, 128]])
            wl = [hT.sub(kc * 512 + s * 128, 128) for kc in range(8)]
            if s % 2:
                S.op('dve', 'tensor_copy', (dst_ap, src_ap), r=[pv], w=wl)
            else:
                act(dst_ap, src_ap, AF.Copy, r=[pv], w=wl)
            rel(b)

    def proj_fm(wb, coff, WST=512):
        b = alloc()
        o = PS(b)
        for kc in range(8):
            l = wb.sub(kc * WST + coff, 128)
            rr = hT.sub(kc * 512, 512)
            mm(o.ap(), l.ap(), rr.ap(), kc == 0, kc == 7, r=[l, rr], w=[o])
        o.bank = b
        return o

    def proj_v(wb, WST=512):
        b = alloc()
        for s in range(4):
            o = PS(b, s * 128, 128)
            for kc in range(8):
                l = hT.sub(kc * 512 + s * 128, 128)
                rr = wb.sub(kc * WST + 128, 128)
                mm(o.ap(), l.ap(), rr.ap(), kc == 0, kc == 7, r=[l, rr], w=[o])
        o = PS(b)
        o.bank = b
        return o

    def ph_P(hs, full):
        prs = []
        for h in hs:
            pr = {}
            if full:
                wb = load_block(HB0 + h)
                pr['z'] = proj_fm(wb, 0)
                pr['v'] = proj_v(wb)
            else:
                wb = wpre.sub(h * 2048, 2048)
                pr['z'] = proj_fm(wb, 0, 256)
                pr['v'] = proj_v(wb, 256)
            if full:
                pr['q'] = proj_fm(wb, 256)
                pr['g'] = proj_fm(wb, 384)
            prs.append(pr)
        return prs

    def ph_A(hs, SL, prs, full):
        if full:
            for g, h in enumerate(hs):
                B = SL[g]
                act(B['qs'].ap(), prs[g]['q'].ap(), AF.Silu, r=[prs[g]['q']], w=[B['qs']])
                act(B['sg'].ap(), prs[g]['g'].ap(), AF.Silu, r=[prs[g]['g']], w=[B['sg']])
                rel(prs[g]['q'].bank, prs[g]['g'].bank)
        for g, h in enumerate(hs):
            B = SL[g]
            act(B['ta'].ap(), prs[g]['z'].ap(), AF.Sigmoid, r=[prs[g]['z']], w=[B['ta']])
            S.op('dve', 'tensor_copy', (B['vh'].ap(), prs[g]['v'].ap()), r=[prs[g]['v']], w=[B['vh']])
            rel(prs[g]['z'].bank, prs[g]['v'].bank)
        for g, h in enumerate(hs):
            B = SL[g]
            lb, oml = col(lbv, h), col(lbv, 8 + h)
            act(B['tb'].ap(), B['ta'].ap(), AF.Ln, r=[B['ta'], lb, oml], w=[B['tb']], bias=lb.ap(), scale=oml.ap())

    def ph_B(hs, SL):
        scanm = cst.sub(CC_SCAN, T)
        for g, h in enumerate(hs):
            B = SL[g]
            S.op('dve', 'tensor_tensor_scan', (B['tc'].ap(), scanm.ap(), B['tb'].ap(), 0.0, ALU.mult, ALU.add),
                 r=[scanm, B['tb']], w=[B['tc']])
        for g, h in enumerate(hs):
            B = SL[g]
            oml, noml = col(lbv, 8 + h), col(lbv, 16 + h)
            S.op('dve', 'tensor_scalar', (B['ta'].ap(), B['ta'].ap(), noml.ap(), oml.ap(), ALU.mult, ALU.add),
                 r=[B['ta'], noml, oml], w=[B['ta']])

    def ph_C(hs, SL):
        for g, h in enumerate(hs):
            B = SL[g]
            act(B['tb'].ap(), B['tc'].ap(), AF.Exp, r=[B['tc']], w=[B['tb']])
            act(B['td'].ap(), B['tc'].ap(), AF.Exp, r=[B['tc']], w=[B['td']], scale=-1.0)

    def ph_D(hs, SL, full):
        for g, h in enumerate(hs):
            B = SL[g]
            tt('pool', B['td'].ap(), B['ta'].ap(), B['td'].ap(), ALU.mult, r=[B['ta'], B['td']], w=[B['td']])
        for g, h in enumerate(hs):
            B = SL[g]
            tt('dve', B['kdT'].ap([[64, 8], [1, 64]]), B['td'].ap([[64, 8], [1, 64]]),
               B['tb'].ap([[64, 8], [0, 64]], off=63), ALU.mult, r=[B['td'], B['tb']], w=[B['kdT']])
        if full:
            for g, h in enumerate(hs):
                B = SL[g]
                tt('pool', B['qb'].ap(), B['qs'].ap(), B['tb'].ap(), ALU.mult, r=[B['qs'], B['tb']], w=[B['qb']])
                act(B['kb'].ap(), B['td'].ap(), AF.Copy, r=[B['td']], w=[B['kb']])

    def ph_E(hs, SL):
        bs = []
        for g, h in enumerate(hs):
            B = SL[g]
            b = alloc()
            bs.append(b)
            for s in range(4):
                o = PSB(b, s * 128, 128)
                i_ = B['kdT'].sub(s * 128, 128)
                S.op('pe', 'transpose', (o.ap(), i_.ap(), identb.ap()), r=[i_, identb], w=[o])
        for g, h in enumerate(hs):
            B = SL[g]
            pv2 = PSB(bs[g], 0, 512)
            S.op('dve', 'tensor_copy', (B['kdtm'].ap(), pv2.ap()), r=[pv2], w=[B['kdtm']])
            rel(bs[g])

    def ph_F(hs, SL, full, cur):
        pds = []
        for g, h in enumerate(hs):
            B = SL[g]
            banks = alloc(2)
            pd = []
            for c in range(8):
                s_, hf = c // 2, c % 2
                o = PS(banks[hf], s_ * 128, 128)
                l = B['kdtm'].sub(s_ * 128, 128)
                rr = B['vh'].sub(s_ * 128, 128)
                mm(o.ap(), l.ap(p0=hf * 64, np_=64), rr.ap(p0=hf * 64, np_=64), True, True, r=[l, rr], w=[o])
                pd.append(o)
            pds.append((pd, banks))
        for c in range(8):
            for g, h in enumerate(hs):
                B = SL[g]
                pd, banks = pds[g]
                Sin_ = (S32 if c % 2 == 0 else S32b).sub(h * 128, 128)
                Sh = (S32b if c % 2 == 0 else S32).sub(h * 128, 128)
                ebl = col(B['tb'], c * 64 + 63)
                S.op('dve', 'scalar_tensor_tensor', (Sh.ap(), Sin_.ap(), ebl.ap(), pd[c].ap(), ALU.mult, ALU.add),
                     r=[Sin_, ebl, pd[c]], w=[Sh])
                if c < 7:
                    if not full:
                        continue
                    dst = B['Sall'].sub(c * 128, 128)
                else:
                    dst = Sc[1 - cur].sub(h * 128, 128)
                if c % 2 == 0:
                    act(dst.ap(), Sh.ap(), AF.Copy, r=[Sh], w=[dst])
                else:
                    S.op('pool', 'tensor_copy', (dst.ap(), Sh.ap()), r=[Sh], w=[dst])
        for g in range(len(hs)):
            rel(*pds[g][1])

    def ph_G1(hs, SL, cur, st):
        maskv = cst.sub(CC_MASK, 128)
        bss = []
        for g, h in enumerate(hs):
            B = SL[g]
            bs = alloc()
            bss.append(bs)
            for s in range(4):
                o = PS(bs, s * 128, 128)
                l = B['kb'].sub(s * 128, 128)
                rr = B['qb'].sub(s * 128, 128)
                mm(o.ap(), l.ap(), rr.ap(), True, True, r=[l, rr], w=[o])
        for g, h in enumerate(hs):
            psS = PS(bss[g])
            tt('dve', scm[g].ap([[128, 4], [1, 128]]), psS.ap([[128, 4], [1, 128]]), maskv.ap([[0, 4], [1, 128]]),
               ALU.mult, r=[psS, maskv], w=[scm[g]])
            rel(bss[g])

    def ph_G2(hs, SL, cur, st):
        bos = []
        for g, h in enumerate(hs):
            B = SL[g]
            bo = alloc()
            bos.append(bo)
            for s in range(4):
                o = PS(bo, s * 128, 128)
                l = B['vh'].sub(s * 128, 128)
                rr = scm[g].sub(s * 128, 128)
                mm(o.ap(), l.ap(), rr.ap(), True, False, r=[l, rr], w=[o])
                for hf in range(2):
                    c = 2 * s + hf
                    Sin = Sc[cur].sub(h * 128, 128) if c == 0 else B['Sall'].sub((c - 1) * 128, 128)
                    o2 = PS(bo, s * 128 + hf * 64, 64)
                    rr2 = B['qb'].sub(c * 64, 64)
                    mm(o2.ap(), Sin.ap(), rr2.ap(), False, hf == 1, r=[Sin, rr2], w=[o2])
        for g, h in enumerate(hs):
            psO = PS(bos[g])
            act(osq[g].ap(), psO.ap(), AF.Square, r=[psO], w=[osq[g]])
        st['bos'] = bos

    def ph_G3(hs, SL, cur, st):
        bms = []
        for g, h in enumerate(hs):
            bm = alloc()
            bms.append(bm)
            psM = PS(bm)
            mm(psM.ap(), onesb.ap(), osq[g].ap(), True, True, r=[onesb, osq[g]], w=[psM])
        for g, h in enumerate(hs):
            psM = PS(bms[g])
            act(rsv[g].ap(), psM.ap(), AF.Ln, r=[psM, epsc], w=[rsv[g]], bias=epsc.ap(), scale=1.0 / 128)
            rel(bms[g])
            act(rsv[g].ap(), rsv[g].ap(), AF.Exp, r=[rsv[g]], w=[rsv[g]], scale=-0.5)

    def ph_G4(hs, SL, cur, st):
        hg = col(par, PC_HG)
        bos = st['bos']
        for g, h in enumerate(hs):
            B = SL[g]
            psO = PS(bos[g])
            tt('dve', rsv[g].ap(), psO.ap(), rsv[g].ap(), ALU.mult, r=[psO, rsv[g]], w=[rsv[g]])
            rel(bos[g])
            oh = o_a.sub(h * T, T)
            S.op('dve', 'scalar_tensor_tensor', (oh.ap(), rsv[g].ap(), hg.ap(), B['sg'].ap(), ALU.mult, ALU.mult),
                 r=[rsv[g], hg, B['sg']], w=[oh])

    onec = col(stat, 13)
    eblv = sb("eblv", 8, F32)

    def pre_front(hs, SL):
        prs = ph_P(hs, False)
        for g, h in enumerate(hs):
            B = SL[g]
            act(B['ta'].ap(), prs[g]['z'].ap(), AF.Sigmoid, r=[prs[g]['z']], w=[B['ta']])
            S.op('dve', 'tensor_copy', (B['vh'].ap(), prs[g]['v'].ap()), r=[prs[g]['v']], w=[B['vh']])
            rel(prs[g]['z'].bank, prs[g]['v'].bank)
        for g, h in enumerate(hs):
            B = SL[g]
            lb, oml = col(lbv, h), col(lbv, 8 + h)
            act(B['tb'].ap(), B['ta'].ap(), AF.Ln, r=[B['ta'], lb, oml], w=[B['tb']], bias=lb.ap(), scale=oml.ap())
        for g, h in enumerate(hs):
            B = SL[g]
            S.op('dve', 'tensor_tensor_scan', (B['tc'].ap(), onec.ap([[0, T]]), B['tb'].ap(), 0.0, ALU.mult, ALU.add),
                 r=[onec, B['tb']], w=[B['tc']])
            bl = col(B['tc'], T - 1)
            S.op('dve', 'tensor_scalar', (B['tb'].ap(), B['tc'].ap(), -1.0, bl.ap(), ALU.mult, ALU.add),
                 r=[B['tc']], w=[B['tb']])
            oml, noml = col(lbv, 8 + h), col(lbv, 16 + h)
            S.op('dve', 'tensor_scalar', (B['ta'].ap(), B['ta'].ap(), noml.ap(), oml.ap(), ALU.mult, ALU.add),
                 r=[B['ta'], noml, oml], w=[B['ta']])
        for g, h in enumerate(hs):
            B = SL[g]
            act(B['tb'].ap(), B['tb'].ap(), AF.Exp, r=[B['tb']], w=[B['tb']])
            bl = col(B['tc'], T - 1)
            act(col(eblv, h).ap(), bl.ap(), AF.Exp, r=[bl], w=[col(eblv, h)])
        for g, h in enumerate(hs):
            B = SL[g]
            tt('pool', B['kdT'].ap(), B['ta'].ap(), B['tb'].ap(), ALU.mult, r=[B['ta'], B['tb']], w=[B['kdT']])

    def pre_back(hs, SL):
        ph_E(hs, SL)
        bds = []
        for g, h in enumerate(hs):
            B = SL[g]
            b = alloc()
            bds.append(b)
            o = PS(b, 0, 128)
            for s in range(4):
                l = B['kdtm'].sub(s * 128, 128)
                rr = B['vh'].sub(s * 128, 128)
                mm(o.ap(), l.ap(), rr.ap(), s == 0, s == 3, r=[l, rr], w=[o])
        for g, h in enumerate(hs):
            o = PS(bds[g], 0, 128)
            Sh = S32.sub(h * 128, 128)
            S.op('dve', 'scalar_tensor_tensor', (Sh.ap(), Sh.ap(), col(eblv, h).ap(), o.ap(), ALU.mult, ALU.add),
                 r=[Sh, col(eblv, h), o], w=[Sh])
            rel(bds[g])

    def pre_finish(cur):
        for h in range(H):
            Sh = S32.sub(h * 128, 128)
            dst = Sc[cur].sub(h * 128, 128)
            S.op('pool', 'tensor_copy', (dst.ap(), Sh.ap()), r=[Sh], w=[dst])

    def hgrn_stage(full, cur):
        assert full
        groups = [[0, 1], [2, 3], [4, 5], [6, 7]]

        def SLs(p):
            return SLF4[2 * (p % 2):2 * (p % 2) + 2]

        prs = ph_P(groups[0], True)
        ph_A(groups[0], SLs(0), prs, True)
        ph_B(groups[0], SLs(0))
        ph_C(groups[0], SLs(0))
        ph_D(groups[0], SLs(0), True)
        ph_E(groups[0], SLs(0))
        for p, hs in enumerate(groups):
            nxt = groups[p + 1] if p + 1 < len(groups) else None
            st = {}
            if nxt:
                prs = ph_P(nxt, True)
                ph_A(nxt, SLs(p + 1), prs, True)
            ph_F(hs, SLs(p), True, cur)
            ph_G1(hs, SLs(p), cur, st)
            if nxt:
                ph_B(nxt, SLs(p + 1))
            ph_G2(hs, SLs(p), cur, st)
            if nxt:
                ph_C(nxt, SLs(p + 1))
            ph_G3(hs, SLs(p), cur, st)
            if nxt:
                ph_D(nxt, SLs(p + 1), True)
            ph_G4(hs, SLs(p), cur, st)
            if nxt:
                ph_E(nxt, SLs(p + 1))

    def glu_stage():
        for j in range(4):
            wb = load_block(GLU0 + j)
            for mi in range(2):
                m = 2 * j + mi
                pa = proj_fm(wb, mi * 256)
                pb = proj_fm(wb, mi * 256 + 128)
                sg_ = sgate[m % 2]
                act(sg_.ap(), pb.ap(), AF.Sigmoid, r=[pb], w=[sg_])
                um = u.sub(m * UW + CW - 1, T)
                tt('dve', um.ap(), pa.ap(), sg_.ap(), ALU.mult, r=[pa, sg_], w=[um])
                rel(pa.bank, pb.bank)

    def halo_shift():
        S.op('pool', 'tensor_copy', (u.ap([[UW, 8], [1, CW - 1]]), u.ap([[UW, 8], [1, CW - 1]], off=T)),
             r=[u.sub(m * UW + T, CW - 1) for m in range(8)], w=[u.sub(m * UW, CW - 1) for m in range(8)])

    def conv_stage():
        bmean, bex2 = alloc(2)
        pmean, pex2 = PS(bmean), PS(bex2)
        build_diag(0)
        for m in range(8):
            if m + 1 < 8:
                build_diag(m + 1)
            b = alloc()
            py = PS(b)
            for j in range(CW):
                l = diag.sub(((m % 2) * CW + j) * 128, 128)
                rr = u.sub(m * UW + j, T)
                mm(py.ap(), l.ap(), rr.ap(), j == 0, j == CW - 1, r=[l, rr], w=[py])
            cbm = col(par, PC_CB + m)
            ym = y32.sub(m * T, T)
            act(ym.ap(), py.ap(), AF.Identity, r=[py, cbm], w=[ym], bias=cbm.ap())
            yb, yq = ybf[m % 2], ysq[m % 2]
            S.op('dve', 'tensor_copy', (yb.ap(), ym.ap()), r=[ym], w=[yb])
            act(yq.ap(), py.ap(), AF.Square, r=[py, cbm], w=[yq], bias=cbm.ap())
            rel(b)
            mm(pmean.ap(), onesb.ap(), yb.ap(), m == 0, m == 7, r=[onesb, yb], w=[pmean])
            mm(pex2.ap(), onesb.ap(), yq.ap(), m == 0, m == 7, r=[onesb, yq], w=[pex2])
        halo_shift()
        return bmean, bex2, pmean, pex2

    def conv_ln(bmean, bex2, pmean, pex2):
        act(meanv.ap(), pmean.ap(), AF.Copy, r=[pmean], w=[meanv], scale=1.0 / D)
        tt('dve', msqv.ap(), meanv.ap(), meanv.ap(), ALU.mult, r=[meanv], w=[msqv])
        S.op('dve', 'scalar_tensor_tensor', (msqv.ap(), pex2.ap(), 1.0 / D, msqv.ap(), ALU.mult, ALU.subtract),
             r=[pex2, msqv], w=[msqv])
        rel(bmean, bex2)
        act(rs2v.ap(), msqv.ap(), AF.Ln, r=[msqv, epsc], w=[rs2v], bias=epsc.ap())
        act(rs2v.ap(), rs2v.ap(), AF.Exp, r=[rs2v], w=[rs2v], scale=-0.5)
        for m in range(8):
            ym = y32.sub(m * T, T)
            tc_ = tcv[m % 2]
            tt('dve', tc_.ap(), ym.ap(), meanv.ap(), ALU.subtract, r=[ym, meanv], w=[tc_])
            tt('pool', tc_.ap(), tc_.ap(), rs2v.ap(), ALU.mult, r=[tc_, rs2v], w=[tc_])
            lg, lbb = col(par, PC_LG + m), col(par, PC_LB + m)
            yn = ynorm.sub(m * T, T)
            act(yn.ap(), tc_.ap(), AF.Silu, r=[tc_, lg, lbb], w=[yn], bias=lbb.ap(), scale=lg.ap())

    def branch_stage(gate0, w0, src, is_b, js=(0, 1)):
        for j in js:
            wg = load_block(gate0 + j)
            ww = load_block(w0 + j)
            for mi in range(4):
                m = 4 * j + mi
                pg = proj_fm(wg, mi * 128)
                b = alloc()
                po = PS(b)
                for kc in range(8):
                    l = ww.sub(kc * 512 + mi * 128, 128)
                    rr = src.sub(kc * T, T)
                    mm(po.ap(), l.ap(), rr.ap(), kc == 0, kc == 7, r=[l, rr], w=[po])
                sg_ = sgate[m % 2]
                act(sg_.ap(), pg.ap(), AF.Sigmoid, r=[pg], w=[sg_])
                rel(pg.bank)
                ma = mergedA.sub(m * T, T)
                if not is_b:
                    tt('dve', ma.ap(), po.ap(), sg_.ap(), ALU.mult, r=[po, sg_], w=[ma])
                else:
                    tb_ = tbv[m % 2]
                    tt('dve', tb_.ap(), po.ap(), sg_.ap(), ALU.mult, r=[po, sg_], w=[tb_])
                    mg = merged.sub(m * T, T)
                    tt('pool', mg.ap(), tb_.ap(), ma.ap(), ALU.add, r=[tb_, ma], w=[mg])
                rel(b)

    def wout_stage():
        for hf in range(2):
            wb = load_block(WO0 + hf)
            for s in range(4):
                b = alloc()
                po = PS(b)
                for kc in range(8):
                    l = merged.sub(kc * T + s * 128, 128)
                    rr = wb.sub(kc * 512, 512)
                    mm(po.ap(), l.ap(), rr.ap(), kc == 0, kc == 7, r=[l, rr], w=[po])
                xs = x.sub(s * D + hf * 512, 512)
                tt('dve', xs.ap(), po.ap(), xs.ap(), ALU.add, r=[po, xs], w=[xs])
                rel(b)

    def mlp_stage(nxt=None):
        for j in range(8):
            wb = load_block(M10 + j)
            for fi in range(4):
                f = 4 * j + fi
                pz = proj_fm(wb, fi * 128)
                rr_ = r32[f % 2]
                act(rr_.ap(), pz.ap(), AF.Relu, r=[pz], w=[rr_])
                rel(pz.bank)
                zf = z2T.sub(f * T, T)
                tt('pool' if f % 2 == 0 else 'dve', zf.ap(), rr_.ap(), rr_.ap(), ALU.mult, r=[rr_], w=[zf])
        if nxt is not None:
            pf_elem(nxt)
        for hf in range(2):
            if hf == 1 and nxt is not None:
                pf_pe()
            banks = alloc(4)
            for kg in range(4):
                wb = load_block(M20 + hf * 4 + kg)
                for s in range(4):
                    po = PS(banks[s])
                    for kc in range(8):
                        l = z2T.sub((kg * 8 + kc) * T + s * 128, 128)
                        rr = wb.sub(kc * 512, 512)
                        mm(po.ap(), l.ap(), rr.ap(), kg == 0 and kc == 0, kg == 3 and kc == 7, r=[l, rr], w=[po])
            for s in range(4):
                po = PS(banks[s])
                xs = x.sub(s * D + hf * 512, 512)
                tt('dve', xs.ap(), po.ap(), xs.ap(), ALU.add, r=[po, xs], w=[xs])
            rel(*banks)
        if nxt is not None:
            preload(HB0 + 0)
            preload(HB0 + 1)

    xstage = [A(36 * K, D, F32), A(40 * K, D, F32)]
    xn4 = [A((44 + 2 * i) * K, D, BF16) for i in range(4)]

    def pf_elem(t):
        for s in range(4):
            xs = xstage[s % 2]
            dma(xs, dram(x_d[t * 4 + s], "xin", 0, 1), "xs%d" % (s % 2))
            rs = rms_stats(xs, s, xn4[s])
            S.op('dve', 'tensor_scalar', (xn4[s].ap(), xs.ap(), rs.ap(), None, ALU.mult), r=[xs, rs], w=[xn4[s]])

    def pf_pe():
        for s in range(4):
            b = alloc()
            xb = xn4[s]
            for kc in range(8):
                o = PSB(b, kc * 128, 128)
                i_ = xb.sub(kc * 128, 128)
                S.op('pe', 'transpose', (o.ap(), i_.ap(), identb.ap()), r=[i_, identb], w=[o])
            pv = PSB(b)
            dst_ap = hT.ap([[512, 8], [1, 128]], off=s * 128)
            src_ap = pv.ap([[128, 8], [1, 128]])
            wl = [hT.sub(kc * 512 + s * 128, 128) for kc in range(8)]
            if s % 2:
                S.op('dve', 'tensor_copy', (dst_ap, src_ap), r=[pv], w=wl)
            else:
                act(dst_ap, src_ap, AF.Copy, r=[pv], w=wl)
            rel(b)

    def final_stage(t):
        toks = []
        for s in range(4):
            xs = x.sub(s * D, D)
            rs = rms_stats(xs, s, xn[s % 2])
            S.op('dve', 'scalar_tensor_tensor', (xs.ap(), xs.ap(), rs.ap(), gfin.ap(), ALU.mult, ALU.mult),
                 r=[xs, rs, gfin], w=[xs])
            toks.append(dma(dram(y_d[t * 4 + s], "y_d", t * 4 + s, t * 4 + s + 1), xs, "yo%d" % s))
        return toks

    def load_x(src, t):
        for s in range(4):
            dma(x.sub(s * D, D), dram(src[t * 4 + s], "xin", 0, 1), "x%d" % s)

    last_tok = []
    S.op('dve', 'memset', (onec.ap(), 1.0), w=[onec])
    cast_blocks(list(range(HB0, HB0 + 8)))
    load_wpre()
    rest = list(range(8, NBLK))
    per = (len(rest) + NT - 1) // NT
    load_x(xp_d, 0)
    prev = None
    gi = 0
    for t in range(NT):
        blks = rest[t * per:(t + 1) * per]
        for half in range(2):
            hs = [4 * half + i for i in range(4)]
            SLg = SLP[4 * (gi % 2):4 * (gi % 2) + 4]
            gi += 1
            if half == 0:
                rmsnorm_T()
                if t + 1 < NT:
                    load_x(xp_d, t + 1)
                else:
                    load_x(x_d, 0)
            pre_front(hs, SLg)
            if prev is not None:
                pre_back(*prev)
            prev = (hs, SLg)
            cast_blocks(blks[:len(blks) // 2] if half == 0 else blks[len(blks) // 2:])
    pre_back(*prev)
    glu_stage()
    halo_shift()
    cur = 0
    pre_finish(cur)
    for t in range(NT):
        if t == 0:
            rmsnorm_T()
        hgrn_stage(True, cur)
        cur = 1 - cur
        glu_stage()
        cs_ = conv_stage()
        branch_stage(GA0, WA0, o_a, False, js=(0,))
        conv_ln(*cs_)
        branch_stage(GA0, WA0, o_a, False, js=(1,))
        branch_stage(GB0, WB0, ynorm, True)
        wout_stage()
        rmsnorm_T()
        mlp_stage(t + 1 if t + 1 < NT else None)
        last_tok = final_stage(t)
        if t + 1 < NT:
            load_x(x_d, t + 1)
    assert len(free_banks) == 8
    S.emit(nc, es, list(last_tok))
    es.close()
    S.stats = {e: len(S.ops[e]) for e in ENGS}
    print('ops per engine:', S.stats, flush=True)
    return nc


_CACHE = {}


def _prep_weights(inp):
    w_in = np.asarray(inp["w_in"][0], dtype=np.float32)
    blocks = []

    def blk(w, cols):
        kt = w.shape[0] // 128
        b = w[:, cols].reshape(kt, 128, len(cols)).transpose(1, 0, 2)
        return np.ascontiguousarray(b).reshape(128, kt * len(cols))

    ar = np.arange
    for h in range(8):
        cols = np.concatenate([1024 + h * 128 + ar(128), 2048 + h * 128 + ar(128), h * 128 + ar(128), 3072 + h * 128 + ar(128)])
        blocks.append(blk(w_in, cols))
    for j in range(4):
        m0, m1 = 2 * j, 2 * j + 1
        cols = np.concatenate([4096 + m0 * 128 + ar(128), 5120 + m0 * 128 + ar(128),
                               4096 + m1 * 128 + ar(128), 5120 + m1 * 128 + ar(128)])
        blocks.append(blk(w_in, cols))
    for j in range(2):
        blocks.append(blk(w_in, 6144 + j * 512 + ar(512)))
    for j in range(2):
        blocks.append(blk(w_in, 7168 + j * 512 + ar(512)))
    for name in ("w_a_out", "w_b_out", "w_out"):
        w = np.asarray(inp[name][0], dtype=np.float32)
        for j in range(2):
            blocks.append(blk(w, j * 512 + ar(512)))
    w1 = np.asarray(inp["w_mlp_in"][0], dtype=np.float32)
    for j in range(8):
        blocks.append(blk(w1, j * 512 + ar(512)))
    w2 = np.asarray(inp["w_mlp_out"][0], dtype=np.float32)
    for hf in range(2):
        for kg in range(4):
            blocks.append(blk(w2[kg * 1024:(kg + 1) * 1024], hf * 512 + ar(512)))
    wall = np.stack(blocks).astype(np.float32)
    assert wall.shape == (NBLK, 128, 4096)

    def fm(v):
        return np.asarray(v, dtype=np.float32).reshape(8, 128).T

    par = np.zeros((128, NPAR), np.float32)
    par[:, PC_GM:PC_GM + 8] = fm(inp["norm_mix_g"][0])
    par[:, PC_GML:PC_GML + 8] = fm(inp["norm_mlp_g"][0])
    par[:, PC_P0:PC_P0 + 8] = fm(inp["lb_param"][0])
    par[:, PC_P1:PC_P1 + 8] = fm(inp["lb_param"][1])
    par[:, PC_HG] = np.asarray(inp["hgrn_norm_g"][0], dtype=np.float32)
    par[:, PC_CB:PC_CB + 8] = fm(inp["conv_b"][0])
    par[:, PC_LG:PC_LG + 8] = fm(inp["conv_ln_g"][0])
    par[:, PC_LB:PC_LB + 8] = fm(inp["conv_ln_b"][0])
    cw = np.asarray(inp["conv_w"][0], dtype=np.float32)
    par[:, PC_CW:] = cw.reshape(CW, 8, 128).transpose(2, 1, 0).reshape(128, 8 * CW)
    gfin = np.ascontiguousarray(np.broadcast_to(np.asarray(inp["norm_final_g"], dtype=np.float32)[None, :], (128, D)))
    cst = np.zeros((128, NCST), np.float32)
    cst[:, CC_ID:CC_ID + 128] = np.eye(128, dtype=np.float32)
    sp, tt = np.meshgrid(ar(128), ar(128), indexing="ij")
    cst[:, CC_MASK:CC_MASK + 128] = ((sp // 64 == tt // 64) & (sp <= tt)).astype(np.float32)
    cst[:, CC_SCAN:] = (ar(T) % 64 != 0).astype(np.float32)[None, :]
    return wall, par, gfin, cst


def kernel(**inp):
    xfull = np.asarray(inp["x"], dtype=np.float32)
    B, Sq, _ = xfull.shape
    half = Sq // 2
    NT = half // T
    assert half % T == 0 and B * 2 == 8
    if NT not in _CACHE:
        _CACHE[NT] = build(NT)
    nc = _CACHE[NT]
    wall, par, gfin, cst = _prep_weights(inp)
    in_maps = []
    zeros = np.zeros((NT * 4, 128, D), np.float32)
    for c in range(8):
        b, hf = c // 2, c % 2
        xc = np.ascontiguousarray(xfull[b, hf * half:(hf + 1) * half]).reshape(NT * 4, 128, D)
        xp = zeros if hf == 0 else np.ascontiguousarray(xfull[b, 0:half]).reshape(NT * 4, 128, D)
        in_maps.append({"x": xc, "xprev": xp, "wall": wall, "par": par, "cst": cst, "gfin": gfin})
    res = run_bass_kernel_spmd(nc, in_maps, core_ids=list(range(8)))
    out = np.empty((B, Sq, D), np.float32)
    for c in range(8):
        b, hf = c // 2, c % 2
        out[b, hf * half:(hf + 1) * half] = np.asarray(res.results[c]["y"]).reshape(half, D)
    return out
```

```python
import contextlib
import numpy as np
import concourse.bass as bass
import concourse.mybir as mybir
from concourse.bass_utils import run_bass_kernel_spmd

F32 = mybir.dt.float32
BF16 = mybir.dt.bfloat16
AF = mybir.ActivationFunctionType
ALU = mybir.AluOpType

P = 128
D = 1024
H = 8
T = 512
EPS = 1e-6
CW = 31
UW = T + CW - 1
NBLK = 38
HB0, GLU0, GA0, GB0, WA0, WB0, WO0, M10, M20 = 0, 8, 12, 14, 16, 18, 20, 22, 30
PC_GM, PC_GML, PC_P0, PC_P1, PC_HG, PC_CB, PC_LG, PC_LB, PC_CW = 0, 8, 16, 24, 32, 33, 41, 49, 57
NPAR = 57 + 8 * CW
CC_ID, CC_MASK, CC_SCAN = 0, 128, 256
NCST = 256 + T

ENGS = ['pe', 'act', 'dve', 'pool', 'sp']


class V:
    def __init__(s, t, W, name, off, n, esz, ap=None, whole=None):
        s.t, s.W, s.name, s.off, s.n, s.esz = t, W, name, off, n, esz
        s._ap = ap
        s.whole = whole

    def res(s):
        if s.whole is not None:
            return (s.name, 0, s.whole)
        return (s.name, s.off * s.esz, (s.off + s.n) * s.esz)

    def sub(s, a, n):
        assert a + n <= s.n
        return V(s.t, s.W, s.name, s.off + a, n, s.esz)

    def ap(s, dims=None, off=0, p0=0, np_=P):
        if s._ap is not None:
            return s._ap
        if dims is None:
            dims = [[1, s.n - off]]
        return bass.AP(s.t, p0 * s.W + s.off + off, [[s.W, np_]] + [list(d) for d in dims])


class Sched:
    def __init__(s):
        s.ops = {e: [] for e in ENGS}
        s.res = {}
        s.seen = {e: {} for e in ENGS}
        s.dma_cnt = {}
        s.nbank = 0
        s.held = set()

    def bank(s):
        while True:
            b = s.nbank % 8
            s.nbank += 1
            if b not in s.held:
                return b

    def op(s, eng, iname, args, kw=None, r=(), w=(), dma=None):
        idx = len(s.ops[eng])
        deps = {}

        def add(d):
            if deps.get(d[0], -1) < d[1]:
                deps[d[0]] = d[1]

        rres = [v.res() for v in r]
        wres = [v.res() for v in w]
        for (name, lo, hi) in rres:
            R = s.res.setdefault(name, {'w': [], 'r': []})
            for (l, h, d) in R['w']:
                if l < hi and lo < h:
                    add(d)
        for (name, lo, hi) in wres:
            R = s.res.setdefault(name, {'w': [], 'r': []})
            for (l, h, d) in R['w']:
                if l < hi and lo < h:
                    add(d)
            for (l, h, d) in R['r']:
                if l < hi and lo < h:
                    add(d)
        if dma is not None:
            s.dma_cnt[dma] = s.dma_cnt.get(dma, 0) + 1
            tok = (('dma', dma), s.dma_cnt[dma] * 16)
        else:
            tok = (eng, idx)
        for (name, lo, hi) in rres:
            R = s.res[name]
            R['r'] = [x for x in R['r'] if not (x[2][0] == tok[0] and lo <= x[0] and x[1] <= hi)]
            R['r'].append((lo, hi, tok))
        for (name, lo, hi) in wres:
            R = s.res[name]
            R['w'] = [x for x in R['w'] if not (lo <= x[0] and x[1] <= hi)]
            R['r'] = [x for x in R['r'] if not (lo <= x[0] and x[1] <= hi)]
            R['w'].append((lo, hi, tok))
        waits = []
        for src, i in deps.items():
            if src == 'pe' and eng == 'pe':
                continue
            if s.seen[eng].get(src, -1) >= i:
                continue
            s.seen[eng][src] = i
            waits.append((src, i))
        s.ops[eng].append({'name': iname, 'args': args, 'kw': kw or {}, 'waits': waits, 'dma': dma})
        return tok

    def emit(s, nc, es, final_waits):
        sig = {e: set() for e in ENGS}
        for e in ENGS:
            for o in s.ops[e]:
                for (src, i) in o['waits']:
                    if isinstance(src, str):
                        sig[src].add(i)
        cnt = {}
        for e in ENGS:
            c = 0
            m = {}
            for i in range(len(s.ops[e])):
                if i in sig[e]:
                    c += 1
                    m[i] = c
            cnt[e] = m
        sems = {}
        for e in ENGS:
            if cnt[e]:
                sems[e] = es.enter_context(nc.semaphore('s_' + e))
        for k in s.dma_cnt:
            sems[('dma', k)] = es.enter_context(nc.semaphore('d_' + k))
        block = es.enter_context(nc.Block())

        def run(ename, eobj):
            for i, o in enumerate(s.ops[ename]):
                for (src, v) in o['waits']:
                    if isinstance(src, str):
                        eobj.wait_ge(sems[src], cnt[src][v])
                    else:
                        eobj.wait_ge(sems[src], v)
                ins = getattr(eobj, o['name'])(*o['args'], **o['kw'])
                if o['dma'] is not None:
                    ins.then_inc(sems[('dma', o['dma'])], 16)
                elif i in cnt[ename]:
                    ins.then_inc(sems[ename], 1)
            if ename == 'sp':
                for (src, v) in final_waits:
                    eobj.wait_ge(sems[src], v)

        @block.tensor
        def _(e):
            run('pe', e)

        @block.scalar
        def _(e):
            run('act', e)

        @block.vector
        def _(e):
            run('dve', e)

        @block.gpsimd
        def _(e):
            run('pool', e)

        @block.sync
        def _(e):
            run('sp', e)


def build(NT):
    import os as _os
    nc = bass.Bass("TRN2", target_bir_lowering=False)
    x_d = nc.dram_tensor("x", [NT * 4, P, D], F32, kind="ExternalInput").ap()
    xp_d = nc.dram_tensor("xprev", [NT * 4, P, D], F32, kind="ExternalInput").ap()
    w_d = nc.dram_tensor("wall", [NBLK, P, 4096], F32, kind="ExternalInput").ap()
    par_d = nc.dram_tensor("par", [P, NPAR], F32, kind="ExternalInput").ap()
    cst_d = nc.dram_tensor("cst", [P, NCST], F32, kind="ExternalInput").ap()
    gfin_d = nc.dram_tensor("gfin", [P, D], F32, kind="ExternalInput").ap()
    gmix_d = nc.dram_tensor("gmix", [P, D], F32, kind="ExternalInput").ap()
    gmlp_d = nc.dram_tensor("gmlp", [P, D], F32, kind="ExternalInput").ap()
    y_d = nc.dram_tensor("y", [NT * 4, P, D], F32, kind="ExternalOutput").ap()
    wbf_d = nc.dram_tensor("wbf", [NBLK, P, 4096], BF16).ap()

    es = contextlib.ExitStack()
    S = Sched()

    def sb(name, n, dt):
        t = es.enter_context(nc.sbuf_tensor("sb_" + name, [P, n], dt))
        return V(t, n, name, 0, n, 4 if dt == F32 else 2)

    par = sb("par", NPAR, F32)
    lbv = sb("lbv", 32, F32)
    cst = sb("cst", NCST, F32)
    identb = sb("identb", 128, BF16)
    onesb = sb("onesb", 128, BF16)
    gfin = sb("gfin", D, F32)
    gmix = sb("gmix", D, F32)
    gmlp = sb("gmlp", D, F32)
    diag = sb("diag", 2 * CW * 128, BF16)
    S32 = sb("S32", H * 128, F32)
    S32b = sb("S32b", H * 128, F32)
    Sc = [sb("Sc0", H * 128, BF16), sb("Sc1", H * 128, BF16)]
    u = sb("u", 8 * UW, BF16)
    x = sb("x", 4 * D, F32)
    hT = sb("hT", 8 * T, BF16)
    wbuf = [sb("wb%d" % i, 4096, BF16) for i in range(3)]
    stat = sb("stat", 16, F32)
    ARENA = 106 * 1024
    ar_t = es.enter_context(nc.sbuf_tensor("arena", [P, ARENA // 2], BF16))
    ar_f = ar_t.bitcast(F32)

    def A(byte_off, n, dt):
        byte_off = int(byte_off)
        if dt == F32:
            assert byte_off % 4 == 0 and byte_off + 4 * n <= ARENA
            return V(ar_f, ARENA // 4, "arena", byte_off // 4, n, 4)
        assert byte_off % 2 == 0 and byte_off + 2 * n <= ARENA
        return V(ar_t, ARENA // 2, "arena", byte_off // 2, n, 2)

    K = 1024
    o_a = A(0, 8 * T, BF16)
    mergedA = A(8 * K, 8 * T, F32)

    def slot_full(g):
        b = 24 * K + g * 19 * K
        return dict(ta=A(b, T, F32), tb=A(b + 2 * K, T, F32), tc=A(b + 4 * K, T, F32), td=A(b + 6 * K, T, F32),
                    qs=A(b + 8 * K, T, F32), sg=A(b + 10 * K, T, F32), qb=A(b + 12 * K, T, BF16),
                    kb=A(b + 13 * K, T, BF16), kdT=A(b + 14 * K, T, BF16), kdtm=A(b + 15 * K, T, BF16),
                    vh=A(b + 16 * K, T, BF16), Sall=A(b + 17 * K, 7 * 128, BF16))

    def slot_pre(g):
        b = g * 8 * K
        return dict(ta=A(b, T, F32), tb=A(b + 2 * K, T, F32), tc=A(b + 4 * K, T, F32),
                    kdT=A(b + 6 * K, T, BF16), kdtm=A(b + 6 * K, T, BF16), vh=A(b + 7 * K, T, BF16))

    SLF4 = [slot_full(g) for g in range(4)]
    wpre = A(64 * K, 8 * 2048, BF16)
    SLP = [slot_pre(g) for g in range(8)]
    osq = [A(8 * K, T, BF16), A(9 * K, T, BF16)]
    scm = [A(10 * K, T, BF16), A(11 * K, T, BF16)]
    rsv = [A(100 * K, T, F32), A(102 * K, T, F32)]
    xnF = [A(12 * K, D, BF16), A(14 * K, D, BF16)]
    xnP = [A(96 * K, D, BF16), A(98 * K, D, BF16)]
    ynorm = A(70 * K, 8 * T, BF16)
    y32 = A(24 * K, 8 * T, F32)
    ybf = [A(40 * K, T, BF16), A(41 * K, T, BF16)]
    ysq = [A(42 * K, T, BF16), A(43 * K, T, BF16)]
    meanv = A(44 * K, T, F32)
    msqv = A(46 * K, T, F32)
    rs2v = A(48 * K, T, F32)
    tcv = [A(50 * K, T, F32), A(52 * K, T, F32)]
    sgate = [A(54 * K, T, F32), A(56 * K, T, F32)]
    tbv = [A(58 * K, T, F32), A(60 * K, T, F32)]
    merged = A(62 * K, 8 * T, BF16)
    z2T = A(0, 32 * T, BF16)
    r32 = [A(32 * K, T, F32), A(34 * K, T, F32)]
    ps_t = [es.enter_context(nc.psum_tensor("ps%d" % i, [P, 512], F32)) for i in range(8)]
    ps_b = [t.bitcast(BF16) for t in ps_t]
    free_banks = list(range(8))

    def alloc(n=1):
        assert len(free_banks) >= n, "out of PSUM banks"
        out = [free_banks.pop(0) for _ in range(n)]
        return out if n > 1 else out[0]

    def rel(*bs):
        for b in bs:
            assert b not in free_banks
            free_banks.append(b)

    def PS(b, off=0, n=512):
        return V(ps_t[b], 512, "ps%d" % b, off, n, 4, whole=2048)

    def PSB(b, off=0, n=1024):
        return V(ps_b[b], 1024, "ps%d" % b, off, n, 2, whole=2048)

    def dram(apx, name, lo, hi):
        return V(None, 0, name, lo, hi - lo, 1, ap=apx)

    def mm(out, lhsT, rhs, start, stop, r, w):
        S.op('pe', 'matmul', (out, lhsT, rhs), dict(start=start, stop=stop), r=r, w=w)

    def act(out_ap, in_ap, func, r, w, bias=None, scale=None, accum=None):
        kw = {}
        if bias is not None:
            kw['bias'] = bias
        if scale is not None:
            kw['scale'] = scale
        if accum is not None:
            kw['accum_out'] = accum
        S.op('act', 'activation', (out_ap, in_ap, func), kw, r=r, w=w)

    def tt(eng, out, in0, in1, op, r, w):
        S.op(eng, 'tensor_tensor', (out, in0, in1, op), r=r, w=w)

    def dma(out_v, in_v, key, eng='sp'):
        return S.op(eng, 'dma_start', (), dict(out=out_v.ap(), in_=in_v.ap()), r=[in_v], w=[out_v], dma=key)

    wslot = [0]

    preloaded = {}

    def preload(b):
        preloaded[b] = load_block(b)

    def load_block(b):
        if b in preloaded:
            return preloaded.pop(b)
        k = wslot[0] % 3
        wslot[0] += 1
        dma(wbuf[k], dram(wbf_d[b], "wbf", b, b + 1), "w%d" % k)
        return wbuf[k]

    def col(v, c):
        return v.sub(c, 1)

    epsc = col(stat, 12)

    dma(par, dram(par_d, "par_d", 0, 1), "c0")
    dma(cst, dram(cst_d, "cst_d", 0, 1), "c1")
    dma(gfin, dram(gfin_d, "gfin_d", 0, 1), "c2")
    dma(gmix, dram(gmix_d, "gmix_d", 0, 1), "c3")
    dma(gmlp, dram(gmlp_d, "gmlp_d", 0, 1), "c4")
    S.op('dve', 'memset', (epsc.ap(), EPS), w=[epsc])
    dv = lbv.sub(24, 8)
    tt('dve', dv.ap(), par.sub(PC_P0, 8).ap(), par.sub(PC_P1, 8).ap(), ALU.subtract, r=[par], w=[dv])
    act(lbv.sub(0, 8).ap(), dv.ap(), AF.Sigmoid, r=[dv], w=[lbv.sub(0, 8)])
    act(lbv.sub(8, 8).ap(), dv.ap(), AF.Sigmoid, r=[dv], w=[lbv.sub(8, 8)], scale=-1.0)
    S.op('dve', 'tensor_scalar', (lbv.sub(16, 8).ap(), lbv.sub(8, 8).ap(), -1.0, None, ALU.mult),
         r=[lbv.sub(8, 8)], w=[lbv.sub(16, 8)])
    S.op('dve', 'tensor_copy', (identb.ap(), cst.sub(CC_ID, 128).ap()), r=[cst], w=[identb])
    S.op('dve', 'memset', (onesb.ap(), 1.0), w=[onesb])
    S.op('dve', 'memset', (S32.ap(), 0.0), w=[S32])
    S.op('pool', 'memset', (u.ap(), 0.0), w=[u])
    identf = cst.sub(CC_ID, 128)

    def build_diag(m):
        for j in range(CW):
            dst = diag.sub(((m % 2) * CW + j) * 128, 128)
            sc = col(par, PC_CW + m * CW + j)
            e3 = j % 3
            if e3 == 0:
                S.op('pool', 'tensor_scalar', (dst.ap(), identf.ap(), sc.ap(), 0.0, ALU.mult, ALU.add), r=[identf, sc], w=[dst])
            elif e3 == 1:
                act(dst.ap(), identf.ap(), AF.Copy, r=[identf, sc], w=[dst], scale=sc.ap())
            else:
                S.op('dve', 'tensor_scalar', (dst.ap(), identf.ap(), sc.ap(), 0.0, ALU.mult, ALU.add), r=[identf, sc], w=[dst])

    def cast_group(blocks, key):
        for b in blocks:
            srcv = dram(w_d[b], "w_d", b, b + 1)
            dstv = dram(wbf_d[b], "wbf", b, b + 1)
            S.op('pool', 'dma_start', (), dict(out=dstv.ap(), in_=srcv.ap()), r=[srcv], w=[dstv], dma=key)
        final = (('dma', key), S.dma_cnt[key] * 16)
        R = S.res["wbf"]
        R['w'] = [(lo, hi, final if tk[0] == ('dma', key) else tk) for (lo, hi, tk) in R['w']]

    def load_wpre():
        for h in range(8):
            srcap = wbf_d[HB0 + h].rearrange("p (k c) -> p k c", c=512)[:, :, 0:256]
            dstv = wpre.sub(h * 2048, 2048)
            S.op('sp', 'dma_start', (), dict(out=dstv.ap([[256, 8], [1, 256]]), in_=srcap),
                 r=[dram(srcap, "wbf", HB0 + h, HB0 + h + 1)], w=[dstv], dma="wp%d" % h)

    def rms_stats(xs, s, junkv):
        ss, ln_, rs = col(stat, s), col(stat, 4 + s), col(stat, 8 + s)
        act(junkv.ap(), xs.ap(), AF.Square, r=[xs], w=[junkv, ss], accum=ss.ap())
        act(ln_.ap(), ss.ap(), AF.Ln, r=[ss, epsc], w=[ln_], bias=epsc.ap(), scale=1.0 / D)
        act(rs.ap(), ln_.ap(), AF.Exp, r=[ln_], w=[rs], scale=-0.5)
        return rs

    def rmsnorm_T(gv, xn):
        for s in range(4):
            xs = x.sub(s * D, D)
            b = alloc()
            xb = xn[s % 2]
            rs = rms_stats(xs, s, xb)
            S.op('dve', 'scalar_tensor_tensor', (xb.ap(), xs.ap(), rs.ap(), gv.ap(), ALU.mult, ALU.mult), r=[xs, rs, gv], w=[xb])
            for kc in range(8):
                o = PSB(b, kc * 128, 128)
                i_ = xb.sub(kc * 128, 128)
                S.op('pe', 'transpose', (o.ap(), i_.ap(), identb.ap()), r=[i_, identb], w=[o])
            pv = PSB(b)
            dst_ap = hT.ap([[512, 8], [1, 128]], off=s * 128)
            src_ap = pv.ap([[128, 8], [1, 128]])
            wl = [hT.sub(kc * 512 + s * 128, 128) for kc in range(8)]
            if s % 2:
                S.op('dve', 'tensor_copy', (dst_ap, src_ap), r=[pv], w=wl)
            else:
                act(dst_ap, src_ap, AF.Copy, r=[pv], w=wl)
            rel(b)

    def proj_fm(wb, coff, WST=512):
        b = alloc()
        o = PS(b)
        for kc in range(8):
            l = wb.sub(kc * WST + coff, 128)
            rr = hT.sub(kc * 512, 512)
            mm(o.ap(), l.ap(), rr.ap(), kc == 0, kc == 7, r=[l, rr], w=[o])
        o.bank = b
        return o

    def proj_v(wb, WST=512):
        b = alloc()
        for s in range(4):
            o = PS(b, s * 128, 128)
            for kc in range(8):
                l = hT.sub(kc * 512 + s * 128, 128)
                rr = wb.sub(kc * WST + 128, 128)
                mm(o.ap(), l.ap(), rr.ap(), kc == 0, kc == 7, r=[l, rr], w=[o])
        o = PS(b)
        o.bank = b
        return o

    def ph_P(hs, full):
        prs = []
        for h in hs:
            pr = {}
            if full:
                wb = load_block(HB0 + h)
                pr['z'] = proj_fm(wb, 0)
                pr['v'] = proj_v(wb)
            else:
                wb = wpre.sub(h * 2048, 2048)
                pr['z'] = proj_fm(wb, 0, 256)
                pr['v'] = proj_v(wb, 256)
            if full:
                pr['q'] = proj_fm(wb, 256)
                pr['g'] = proj_fm(wb, 384)
            prs.append(pr)
        return prs

    def ph_A(hs, SL, prs, full):
        if full:
            for g, h in enumerate(hs):
                B = SL[g]
                act(B['qs'].ap(), prs[g]['q'].ap(), AF.Silu, r=[prs[g]['q']], w=[B['qs']])
                act(B['sg'].ap(), prs[g]['g'].ap(), AF.Silu, r=[prs[g]['g']], w=[B['sg']])
                rel(prs[g]['q'].bank, prs[g]['g'].bank)
        for g, h in enumerate(hs):
            B = SL[g]
            act(B['ta'].ap(), prs[g]['z'].ap(), AF.Sigmoid, r=[prs[g]['z']], w=[B['ta']])
            S.op('dve', 'tensor_copy', (B['vh'].ap(), prs[g]['v'].ap()), r=[prs[g]['v']], w=[B['vh']])
            rel(prs[g]['z'].bank, prs[g]['v'].bank)
        for g, h in enumerate(hs):
            B = SL[g]
            lb, oml = col(lbv, h), col(lbv, 8 + h)
            act(B['tb'].ap(), B['ta'].ap(), AF.Ln, r=[B['ta'], lb, oml], w=[B['tb']], bias=lb.ap(), scale=oml.ap())

    def ph_B(hs, SL):
        scanm = cst.sub(CC_SCAN, T)
        for g, h in enumerate(hs):
            B = SL[g]
            S.op('dve', 'tensor_tensor_scan', (B['tc'].ap(), scanm.ap(), B['tb'].ap(), 0.0, ALU.mult, ALU.add),
                 r=[scanm, B['tb']], w=[B['tc']])
        for g, h in enumerate(hs):
            B = SL[g]
            oml, noml = col(lbv, 8 + h), col(lbv, 16 + h)
            S.op('dve', 'tensor_scalar', (B['ta'].ap(), B['ta'].ap(), noml.ap(), oml.ap(), ALU.mult, ALU.add),
                 r=[B['ta'], noml, oml], w=[B['ta']])

    def ph_C(hs, SL):
        for g, h in enumerate(hs):
            B = SL[g]
            act(B['tb'].ap(), B['tc'].ap(), AF.Exp, r=[B['tc']], w=[B['tb']])
            act(B['td'].ap(), B['tc'].ap(), AF.Exp, r=[B['tc']], w=[B['td']], scale=-1.0)

    def ph_D(hs, SL, full):
        for g, h in enumerate(hs):
            B = SL[g]
            tt('pool', B['td'].ap(), B['ta'].ap(), B['td'].ap(), ALU.mult, r=[B['ta'], B['td']], w=[B['td']])
        for g, h in enumerate(hs):
            B = SL[g]
            tt('dve', B['kdT'].ap([[64, 8], [1, 64]]), B['td'].ap([[64, 8], [1, 64]]),
               B['tb'].ap([[64, 8], [0, 64]], off=63), ALU.mult, r=[B['td'], B['tb']], w=[B['kdT']])
        if full:
            for g, h in enumerate(hs):
                B = SL[g]
                tt('pool', B['qb'].ap(), B['qs'].ap(), B['tb'].ap(), ALU.mult, r=[B['qs'], B['tb']], w=[B['qb']])
                act(B['kb'].ap(), B['td'].ap(), AF.Copy, r=[B['td']], w=[B['kb']])

    def ph_E(hs, SL):
        bs = []
        for g, h in enumerate(hs):
            B = SL[g]
            b = alloc()
            bs.append(b)
            for s in range(4):
                o = PSB(b, s * 128, 128)
                i_ = B['kdT'].sub(s * 128, 128)
                S.op('pe', 'transpose', (o.ap(), i_.ap(), identb.ap()), r=[i_, identb], w=[o])
        for g, h in enumerate(hs):
            B = SL[g]
            pv2 = PSB(bs[g], 0, 512)
            S.op('dve', 'tensor_copy', (B['kdtm'].ap(), pv2.ap()), r=[pv2], w=[B['kdtm']])
            rel(bs[g])

    def ph_F(hs, SL, full, cur):
        pds = []
        for g, h in enumerate(hs):
            B = SL[g]
            banks = alloc(2)
            pd = []
            for c in range(8):
                s_, hf = c // 2, c % 2
                o = PS(banks[hf], s_ * 128, 128)
                l = B['kdtm'].sub(s_ * 128, 128)
                rr = B['vh'].sub(s_ * 128, 128)
                mm(o.ap(), l.ap(p0=hf * 64, np_=64), rr.ap(p0=hf * 64, np_=64), True, True, r=[l, rr], w=[o])
                pd.append(o)
            pds.append((pd, banks))
        for c in range(8):
            for g, h in enumerate(hs):
                B = SL[g]
                pd, banks = pds[g]
                Sin_ = (S32 if c % 2 == 0 else S32b).sub(h * 128, 128)
                Sh = (S32b if c % 2 == 0 else S32).sub(h * 128, 128)
                ebl = col(B['tb'], c * 64 + 63)
                S.op('dve', 'scalar_tensor_tensor', (Sh.ap(), Sin_.ap(), ebl.ap(), pd[c].ap(), ALU.mult, ALU.add),
                     r=[Sin_, ebl, pd[c]], w=[Sh])
                if c < 7:
                    if not full:
                        continue
                    dst = B['Sall'].sub(c * 128, 128)
                else:
                    dst = Sc[1 - cur].sub(h * 128, 128)
                if c % 2 == 0:
                    act(dst.ap(), Sh.ap(), AF.Copy, r=[Sh], w=[dst])
                else:
                    S.op('pool', 'tensor_copy', (dst.ap(), Sh.ap()), r=[Sh], w=[dst])
        for g in range(len(hs)):
            rel(*pds[g][1])

    def ph_G1(hs, SL, cur, st):
        maskv = cst.sub(CC_MASK, 128)
        bss = []
        for g, h in enumerate(hs):
            B = SL[g]
            bs = alloc()
            bss.append(bs)
            for s in range(4):
                o = PS(bs, s * 128, 128)
                l = B['kb'].sub(s * 128, 128)
                rr = B['qb'].sub(s * 128, 128)
                mm(o.ap(), l.ap(), rr.ap(), True, True, r=[l, rr], w=[o])
        for g, h in enumerate(hs):
            psS = PS(bss[g])
            tt('dve', scm[g].ap([[128, 4], [1, 128]]), psS.ap([[128, 4], [1, 128]]), maskv.ap([[0, 4], [1, 128]]),
               ALU.mult, r=[psS, maskv], w=[scm[g]])
            rel(bss[g])

    def ph_G2(hs, SL, cur, st):
        bos = []
        for g, h in enumerate(hs):
            B = SL[g]
            bo = alloc()
            bos.append(bo)
            for s in range(4):
                o = PS(bo, s * 128, 128)
                l = B['vh'].sub(s * 128, 128)
                rr = scm[g].sub(s * 128, 128)
                mm(o.ap(), l.ap(), rr.ap(), True, False, r=[l, rr], w=[o])
                for hf in range(2):
                    c = 2 * s + hf
                    Sin = Sc[cur].sub(h * 128, 128) if c == 0 else B['Sall'].sub((c - 1) * 128, 128)
                    o2 = PS(bo, s * 128 + hf * 64, 64)
                    rr2 = B['qb'].sub(c * 64, 64)
                    mm(o2.ap(), Sin.ap(), rr2.ap(), False, hf == 1, r=[Sin, rr2], w=[o2])
        for g, h in enumerate(hs):
            psO = PS(bos[g])
            act(osq[g].ap(), psO.ap(), AF.Square, r=[psO], w=[osq[g]])
        st['bos'] = bos

    def ph_G3(hs, SL, cur, st):
        bms = []
        for g, h in enumerate(hs):
            bm = alloc()
            bms.append(bm)
            psM = PS(bm)
            mm(psM.ap(), onesb.ap(), osq[g].ap(), True, True, r=[onesb, osq[g]], w=[psM])
        for g, h in enumerate(hs):
            psM = PS(bms[g])
            act(rsv[g].ap(), psM.ap(), AF.Ln, r=[psM, epsc], w=[rsv[g]], bias=epsc.ap(), scale=1.0 / 128)
            rel(bms[g])
            act(rsv[g].ap(), rsv[g].ap(), AF.Exp, r=[rsv[g]], w=[rsv[g]], scale=-0.5)

    def ph_G4(hs, SL, cur, st):
        hg = col(par, PC_HG)
        bos = st['bos']
        for g, h in enumerate(hs):
            B = SL[g]
            psO = PS(bos[g])
            tt('dve', rsv[g].ap(), psO.ap(), rsv[g].ap(), ALU.mult, r=[psO, rsv[g]], w=[rsv[g]])
            rel(bos[g])
            oh = o_a.sub(h * T, T)
            S.op('dve', 'scalar_tensor_tensor', (oh.ap(), rsv[g].ap(), hg.ap(), B['sg'].ap(), ALU.mult, ALU.mult),
                 r=[rsv[g], hg, B['sg']], w=[oh])

    onec = col(stat, 13)
    eblv = sb("eblv", 8, F32)

    def pre_front(hs, SL):
        prs = ph_P(hs, False)
        for g, h in enumerate(hs):
            B = SL[g]
            act(B['ta'].ap(), prs[g]['z'].ap(), AF.Sigmoid, r=[prs[g]['z']], w=[B['ta']])
            S.op('dve', 'tensor_copy', (B['vh'].ap(), prs[g]['v'].ap()), r=[prs[g]['v']], w=[B['vh']])
            rel(prs[g]['z'].bank, prs[g]['v'].bank)
        for g, h in enumerate(hs):
            B = SL[g]
            lb, oml = col(lbv, h), col(lbv, 8 + h)
            act(B['tb'].ap(), B['ta'].ap(), AF.Ln, r=[B['ta'], lb, oml], w=[B['tb']], bias=lb.ap(), scale=oml.ap())
        for g, h in enumerate(hs):
            B = SL[g]
            S.op('dve', 'tensor_tensor_scan', (B['tc'].ap(), onec.ap([[0, T]]), B['tb'].ap(), 0.0, ALU.mult, ALU.add),
                 r=[onec, B['tb']], w=[B['tc']])
            bl = col(B['tc'], T - 1)
            S.op('dve', 'tensor_scalar', (B['tb'].ap(), B['tc'].ap(), -1.0, bl.ap(), ALU.mult, ALU.add),
                 r=[B['tc']], w=[B['tb']])
            oml, noml = col(lbv, 8 + h), col(lbv, 16 + h)
            S.op('dve', 'tensor_scalar', (B['ta'].ap(), B['ta'].ap(), noml.ap(), oml.ap(), ALU.mult, ALU.add),
                 r=[B['ta'], noml, oml], w=[B['ta']])
        for g, h in enumerate(hs):
            B = SL[g]
            act(B['tb'].ap(), B['tb'].ap(), AF.Exp, r=[B['tb']], w=[B['tb']])
            bl = col(B['tc'], T - 1)
            act(col(eblv, h).ap(), bl.ap(), AF.Exp, r=[bl], w=[col(eblv, h)])
        for g, h in enumerate(hs):
            B = SL[g]
            tt('pool', B['kdT'].ap(), B['ta'].ap(), B['tb'].ap(), ALU.mult, r=[B['ta'], B['tb']], w=[B['kdT']])

    def pre_back(hs, SL):
        ph_E(hs, SL)
        bds = []
        for g, h in enumerate(hs):
            B = SL[g]
            b = alloc()
            bds.append(b)
            o = PS(b, 0, 128)
            for s in range(4):
                l = B['kdtm'].sub(s * 128, 128)
                rr = B['vh'].sub(s * 128, 128)
                mm(o.ap(), l.ap(), rr.ap(), s == 0, s == 3, r=[l, rr], w=[o])
        for g, h in enumerate(hs):
            o = PS(bds[g], 0, 128)
            Sh = S32.sub(h * 128, 128)
            S.op('dve', 'scalar_tensor_tensor', (Sh.ap(), Sh.ap(), col(eblv, h).ap(), o.ap(), ALU.mult, ALU.add),
                 r=[Sh, col(eblv, h), o], w=[Sh])
            rel(bds[g])

    def pre_finish(cur):
        for h in range(H):
            Sh = S32.sub(h * 128, 128)
            dst = Sc[cur].sub(h * 128, 128)
            S.op('pool', 'tensor_copy', (dst.ap(), Sh.ap()), r=[Sh], w=[dst])

    def hgrn_stage(full, cur):
        assert full
        groups = [[0, 1], [2, 3], [4, 5], [6, 7]]

        def SLs(p):
            return SLF4[2 * (p % 2):2 * (p % 2) + 2]

        prs = ph_P(groups[0], True)
        ph_A(groups[0], SLs(0), prs, True)
        ph_B(groups[0], SLs(0))
        ph_C(groups[0], SLs(0))
        ph_D(groups[0], SLs(0), True)
        ph_E(groups[0], SLs(0))
        for p, hs in enumerate(groups):
            nxt = groups[p + 1] if p + 1 < len(groups) else None
            st = {}
            if nxt:
                prs = ph_P(nxt, True)
                ph_A(nxt, SLs(p + 1), prs, True)
            ph_F(hs, SLs(p), True, cur)
            ph_G1(hs, SLs(p), cur, st)
            if nxt:
                ph_B(nxt, SLs(p + 1))
            ph_G2(hs, SLs(p), cur, st)
            if nxt:
                ph_C(nxt, SLs(p + 1))
            ph_G3(hs, SLs(p), cur, st)
            if nxt:
                ph_D(nxt, SLs(p + 1), True)
            ph_G4(hs, SLs(p), cur, st)
            if nxt:
                ph_E(nxt, SLs(p + 1))

    def glu_stage():
        for j in range(4):
            wb = load_block(GLU0 + j)
            for mi in range(2):
                m = 2 * j + mi
                pa = proj_fm(wb, mi * 256)
                pb = proj_fm(wb, mi * 256 + 128)
                sg_ = sgate[m % 2]
                act(sg_.ap(), pb.ap(), AF.Sigmoid, r=[pb], w=[sg_])
                um = u.sub(m * UW + CW - 1, T)
                tt('dve', um.ap(), pa.ap(), sg_.ap(), ALU.mult, r=[pa, sg_], w=[um])
                rel(pa.bank, pb.bank)

    def halo_shift():
        S.op('pool', 'tensor_copy', (u.ap([[UW, 8], [1, CW - 1]]), u.ap([[UW, 8], [1, CW - 1]], off=T)),
             r=[u.sub(m * UW + T, CW - 1) for m in range(8)], w=[u.sub(m * UW, CW - 1) for m in range(8)])

    def conv_stage():
        bmean, bex2 = alloc(2)
        pmean, pex2 = PS(bmean), PS(bex2)
        build_diag(0)
        for m in range(8):
            if m + 1 < 8:
                build_diag(m + 1)
            b = alloc()
            py = PS(b)
            for j in range(CW):
                l = diag.sub(((m % 2) * CW + j) * 128, 128)
                rr = u.sub(m * UW + j, T)
                mm(py.ap(), l.ap(), rr.ap(), j == 0, j == CW - 1, r=[l, rr], w=[py])
            cbm = col(par, PC_CB + m)
            ym = y32.sub(m * T, T)
            act(ym.ap(), py.ap(), AF.Identity, r=[py, cbm], w=[ym], bias=cbm.ap())
            yb, yq = ybf[m % 2], ysq[m % 2]
            S.op('dve', 'tensor_copy', (yb.ap(), ym.ap()), r=[ym], w=[yb])
            act(yq.ap(), py.ap(), AF.Square, r=[py, cbm], w=[yq], bias=cbm.ap())
            rel(b)
            mm(pmean.ap(), onesb.ap(), yb.ap(), m == 0, m == 7, r=[onesb, yb], w=[pmean])
            mm(pex2.ap(), onesb.ap(), yq.ap(), m == 0, m == 7, r=[onesb, yq], w=[pex2])
        halo_shift()
        return bmean, bex2, pmean, pex2

    def conv_ln(bmean, bex2, pmean, pex2):
        act(meanv.ap(), pmean.ap(), AF.Copy, r=[pmean], w=[meanv], scale=1.0 / D)
        tt('dve', msqv.ap(), meanv.ap(), meanv.ap(), ALU.mult, r=[meanv], w=[msqv])
        S.op('dve', 'scalar_tensor_tensor', (msqv.ap(), pex2.ap(), 1.0 / D, msqv.ap(), ALU.mult, ALU.subtract),
             r=[pex2, msqv], w=[msqv])
        rel(bmean, bex2)
        act(rs2v.ap(), msqv.ap(), AF.Ln, r=[msqv, epsc], w=[rs2v], bias=epsc.ap())
        act(rs2v.ap(), rs2v.ap(), AF.Exp, r=[rs2v], w=[rs2v], scale=-0.5)
        for m in range(8):
            ym = y32.sub(m * T, T)
            tc_ = tcv[m % 2]
            tt('dve', tc_.ap(), ym.ap(), meanv.ap(), ALU.subtract, r=[ym, meanv], w=[tc_])
            tt('pool', tc_.ap(), tc_.ap(), rs2v.ap(), ALU.mult, r=[tc_, rs2v], w=[tc_])
            lg, lbb = col(par, PC_LG + m), col(par, PC_LB + m)
            yn = ynorm.sub(m * T, T)
            act(yn.ap(), tc_.ap(), AF.Silu, r=[tc_, lg, lbb], w=[yn], bias=lbb.ap(), scale=lg.ap())

    def branch_stage(gate0, w0, src, is_b, js=(0, 1)):
        for j in js:
            wg = load_block(gate0 + j)
            ww = load_block(w0 + j)
            for mi in range(4):
                m = 4 * j + mi
                pg = proj_fm(wg, mi * 128)
                b = alloc()
                po = PS(b)
                for kc in range(8):
                    l = ww.sub(kc * 512 + mi * 128, 128)
                    rr = src.sub(kc * T, T)
                    mm(po.ap(), l.ap(), rr.ap(), kc == 0, kc == 7, r=[l, rr], w=[po])
                sg_ = sgate[m % 2]
                act(sg_.ap(), pg.ap(), AF.Sigmoid, r=[pg], w=[sg_])
                rel(pg.bank)
                ma = mergedA.sub(m * T, T)
                if not is_b:
                    tt('dve', ma.ap(), po.ap(), sg_.ap(), ALU.mult, r=[po, sg_], w=[ma])
                else:
                    tb_ = tbv[m % 2]
                    tt('dve', tb_.ap(), po.ap(), sg_.ap(), ALU.mult, r=[po, sg_], w=[tb_])
                    mg = merged.sub(m * T, T)
                    tt('pool', mg.ap(), tb_.ap(), ma.ap(), ALU.add, r=[tb_, ma], w=[mg])
                rel(b)

    def wout_stage():
        for hf in range(2):
            wb = load_block(WO0 + hf)
            for s in range(4):
                b = alloc()
                po = PS(b)
                for kc in range(8):
                    l = merged.sub(kc * T + s * 128, 128)
                    rr = wb.sub(kc * 512, 512)
                    mm(po.ap(), l.ap(), rr.ap(), kc == 0, kc == 7, r=[l, rr], w=[po])
                xs = x.sub(s * D + hf * 512, 512)
                tt('dve', xs.ap(), po.ap(), xs.ap(), ALU.add, r=[po, xs], w=[xs])
                rel(b)

    def mlp_stage(nxt=None):
        for j in range(8):
            wb = load_block(M10 + j)
            for fi in range(4):
                f = 4 * j + fi
                pz = proj_fm(wb, fi * 128)
                rr_ = r32[f % 2]
                act(rr_.ap(), pz.ap(), AF.Relu, r=[pz], w=[rr_])
                rel(pz.bank)
                zf = z2T.sub(f * T, T)
                tt('pool' if f % 2 == 0 else 'dve', zf.ap(), rr_.ap(), rr_.ap(), ALU.mult, r=[rr_], w=[zf])
        if nxt is not None:
            pf_elem(nxt)
        for hf in range(2):
            if hf == 1 and nxt is not None:
                pf_pe()
            banks = alloc(4)
            for kg in range(4):
                wb = load_block(M20 + hf * 4 + kg)
                for s in range(4):
                    po = PS(banks[s])
                    for kc in range(8):
                        l = z2T.sub((kg * 8 + kc) * T + s * 128, 128)
                        rr = wb.sub(kc * 512, 512)
                        mm(po.ap(), l.ap(), rr.ap(), kg == 0 and kc == 0, kg == 3 and kc == 7, r=[l, rr], w=[po])
            for s in range(4):
                po = PS(banks[s])
                xs = x.sub(s * D + hf * 512, 512)
                tt('dve', xs.ap(), po.ap(), xs.ap(), ALU.add, r=[po, xs], w=[xs])
            rel(*banks)
        if nxt is not None:
            preload(HB0 + 0)
            preload(HB0 + 1)

    xstage = [A(36 * K, D, F32), A(40 * K, D, F32)]
    xn4 = [A((44 + 2 * i) * K, D, BF16) for i in range(4)]

    def pf_elem(t):
        for s in range(4):
            xs = xstage[s % 2]
            dma(xs, dram(x_d[t * 4 + s], "xin", 0, 1), "xs%d" % (s % 2))
            rs = rms_stats(xs, s, xn4[s])
            S.op('dve', 'scalar_tensor_tensor', (xn4[s].ap(), xs.ap(), rs.ap(), gmix.ap(), ALU.mult, ALU.mult), r=[xs, rs, gmix], w=[xn4[s]])

    def pf_pe():
        for s in range(4):
            b = alloc()
            xb = xn4[s]
            for kc in range(8):
                o = PSB(b, kc * 128, 128)
                i_ = xb.sub(kc * 128, 128)
                S.op('pe', 'transpose', (o.ap(), i_.ap(), identb.ap()), r=[i_, identb], w=[o])
            pv = PSB(b)
            dst_ap = hT.ap([[512, 8], [1, 128]], off=s * 128)
            src_ap = pv.ap([[128, 8], [1, 128]])
            wl = [hT.sub(kc * 512 + s * 128, 128) for kc in range(8)]
            if s % 2:
                S.op('dve', 'tensor_copy', (dst_ap, src_ap), r=[pv], w=wl)
            else:
                act(dst_ap, src_ap, AF.Copy, r=[pv], w=wl)
            rel(b)

    def final_stage(t):
        toks = []
        for s in range(4):
            xs = x.sub(s * D, D)
            rs = rms_stats(xs, s, xnF[s % 2])
            S.op('dve', 'scalar_tensor_tensor', (xs.ap(), xs.ap(), rs.ap(), gfin.ap(), ALU.mult, ALU.mult),
                 r=[xs, rs, gfin], w=[xs])
            toks.append(dma(dram(y_d[t * 4 + s], "y_d", t * 4 + s, t * 4 + s + 1), xs, "yo%d" % s))
        return toks

    def load_x(src, t):
        for s in range(4):
            dma(x.sub(s * D, D), dram(src[t * 4 + s], "xin", 0, 1), "x%d" % s)

    last_tok = []
    S.op('dve', 'memset', (onec.ap(), 1.0), w=[onec])
    for h in range(8):
        cast_group([HB0 + h], "cb%d" % h)
    cast_group([8, 9, 10, 11], "cg1")
    cast_group([12, 13, 16, 17], "cg2")
    cast_group([14, 15, 18, 19, 20, 21], "cg3")
    cast_group(list(range(22, 30)), "cg4")
    cast_group(list(range(30, 38)), "cg5")
    load_wpre()
    load_x(xp_d, 0)
    prev = None
    gi = 0
    for t in range(NT):
        for half in range(2):
            hs = [4 * half + i for i in range(4)]
            SLg = SLP[4 * (gi % 2):4 * (gi % 2) + 4]
            gi += 1
            if half == 0:
                rmsnorm_T(gmix, xnP)
                if t + 1 < NT:
                    load_x(xp_d, t + 1)
                else:
                    load_x(x_d, 0)
            pre_front(hs, SLg)
            if prev is not None:
                pre_back(*prev)
            prev = (hs, SLg)
    pre_back(*prev)
    glu_stage()
    halo_shift()
    cur = 0
    pre_finish(cur)
    for t in range(NT):
        if t == 0:
            rmsnorm_T(gmix, xnF)
        hgrn_stage(True, cur)
        cur = 1 - cur
        glu_stage()
        cs_ = conv_stage()
        branch_stage(GA0, WA0, o_a, False, js=(0,))
        conv_ln(*cs_)
        branch_stage(GA0, WA0, o_a, False, js=(1,))
        branch_stage(GB0, WB0, ynorm, True)
        wout_stage()
        rmsnorm_T(gmlp, xnF)
        mlp_stage(t + 1 if t + 1 < NT else None)
        last_tok = final_stage(t)
        if t + 1 < NT:
            load_x(x_d, t + 1)
    assert len(free_banks) == 8
    S.emit(nc, es, list(last_tok))
    es.close()
    S.stats = {e: len(S.ops[e]) for e in ENGS}
    print('ops per engine:', S.stats, flush=True)
    return nc


_CACHE = {}


def _prep_weights(inp):
    w_in = np.asarray(inp["w_in"][0], dtype=np.float32)
    blocks = []

    def blk(w, cols):
        kt = w.shape[0] // 128
        b = w[:, cols].reshape(kt, 128, len(cols)).transpose(1, 0, 2)
        return np.ascontiguousarray(b).reshape(128, kt * len(cols))

    ar = np.arange
    for h in range(8):
        cols = np.concatenate([1024 + h * 128 + ar(128), 2048 + h * 128 + ar(128), h * 128 + ar(128), 3072 + h * 128 + ar(128)])
        blocks.append(blk(w_in, cols))
    for j in range(4):
        m0, m1 = 2 * j, 2 * j + 1
        cols = np.concatenate([4096 + m0 * 128 + ar(128), 5120 + m0 * 128 + ar(128),
                               4096 + m1 * 128 + ar(128), 5120 + m1 * 128 + ar(128)])
        blocks.append(blk(w_in, cols))
    for j in range(2):
        blocks.append(blk(w_in, 6144 + j * 512 + ar(512)))
    for j in range(2):
        blocks.append(blk(w_in, 7168 + j * 512 + ar(512)))
    for name in ("w_a_out", "w_b_out", "w_out"):
        w = np.asarray(inp[name][0], dtype=np.float32)
        for j in range(2):
            blocks.append(blk(w, j * 512 + ar(512)))
    w1 = np.asarray(inp["w_mlp_in"][0], dtype=np.float32)
    for j in range(8):
        blocks.append(blk(w1, j * 512 + ar(512)))
    w2 = np.asarray(inp["w_mlp_out"][0], dtype=np.float32)
    for hf in range(2):
        for kg in range(4):
            blocks.append(blk(w2[kg * 1024:(kg + 1) * 1024], hf * 512 + ar(512)))
    wall = np.stack(blocks).astype(np.float32)
    assert wall.shape == (NBLK, 128, 4096)

    def fm(v):
        return np.asarray(v, dtype=np.float32).reshape(8, 128).T

    par = np.zeros((128, NPAR), np.float32)
    par[:, PC_GM:PC_GM + 8] = fm(inp["norm_mix_g"][0])
    par[:, PC_GML:PC_GML + 8] = fm(inp["norm_mlp_g"][0])
    par[:, PC_P0:PC_P0 + 8] = fm(inp["lb_param"][0])
    par[:, PC_P1:PC_P1 + 8] = fm(inp["lb_param"][1])
    par[:, PC_HG] = np.asarray(inp["hgrn_norm_g"][0], dtype=np.float32)
    par[:, PC_CB:PC_CB + 8] = fm(inp["conv_b"][0])
    par[:, PC_LG:PC_LG + 8] = fm(inp["conv_ln_g"][0])
    par[:, PC_LB:PC_LB + 8] = fm(inp["conv_ln_b"][0])
    cw = np.asarray(inp["conv_w"][0], dtype=np.float32)
    par[:, PC_CW:] = cw.reshape(CW, 8, 128).transpose(2, 1, 0).reshape(128, 8 * CW)
    gfin = np.ascontiguousarray(np.broadcast_to(np.asarray(inp["norm_final_g"], dtype=np.float32)[None, :], (128, D)))
    gmix = np.ascontiguousarray(np.broadcast_to(np.asarray(inp["norm_mix_g"][0], dtype=np.float32)[None, :], (128, D)))
    gmlp = np.ascontiguousarray(np.broadcast_to(np.asarray(inp["norm_mlp_g"][0], dtype=np.float32)[None, :], (128, D)))
    cst = np.zeros((128, NCST), np.float32)
    cst[:, CC_ID:CC_ID + 128] = np.eye(128, dtype=np.float32)
    sp, tt = np.meshgrid(ar(128), ar(128), indexing="ij")
    cst[:, CC_MASK:CC_MASK + 128] = ((sp // 64 == tt // 64) & (sp <= tt)).astype(np.float32)
    cst[:, CC_SCAN:] = (ar(T) % 64 != 0).astype(np.float32)[None, :]
    return wall, par, gfin, cst, gmix, gmlp


def kernel(**inp):
    xfull = np.asarray(inp["x"], dtype=np.float32)
    B, Sq, _ = xfull.shape
    half = Sq // 2
    NT = half // T
    assert half % T == 0 and B * 2 == 8
    if NT not in _CACHE:
        _CACHE[NT] = build(NT)
    nc = _CACHE[NT]
    wall, par, gfin, cst, gmix, gmlp = _prep_weights(inp)
    in_maps = []
    zeros = np.zeros((NT * 4, 128, D), np.float32)
    for c in range(8):
        b, hf = c // 2, c % 2
        xc = np.ascontiguousarray(xfull[b, hf * half:(hf + 1) * half]).reshape(NT * 4, 128, D)
        xp = zeros if hf == 0 else np.ascontiguousarray(xfull[b, 0:half]).reshape(NT * 4, 128, D)
        in_maps.append({"x": xc, "xprev": xp, "wall": wall, "par": par, "cst": cst, "gfin": gfin, "gmix": gmix, "gmlp": gmlp})
    res = run_bass_kernel_spmd(nc, in_maps, core_ids=list(range(8)))
    out = np.empty((B, Sq, D), np.float32)
    for c in range(8):
        b, hf = c // 2, c % 2
        out[b, hf * half:(hf + 1) * half] = np.asarray(res.results[c]["y"]).reshape(half, D)
    return out
```

```python
import contextlib
import numpy as np
import concourse.bass as bass
import concourse.mybir as mybir
from concourse.bass_utils import run_bass_kernel_spmd

F32 = mybir.dt.float32
BF16 = mybir.dt.bfloat16
AF = mybir.ActivationFunctionType
ALU = mybir.AluOpType

P = 128
D = 1024
H = 8
T = 512
EPS = 1e-6
CW = 31
UW = T + CW - 1
NBLK = 38
HB0, GLU0, GA0, GB0, WA0, WB0, WO0, M10, M20 = 0, 8, 12, 14, 16, 18, 20, 22, 30
PC_GM, PC_GML, PC_P0, PC_P1, PC_HG, PC_CB, PC_LG, PC_LB, PC_CW = 0, 8, 16, 24, 32, 33, 41, 49, 57
NPAR = 57 + 8 * CW
CC_ID, CC_MASK, CC_SCAN = 0, 128, 256
NCST = 256 + T

ENGS = ['pe', 'act', 'dve', 'pool', 'sp']


class V:
    def __init__(s, t, W, name, off, n, esz, ap=None, whole=None):
        s.t, s.W, s.name, s.off, s.n, s.esz = t, W, name, off, n, esz
        s._ap = ap
        s.whole = whole

    def res(s):
        if s.whole is not None:
            return (s.name, 0, s.whole)
        return (s.name, s.off * s.esz, (s.off + s.n) * s.esz)

    def sub(s, a, n):
        assert a + n <= s.n
        return V(s.t, s.W, s.name, s.off + a, n, s.esz)

    def ap(s, dims=None, off=0, p0=0, np_=P):
        if s._ap is not None:
            return s._ap
        if dims is None:
            dims = [[1, s.n - off]]
        return bass.AP(s.t, p0 * s.W + s.off + off, [[s.W, np_]] + [list(d) for d in dims])


class Sched:
    def __init__(s):
        s.ops = {e: [] for e in ENGS}
        s.res = {}
        s.seen = {e: {} for e in ENGS}
        s.dma_cnt = {}
        s.nbank = 0
        s.held = set()

    def bank(s):
        while True:
            b = s.nbank % 8
            s.nbank += 1
            if b not in s.held:
                return b

    def op(s, eng, iname, args, kw=None, r=(), w=(), dma=None):
        idx = len(s.ops[eng])
        deps = {}

        def add(d):
            if deps.get(d[0], -1) < d[1]:
                deps[d[0]] = d[1]

        rres = [v.res() for v in r]
        wres = [v.res() for v in w]
        for (name, lo, hi) in rres:
            R = s.res.setdefault(name, {'w': [], 'r': []})
            for (l, h, d) in R['w']:
                if l < hi and lo < h:
                    add(d)
        for (name, lo, hi) in wres:
            R = s.res.setdefault(name, {'w': [], 'r': []})
            for (l, h, d) in R['w']:
                if l < hi and lo < h:
                    add(d)
            for (l, h, d) in R['r']:
                if l < hi and lo < h:
                    add(d)
        if dma is not None:
            s.dma_cnt[dma] = s.dma_cnt.get(dma, 0) + 1
            tok = (('dma', dma), s.dma_cnt[dma] * 16)
        else:
            tok = (eng, idx)
        for (name, lo, hi) in rres:
            R = s.res[name]
            R['r'] = [x for x in R['r'] if not (x[2][0] == tok[0] and lo <= x[0] and x[1] <= hi)]
            R['r'].append((lo, hi, tok))
        for (name, lo, hi) in wres:
            R = s.res[name]
            R['w'] = [x for x in R['w'] if not (lo <= x[0] and x[1] <= hi)]
            R['r'] = [x for x in R['r'] if not (lo <= x[0] and x[1] <= hi)]
            R['w'].append((lo, hi, tok))
        waits = []
        for src, i in deps.items():
            if src == 'pe' and eng == 'pe':
                continue
            if s.seen[eng].get(src, -1) >= i:
                continue
            s.seen[eng][src] = i
            waits.append((src, i))
        s.ops[eng].append({'name': iname, 'args': args, 'kw': kw or {}, 'waits': waits, 'dma': dma})
        return tok

    def emit(s, nc, es, final_waits):
        sig = {e: set() for e in ENGS}
        for e in ENGS:
            for o in s.ops[e]:
                for (src, i) in o['waits']:
                    if isinstance(src, str):
                        sig[src].add(i)
        cnt = {}
        for e in ENGS:
            c = 0
            m = {}
            for i in range(len(s.ops[e])):
                if i in sig[e]:
                    c += 1
                    m[i] = c
            cnt[e] = m
        sems = {}
        for e in ENGS:
            if cnt[e]:
                sems[e] = es.enter_context(nc.semaphore('s_' + e))
        for k in s.dma_cnt:
            sems[('dma', k)] = es.enter_context(nc.semaphore('d_' + k))
        block = es.enter_context(nc.Block())

        def run(ename, eobj):
            for i, o in enumerate(s.ops[ename]):
                for (src, v) in o['waits']:
                    if isinstance(src, str):
                        eobj.wait_ge(sems[src], cnt[src][v])
                    else:
                        eobj.wait_ge(sems[src], v)
                ins = getattr(eobj, o['name'])(*o['args'], **o['kw'])
                if o['dma'] is not None:
                    ins.then_inc(sems[('dma', o['dma'])], 16)
                elif i in cnt[ename]:
                    ins.then_inc(sems[ename], 1)
            if ename == 'sp':
                for (src, v) in final_waits:
                    eobj.wait_ge(sems[src], v)

        @block.tensor
        def _(e):
            run('pe', e)

        @block.scalar
        def _(e):
            run('act', e)

        @block.vector
        def _(e):
            run('dve', e)

        @block.gpsimd
        def _(e):
            run('pool', e)

        @block.sync
        def _(e):
            run('sp', e)


def build(NT):
    import os as _os
    nc = bass.Bass("TRN2", target_bir_lowering=False)
    x_d = nc.dram_tensor("x", [NT * 4, P, D], F32, kind="ExternalInput").ap()
    xp_d = nc.dram_tensor("xprev", [NT * 4, P, D], F32, kind="ExternalInput").ap()
    w_d = nc.dram_tensor("wall", [NBLK, P, 4096], F32, kind="ExternalInput").ap()
    par_d = nc.dram_tensor("par", [P, NPAR], F32, kind="ExternalInput").ap()
    cst_d = nc.dram_tensor("cst", [P, NCST], F32, kind="ExternalInput").ap()
    gfin_d = nc.dram_tensor("gfin", [P, D], F32, kind="ExternalInput").ap()
    gmix_d = nc.dram_tensor("gmix", [P, D], F32, kind="ExternalInput").ap()
    gmlp_d = nc.dram_tensor("gmlp", [P, D], F32, kind="ExternalInput").ap()
    y_d = nc.dram_tensor("y", [NT * 4, P, D], F32, kind="ExternalOutput").ap()
    wbf_d = nc.dram_tensor("wbf", [NBLK, P, 4096], BF16).ap()

    es = contextlib.ExitStack()
    S = Sched()

    def sb(name, n, dt):
        t = es.enter_context(nc.sbuf_tensor("sb_" + name, [P, n], dt))
        return V(t, n, name, 0, n, 4 if dt == F32 else 2)

    par = sb("par", NPAR, F32)
    lbv = sb("lbv", 32, F32)
    cst = sb("cst", NCST, F32)
    identb = sb("identb", 128, BF16)
    onesb = sb("onesb", 128, BF16)
    gfin = sb("gfin", D, F32)
    gmix = sb("gmix", D, F32)
    gmlp = sb("gmlp", D, F32)
    diag = sb("diag", 2 * CW * 128, BF16)
    S32 = sb("S32", H * 128, F32)
    S32b = sb("S32b", H * 128, F32)
    Sc = [sb("Sc0", H * 128, BF16), sb("Sc1", H * 128, BF16)]
    u = sb("u", 8 * UW, BF16)
    x = sb("x", 4 * D, F32)
    hT = sb("hT", 8 * T, BF16)
    wbuf = [sb("wb%d" % i, 4096, BF16) for i in range(3)]
    stat = sb("stat", 16, F32)
    ARENA = 106 * 1024
    ar_t = es.enter_context(nc.sbuf_tensor("arena", [P, ARENA // 2], BF16))
    ar_f = ar_t.bitcast(F32)

    def A(byte_off, n, dt):
        byte_off = int(byte_off)
        if dt == F32:
            assert byte_off % 4 == 0 and byte_off + 4 * n <= ARENA
            return V(ar_f, ARENA // 4, "arena", byte_off // 4, n, 4)
        assert byte_off % 2 == 0 and byte_off + 2 * n <= ARENA
        return V(ar_t, ARENA // 2, "arena", byte_off // 2, n, 2)

    K = 1024
    o_a = A(0, 8 * T, BF16)
    mergedA = A(8 * K, 8 * T, F32)

    def slot_full(g):
        b = 24 * K + g * 19 * K
        return dict(ta=A(b, T, F32), tb=A(b + 2 * K, T, F32), tc=A(b + 4 * K, T, F32), td=A(b + 6 * K, T, F32),
                    qs=A(b + 8 * K, T, F32), sg=A(b + 10 * K, T, F32), qb=A(b + 12 * K, T, BF16),
                    kb=A(b + 13 * K, T, BF16), kdT=A(b + 14 * K, T, BF16), kdtm=A(b + 15 * K, T, BF16),
                    vh=A(b + 16 * K, T, BF16), Sall=A(b + 17 * K, 7 * 128, BF16))

    def slot_pre(g):
        b = g * 8 * K
        return dict(ta=A(b, T, F32), tb=A(b + 2 * K, T, F32), tc=A(b + 4 * K, T, F32),
                    kdT=A(b + 6 * K, T, BF16), kdtm=A(b + 6 * K, T, BF16), vh=A(b + 7 * K, T, BF16))

    SLF4 = [slot_full(g) for g in range(4)]
    wpre = A(64 * K, 8 * 2048, BF16)
    SLP = [slot_pre(g) for g in range(8)]
    osq = [A(8 * K, T, BF16), A(9 * K, T, BF16)]
    scm = [A(10 * K, T, BF16), A(11 * K, T, BF16)]
    rsv = [A(100 * K, T, F32), A(102 * K, T, F32)]
    xnF = [A(12 * K, D, BF16), A(14 * K, D, BF16)]
    xnP = [A(96 * K, D, BF16), A(98 * K, D, BF16)]
    ynorm = A(70 * K, 8 * T, BF16)
    y32 = A(24 * K, 8 * T, F32)
    ybf = [A(40 * K, T, BF16), A(41 * K, T, BF16)]
    ysq = [A(42 * K, T, BF16), A(43 * K, T, BF16)]
    meanv = A(44 * K, T, F32)
    msqv = A(46 * K, T, F32)
    rs2v = A(48 * K, T, F32)
    tcv = [A(50 * K, T, F32), A(52 * K, T, F32)]
    sgate = [A(54 * K, T, F32), A(56 * K, T, F32)]
    tbv = [A(58 * K, T, F32), A(60 * K, T, F32)]
    merged = A(62 * K, 8 * T, BF16)
    z2T = A(0, 32 * T, BF16)
    r32 = [A(32 * K, T, F32), A(34 * K, T, F32)]
    ps_t = [es.enter_context(nc.psum_tensor("ps%d" % i, [P, 512], F32)) for i in range(8)]
    ps_b = [t.bitcast(BF16) for t in ps_t]
    free_banks = list(range(8))

    def alloc(n=1):
        assert len(free_banks) >= n, "out of PSUM banks"
        out = [free_banks.pop(0) for _ in range(n)]
        return out if n > 1 else out[0]

    def rel(*bs):
        for b in bs:
            assert b not in free_banks
            free_banks.append(b)

    def PS(b, off=0, n=512):
        return V(ps_t[b], 512, "ps%d" % b, off, n, 4, whole=2048)

    def PSB(b, off=0, n=1024):
        return V(ps_b[b], 1024, "ps%d" % b, off, n, 2, whole=2048)

    def dram(apx, name, lo, hi):
        return V(None, 0, name, lo, hi - lo, 1, ap=apx)

    def mm(out, lhsT, rhs, start, stop, r, w):
        S.op('pe', 'matmul', (out, lhsT, rhs), dict(start=start, stop=stop), r=r, w=w)

    def act(out_ap, in_ap, func, r, w, bias=None, scale=None, accum=None):
        kw = {}
        if bias is not None:
            kw['bias'] = bias
        if scale is not None:
            kw['scale'] = scale
        if accum is not None:
            kw['accum_out'] = accum
        S.op('act', 'activation', (out_ap, in_ap, func), kw, r=r, w=w)

    def tt(eng, out, in0, in1, op, r, w):
        S.op(eng, 'tensor_tensor', (out, in0, in1, op), r=r, w=w)

    def dma(out_v, in_v, key, eng='sp'):
        return S.op(eng, 'dma_start', (), dict(out=out_v.ap(), in_=in_v.ap()), r=[in_v], w=[out_v], dma=key)

    wslot = [0]

    preloaded = {}

    def preload(b):
        preloaded[b] = load_block(b)

    def load_block(b):
        if b in preloaded:
            return preloaded.pop(b)
        k = wslot[0] % 3
        wslot[0] += 1
        dma(wbuf[k], dram(wbf_d[b], "wbf", b, b + 1), "w%d" % k)
        return wbuf[k]

    def col(v, c):
        return v.sub(c, 1)

    epsc = col(stat, 12)

    dma(par, dram(par_d, "par_d", 0, 1), "c0")
    dma(cst, dram(cst_d, "cst_d", 0, 1), "c1")
    dma(gfin, dram(gfin_d, "gfin_d", 0, 1), "c2")
    dma(gmix, dram(gmix_d, "gmix_d", 0, 1), "c3")
    dma(gmlp, dram(gmlp_d, "gmlp_d", 0, 1), "c4")
    S.op('dve', 'memset', (epsc.ap(), EPS), w=[epsc])
    dv = lbv.sub(24, 8)
    tt('dve', dv.ap(), par.sub(PC_P0, 8).ap(), par.sub(PC_P1, 8).ap(), ALU.subtract, r=[par], w=[dv])
    act(lbv.sub(0, 8).ap(), dv.ap(), AF.Sigmoid, r=[dv], w=[lbv.sub(0, 8)])
    act(lbv.sub(8, 8).ap(), dv.ap(), AF.Sigmoid, r=[dv], w=[lbv.sub(8, 8)], scale=-1.0)
    S.op('dve', 'tensor_scalar', (lbv.sub(16, 8).ap(), lbv.sub(8, 8).ap(), -1.0, None, ALU.mult),
         r=[lbv.sub(8, 8)], w=[lbv.sub(16, 8)])
    S.op('dve', 'tensor_copy', (identb.ap(), cst.sub(CC_ID, 128).ap()), r=[cst], w=[identb])
    S.op('dve', 'memset', (onesb.ap(), 1.0), w=[onesb])
    S.op('dve', 'memset', (S32.ap(), 0.0), w=[S32])
    S.op('pool', 'memset', (u.ap(), 0.0), w=[u])
    identf = cst.sub(CC_ID, 128)

    def build_diag(m):
        for j in range(CW):
            dst = diag.sub(((m % 2) * CW + j) * 128, 128)
            sc = col(par, PC_CW + m * CW + j)
            e3 = j % 3
            if e3 == 0:
                S.op('pool', 'tensor_scalar', (dst.ap(), identf.ap(), sc.ap(), 0.0, ALU.mult, ALU.add), r=[identf, sc], w=[dst])
            elif e3 == 1:
                act(dst.ap(), identf.ap(), AF.Copy, r=[identf, sc], w=[dst], scale=sc.ap())
            else:
                S.op('dve', 'tensor_scalar', (dst.ap(), identf.ap(), sc.ap(), 0.0, ALU.mult, ALU.add), r=[identf, sc], w=[dst])

    def cast_group(blocks, key):
        for b in blocks:
            srcv = dram(w_d[b], "w_d", b, b + 1)
            dstv = dram(wbf_d[b], "wbf", b, b + 1)
            S.op('pool', 'dma_start', (), dict(out=dstv.ap(), in_=srcv.ap()), r=[srcv], w=[dstv], dma=key)
        final = (('dma', key), S.dma_cnt[key] * 16)
        R = S.res["wbf"]
        R['w'] = [(lo, hi, final if tk[0] == ('dma', key) else tk) for (lo, hi, tk) in R['w']]

    def load_wpre():
        for h in range(8):
            srcap = wbf_d[HB0 + h].rearrange("p (k c) -> p k c", c=512)[:, :, 0:256]
            dstv = wpre.sub(h * 2048, 2048)
            S.op('sp', 'dma_start', (), dict(out=dstv.ap([[256, 8], [1, 256]]), in_=srcap),
                 r=[dram(srcap, "wbf", HB0 + h, HB0 + h + 1)], w=[dstv], dma="wp%d" % h)

    def rms_stats(xs, s, junkv):
        ss, ln_, rs = col(stat, s), col(stat, 4 + s), col(stat, 8 + s)
        act(junkv.ap(), xs.ap(), AF.Square, r=[xs], w=[junkv, ss], accum=ss.ap())
        act(ln_.ap(), ss.ap(), AF.Ln, r=[ss, epsc], w=[ln_], bias=epsc.ap(), scale=1.0 / D)
        act(rs.ap(), ln_.ap(), AF.Exp, r=[ln_], w=[rs], scale=-0.5)
        return rs

    def rmsnorm_T(gv, xn):
        for s in range(4):
            xs = x.sub(s * D, D)
            b = alloc()
            xb = xn[s % 2]
            rs = rms_stats(xs, s, xb)
            S.op('dve', 'scalar_tensor_tensor', (xb.ap(), xs.ap(), rs.ap(), gv.ap(), ALU.mult, ALU.mult), r=[xs, rs, gv], w=[xb])
            for kc in range(8):
                o = PSB(b, kc * 128, 128)
                i_ = xb.sub(kc * 128, 128)
                S.op('pe', 'transpose', (o.ap(), i_.ap(), identb.ap()), r=[i_, identb], w=[o])
            pv = PSB(b)
            dst_ap = hT.ap([[512, 8], [1, 128]], off=s * 128)
            src_ap = pv.ap([[128, 8], [1, 128]])
            wl = [hT.sub(kc * 512 + s * 128, 128) for kc in range(8)]
            if s % 2:
                S.op('dve', 'tensor_copy', (dst_ap, src_ap), r=[pv], w=wl)
            else:
                act(dst_ap, src_ap, AF.Copy, r=[pv], w=wl)
            rel(b)

    def proj_fm(wb, coff, WST=512):
        b = alloc()
        o = PS(b)
        for kc in range(8):
            l = wb.sub(kc * WST + coff, 128)
            rr = hT.sub(kc * 512, 512)
            mm(o.ap(), l.ap(), rr.ap(), kc == 0, kc == 7, r=[l, rr], w=[o])
        o.bank = b
        return o

    def proj_v(wb, WST=512):
        b = alloc()
        for s in range(4):
            o = PS(b, s * 128, 128)
            for kc in range(8):
                l = hT.sub(kc * 512 + s * 128, 128)
                rr = wb.sub(kc * WST + 128, 128)
                mm(o.ap(), l.ap(), rr.ap(), kc == 0, kc == 7, r=[l, rr], w=[o])
        o = PS(b)
        o.bank = b
        return o

    def ph_P(hs, full):
        prs = []
        for h in hs:
            pr = {}
            if full:
                wb = load_block(HB0 + h)
                pr['z'] = proj_fm(wb, 0)
                pr['v'] = proj_v(wb)
            else:
                wb = wpre.sub(h * 2048, 2048)
                pr['z'] = proj_fm(wb, 0, 256)
                pr['v'] = proj_v(wb, 256)
            if full:
                pr['q'] = proj_fm(wb, 256)
                pr['g'] = proj_fm(wb, 384)
            prs.append(pr)
        return prs

    def ph_A(hs, SL, prs, full):
        if full:
            for g, h in enumerate(hs):
                B = SL[g]
                act(B['qs'].ap(), prs[g]['q'].ap(), AF.Silu, r=[prs[g]['q']], w=[B['qs']])
                act(B['sg'].ap(), prs[g]['g'].ap(), AF.Silu, r=[prs[g]['g']], w=[B['sg']])
                rel(prs[g]['q'].bank, prs[g]['g'].bank)
        for g, h in enumerate(hs):
            B = SL[g]
            act(B['ta'].ap(), prs[g]['z'].ap(), AF.Sigmoid, r=[prs[g]['z']], w=[B['ta']])
            S.op('dve', 'tensor_copy', (B['vh'].ap(), prs[g]['v'].ap()), r=[prs[g]['v']], w=[B['vh']])
            rel(prs[g]['z'].bank, prs[g]['v'].bank)
        for g, h in enumerate(hs):
            B = SL[g]
            lb, oml = col(lbv, h), col(lbv, 8 + h)
            act(B['tb'].ap(), B['ta'].ap(), AF.Ln, r=[B['ta'], lb, oml], w=[B['tb']], bias=lb.ap(), scale=oml.ap())

    def ph_B(hs, SL):
        scanm = cst.sub(CC_SCAN, T)
        for g, h in enumerate(hs):
            B = SL[g]
            S.op('dve', 'tensor_tensor_scan', (B['tc'].ap(), scanm.ap(), B['tb'].ap(), 0.0, ALU.mult, ALU.add),
                 r=[scanm, B['tb']], w=[B['tc']])
        for g, h in enumerate(hs):
            B = SL[g]
            oml, noml = col(lbv, 8 + h), col(lbv, 16 + h)
            S.op('dve', 'tensor_scalar', (B['ta'].ap(), B['ta'].ap(), noml.ap(), oml.ap(), ALU.mult, ALU.add),
                 r=[B['ta'], noml, oml], w=[B['ta']])

    def ph_C(hs, SL):
        for g, h in enumerate(hs):
            B = SL[g]
            act(B['tb'].ap(), B['tc'].ap(), AF.Exp, r=[B['tc']], w=[B['tb']])
            act(B['td'].ap(), B['tc'].ap(), AF.Exp, r=[B['tc']], w=[B['td']], scale=-1.0)

    def ph_D(hs, SL, full):
        for g, h in enumerate(hs):
            B = SL[g]
            tt('pool', B['td'].ap(), B['ta'].ap(), B['td'].ap(), ALU.mult, r=[B['ta'], B['td']], w=[B['td']])
        for g, h in enumerate(hs):
            B = SL[g]
            tt('dve', B['kdT'].ap([[64, 8], [1, 64]]), B['td'].ap([[64, 8], [1, 64]]),
               B['tb'].ap([[64, 8], [0, 64]], off=63), ALU.mult, r=[B['td'], B['tb']], w=[B['kdT']])
        if full:
            for g, h in enumerate(hs):
                B = SL[g]
                tt('pool', B['qb'].ap(), B['qs'].ap(), B['tb'].ap(), ALU.mult, r=[B['qs'], B['tb']], w=[B['qb']])
                act(B['kb'].ap(), B['td'].ap(), AF.Copy, r=[B['td']], w=[B['kb']])

    def ph_E(hs, SL):
        bs = []
        for g, h in enumerate(hs):
            B = SL[g]
            b = alloc()
            bs.append(b)
            for s in range(4):
                o = PSB(b, s * 128, 128)
                i_ = B['kdT'].sub(s * 128, 128)
                S.op('pe', 'transpose', (o.ap(), i_.ap(), identb.ap()), r=[i_, identb], w=[o])
        for g, h in enumerate(hs):
            B = SL[g]
            pv2 = PSB(bs[g], 0, 512)
            S.op('dve', 'tensor_copy', (B['kdtm'].ap(), pv2.ap()), r=[pv2], w=[B['kdtm']])
            rel(bs[g])

    def ph_F(hs, SL, full, cur):
        pds = []
        for g, h in enumerate(hs):
            B = SL[g]
            banks = alloc(2)
            pd = []
            for c in range(8):
                s_, hf = c // 2, c % 2
                o = PS(banks[hf], s_ * 128, 128)
                l = B['kdtm'].sub(s_ * 128, 128)
                rr = B['vh'].sub(s_ * 128, 128)
                mm(o.ap(), l.ap(p0=hf * 64, np_=64), rr.ap(p0=hf * 64, np_=64), True, True, r=[l, rr], w=[o])
                pd.append(o)
            pds.append((pd, banks))
        for c in range(8):
            for g, h in enumerate(hs):
                B = SL[g]
                pd, banks = pds[g]
                Sin_ = (S32 if c % 2 == 0 else S32b).sub(h * 128, 128)
                Sh = (S32b if c % 2 == 0 else S32).sub(h * 128, 128)
                ebl = col(B['tb'], c * 64 + 63)
                S.op('dve', 'scalar_tensor_tensor', (Sh.ap(), Sin_.ap(), ebl.ap(), pd[c].ap(), ALU.mult, ALU.add),
                     r=[Sin_, ebl, pd[c]], w=[Sh])
                if c < 7:
                    if not full:
                        continue
                    dst = B['Sall'].sub(c * 128, 128)
                else:
                    dst = Sc[1 - cur].sub(h * 128, 128)
                if c % 2 == 0:
                    act(dst.ap(), Sh.ap(), AF.Copy, r=[Sh], w=[dst])
                else:
                    S.op('pool', 'tensor_copy', (dst.ap(), Sh.ap()), r=[Sh], w=[dst])
        for g in range(len(hs)):
            rel(*pds[g][1])

    def ph_G1(hs, SL, cur, st):
        maskv = cst.sub(CC_MASK, 128)
        bss = []
        for g, h in enumerate(hs):
            B = SL[g]
            bs = alloc()
            bss.append(bs)
            for s in range(4):
                o = PS(bs, s * 128, 128)
                l = B['kb'].sub(s * 128, 128)
                rr = B['qb'].sub(s * 128, 128)
                mm(o.ap(), l.ap(), rr.ap(), True, True, r=[l, rr], w=[o])
        for g, h in enumerate(hs):
            psS = PS(bss[g])
            tt('dve', scm[g].ap([[128, 4], [1, 128]]), psS.ap([[128, 4], [1, 128]]), maskv.ap([[0, 4], [1, 128]]),
               ALU.mult, r=[psS, maskv], w=[scm[g]])
            rel(bss[g])

    def ph_G2(hs, SL, cur, st):
        bos = []
        for g, h in enumerate(hs):
            B = SL[g]
            bo = alloc()
            bos.append(bo)
            for s in range(4):
                o = PS(bo, s * 128, 128)
                l = B['vh'].sub(s * 128, 128)
                rr = scm[g].sub(s * 128, 128)
                mm(o.ap(), l.ap(), rr.ap(), True, False, r=[l, rr], w=[o])
                for hf in range(2):
                    c = 2 * s + hf
                    Sin = Sc[cur].sub(h * 128, 128) if c == 0 else B['Sall'].sub((c - 1) * 128, 128)
                    o2 = PS(bo, s * 128 + hf * 64, 64)
                    rr2 = B['qb'].sub(c * 64, 64)
                    mm(o2.ap(), Sin.ap(), rr2.ap(), False, hf == 1, r=[Sin, rr2], w=[o2])
        for g, h in enumerate(hs):
            psO = PS(bos[g])
            act(osq[g].ap(), psO.ap(), AF.Square, r=[psO], w=[osq[g]])
        st['bos'] = bos

    def ph_G3(hs, SL, cur, st):
        bms = []
        for g, h in enumerate(hs):
            bm = alloc()
            bms.append(bm)
            psM = PS(bm)
            mm(psM.ap(), onesb.ap(), osq[g].ap(), True, True, r=[onesb, osq[g]], w=[psM])
        for g, h in enumerate(hs):
            psM = PS(bms[g])
            act(rsv[g].ap(), psM.ap(), AF.Ln, r=[psM, epsc], w=[rsv[g]], bias=epsc.ap(), scale=1.0 / 128)
            rel(bms[g])
            act(rsv[g].ap(), rsv[g].ap(), AF.Exp, r=[rsv[g]], w=[rsv[g]], scale=-0.5)

    def ph_G4(hs, SL, cur, st):
        hg = col(par, PC_HG)
        bos = st['bos']
        for g, h in enumerate(hs):
            B = SL[g]
            psO = PS(bos[g])
            tt('dve', rsv[g].ap(), psO.ap(), rsv[g].ap(), ALU.mult, r=[psO, rsv[g]], w=[rsv[g]])
            rel(bos[g])
            oh = o_a.sub(h * T, T)
            S.op('dve', 'scalar_tensor_tensor', (oh.ap(), rsv[g].ap(), hg.ap(), B['sg'].ap(), ALU.mult, ALU.mult),
                 r=[rsv[g], hg, B['sg']], w=[oh])

    onec = col(stat, 13)
    eblv = sb("eblv", 8, F32)

    def pre_front(hs, SL):
        prs = ph_P(hs, False)
        for g, h in enumerate(hs):
            B = SL[g]
            act(B['ta'].ap(), prs[g]['z'].ap(), AF.Sigmoid, r=[prs[g]['z']], w=[B['ta']])
            S.op('dve', 'tensor_copy', (B['vh'].ap(), prs[g]['v'].ap()), r=[prs[g]['v']], w=[B['vh']])
            rel(prs[g]['z'].bank, prs[g]['v'].bank)
        for g, h in enumerate(hs):
            B = SL[g]
            lb, oml = col(lbv, h), col(lbv, 8 + h)
            act(B['tb'].ap(), B['ta'].ap(), AF.Ln, r=[B['ta'], lb, oml], w=[B['tb']], bias=lb.ap(), scale=oml.ap())
        for g, h in enumerate(hs):
            B = SL[g]
            S.op('dve', 'tensor_tensor_scan', (B['tc'].ap(), onec.ap([[0, T]]), B['tb'].ap(), 0.0, ALU.mult, ALU.add),
                 r=[onec, B['tb']], w=[B['tc']])
            bl = col(B['tc'], T - 1)
            S.op('dve', 'tensor_scalar', (B['tb'].ap(), B['tc'].ap(), -1.0, bl.ap(), ALU.mult, ALU.add),
                 r=[B['tc']], w=[B['tb']])
            oml, noml = col(lbv, 8 + h), col(lbv, 16 + h)
            S.op('dve', 'tensor_scalar', (B['ta'].ap(), B['ta'].ap(), noml.ap(), oml.ap(), ALU.mult, ALU.add),
                 r=[B['ta'], noml, oml], w=[B['ta']])
        for g, h in enumerate(hs):
            B = SL[g]
            act(B['tb'].ap(), B['tb'].ap(), AF.Exp, r=[B['tb']], w=[B['tb']])
            bl = col(B['tc'], T - 1)
            act(col(eblv, h).ap(), bl.ap(), AF.Exp, r=[bl], w=[col(eblv, h)])
        for g, h in enumerate(hs):
            B = SL[g]
            tt('pool', B['kdT'].ap(), B['ta'].ap(), B['tb'].ap(), ALU.mult, r=[B['ta'], B['tb']], w=[B['kdT']])

    def pre_back(hs, SL):
        ph_E(hs, SL)
        bds = []
        for g, h in enumerate(hs):
            B = SL[g]
            b = alloc()
            bds.append(b)
            o = PS(b, 0, 128)
            for s in range(4):
                l = B['kdtm'].sub(s * 128, 128)
                rr = B['vh'].sub(s * 128, 128)
                mm(o.ap(), l.ap(), rr.ap(), s == 0, s == 3, r=[l, rr], w=[o])
        for g, h in enumerate(hs):
            o = PS(bds[g], 0, 128)
            Sh = S32.sub(h * 128, 128)
            S.op('dve', 'scalar_tensor_tensor', (Sh.ap(), Sh.ap(), col(eblv, h).ap(), o.ap(), ALU.mult, ALU.add),
                 r=[Sh, col(eblv, h), o], w=[Sh])
            rel(bds[g])

    def pre_finish(cur):
        for h in range(H):
            Sh = S32.sub(h * 128, 128)
            dst = Sc[cur].sub(h * 128, 128)
            S.op('pool', 'tensor_copy', (dst.ap(), Sh.ap()), r=[Sh], w=[dst])

    def hgrn_stage(full, cur):
        assert full
        groups = [[0, 1], [2, 3], [4, 5], [6, 7]]

        def SLs(p):
            return SLF4[2 * (p % 2):2 * (p % 2) + 2]

        prs = ph_P(groups[0], True)
        ph_A(groups[0], SLs(0), prs, True)
        ph_B(groups[0], SLs(0))
        ph_C(groups[0], SLs(0))
        ph_D(groups[0], SLs(0), True)
        ph_E(groups[0], SLs(0))
        for p, hs in enumerate(groups):
            nxt = groups[p + 1] if p + 1 < len(groups) else None
            st = {}
            if nxt:
                prs = ph_P(nxt, True)
                ph_A(nxt, SLs(p + 1), prs, True)
            ph_F(hs, SLs(p), True, cur)
            ph_G1(hs, SLs(p), cur, st)
            if nxt:
                ph_B(nxt, SLs(p + 1))
            ph_G2(hs, SLs(p), cur, st)
            if nxt:
                ph_C(nxt, SLs(p + 1))
            ph_G3(hs, SLs(p), cur, st)
            if nxt:
                ph_D(nxt, SLs(p + 1), True)
            ph_G4(hs, SLs(p), cur, st)
            if nxt:
                ph_E(nxt, SLs(p + 1))

    def glu_stage():
        for j in range(4):
            wb = load_block(GLU0 + j)
            for mi in range(2):
                m = 2 * j + mi
                pa = proj_fm(wb, mi * 256)
                pb = proj_fm(wb, mi * 256 + 128)
                sg_ = sgate[m % 2]
                act(sg_.ap(), pb.ap(), AF.Sigmoid, r=[pb], w=[sg_])
                um = u.sub(m * UW + CW - 1, T)
                tt('dve', um.ap(), pa.ap(), sg_.ap(), ALU.mult, r=[pa, sg_], w=[um])
                rel(pa.bank, pb.bank)

    def halo_shift():
        S.op('pool', 'tensor_copy', (u.ap([[UW, 8], [1, CW - 1]]), u.ap([[UW, 8], [1, CW - 1]], off=T)),
             r=[u.sub(m * UW + T, CW - 1) for m in range(8)], w=[u.sub(m * UW, CW - 1) for m in range(8)])

    def conv_stage():
        bmean, bex2 = alloc(2)
        pmean, pex2 = PS(bmean), PS(bex2)
        build_diag(0)
        for m in range(8):
            if m + 1 < 8:
                build_diag(m + 1)
            b = alloc()
            py = PS(b)
            for j in range(CW):
                l = diag.sub(((m % 2) * CW + j) * 128, 128)
                rr = u.sub(m * UW + j, T)
                mm(py.ap(), l.ap(), rr.ap(), j == 0, j == CW - 1, r=[l, rr], w=[py])
            cbm = col(par, PC_CB + m)
            ym = y32.sub(m * T, T)
            act(ym.ap(), py.ap(), AF.Identity, r=[py, cbm], w=[ym], bias=cbm.ap())
            yb, yq = ybf[m % 2], ysq[m % 2]
            S.op('dve', 'tensor_copy', (yb.ap(), ym.ap()), r=[ym], w=[yb])
            act(yq.ap(), py.ap(), AF.Square, r=[py, cbm], w=[yq], bias=cbm.ap())
            rel(b)
            mm(pmean.ap(), onesb.ap(), yb.ap(), m == 0, m == 7, r=[onesb, yb], w=[pmean])
            mm(pex2.ap(), onesb.ap(), yq.ap(), m == 0, m == 7, r=[onesb, yq], w=[pex2])
        halo_shift()
        return bmean, bex2, pmean, pex2

    def conv_ln(bmean, bex2, pmean, pex2):
        act(meanv.ap(), pmean.ap(), AF.Copy, r=[pmean], w=[meanv], scale=1.0 / D)
        tt('dve', msqv.ap(), meanv.ap(), meanv.ap(), ALU.mult, r=[meanv], w=[msqv])
        S.op('dve', 'scalar_tensor_tensor', (msqv.ap(), pex2.ap(), 1.0 / D, msqv.ap(), ALU.mult, ALU.subtract),
             r=[pex2, msqv], w=[msqv])
        rel(bmean, bex2)
        act(rs2v.ap(), msqv.ap(), AF.Ln, r=[msqv, epsc], w=[rs2v], bias=epsc.ap())
        act(rs2v.ap(), rs2v.ap(), AF.Exp, r=[rs2v], w=[rs2v], scale=-0.5)
        for m in range(8):
            ym = y32.sub(m * T, T)
            tc_ = tcv[m % 2]
            tt('dve', tc_.ap(), ym.ap(), meanv.ap(), ALU.subtract, r=[ym, meanv], w=[tc_])
            tt('pool', tc_.ap(), tc_.ap(), rs2v.ap(), ALU.mult, r=[tc_, rs2v], w=[tc_])
            lg, lbb = col(par, PC_LG + m), col(par, PC_LB + m)
            yn = ynorm.sub(m * T, T)
            act(yn.ap(), tc_.ap(), AF.Silu, r=[tc_, lg, lbb], w=[yn], bias=lbb.ap(), scale=lg.ap())

    def branch_stage(gate0, w0, src, is_b, js=(0, 1)):
        for j in js:
            wg = load_block(gate0 + j)
            ww = load_block(w0 + j)
            for mi in range(4):
                m = 4 * j + mi
                pg = proj_fm(wg, mi * 128)
                b = alloc()
                po = PS(b)
                for kc in range(8):
                    l = ww.sub(kc * 512 + mi * 128, 128)
                    rr = src.sub(kc * T, T)
                    mm(po.ap(), l.ap(), rr.ap(), kc == 0, kc == 7, r=[l, rr], w=[po])
                sg_ = sgate[m % 2]
                act(sg_.ap(), pg.ap(), AF.Sigmoid, r=[pg], w=[sg_])
                rel(pg.bank)
                ma = mergedA.sub(m * T, T)
                if not is_b:
                    tt('dve', ma.ap(), po.ap(), sg_.ap(), ALU.mult, r=[po, sg_], w=[ma])
                else:
                    tb_ = tbv[m % 2]
                    tt('dve', tb_.ap(), po.ap(), sg_.ap(), ALU.mult, r=[po, sg_], w=[tb_])
                    mg = merged.sub(m * T, T)
                    tt('pool', mg.ap(), tb_.ap(), ma.ap(), ALU.add, r=[tb_, ma], w=[mg])
                rel(b)

    def wout_stage():
        for hf in range(2):
            wb = load_block(WO0 + hf)
            for s in range(4):
                b = alloc()
                po = PS(b)
                for kc in range(8):
                    l = merged.sub(kc * T + s * 128, 128)
                    rr = wb.sub(kc * 512, 512)
                    mm(po.ap(), l.ap(), rr.ap(), kc == 0, kc == 7, r=[l, rr], w=[po])
                xs = x.sub(s * D + hf * 512, 512)
                tt('dve', xs.ap(), po.ap(), xs.ap(), ALU.add, r=[po, xs], w=[xs])
                rel(b)

    def mlp_stage(nxt=None):
        for j in range(8):
            wb = load_block(M10 + j)
            for fi in range(4):
                f = 4 * j + fi
                pz = proj_fm(wb, fi * 128)
                rr_ = r32[f % 2]
                act(rr_.ap(), pz.ap(), AF.Relu, r=[pz], w=[rr_])
                rel(pz.bank)
                zf = z2T.sub(f * T, T)
                tt('pool' if f % 2 == 0 else 'dve', zf.ap(), rr_.ap(), rr_.ap(), ALU.mult, r=[rr_], w=[zf])
        if nxt is not None:
            pf_elem(nxt)
        for hf in range(2):
            if hf == 1 and nxt is not None:
                pf_pe()
            banks = alloc(4)
            for kg in range(4):
                wb = load_block(M20 + hf * 4 + kg)
                for s in range(4):
                    po = PS(banks[s])
                    for kc in range(8):
                        l = z2T.sub((kg * 8 + kc) * T + s * 128, 128)
                        rr = wb.sub(kc * 512, 512)
                        mm(po.ap(), l.ap(), rr.ap(), kg == 0 and kc == 0, kg == 3 and kc == 7, r=[l, rr], w=[po])
            for s in range(4):
                po = PS(banks[s])
                xs = x.sub(s * D + hf * 512, 512)
                tt('dve', xs.ap(), po.ap(), xs.ap(), ALU.add, r=[po, xs], w=[xs])
            rel(*banks)
        if nxt is not None:
            preload(HB0 + 0)
            preload(HB0 + 1)

    xstage = [A(36 * K, D, F32), A(40 * K, D, F32)]
    xn4 = [A((44 + 2 * i) * K, D, BF16) for i in range(4)]

    def pf_elem(t):
        for s in range(4):
            xs = xstage[s % 2]
            dma(xs, dram(x_d[t * 4 + s], "xin", 0, 1), "xs%d" % (s % 2))
            rs = rms_stats(xs, s, xn4[s])
            S.op('dve', 'scalar_tensor_tensor', (xn4[s].ap(), xs.ap(), rs.ap(), gmix.ap(), ALU.mult, ALU.mult), r=[xs, rs, gmix], w=[xn4[s]])

    def pf_pe():
        for s in range(4):
            b = alloc()
            xb = xn4[s]
            for kc in range(8):
                o = PSB(b, kc * 128, 128)
                i_ = xb.sub(kc * 128, 128)
                S.op('pe', 'transpose', (o.ap(), i_.ap(), identb.ap()), r=[i_, identb], w=[o])
            pv = PSB(b)
            dst_ap = hT.ap([[512, 8], [1, 128]], off=s * 128)
            src_ap = pv.ap([[128, 8], [1, 128]])
            wl = [hT.sub(kc * 512 + s * 128, 128) for kc in range(8)]
            if s % 2:
                S.op('dve', 'tensor_copy', (dst_ap, src_ap), r=[pv], w=wl)
            else:
                act(dst_ap, src_ap, AF.Copy, r=[pv], w=wl)
            rel(b)

    def final_stage(t):
        toks = []
        for s in range(4):
            xs = x.sub(s * D, D)
            rs = rms_stats(xs, s, xnF[s % 2])
            S.op('dve', 'scalar_tensor_tensor', (xs.ap(), xs.ap(), rs.ap(), gfin.ap(), ALU.mult, ALU.mult),
                 r=[xs, rs, gfin], w=[xs])
            toks.append(dma(dram(y_d[t * 4 + s], "y_d", t * 4 + s, t * 4 + s + 1), xs, "yo%d" % s))
        return toks

    def load_x(src, t):
        for s in range(4):
            dma(x.sub(s * D, D), dram(src[t * 4 + s], "xin", 0, 1), "x%d" % s)

    last_tok = []
    S.op('dve', 'memset', (onec.ap(), 1.0), w=[onec])
    for h in range(8):
        cast_group([HB0 + h], "cb%d" % h)
    later = [([8, 9, 10, 11], "cg1"), ([12, 13, 16, 17], "cg2"), ([14, 15, 18, 19, 20, 21], "cg3"),
             (list(range(22, 30)), "cg4"), (list(range(30, 38)), "cg5")]
    load_wpre()
    load_x(xp_d, 0)
    prev = None
    gi = 0
    for t in range(NT):
        for half in range(2):
            hs = [4 * half + i for i in range(4)]
            SLg = SLP[4 * (gi % 2):4 * (gi % 2) + 4]
            gi += 1
            if half == 0:
                rmsnorm_T(gmix, xnP)
                if t + 1 < NT:
                    load_x(xp_d, t + 1)
                else:
                    load_x(x_d, 0)
            pre_front(hs, SLg)
            if prev is not None:
                pre_back(*prev)
            prev = (hs, SLg)
            if half == 0 and later and (t >= 1 or NT == 1):
                cast_group(*later.pop(0))
    while later:
        cast_group(*later.pop(0))
    pre_back(*prev)
    glu_stage()
    halo_shift()
    cur = 0
    pre_finish(cur)
    for t in range(NT):
        if t == 0:
            rmsnorm_T(gmix, xnF)
        hgrn_stage(True, cur)
        cur = 1 - cur
        glu_stage()
        cs_ = conv_stage()
        branch_stage(GA0, WA0, o_a, False, js=(0,))
        conv_ln(*cs_)
        branch_stage(GA0, WA0, o_a, False, js=(1,))
        branch_stage(GB0, WB0, ynorm, True)
        wout_stage()
        rmsnorm_T(gmlp, xnF)
        mlp_stage(t + 1 if t + 1 < NT else None)
        last_tok = final_stage(t)
        if t + 1 < NT:
            load_x(x_d, t + 1)
    assert len(free_banks) == 8
    S.emit(nc, es, list(last_tok))
    es.close()
    S.stats = {e: len(S.ops[e]) for e in ENGS}
    print('ops per engine:', S.stats, flush=True)
    return nc


_CACHE = {}


def _prep_weights(inp):
    w_in = np.asarray(inp["w_in"][0], dtype=np.float32)
    blocks = []

    def blk(w, cols):
        kt = w.shape[0] // 128
        b = w[:, cols].reshape(kt, 128, len(cols)).transpose(1, 0, 2)
        return np.ascontiguousarray(b).reshape(128, kt * len(cols))

    ar = np.arange
    for h in range(8):
        cols = np.concatenate([1024 + h * 128 + ar(128), 2048 + h * 128 + ar(128), h * 128 + ar(128), 3072 + h * 128 + ar(128)])
        blocks.append(blk(w_in, cols))
    for j in range(4):
        m0, m1 = 2 * j, 2 * j + 1
        cols = np.concatenate([4096 + m0 * 128 + ar(128), 5120 + m0 * 128 + ar(128),
                               4096 + m1 * 128 + ar(128), 5120 + m1 * 128 + ar(128)])
        blocks.append(blk(w_in, cols))
    for j in range(2):
        blocks.append(blk(w_in, 6144 + j * 512 + ar(512)))
    for j in range(2):
        blocks.append(blk(w_in, 7168 + j * 512 + ar(512)))
    for name in ("w_a_out", "w_b_out", "w_out"):
        w = np.asarray(inp[name][0], dtype=np.float32)
        for j in range(2):
            blocks.append(blk(w, j * 512 + ar(512)))
    w1 = np.asarray(inp["w_mlp_in"][0], dtype=np.float32)
    for j in range(8):
        blocks.append(blk(w1, j * 512 + ar(512)))
    w2 = np.asarray(inp["w_mlp_out"][0], dtype=np.float32)
    for hf in range(2):
        for kg in range(4):
            blocks.append(blk(w2[kg * 1024:(kg + 1) * 1024], hf * 512 + ar(512)))
    wall = np.stack(blocks).astype(np.float32)
    assert wall.shape == (NBLK, 128, 4096)

    def fm(v):
        return np.asarray(v, dtype=np.float32).reshape(8, 128).T

    par = np.zeros((128, NPAR), np.float32)
    par[:, PC_GM:PC_GM + 8] = fm(inp["norm_mix_g"][0])
    par[:, PC_GML:PC_GML + 8] = fm(inp["norm_mlp_g"][0])
    par[:, PC_P0:PC_P0 + 8] = fm(inp["lb_param"][0])
    par[:, PC_P1:PC_P1 + 8] = fm(inp["lb_param"][1])
    par[:, PC_HG] = np.asarray(inp["hgrn_norm_g"][0], dtype=np.float32)
    par[:, PC_CB:PC_CB + 8] = fm(inp["conv_b"][0])
    par[:, PC_LG:PC_LG + 8] = fm(inp["conv_ln_g"][0])
    par[:, PC_LB:PC_LB + 8] = fm(inp["conv_ln_b"][0])
    cw = np.asarray(inp["conv_w"][0], dtype=np.float32)
    par[:, PC_CW:] = cw.reshape(CW, 8, 128).transpose(2, 1, 0).reshape(128, 8 * CW)
    gfin = np.ascontiguousarray(np.broadcast_to(np.asarray(inp["norm_final_g"], dtype=np.float32)[None, :], (128, D)))
    gmix = np.ascontiguousarray(np.broadcast_to(np.asarray(inp["norm_mix_g"][0], dtype=np.float32)[None, :], (128, D)))
    gmlp = np.ascontiguousarray(np.broadcast_to(np.asarray(inp["norm_mlp_g"][0], dtype=np.float32)[None, :], (128, D)))
    cst = np.zeros((128, NCST), np.float32)
    cst[:, CC_ID:CC_ID + 128] = np.eye(128, dtype=np.float32)
    sp, tt = np.meshgrid(ar(128), ar(128), indexing="ij")
    cst[:, CC_MASK:CC_MASK + 128] = ((sp // 64 == tt // 64) & (sp <= tt)).astype(np.float32)
    cst[:, CC_SCAN:] = (ar(T) % 64 != 0).astype(np.float32)[None, :]
    return wall, par, gfin, cst, gmix, gmlp


def kernel(**inp):
    xfull = np.asarray(inp["x"], dtype=np.float32)
    B, Sq, _ = xfull.shape
    half = Sq // 2
    NT = half // T
    assert half % T == 0 and B * 2 == 8
    if NT not in _CACHE:
        _CACHE[NT] = build(NT)
    nc = _CACHE[NT]
    wall, par, gfin, cst, gmix, gmlp = _prep_weights(inp)
    in_maps = []
    zeros = np.zeros((NT * 4, 128, D), np.float32)
    for c in range(8):
        b, hf = c // 2, c % 2
        xc = np.ascontiguousarray(xfull[b, hf * half:(hf + 1) * half]).reshape(NT * 4, 128, D)
        xp = zeros if hf == 0 else np.ascontiguousarray(xfull[b, 0:half]).reshape(NT * 4, 128, D)
        in_maps.append({"x": xc, "xprev": xp, "wall": wall, "par": par, "cst": cst, "gfin": gfin, "gmix": gmix, "gmlp": gmlp})
    res = run_bass_kernel_spmd(nc, in_maps, core_ids=list(range(8)))
    out = np.empty((B, Sq, D), np.float32)
    for c in range(8):
        b, hf = c // 2, c % 2
        out[b, hf * half:(hf + 1) * half] = np.asarray(res.results[c]["y"]).reshape(half, D)
    return out
```

```python
import contextlib
import numpy as np
import concourse.bass as bass
import concourse.mybir as mybir
from concourse.bass_utils import run_bass_kernel_spmd

F32 = mybir.dt.float32
BF16 = mybir.dt.bfloat16
AF = mybir.ActivationFunctionType
ALU = mybir.AluOpType

P = 128
D = 1024
H = 8
T = 512
EPS = 1e-6
CW = 31
UW = T + CW - 1
NBLK = 38
HB0, GLU0, GA0, GB0, WA0, WB0, WO0, M10, M20 = 0, 8, 12, 14, 16, 18, 20, 22, 30
PC_GM, PC_GML, PC_P0, PC_P1, PC_HG, PC_CB, PC_LG, PC_LB, PC_CW = 0, 8, 16, 24, 32, 33, 41, 49, 57
NPAR = 57 + 8 * CW
CC_ID, CC_MASK, CC_SCAN = 0, 128, 256
NCST = 256 + T

ENGS = ['pe', 'act', 'dve', 'pool', 'sp']


class V:
    def __init__(s, t, W, name, off, n, esz, ap=None, whole=None):
        s.t, s.W, s.name, s.off, s.n, s.esz = t, W, name, off, n, esz
        s._ap = ap
        s.whole = whole

    def res(s):
        if s.whole is not None:
            return (s.name, 0, s.whole)
        return (s.name, s.off * s.esz, (s.off + s.n) * s.esz)

    def sub(s, a, n):
        assert a + n <= s.n
        return V(s.t, s.W, s.name, s.off + a, n, s.esz)

    def ap(s, dims=None, off=0, p0=0, np_=P):
        if s._ap is not None:
            return s._ap
        if dims is None:
            dims = [[1, s.n - off]]
        return bass.AP(s.t, p0 * s.W + s.off + off, [[s.W, np_]] + [list(d) for d in dims])


class Sched:
    def __init__(s):
        s.ops = {e: [] for e in ENGS}
        s.res = {}
        s.seen = {e: {} for e in ENGS}
        s.dma_cnt = {}
        s.nbank = 0
        s.held = set()

    def bank(s):
        while True:
            b = s.nbank % 8
            s.nbank += 1
            if b not in s.held:
                return b

    def op(s, eng, iname, args, kw=None, r=(), w=(), dma=None):
        idx = len(s.ops[eng])
        deps = {}

        def add(d):
            if deps.get(d[0], -1) < d[1]:
                deps[d[0]] = d[1]

        rres = [v.res() for v in r]
        wres = [v.res() for v in w]
        for (name, lo, hi) in rres:
            R = s.res.setdefault(name, {'w': [], 'r': []})
            for (l, h, d) in R['w']:
                if l < hi and lo < h:
                    add(d)
        for (name, lo, hi) in wres:
            R = s.res.setdefault(name, {'w': [], 'r': []})
            for (l, h, d) in R['w']:
                if l < hi and lo < h:
                    add(d)
            for (l, h, d) in R['r']:
                if l < hi and lo < h:
                    add(d)
        if dma is not None:
            s.dma_cnt[dma] = s.dma_cnt.get(dma, 0) + 1
            tok = (('dma', dma), s.dma_cnt[dma] * 16)
        else:
            tok = (eng, idx)
        for (name, lo, hi) in rres:
            R = s.res[name]
            R['r'] = [x for x in R['r'] if not (x[2][0] == tok[0] and lo <= x[0] and x[1] <= hi)]
            R['r'].append((lo, hi, tok))
        for (name, lo, hi) in wres:
            R = s.res[name]
            R['w'] = [x for x in R['w'] if not (lo <= x[0] and x[1] <= hi)]
            R['r'] = [x for x in R['r'] if not (lo <= x[0] and x[1] <= hi)]
            R['w'].append((lo, hi, tok))
        waits = []
        for src, i in deps.items():
            if src == 'pe' and eng == 'pe':
                continue
            if s.seen[eng].get(src, -1) >= i:
                continue
            s.seen[eng][src] = i
            waits.append((src, i))
        s.ops[eng].append({'name': iname, 'args': args, 'kw': kw or {}, 'waits': waits, 'dma': dma})
        return tok

    def emit(s, nc, es, final_waits):
        sig = {e: set() for e in ENGS}
        for e in ENGS:
            for o in s.ops[e]:
                for (src, i) in o['waits']:
                    if isinstance(src, str):
                        sig[src].add(i)
        cnt = {}
        for e in ENGS:
            c = 0
            m = {}
            for i in range(len(s.ops[e])):
                if i in sig[e]:
                    c += 1
                    m[i] = c
            cnt[e] = m
        sems = {}
        for e in ENGS:
            if cnt[e]:
                sems[e] = es.enter_context(nc.semaphore('s_' + e))
        for k in s.dma_cnt:
            sems[('dma', k)] = es.enter_context(nc.semaphore('d_' + k))
        block = es.enter_context(nc.Block())

        def run(ename, eobj):
            for i, o in enumerate(s.ops[ename]):
                for (src, v) in o['waits']:
                    if isinstance(src, str):
                        eobj.wait_ge(sems[src], cnt[src][v])
                    else:
                        eobj.wait_ge(sems[src], v)
                ins = getattr(eobj, o['name'])(*o['args'], **o['kw'])
                if o['dma'] is not None:
                    ins.then_inc(sems[('dma', o['dma'])], 16)
                elif i in cnt[ename]:
                    ins.then_inc(sems[ename], 1)
            if ename == 'sp':
                for (src, v) in final_waits:
                    eobj.wait_ge(sems[src], v)

        @block.tensor
        def _(e):
            run('pe', e)

        @block.scalar
        def _(e):
            run('act', e)

        @block.vector
        def _(e):
            run('dve', e)

        @block.gpsimd
        def _(e):
            run('pool', e)

        @block.sync
        def _(e):
            run('sp', e)


def build(NT):
    import os as _os
    nc = bass.Bass("TRN2", target_bir_lowering=False)
    x_d = nc.dram_tensor("x", [NT * 4, P, D], F32, kind="ExternalInput").ap()
    xp_d = nc.dram_tensor("xprev", [NT * 4, P, D], F32, kind="ExternalInput").ap()
    w_d = nc.dram_tensor("wall", [NBLK, P, 4096], F32, kind="ExternalInput").ap()
    par_d = nc.dram_tensor("par", [P, NPAR], F32, kind="ExternalInput").ap()
    cst_d = nc.dram_tensor("cst", [P, NCST], F32, kind="ExternalInput").ap()
    gfin_d = nc.dram_tensor("gfin", [P, D], F32, kind="ExternalInput").ap()
    gmix_d = nc.dram_tensor("gmix", [P, D], F32, kind="ExternalInput").ap()
    gmlp_d = nc.dram_tensor("gmlp", [P, D], F32, kind="ExternalInput").ap()
    y_d = nc.dram_tensor("y", [NT * 4, P, D], F32, kind="ExternalOutput").ap()
    wbf_d = nc.dram_tensor("wbf", [NBLK, P, 4096], BF16).ap()

    es = contextlib.ExitStack()
    S = Sched()

    def sb(name, n, dt):
        t = es.enter_context(nc.sbuf_tensor("sb_" + name, [P, n], dt))
        return V(t, n, name, 0, n, 4 if dt == F32 else 2)

    par = sb("par", NPAR, F32)
    lbv = sb("lbv", 32, F32)
    cst = sb("cst", NCST, F32)
    identb = sb("identb", 128, BF16)
    onesb = sb("onesb", 128, BF16)
    gfin = sb("gfin", D, F32)
    gmix = sb("gmix", D, F32)
    gmlp = sb("gmlp", D, F32)
    diag = sb("diag", 2 * CW * 128, BF16)
    S32 = sb("S32", H * 128, F32)
    S32b = sb("S32b", H * 128, F32)
    Sc = [sb("Sc0", H * 128, BF16), sb("Sc1", H * 128, BF16)]
    u = sb("u", 8 * UW, BF16)
    x = sb("x", 4 * D, F32)
    hT = sb("hT", 8 * T, BF16)
    wbuf = [sb("wb%d" % i, 4096, BF16) for i in range(3)]
    stat = sb("stat", 16, F32)
    ARENA = 106 * 1024
    ar_t = es.enter_context(nc.sbuf_tensor("arena", [P, ARENA // 2], BF16))
    ar_f = ar_t.bitcast(F32)

    def A(byte_off, n, dt):
        byte_off = int(byte_off)
        if dt == F32:
            assert byte_off % 4 == 0 and byte_off + 4 * n <= ARENA
            return V(ar_f, ARENA // 4, "arena", byte_off // 4, n, 4)
        assert byte_off % 2 == 0 and byte_off + 2 * n <= ARENA
        return V(ar_t, ARENA // 2, "arena", byte_off // 2, n, 2)

    K = 1024
    o_a = A(0, 8 * T, BF16)
    mergedA = A(8 * K, 8 * T, F32)

    def slot_full(g):
        b = 24 * K + g * 19 * K
        return dict(ta=A(b, T, F32), tb=A(b + 2 * K, T, F32), tc=A(b + 4 * K, T, F32), td=A(b + 6 * K, T, F32),
                    qs=A(b + 8 * K, T, F32), sg=A(b + 10 * K, T, F32), qb=A(b + 12 * K, T, BF16),
                    kb=A(b + 13 * K, T, BF16), kdT=A(b + 14 * K, T, BF16), kdtm=A(b + 15 * K, T, BF16),
                    vh=A(b + 16 * K, T, BF16), Sall=A(b + 17 * K, 7 * 128, BF16))

    def slot_pre(g):
        b = g * 8 * K
        return dict(ta=A(b, T, F32), tb=A(b + 2 * K, T, F32), tc=A(b + 4 * K, T, F32),
                    kdT=A(b + 6 * K, T, BF16), kdtm=A(b + 6 * K, T, BF16), vh=A(b + 7 * K, T, BF16))

    SLF4 = [slot_full(g) for g in range(4)]
    wpre = A(64 * K, 8 * 2048, BF16)
    SLP = [slot_pre(g) for g in range(8)]
    osq = [A(8 * K, T, BF16), A(9 * K, T, BF16)]
    scm = [A(10 * K, T, BF16), A(11 * K, T, BF16)]
    rsv = [A(100 * K, T, F32), A(102 * K, T, F32)]
    xnF = [A(12 * K, D, BF16), A(14 * K, D, BF16)]
    xnP = [A(96 * K, D, BF16), A(98 * K, D, BF16)]
    ynorm = A(70 * K, 8 * T, BF16)
    y32 = A(24 * K, 8 * T, F32)
    ybf = [A(40 * K, T, BF16), A(41 * K, T, BF16)]
    ysq = [A(42 * K, T, BF16), A(43 * K, T, BF16)]
    meanv = A(44 * K, T, F32)
    msqv = A(46 * K, T, F32)
    rs2v = A(48 * K, T, F32)
    tcv = [A(50 * K, T, F32), A(52 * K, T, F32)]
    sgate = [A(54 * K, T, F32), A(56 * K, T, F32)]
    tbv = [A(58 * K, T, F32), A(60 * K, T, F32)]
    merged = A(62 * K, 8 * T, BF16)
    z2T = A(0, 32 * T, BF16)
    r32 = [A(32 * K, T, F32), A(34 * K, T, F32)]
    ps_t = [es.enter_context(nc.psum_tensor("ps%d" % i, [P, 512], F32)) for i in range(8)]
    ps_b = [t.bitcast(BF16) for t in ps_t]
    free_banks = list(range(8))

    def alloc(n=1):
        assert len(free_banks) >= n, "out of PSUM banks"
        out = [free_banks.pop(0) for _ in range(n)]
        return out if n > 1 else out[0]

    def rel(*bs):
        for b in bs:
            assert b not in free_banks
            free_banks.append(b)

    def PS(b, off=0, n=512):
        return V(ps_t[b], 512, "ps%d" % b, off, n, 4, whole=2048)

    def PSB(b, off=0, n=1024):
        return V(ps_b[b], 1024, "ps%d" % b, off, n, 2, whole=2048)

    def dram(apx, name, lo, hi):
        return V(None, 0, name, lo, hi - lo, 1, ap=apx)

    def mm(out, lhsT, rhs, start, stop, r, w):
        S.op('pe', 'matmul', (out, lhsT, rhs), dict(start=start, stop=stop), r=r, w=w)

    def act(out_ap, in_ap, func, r, w, bias=None, scale=None, accum=None):
        kw = {}
        if bias is not None:
            kw['bias'] = bias
        if scale is not None:
            kw['scale'] = scale
        if accum is not None:
            kw['accum_out'] = accum
        S.op('act', 'activation', (out_ap, in_ap, func), kw, r=r, w=w)

    def tt(eng, out, in0, in1, op, r, w):
        S.op(eng, 'tensor_tensor', (out, in0, in1, op), r=r, w=w)

    def dma(out_v, in_v, key, eng='sp'):
        return S.op(eng, 'dma_start', (), dict(out=out_v.ap(), in_=in_v.ap()), r=[in_v], w=[out_v], dma=key)

    wslot = [0]

    preloaded = {}

    def preload(b):
        preloaded[b] = load_block(b)

    def load_block(b):
        if b in preloaded:
            return preloaded.pop(b)
        k = wslot[0] % 3
        wslot[0] += 1
        dma(wbuf[k], dram(wbf_d[b], "wbf", b, b + 1), "w%d" % k)
        return wbuf[k]

    def col(v, c):
        return v.sub(c, 1)

    epsc = col(stat, 12)

    dma(par, dram(par_d, "par_d", 0, 1), "c0")
    dma(cst, dram(cst_d, "cst_d", 0, 1), "c1")
    dma(gfin, dram(gfin_d, "gfin_d", 0, 1), "c2")
    dma(gmix, dram(gmix_d, "gmix_d", 0, 1), "c3")
    dma(gmlp, dram(gmlp_d, "gmlp_d", 0, 1), "c4")
    S.op('dve', 'memset', (epsc.ap(), EPS), w=[epsc])
    dv = lbv.sub(24, 8)
    tt('dve', dv.ap(), par.sub(PC_P0, 8).ap(), par.sub(PC_P1, 8).ap(), ALU.subtract, r=[par], w=[dv])
    act(lbv.sub(0, 8).ap(), dv.ap(), AF.Sigmoid, r=[dv], w=[lbv.sub(0, 8)])
    act(lbv.sub(8, 8).ap(), dv.ap(), AF.Sigmoid, r=[dv], w=[lbv.sub(8, 8)], scale=-1.0)
    S.op('dve', 'tensor_scalar', (lbv.sub(16, 8).ap(), lbv.sub(8, 8).ap(), -1.0, None, ALU.mult),
         r=[lbv.sub(8, 8)], w=[lbv.sub(16, 8)])
    S.op('dve', 'tensor_copy', (identb.ap(), cst.sub(CC_ID, 128).ap()), r=[cst], w=[identb])
    S.op('dve', 'memset', (onesb.ap(), 1.0), w=[onesb])
    S.op('dve', 'memset', (S32.ap(), 0.0), w=[S32])
    S.op('pool', 'memset', (u.ap(), 0.0), w=[u])
    identf = cst.sub(CC_ID, 128)

    def build_diag(m):
        for j in range(CW):
            dst = diag.sub(((m % 2) * CW + j) * 128, 128)
            sc = col(par, PC_CW + m * CW + j)
            e3 = j % 3
            if e3 == 0:
                S.op('pool', 'tensor_scalar', (dst.ap(), identf.ap(), sc.ap(), 0.0, ALU.mult, ALU.add), r=[identf, sc], w=[dst])
            elif e3 == 1:
                act(dst.ap(), identf.ap(), AF.Copy, r=[identf, sc], w=[dst], scale=sc.ap())
            else:
                S.op('dve', 'tensor_scalar', (dst.ap(), identf.ap(), sc.ap(), 0.0, ALU.mult, ALU.add), r=[identf, sc], w=[dst])

    def cast_group(blocks, key):
        for b in blocks:
            srcv = dram(w_d[b], "w_d", b, b + 1)
            dstv = dram(wbf_d[b], "wbf", b, b + 1)
            S.op('pool', 'dma_start', (), dict(out=dstv.ap(), in_=srcv.ap()), r=[srcv], w=[dstv], dma=key)
        final = (('dma', key), S.dma_cnt[key] * 16)
        R = S.res["wbf"]
        R['w'] = [(lo, hi, final if tk[0] == ('dma', key) else tk) for (lo, hi, tk) in R['w']]

    def load_wpre():
        for h in range(8):
            srcap = wbf_d[HB0 + h].rearrange("p (k c) -> p k c", c=512)[:, :, 0:256]
            dstv = wpre.sub(h * 2048, 2048)
            S.op('sp', 'dma_start', (), dict(out=dstv.ap([[256, 8], [1, 256]]), in_=srcap),
                 r=[dram(srcap, "wbf", HB0 + h, HB0 + h + 1)], w=[dstv], dma="wp%d" % h)

    def rms_stats(xs, s, junkv):
        ss, ln_, rs = col(stat, s), col(stat, 4 + s), col(stat, 8 + s)
        act(junkv.ap(), xs.ap(), AF.Square, r=[xs], w=[junkv, ss], accum=ss.ap())
        act(ln_.ap(), ss.ap(), AF.Ln, r=[ss, epsc], w=[ln_], bias=epsc.ap(), scale=1.0 / D)
        act(rs.ap(), ln_.ap(), AF.Exp, r=[ln_], w=[rs], scale=-0.5)
        return rs

    def rmsnorm_T(gv, xn):
        for s in range(4):
            xs = x.sub(s * D, D)
            b = alloc()
            xb = xn[s % 2]
            rs = rms_stats(xs, s, xb)
            S.op('dve', 'scalar_tensor_tensor', (xb.ap(), xs.ap(), rs.ap(), gv.ap(), ALU.mult, ALU.mult), r=[xs, rs, gv], w=[xb])
            for kc in range(8):
                o = PSB(b, kc * 128, 128)
                i_ = xb.sub(kc * 128, 128)
                S.op('pe', 'transpose', (o.ap(), i_.ap(), identb.ap()), r=[i_, identb], w=[o])
            pv = PSB(b)
            dst_ap = hT.ap([[512, 8], [1, 128]], off=s * 128)
            src_ap = pv.ap([[128, 8], [1, 128]])
            wl = [hT.sub(kc * 512 + s * 128, 128) for kc in range(8)]
            if s % 2:
                S.op('dve', 'tensor_copy', (dst_ap, src_ap), r=[pv], w=wl)
            else:
                act(dst_ap, src_ap, AF.Copy, r=[pv], w=wl)
            rel(b)

    def proj_fm(wb, coff, WST=512):
        b = alloc()
        o = PS(b)
        for kc in range(8):
            l = wb.sub(kc * WST + coff, 128)
            rr = hT.sub(kc * 512, 512)
            mm(o.ap(), l.ap(), rr.ap(), kc == 0, kc == 7, r=[l, rr], w=[o])
        o.bank = b
        return o

    def proj_v(wb, WST=512):
        b = alloc()
        for s in range(4):
            o = PS(b, s * 128, 128)
            for kc in range(8):
                l = hT.sub(kc * 512 + s * 128, 128)
                rr = wb.sub(kc * WST + 128, 128)
                mm(o.ap(), l.ap(), rr.ap(), kc == 0, kc == 7, r=[l, rr], w=[o])
        o = PS(b)
        o.bank = b
        return o

    def ph_P(hs, full):
        prs = []
        for h in hs:
            pr = {}
            if full:
                wb = load_block(HB0 + h)
                pr['z'] = proj_fm(wb, 0)
                pr['v'] = proj_v(wb)
            else:
                wb = wpre.sub(h * 2048, 2048)
                pr['z'] = proj_fm(wb, 0, 256)
                pr['v'] = proj_v(wb, 256)
            if full:
                pr['q'] = proj_fm(wb, 256)
                pr['g'] = proj_fm(wb, 384)
            prs.append(pr)
        return prs

    def ph_A(hs, SL, prs, full):
        if full:
            for g, h in enumerate(hs):
                B = SL[g]
                act(B['qs'].ap(), prs[g]['q'].ap(), AF.Silu, r=[prs[g]['q']], w=[B['qs']])
                act(B['sg'].ap(), prs[g]['g'].ap(), AF.Silu, r=[prs[g]['g']], w=[B['sg']])
                rel(prs[g]['q'].bank, prs[g]['g'].bank)
        for g, h in enumerate(hs):
            B = SL[g]
            act(B['ta'].ap(), prs[g]['z'].ap(), AF.Sigmoid, r=[prs[g]['z']], w=[B['ta']])
            S.op('dve', 'tensor_copy', (B['vh'].ap(), prs[g]['v'].ap()), r=[prs[g]['v']], w=[B['vh']])
            rel(prs[g]['z'].bank, prs[g]['v'].bank)
        for g, h in enumerate(hs):
            B = SL[g]
            lb, oml = col(lbv, h), col(lbv, 8 + h)
            act(B['tb'].ap(), B['ta'].ap(), AF.Ln, r=[B['ta'], lb, oml], w=[B['tb']], bias=lb.ap(), scale=oml.ap())

    def ph_B(hs, SL):
        scanm = cst.sub(CC_SCAN, T)
        for g, h in enumerate(hs):
            B = SL[g]
            S.op('dve', 'tensor_tensor_scan', (B['tc'].ap(), scanm.ap(), B['tb'].ap(), 0.0, ALU.mult, ALU.add),
                 r=[scanm, B['tb']], w=[B['tc']])
        for g, h in enumerate(hs):
            B = SL[g]
            oml, noml = col(lbv, 8 + h), col(lbv, 16 + h)
            S.op('dve', 'tensor_scalar', (B['ta'].ap(), B['ta'].ap(), noml.ap(), oml.ap(), ALU.mult, ALU.add),
                 r=[B['ta'], noml, oml], w=[B['ta']])

    def ph_C(hs, SL):
        for g, h in enumerate(hs):
            B = SL[g]
            act(B['tb'].ap(), B['tc'].ap(), AF.Exp, r=[B['tc']], w=[B['tb']])
            act(B['td'].ap(), B['tc'].ap(), AF.Exp, r=[B['tc']], w=[B['td']], scale=-1.0)

    def ph_D(hs, SL, full):
        for g, h in enumerate(hs):
            B = SL[g]
            tt('pool', B['td'].ap(), B['ta'].ap(), B['td'].ap(), ALU.mult, r=[B['ta'], B['td']], w=[B['td']])
        for g, h in enumerate(hs):
            B = SL[g]
            tt('dve', B['kdT'].ap([[64, 8], [1, 64]]), B['td'].ap([[64, 8], [1, 64]]),
               B['tb'].ap([[64, 8], [0, 64]], off=63), ALU.mult, r=[B['td'], B['tb']], w=[B['kdT']])
        if full:
            for g, h in enumerate(hs):
                B = SL[g]
                tt('pool', B['qb'].ap(), B['qs'].ap(), B['tb'].ap(), ALU.mult, r=[B['qs'], B['tb']], w=[B['qb']])
                act(B['kb'].ap(), B['td'].ap(), AF.Copy, r=[B['td']], w=[B['kb']])

    def ph_E(hs, SL):
        bs = []
        for g, h in enumerate(hs):
            B = SL[g]
            b = alloc()
            bs.append(b)
            for s in range(4):
                o = PSB(b, s * 128, 128)
                i_ = B['kdT'].sub(s * 128, 128)
                S.op('pe', 'transpose', (o.ap(), i_.ap(), identb.ap()), r=[i_, identb], w=[o])
        for g, h in enumerate(hs):
            B = SL[g]
            pv2 = PSB(bs[g], 0, 512)
            S.op('dve', 'tensor_copy', (B['kdtm'].ap(), pv2.ap()), r=[pv2], w=[B['kdtm']])
            rel(bs[g])

    def ph_F(hs, SL, full, cur):
        pds = []
        for g, h in enumerate(hs):
            B = SL[g]
            banks = alloc(2)
            pd = []
            for c in range(8):
                s_, hf = c // 2, c % 2
                o = PS(banks[hf], s_ * 128, 128)
                l = B['kdtm'].sub(s_ * 128, 128)
                rr = B['vh'].sub(s_ * 128, 128)
                mm(o.ap(), l.ap(p0=hf * 64, np_=64), rr.ap(p0=hf * 64, np_=64), True, True, r=[l, rr], w=[o])
                pd.append(o)
            pds.append((pd, banks))
        for c in range(8):
            for g, h in enumerate(hs):
                B = SL[g]
                pd, banks = pds[g]
                Sin_ = (S32 if c % 2 == 0 else S32b).sub(h * 128, 128)
                Sh = (S32b if c % 2 == 0 else S32).sub(h * 128, 128)
                ebl = col(B['tb'], c * 64 + 63)
                S.op('dve', 'scalar_tensor_tensor', (Sh.ap(), Sin_.ap(), ebl.ap(), pd[c].ap(), ALU.mult, ALU.add),
                     r=[Sin_, ebl, pd[c]], w=[Sh])
                if c < 7:
                    if not full:
                        continue
                    dst = B['Sall'].sub(c * 128, 128)
                else:
                    dst = Sc[1 - cur].sub(h * 128, 128)
                if c % 2 == 0:
                    act(dst.ap(), Sh.ap(), AF.Copy, r=[Sh], w=[dst])
                else:
                    S.op('pool', 'tensor_copy', (dst.ap(), Sh.ap()), r=[Sh], w=[dst])
        for g in range(len(hs)):
            rel(*pds[g][1])

    def ph_G1(hs, SL, cur, st):
        maskv = cst.sub(CC_MASK, 128)
        bss = []
        for g, h in enumerate(hs):
            B = SL[g]
            bs = alloc()
            bss.append(bs)
            for s in range(4):
                o = PS(bs, s * 128, 128)
                l = B['kb'].sub(s * 128, 128)
                rr = B['qb'].sub(s * 128, 128)
                mm(o.ap(), l.ap(), rr.ap(), True, True, r=[l, rr], w=[o])
        for g, h in enumerate(hs):
            psS = PS(bss[g])
            tt('dve', scm[g].ap([[128, 4], [1, 128]]), psS.ap([[128, 4], [1, 128]]), maskv.ap([[0, 4], [1, 128]]),
               ALU.mult, r=[psS, maskv], w=[scm[g]])
            rel(bss[g])

    def ph_G2(hs, SL, cur, st):
        bos = []
        for g, h in enumerate(hs):
            B = SL[g]
            bo = alloc()
            bos.append(bo)
            for s in range(4):
                o = PS(bo, s * 128, 128)
                l = B['vh'].sub(s * 128, 128)
                rr = scm[g].sub(s * 128, 128)
                mm(o.ap(), l.ap(), rr.ap(), True, False, r=[l, rr], w=[o])
                for hf in range(2):
                    c = 2 * s + hf
                    Sin = Sc[cur].sub(h * 128, 128) if c == 0 else B['Sall'].sub((c - 1) * 128, 128)
                    o2 = PS(bo, s * 128 + hf * 64, 64)
                    rr2 = B['qb'].sub(c * 64, 64)
                    mm(o2.ap(), Sin.ap(), rr2.ap(), False, hf == 1, r=[Sin, rr2], w=[o2])
        for g, h in enumerate(hs):
            psO = PS(bos[g])
            act(osq[g].ap(), psO.ap(), AF.Square, r=[psO], w=[osq[g]])
        st['bos'] = bos

    def ph_G3(hs, SL, cur, st):
        bms = []
        for g, h in enumerate(hs):
            bm = alloc()
            bms.append(bm)
            psM = PS(bm)
            mm(psM.ap(), onesb.ap(), osq[g].ap(), True, True, r=[onesb, osq[g]], w=[psM])
        for g, h in enumerate(hs):
            psM = PS(bms[g])
            act(rsv[g].ap(), psM.ap(), AF.Ln, r=[psM, epsc], w=[rsv[g]], bias=epsc.ap(), scale=1.0 / 128)
            rel(bms[g])
            act(rsv[g].ap(), rsv[g].ap(), AF.Exp, r=[rsv[g]], w=[rsv[g]], scale=-0.5)

    def ph_G4(hs, SL, cur, st):
        hg = col(par, PC_HG)
        bos = st['bos']
        for g, h in enumerate(hs):
            B = SL[g]
            psO = PS(bos[g])
            tt('dve', rsv[g].ap(), psO.ap(), rsv[g].ap(), ALU.mult, r=[psO, rsv[g]], w=[rsv[g]])
            rel(bos[g])
            oh = o_a.sub(h * T, T)
            S.op('dve', 'scalar_tensor_tensor', (oh.ap(), rsv[g].ap(), hg.ap(), B['sg'].ap(), ALU.mult, ALU.mult),
                 r=[rsv[g], hg, B['sg']], w=[oh])

    onec = col(stat, 13)
    eblv = sb("eblv", 8, F32)

    def pre_P(hs):
        return ph_P(hs, False)

    def pre_A(hs, SL, prs):
        for g, h in enumerate(hs):
            B = SL[g]
            act(B['ta'].ap(), prs[g]['z'].ap(), AF.Sigmoid, r=[prs[g]['z']], w=[B['ta']])
            S.op('dve', 'tensor_copy', (B['vh'].ap(), prs[g]['v'].ap()), r=[prs[g]['v']], w=[B['vh']])
            rel(prs[g]['z'].bank, prs[g]['v'].bank)
        for g, h in enumerate(hs):
            B = SL[g]
            lb, oml = col(lbv, h), col(lbv, 8 + h)
            act(B['tb'].ap(), B['ta'].ap(), AF.Ln, r=[B['ta'], lb, oml], w=[B['tb']], bias=lb.ap(), scale=oml.ap())

    def pre_BCD(hs, SL):
        for g, h in enumerate(hs):
            B = SL[g]
            S.op('dve', 'tensor_tensor_scan', (B['tc'].ap(), onec.ap([[0, T]]), B['tb'].ap(), 0.0, ALU.mult, ALU.add),
                 r=[onec, B['tb']], w=[B['tc']])
            bl = col(B['tc'], T - 1)
            S.op('dve', 'tensor_scalar', (B['tb'].ap(), B['tc'].ap(), -1.0, bl.ap(), ALU.mult, ALU.add),
                 r=[B['tc']], w=[B['tb']])
            oml, noml = col(lbv, 8 + h), col(lbv, 16 + h)
            S.op('dve', 'tensor_scalar', (B['ta'].ap(), B['ta'].ap(), noml.ap(), oml.ap(), ALU.mult, ALU.add),
                 r=[B['ta'], noml, oml], w=[B['ta']])
        for g, h in enumerate(hs):
            B = SL[g]
            act(B['tb'].ap(), B['tb'].ap(), AF.Exp, r=[B['tb']], w=[B['tb']])
            bl = col(B['tc'], T - 1)
            act(col(eblv, h).ap(), bl.ap(), AF.Exp, r=[bl], w=[col(eblv, h)])
        for g, h in enumerate(hs):
            B = SL[g]
            tt('pool', B['kdT'].ap(), B['ta'].ap(), B['tb'].ap(), ALU.mult, r=[B['ta'], B['tb']], w=[B['kdT']])

    def pre_back(hs, SL):
        ph_E(hs, SL)
        bds = []
        for g, h in enumerate(hs):
            B = SL[g]
            b = alloc()
            bds.append(b)
            o = PS(b, 0, 128)
            for s in range(4):
                l = B['kdtm'].sub(s * 128, 128)
                rr = B['vh'].sub(s * 128, 128)
                mm(o.ap(), l.ap(), rr.ap(), s == 0, s == 3, r=[l, rr], w=[o])
        for g, h in enumerate(hs):
            o = PS(bds[g], 0, 128)
            Sh = S32.sub(h * 128, 128)
            S.op('dve', 'scalar_tensor_tensor', (Sh.ap(), Sh.ap(), col(eblv, h).ap(), o.ap(), ALU.mult, ALU.add),
                 r=[Sh, col(eblv, h), o], w=[Sh])
            rel(bds[g])

    def pre_finish(cur):
        for h in range(H):
            Sh = S32.sub(h * 128, 128)
            dst = Sc[cur].sub(h * 128, 128)
            S.op('pool', 'tensor_copy', (dst.ap(), Sh.ap()), r=[Sh], w=[dst])

    def hgrn_stage(full, cur):
        assert full
        groups = [[0, 1], [2, 3], [4, 5], [6, 7]]

        def SLs(p):
            return SLF4[2 * (p % 2):2 * (p % 2) + 2]

        prs = ph_P(groups[0], True)
        ph_A(groups[0], SLs(0), prs, True)
        ph_B(groups[0], SLs(0))
        ph_C(groups[0], SLs(0))
        ph_D(groups[0], SLs(0), True)
        ph_E(groups[0], SLs(0))
        for p, hs in enumerate(groups):
            nxt = groups[p + 1] if p + 1 < len(groups) else None
            st = {}
            if nxt:
                prs = ph_P(nxt, True)
                ph_A(nxt, SLs(p + 1), prs, True)
            ph_F(hs, SLs(p), True, cur)
            ph_G1(hs, SLs(p), cur, st)
            if nxt:
                ph_B(nxt, SLs(p + 1))
            ph_G2(hs, SLs(p), cur, st)
            if nxt:
                ph_C(nxt, SLs(p + 1))
            ph_G3(hs, SLs(p), cur, st)
            if nxt:
                ph_D(nxt, SLs(p + 1), True)
            ph_G4(hs, SLs(p), cur, st)
            if nxt:
                ph_E(nxt, SLs(p + 1))

    def glu_stage():
        for j in range(4):
            wb = load_block(GLU0 + j)
            for mi in range(2):
                m = 2 * j + mi
                pa = proj_fm(wb, mi * 256)
                pb = proj_fm(wb, mi * 256 + 128)
                sg_ = sgate[m % 2]
                act(sg_.ap(), pb.ap(), AF.Sigmoid, r=[pb], w=[sg_])
                um = u.sub(m * UW + CW - 1, T)
                tt('dve', um.ap(), pa.ap(), sg_.ap(), ALU.mult, r=[pa, sg_], w=[um])
                rel(pa.bank, pb.bank)

    def halo_shift():
        S.op('pool', 'tensor_copy', (u.ap([[UW, 8], [1, CW - 1]]), u.ap([[UW, 8], [1, CW - 1]], off=T)),
             r=[u.sub(m * UW + T, CW - 1) for m in range(8)], w=[u.sub(m * UW, CW - 1) for m in range(8)])

    def conv_stage():
        bmean, bex2 = alloc(2)
        pmean, pex2 = PS(bmean), PS(bex2)
        build_diag(0)
        for m in range(8):
            if m + 1 < 8:
                build_diag(m + 1)
            b = alloc()
            py = PS(b)
            for j in range(CW):
                l = diag.sub(((m % 2) * CW + j) * 128, 128)
                rr = u.sub(m * UW + j, T)
                mm(py.ap(), l.ap(), rr.ap(), j == 0, j == CW - 1, r=[l, rr], w=[py])
            cbm = col(par, PC_CB + m)
            ym = y32.sub(m * T, T)
            act(ym.ap(), py.ap(), AF.Identity, r=[py, cbm], w=[ym], bias=cbm.ap())
            yb, yq = ybf[m % 2], ysq[m % 2]
            S.op('dve', 'tensor_copy', (yb.ap(), ym.ap()), r=[ym], w=[yb])
            act(yq.ap(), py.ap(), AF.Square, r=[py, cbm], w=[yq], bias=cbm.ap())
            rel(b)
            mm(pmean.ap(), onesb.ap(), yb.ap(), m == 0, m == 7, r=[onesb, yb], w=[pmean])
            mm(pex2.ap(), onesb.ap(), yq.ap(), m == 0, m == 7, r=[onesb, yq], w=[pex2])
        halo_shift()
        return bmean, bex2, pmean, pex2

    def conv_ln(bmean, bex2, pmean, pex2):
        act(meanv.ap(), pmean.ap(), AF.Copy, r=[pmean], w=[meanv], scale=1.0 / D)
        tt('dve', msqv.ap(), meanv.ap(), meanv.ap(), ALU.mult, r=[meanv], w=[msqv])
        S.op('dve', 'scalar_tensor_tensor', (msqv.ap(), pex2.ap(), 1.0 / D, msqv.ap(), ALU.mult, ALU.subtract),
             r=[pex2, msqv], w=[msqv])
        rel(bmean, bex2)
        act(rs2v.ap(), msqv.ap(), AF.Ln, r=[msqv, epsc], w=[rs2v], bias=epsc.ap())
        act(rs2v.ap(), rs2v.ap(), AF.Exp, r=[rs2v], w=[rs2v], scale=-0.5)
        for m in range(8):
            ym = y32.sub(m * T, T)
            tc_ = tcv[m % 2]
            tt('dve', tc_.ap(), ym.ap(), meanv.ap(), ALU.subtract, r=[ym, meanv], w=[tc_])
            tt('pool', tc_.ap(), tc_.ap(), rs2v.ap(), ALU.mult, r=[tc_, rs2v], w=[tc_])
            lg, lbb = col(par, PC_LG + m), col(par, PC_LB + m)
            yn = ynorm.sub(m * T, T)
            act(yn.ap(), tc_.ap(), AF.Silu, r=[tc_, lg, lbb], w=[yn], bias=lbb.ap(), scale=lg.ap())

    def branch_stage(gate0, w0, src, is_b, js=(0, 1)):
        for j in js:
            wg = load_block(gate0 + j)
            ww = load_block(w0 + j)
            for mi in range(4):
                m = 4 * j + mi
                pg = proj_fm(wg, mi * 128)
                b = alloc()
                po = PS(b)
                for kc in range(8):
                    l = ww.sub(kc * 512 + mi * 128, 128)
                    rr = src.sub(kc * T, T)
                    mm(po.ap(), l.ap(), rr.ap(), kc == 0, kc == 7, r=[l, rr], w=[po])
                sg_ = sgate[m % 2]
                act(sg_.ap(), pg.ap(), AF.Sigmoid, r=[pg], w=[sg_])
                rel(pg.bank)
                ma = mergedA.sub(m * T, T)
                if not is_b:
                    tt('dve', ma.ap(), po.ap(), sg_.ap(), ALU.mult, r=[po, sg_], w=[ma])
                else:
                    tb_ = tbv[m % 2]
                    tt('dve', tb_.ap(), po.ap(), sg_.ap(), ALU.mult, r=[po, sg_], w=[tb_])
                    mg = merged.sub(m * T, T)
                    tt('pool', mg.ap(), tb_.ap(), ma.ap(), ALU.add, r=[tb_, ma], w=[mg])
                rel(b)

    def wout_stage():
        for hf in range(2):
            wb = load_block(WO0 + hf)
            for s in range(4):
                b = alloc()
                po = PS(b)
                for kc in range(8):
                    l = merged.sub(kc * T + s * 128, 128)
                    rr = wb.sub(kc * 512, 512)
                    mm(po.ap(), l.ap(), rr.ap(), kc == 0, kc == 7, r=[l, rr], w=[po])
                xs = x.sub(s * D + hf * 512, 512)
                tt('dve', xs.ap(), po.ap(), xs.ap(), ALU.add, r=[po, xs], w=[xs])
                rel(b)

    def mlp_stage(nxt=None):
        for j in range(8):
            wb = load_block(M10 + j)
            for fi in range(4):
                f = 4 * j + fi
                pz = proj_fm(wb, fi * 128)
                rr_ = r32[f % 2]
                act(rr_.ap(), pz.ap(), AF.Relu, r=[pz], w=[rr_])
                rel(pz.bank)
                zf = z2T.sub(f * T, T)
                tt('pool' if f % 2 == 0 else 'dve', zf.ap(), rr_.ap(), rr_.ap(), ALU.mult, r=[rr_], w=[zf])
        if nxt is not None:
            pf_elem(nxt)
        for hf in range(2):
            if hf == 1 and nxt is not None:
                pf_pe()
            banks = alloc(4)
            for kg in range(4):
                wb = load_block(M20 + hf * 4 + kg)
                for s in range(4):
                    po = PS(banks[s])
                    for kc in range(8):
                        l = z2T.sub((kg * 8 + kc) * T + s * 128, 128)
                        rr = wb.sub(kc * 512, 512)
                        mm(po.ap(), l.ap(), rr.ap(), kg == 0 and kc == 0, kg == 3 and kc == 7, r=[l, rr], w=[po])
            for s in range(4):
                po = PS(banks[s])
                xs = x.sub(s * D + hf * 512, 512)
                tt('dve', xs.ap(), po.ap(), xs.ap(), ALU.add, r=[po, xs], w=[xs])
            rel(*banks)
        if nxt is not None:
            preload(HB0 + 0)
            preload(HB0 + 1)

    xstage = [A(36 * K, D, F32), A(40 * K, D, F32)]
    xn4 = [A((44 + 2 * i) * K, D, BF16) for i in range(4)]

    def pf_elem(t):
        for s in range(4):
            xs = xstage[s % 2]
            dma(xs, dram(x_d[t * 4 + s], "xin", 0, 1), "xs%d" % (s % 2))
            rs = rms_stats(xs, s, xn4[s])
            S.op('dve', 'scalar_tensor_tensor', (xn4[s].ap(), xs.ap(), rs.ap(), gmix.ap(), ALU.mult, ALU.mult), r=[xs, rs, gmix], w=[xn4[s]])

    def pf_pe():
        for s in range(4):
            b = alloc()
            xb = xn4[s]
            for kc in range(8):
                o = PSB(b, kc * 128, 128)
                i_ = xb.sub(kc * 128, 128)
                S.op('pe', 'transpose', (o.ap(), i_.ap(), identb.ap()), r=[i_, identb], w=[o])
            pv = PSB(b)
            dst_ap = hT.ap([[512, 8], [1, 128]], off=s * 128)
            src_ap = pv.ap([[128, 8], [1, 128]])
            wl = [hT.sub(kc * 512 + s * 128, 128) for kc in range(8)]
            if s % 2:
                S.op('dve', 'tensor_copy', (dst_ap, src_ap), r=[pv], w=wl)
            else:
                act(dst_ap, src_ap, AF.Copy, r=[pv], w=wl)
            rel(b)

    def final_stage(t):
        toks = []
        for s in range(4):
            xs = x.sub(s * D, D)
            rs = rms_stats(xs, s, xnF[s % 2])
            S.op('dve', 'scalar_tensor_tensor', (xs.ap(), xs.ap(), rs.ap(), gfin.ap(), ALU.mult, ALU.mult),
                 r=[xs, rs, gfin], w=[xs])
            toks.append(dma(dram(y_d[t * 4 + s], "y_d", t * 4 + s, t * 4 + s + 1), xs, "yo%d" % s))
        return toks

    def load_x(src, t):
        for s in range(4):
            dma(x.sub(s * D, D), dram(src[t * 4 + s], "xin", 0, 1), "x%d" % s)

    last_tok = []
    S.op('dve', 'memset', (onec.ap(), 1.0), w=[onec])
    for h in range(8):
        cast_group([HB0 + h], "cb%d" % h)
    later = [([8, 9, 10, 11], "cg1"), ([12, 13, 16, 17], "cg2"), ([14, 15, 18, 19, 20, 21], "cg3"),
             (list(range(22, 30)), "cg4"), (list(range(30, 38)), "cg5")]
    load_wpre()
    load_x(xp_d, 0)
    NG = 2 * NT

    def grp(k):
        half = k % 2
        return [4 * half + i for i in range(4)], SLP[4 * (k % 2):4 * (k % 2) + 4]

    def tile_start(k):
        if k % 2 == 0:
            t = k // 2
            rmsnorm_T(gmix, xnP)
            if t + 1 < NT:
                load_x(xp_d, t + 1)
            else:
                load_x(x_d, 0)
            if later and (t >= 1 or NT == 1):
                cast_group(*later.pop(0))

    tile_start(0)
    prs_k = pre_P(grp(0)[0])
    pre_A(grp(0)[0], grp(0)[1], prs_k)
    for k in range(NG):
        hs, SLg = grp(k)
        if k + 1 < NG:
            tile_start(k + 1)
            prs_n = pre_P(grp(k + 1)[0])
        pre_BCD(hs, SLg)
        if k + 1 < NG:
            pre_A(grp(k + 1)[0], grp(k + 1)[1], prs_n)
        pre_back(hs, SLg)
    while later:
        cast_group(*later.pop(0))
    glu_stage()
    halo_shift()
    cur = 0
    pre_finish(cur)
    for t in range(NT):
        if t == 0:
            rmsnorm_T(gmix, xnF)
        hgrn_stage(True, cur)
        cur = 1 - cur
        glu_stage()
        cs_ = conv_stage()
        branch_stage(GA0, WA0, o_a, False, js=(0,))
        conv_ln(*cs_)
        branch_stage(GA0, WA0, o_a, False, js=(1,))
        branch_stage(GB0, WB0, ynorm, True)
        wout_stage()
        rmsnorm_T(gmlp, xnF)
        mlp_stage(t + 1 if t + 1 < NT else None)
        last_tok = final_stage(t)
        if t + 1 < NT:
            load_x(x_d, t + 1)
    assert len(free_banks) == 8
    S.emit(nc, es, list(last_tok))
    es.close()
    S.stats = {e: len(S.ops[e]) for e in ENGS}
    print('ops per engine:', S.stats, flush=True)
    return nc


_CACHE = {}


def _prep_weights(inp):
    w_in = np.asarray(inp["w_in"][0], dtype=np.float32)
    blocks = []

    def blk(w, cols):
        kt = w.shape[0] // 128
        b = w[:, cols].reshape(kt, 128, len(cols)).transpose(1, 0, 2)
        return np.ascontiguousarray(b).reshape(128, kt * len(cols))

    ar = np.arange
    for h in range(8):
        cols = np.concatenate([1024 + h * 128 + ar(128), 2048 + h * 128 + ar(128), h * 128 + ar(128), 3072 + h * 128 + ar(128)])
        blocks.append(blk(w_in, cols))
    for j in range(4):
        m0, m1 = 2 * j, 2 * j + 1
        cols = np.concatenate([4096 + m0 * 128 + ar(128), 5120 + m0 * 128 + ar(128),
                               4096 + m1 * 128 + ar(128), 5120 + m1 * 128 + ar(128)])
        blocks.append(blk(w_in, cols))
    for j in range(2):
        blocks.append(blk(w_in, 6144 + j * 512 + ar(512)))
    for j in range(2):
        blocks.append(blk(w_in, 7168 + j * 512 + ar(512)))
    for name in ("w_a_out", "w_b_out", "w_out"):
        w = np.asarray(inp[name][0], dtype=np.float32)
        for j in range(2):
            blocks.append(blk(w, j * 512 + ar(512)))
    w1 = np.asarray(inp["w_mlp_in"][0], dtype=np.float32)
    for j in range(8):
        blocks.append(blk(w1, j * 512 + ar(512)))
    w2 = np.asarray(inp["w_mlp_out"][0], dtype=np.float32)
    for hf in range(2):
        for kg in range(4):
            blocks.append(blk(w2[kg * 1024:(kg + 1) * 1024], hf * 512 + ar(512)))
    wall = np.stack(blocks).astype(np.float32)
    assert wall.shape == (NBLK, 128, 4096)

    def fm(v):
        return np.asarray(v, dtype=np.float32).reshape(8, 128).T

    par = np.zeros((128, NPAR), np.float32)
    par[:, PC_GM:PC_GM + 8] = fm(inp["norm_mix_g"][0])
    par[:, PC_GML:PC_GML + 8] = fm(inp["norm_mlp_g"][0])
    par[:, PC_P0:PC_P0 + 8] = fm(inp["lb_param"][0])
    par[:, PC_P1:PC_P1 + 8] = fm(inp["lb_param"][1])
    par[:, PC_HG] = np.asarray(inp["hgrn_norm_g"][0], dtype=np.float32)
    par[:, PC_CB:PC_CB + 8] = fm(inp["conv_b"][0])
    par[:, PC_LG:PC_LG + 8] = fm(inp["conv_ln_g"][0])
    par[:, PC_LB:PC_LB + 8] = fm(inp["conv_ln_b"][0])
    cw = np.asarray(inp["conv_w"][0], dtype=np.float32)
    par[:, PC_CW:] = cw.reshape(CW, 8, 128).transpose(2, 1, 0).reshape(128, 8 * CW)
    gfin = np.ascontiguousarray(np.broadcast_to(np.asarray(inp["norm_final_g"], dtype=np.float32)[None, :], (128, D)))
    gmix = np.ascontiguousarray(np.broadcast_to(np.asarray(inp["norm_mix_g"][0], dtype=np.float32)[None, :], (128, D)))
    gmlp = np.ascontiguousarray(np.broadcast_to(np.asarray(inp["norm_mlp_g"][0], dtype=np.float32)[None, :], (128, D)))
    cst = np.zeros((128, NCST), np.float32)
    cst[:, CC_ID:CC_ID + 128] = np.eye(128, dtype=np.float32)
    sp, tt = np.meshgrid(ar(128), ar(128), indexing="ij")
    cst[:, CC_MASK:CC_MASK + 128] = ((sp // 64 == tt // 64) & (sp <= tt)).astype(np.float32)
    cst[:, CC_SCAN:] = (ar(T) % 64 != 0).astype(np.float32)[None, :]
    return wall, par, gfin, cst, gmix, gmlp


def kernel(**inp):
    xfull = np.asarray(inp["x"], dtype=np.float32)
    B, Sq, _ = xfull.shape
    half = Sq // 2
    NT = half // T
    assert half % T == 0 and B * 2 == 8
    if NT not in _CACHE:
        _CACHE[NT] = build(NT)
    nc = _CACHE[NT]
    wall, par, gfin, cst, gmix, gmlp = _prep_weights(inp)
    in_maps = []
    zeros = np.zeros((NT * 4, 128, D), np.float32)
    for c in range(8):
        b, hf = c // 2, c % 2
        xc = np.ascontiguousarray(xfull[b, hf * half:(hf + 1) * half]).reshape(NT * 4, 128, D)
        xp = zeros if hf == 0 else np.ascontiguousarray(xfull[b, 0:half]).reshape(NT * 4, 128, D)
        in_maps.append({"x": xc, "xprev": xp, "wall": wall, "par": par, "cst": cst, "gfin": gfin, "gmix": gmix, "gmlp": gmlp})
    res = run_bass_kernel_spmd(nc, in_maps, core_ids=list(range(8)))
    out = np.empty((B, Sq, D), np.float32)
    for c in range(8):
        b, hf = c // 2, c % 2
        out[b, hf * half:(hf + 1) * half] = np.asarray(res.results[c]["y"]).reshape(half, D)
    return out
```

```python
import contextlib
import numpy as np
import concourse.bass as bass
import concourse.mybir as mybir
from concourse.bass_utils import run_bass_kernel_spmd

F32 = mybir.dt.float32
BF16 = mybir.dt.bfloat16
AF = mybir.ActivationFunctionType
ALU = mybir.AluOpType

P = 128
D = 1024
H = 8
T = 512
EPS = 1e-6
CW = 31
UW = T + CW - 1
NBLK = 38
HB0, GLU0, GA0, GB0, WA0, WB0, WO0, M10, M20 = 0, 8, 12, 14, 16, 18, 20, 22, 30
PC_GM, PC_GML, PC_P0, PC_P1, PC_HG, PC_CB, PC_LG, PC_LB, PC_CW = 0, 8, 16, 24, 32, 33, 41, 49, 57
NPAR = 57 + 8 * CW
CC_ID, CC_MASK, CC_SCAN = 0, 128, 256
NCST = 256 + T

ENGS = ['pe', 'act', 'dve', 'pool', 'sp']


class V:
    def __init__(s, t, W, name, off, n, esz, ap=None, whole=None):
        s.t, s.W, s.name, s.off, s.n, s.esz = t, W, name, off, n, esz
        s._ap = ap
        s.whole = whole

    def res(s):
        if s.whole is not None:
            return (s.name, 0, s.whole)
        return (s.name, s.off * s.esz, (s.off + s.n) * s.esz)

    def sub(s, a, n):
        assert a + n <= s.n
        return V(s.t, s.W, s.name, s.off + a, n, s.esz)

    def ap(s, dims=None, off=0, p0=0, np_=P):
        if s._ap is not None:
            return s._ap
        if dims is None:
            dims = [[1, s.n - off]]
        return bass.AP(s.t, p0 * s.W + s.off + off, [[s.W, np_]] + [list(d) for d in dims])


class Sched:
    def __init__(s):
        s.ops = {e: [] for e in ENGS}
        s.res = {}
        s.seen = {e: {} for e in ENGS}
        s.dma_cnt = {}
        s.nbank = 0
        s.held = set()

    def bank(s):
        while True:
            b = s.nbank % 8
            s.nbank += 1
            if b not in s.held:
                return b

    def op(s, eng, iname, args, kw=None, r=(), w=(), dma=None):
        idx = len(s.ops[eng])
        deps = {}

        def add(d):
            if deps.get(d[0], -1) < d[1]:
                deps[d[0]] = d[1]

        rres = [v.res() for v in r]
        wres = [v.res() for v in w]
        for (name, lo, hi) in rres:
            R = s.res.setdefault(name, {'w': [], 'r': []})
            for (l, h, d) in R['w']:
                if l < hi and lo < h:
                    add(d)
        for (name, lo, hi) in wres:
            R = s.res.setdefault(name, {'w': [], 'r': []})
            for (l, h, d) in R['w']:
                if l < hi and lo < h:
                    add(d)
            for (l, h, d) in R['r']:
                if l < hi and lo < h:
                    add(d)
        if dma is not None:
            s.dma_cnt[dma] = s.dma_cnt.get(dma, 0) + 1
            tok = (('dma', dma), s.dma_cnt[dma] * 16)
        else:
            tok = (eng, idx)
        for (name, lo, hi) in rres:
            R = s.res[name]
            R['r'] = [x for x in R['r'] if not (x[2][0] == tok[0] and lo <= x[0] and x[1] <= hi)]
            R['r'].append((lo, hi, tok))
        for (name, lo, hi) in wres:
            R = s.res[name]
            R['w'] = [x for x in R['w'] if not (lo <= x[0] and x[1] <= hi)]
            R['r'] = [x for x in R['r'] if not (lo <= x[0] and x[1] <= hi)]
            R['w'].append((lo, hi, tok))
        waits = []
        for src, i in deps.items():
            if src == 'pe' and eng == 'pe':
                continue
            if s.seen[eng].get(src, -1) >= i:
                continue
            s.seen[eng][src] = i
            waits.append((src, i))
        s.ops[eng].append({'name': iname, 'args': args, 'kw': kw or {}, 'waits': waits, 'dma': dma})
        return tok

    def emit(s, nc, es, final_waits):
        sig = {e: set() for e in ENGS}
        for e in ENGS:
            for o in s.ops[e]:
                for (src, i) in o['waits']:
                    if isinstance(src, str):
                        sig[src].add(i)
        cnt = {}
        for e in ENGS:
            c = 0
            m = {}
            for i in range(len(s.ops[e])):
                if i in sig[e]:
                    c += 1
                    m[i] = c
            cnt[e] = m
        sems = {}
        for e in ENGS:
            if cnt[e]:
                sems[e] = es.enter_context(nc.semaphore('s_' + e))
        for k in s.dma_cnt:
            sems[('dma', k)] = es.enter_context(nc.semaphore('d_' + k))
        block = es.enter_context(nc.Block())

        def run(ename, eobj):
            for i, o in enumerate(s.ops[ename]):
                for (src, v) in o['waits']:
                    if isinstance(src, str):
                        eobj.wait_ge(sems[src], cnt[src][v])
                    else:
                        eobj.wait_ge(sems[src], v)
                ins = getattr(eobj, o['name'])(*o['args'], **o['kw'])
                if o['dma'] is not None:
                    ins.then_inc(sems[('dma', o['dma'])], 16)
                elif i in cnt[ename]:
                    ins.then_inc(sems[ename], 1)
            if ename == 'sp':
                for (src, v) in final_waits:
                    eobj.wait_ge(sems[src], v)

        @block.tensor
        def _(e):
            run('pe', e)

        @block.scalar
        def _(e):
            run('act', e)

        @block.vector
        def _(e):
            run('dve', e)

        @block.gpsimd
        def _(e):
            run('pool', e)

        @block.sync
        def _(e):
            run('sp', e)


def build(NT):
    import os as _os
    nc = bass.Bass("TRN2", target_bir_lowering=False)
    x_d = nc.dram_tensor("x", [NT * 4, P, D], F32, kind="ExternalInput").ap()
    xp_d = nc.dram_tensor("xprev", [NT * 4, P, D], F32, kind="ExternalInput").ap()
    w_d = nc.dram_tensor("wall", [NBLK, P, 4096], F32, kind="ExternalInput").ap()
    par_d = nc.dram_tensor("par", [P, NPAR], F32, kind="ExternalInput").ap()
    cst_d = nc.dram_tensor("cst", [P, NCST], F32, kind="ExternalInput").ap()
    gfin_d = nc.dram_tensor("gfin", [P, D], F32, kind="ExternalInput").ap()
    gmix_d = nc.dram_tensor("gmix", [P, D], F32, kind="ExternalInput").ap()
    gmlp_d = nc.dram_tensor("gmlp", [P, D], F32, kind="ExternalInput").ap()
    y_d = nc.dram_tensor("y", [NT * 4, P, D], F32, kind="ExternalOutput").ap()
    wbf_d = nc.dram_tensor("wbf", [NBLK, P, 4096], BF16).ap()

    es = contextlib.ExitStack()
    S = Sched()

    def sb(name, n, dt):
        t = es.enter_context(nc.sbuf_tensor("sb_" + name, [P, n], dt))
        return V(t, n, name, 0, n, 4 if dt == F32 else 2)

    par = sb("par", NPAR, F32)
    lbv = sb("lbv", 32, F32)
    cst = sb("cst", NCST, F32)
    identb = sb("identb", 128, BF16)
    onesb = sb("onesb", 128, BF16)
    gfin = sb("gfin", D, F32)
    gmix = sb("gmix", D, F32)
    gmlp = sb("gmlp", D, F32)
    diag = sb("diag", 2 * CW * 128, BF16)
    S32 = sb("S32", H * 128, F32)
    S32b = sb("S32b", H * 128, F32)
    Sc = [sb("Sc0", H * 128, BF16), sb("Sc1", H * 128, BF16)]
    u = sb("u", 8 * UW, BF16)
    x = sb("x", 4 * D, F32)
    hT = sb("hT", 8 * T, BF16)
    wbuf = [sb("wb%d" % i, 4096, BF16) for i in range(3)]
    stat = sb("stat", 16, F32)
    ARENA = 106 * 1024
    ar_t = es.enter_context(nc.sbuf_tensor("arena", [P, ARENA // 2], BF16))
    ar_f = ar_t.bitcast(F32)

    def A(byte_off, n, dt):
        byte_off = int(byte_off)
        if dt == F32:
            assert byte_off % 4 == 0 and byte_off + 4 * n <= ARENA
            return V(ar_f, ARENA // 4, "arena", byte_off // 4, n, 4)
        assert byte_off % 2 == 0 and byte_off + 2 * n <= ARENA
        return V(ar_t, ARENA // 2, "arena", byte_off // 2, n, 2)

    K = 1024
    o_a = A(0, 8 * T, BF16)
    mergedA = A(8 * K, 8 * T, F32)

    def slot_full(g):
        b = 24 * K + g * 19 * K
        return dict(ta=A(b, T, F32), tb=A(b + 2 * K, T, F32), tc=A(b + 4 * K, T, F32), td=A(b + 6 * K, T, F32),
                    qs=A(b + 8 * K, T, F32), sg=A(b + 10 * K, T, F32), qb=A(b + 12 * K, T, BF16),
                    kb=A(b + 13 * K, T, BF16), kdT=A(b + 14 * K, T, BF16), kdtm=A(b + 15 * K, T, BF16),
                    vh=A(b + 16 * K, T, BF16), Sall=A(b + 17 * K, 7 * 128, BF16))

    def slot_pre(g):
        b = g * 8 * K
        return dict(ta=A(b, T, F32), tb=A(b + 2 * K, T, F32), tc=A(b + 4 * K, T, F32),
                    kdT=A(b + 6 * K, T, BF16), kdtm=A(b + 6 * K, T, BF16), vh=A(b + 7 * K, T, BF16))

    SLF4 = [slot_full(g) for g in range(4)]
    wpre = A(64 * K, 8 * 2048, BF16)
    SLP = [slot_pre(g) for g in range(8)]
    osq = [A(8 * K, T, BF16), A(9 * K, T, BF16)]
    scm = [A(10 * K, T, BF16), A(11 * K, T, BF16)]
    rsv = [A(100 * K, T, F32), A(102 * K, T, F32)]
    xnF = [A(12 * K, D, BF16), A(14 * K, D, BF16)]
    xnP = [A(96 * K, D, BF16), A(98 * K, D, BF16)]
    ynorm = A(70 * K, 8 * T, BF16)
    y32 = A(24 * K, 8 * T, F32)
    ybf = [A(40 * K, T, BF16), A(41 * K, T, BF16)]
    ysq = [A(42 * K, T, BF16), A(43 * K, T, BF16)]
    meanv = A(44 * K, T, F32)
    msqv = A(46 * K, T, F32)
    rs2v = A(48 * K, T, F32)
    tcv = [A(50 * K, T, F32), A(52 * K, T, F32)]
    sgate = [A(54 * K, T, F32), A(56 * K, T, F32)]
    tbv = [A(58 * K, T, F32), A(60 * K, T, F32)]
    merged = A(62 * K, 8 * T, BF16)
    z2T = A(0, 32 * T, BF16)
    r32 = [A(32 * K, T, F32), A(34 * K, T, F32)]
    ps_t = [es.enter_context(nc.psum_tensor("ps%d" % i, [P, 512], F32)) for i in range(8)]
    ps_b = [t.bitcast(BF16) for t in ps_t]
    free_banks = list(range(8))

    def alloc(n=1):
        assert len(free_banks) >= n, "out of PSUM banks"
        out = [free_banks.pop(0) for _ in range(n)]
        return out if n > 1 else out[0]

    def rel(*bs):
        for b in bs:
            assert b not in free_banks
            free_banks.append(b)

    def PS(b, off=0, n=512):
        return V(ps_t[b], 512, "ps%d" % b, off, n, 4, whole=2048)

    def PSB(b, off=0, n=1024):
        return V(ps_b[b], 1024, "ps%d" % b, off, n, 2, whole=2048)

    def dram(apx, name, lo, hi):
        return V(None, 0, name, lo, hi - lo, 1, ap=apx)

    def mm(out, lhsT, rhs, start, stop, r, w):
        S.op('pe', 'matmul', (out, lhsT, rhs), dict(start=start, stop=stop), r=r, w=w)

    def act(out_ap, in_ap, func, r, w, bias=None, scale=None, accum=None):
        kw = {}
        if bias is not None:
            kw['bias'] = bias
        if scale is not None:
            kw['scale'] = scale
        if accum is not None:
            kw['accum_out'] = accum
        S.op('act', 'activation', (out_ap, in_ap, func), kw, r=r, w=w)

    def tt(eng, out, in0, in1, op, r, w):
        S.op(eng, 'tensor_tensor', (out, in0, in1, op), r=r, w=w)

    def dma(out_v, in_v, key, eng='sp'):
        return S.op(eng, 'dma_start', (), dict(out=out_v.ap(), in_=in_v.ap()), r=[in_v], w=[out_v], dma=key)

    wslot = [0]

    preloaded = {}

    def preload(b):
        preloaded[b] = load_block(b)

    def load_block(b):
        if b in preloaded:
            return preloaded.pop(b)
        k = wslot[0] % 3
        wslot[0] += 1
        dma(wbuf[k], dram(wbf_d[b], "wbf", b, b + 1), "w%d" % k)
        return wbuf[k]

    def col(v, c):
        return v.sub(c, 1)

    epsc = col(stat, 12)

    dma(par, dram(par_d, "par_d", 0, 1), "c0")
    dma(cst, dram(cst_d, "cst_d", 0, 1), "c1")
    dma(gfin, dram(gfin_d, "gfin_d", 0, 1), "c2")
    dma(gmix, dram(gmix_d, "gmix_d", 0, 1), "c3")
    dma(gmlp, dram(gmlp_d, "gmlp_d", 0, 1), "c4")
    S.op('dve', 'memset', (epsc.ap(), EPS), w=[epsc])
    dv = lbv.sub(24, 8)
    tt('dve', dv.ap(), par.sub(PC_P0, 8).ap(), par.sub(PC_P1, 8).ap(), ALU.subtract, r=[par], w=[dv])
    act(lbv.sub(0, 8).ap(), dv.ap(), AF.Sigmoid, r=[dv], w=[lbv.sub(0, 8)])
    act(lbv.sub(8, 8).ap(), dv.ap(), AF.Sigmoid, r=[dv], w=[lbv.sub(8, 8)], scale=-1.0)
    S.op('dve', 'tensor_scalar', (lbv.sub(16, 8).ap(), lbv.sub(8, 8).ap(), -1.0, None, ALU.mult),
         r=[lbv.sub(8, 8)], w=[lbv.sub(16, 8)])
    S.op('dve', 'tensor_copy', (identb.ap(), cst.sub(CC_ID, 128).ap()), r=[cst], w=[identb])
    S.op('dve', 'memset', (onesb.ap(), 1.0), w=[onesb])
    S.op('dve', 'memset', (S32.ap(), 0.0), w=[S32])
    S.op('pool', 'memset', (u.ap(), 0.0), w=[u])
    identf = cst.sub(CC_ID, 128)

    def build_diag(m):
        for j in range(CW):
            dst = diag.sub(((m % 2) * CW + j) * 128, 128)
            sc = col(par, PC_CW + m * CW + j)
            e3 = j % 3
            if e3 == 0:
                S.op('pool', 'tensor_scalar', (dst.ap(), identf.ap(), sc.ap(), 0.0, ALU.mult, ALU.add), r=[identf, sc], w=[dst])
            elif e3 == 1:
                act(dst.ap(), identf.ap(), AF.Copy, r=[identf, sc], w=[dst], scale=sc.ap())
            else:
                S.op('dve', 'tensor_scalar', (dst.ap(), identf.ap(), sc.ap(), 0.0, ALU.mult, ALU.add), r=[identf, sc], w=[dst])

    def cast_group(blocks, key):
        for b in blocks:
            srcv = dram(w_d[b], "w_d", b, b + 1)
            dstv = dram(wbf_d[b], "wbf", b, b + 1)
            S.op('pool', 'dma_start', (), dict(out=dstv.ap(), in_=srcv.ap()), r=[srcv], w=[dstv], dma=key)
        final = (('dma', key), S.dma_cnt[key] * 16)
        R = S.res["wbf"]
        R['w'] = [(lo, hi, final if tk[0] == ('dma', key) else tk) for (lo, hi, tk) in R['w']]

    def load_wpre():
        for h in range(8):
            srcap = wbf_d[HB0 + h].rearrange("p (k c) -> p k c", c=512)[:, :, 0:256]
            dstv = wpre.sub(h * 2048, 2048)
            S.op('sp', 'dma_start', (), dict(out=dstv.ap([[256, 8], [1, 256]]), in_=srcap),
                 r=[dram(srcap, "wbf", HB0 + h, HB0 + h + 1)], w=[dstv], dma="wp%d" % h)

    def rms_stats(xs, s, junkv):
        ss, ln_, rs = col(stat, s), col(stat, 4 + s), col(stat, 8 + s)
        act(junkv.ap(), xs.ap(), AF.Square, r=[xs], w=[junkv, ss], accum=ss.ap())
        act(ln_.ap(), ss.ap(), AF.Ln, r=[ss, epsc], w=[ln_], bias=epsc.ap(), scale=1.0 / D)
        act(rs.ap(), ln_.ap(), AF.Exp, r=[ln_], w=[rs], scale=-0.5)
        return rs

    def rmsnorm_sub(s, gv, xn):
        xs = x.sub(s * D, D)
        b = alloc()
        xb = xn[s % 2]
        rs = rms_stats(xs, s, xb)
        S.op('dve', 'scalar_tensor_tensor', (xb.ap(), xs.ap(), rs.ap(), gv.ap(), ALU.mult, ALU.mult), r=[xs, rs, gv], w=[xb])
        for kc in range(8):
            o = PSB(b, kc * 128, 128)
            i_ = xb.sub(kc * 128, 128)
            S.op('pe', 'transpose', (o.ap(), i_.ap(), identb.ap()), r=[i_, identb], w=[o])
        pv = PSB(b)
        dst_ap = hT.ap([[512, 8], [1, 128]], off=s * 128)
        src_ap = pv.ap([[128, 8], [1, 128]])
        wl = [hT.sub(kc * 512 + s * 128, 128) for kc in range(8)]
        if s % 2:
            S.op('dve', 'tensor_copy', (dst_ap, src_ap), r=[pv], w=wl)
        else:
            act(dst_ap, src_ap, AF.Copy, r=[pv], w=wl)
        rel(b)

    def rmsnorm_T(gv, xn):
        for s in range(4):
            rmsnorm_sub(s, gv, xn)

    def proj_fm(wb, coff, WST=512):
        b = alloc()
        o = PS(b)
        for kc in range(8):
            l = wb.sub(kc * WST + coff, 128)
            rr = hT.sub(kc * 512, 512)
            mm(o.ap(), l.ap(), rr.ap(), kc == 0, kc == 7, r=[l, rr], w=[o])
        o.bank = b
        return o

    def proj_v(wb, WST=512):
        b = alloc()
        for s in range(4):
            o = PS(b, s * 128, 128)
            for kc in range(8):
                l = hT.sub(kc * 512 + s * 128, 128)
                rr = wb.sub(kc * WST + 128, 128)
                mm(o.ap(), l.ap(), rr.ap(), kc == 0, kc == 7, r=[l, rr], w=[o])
        o = PS(b)
        o.bank = b
        return o

    def ph_P(hs, full):
        prs = []
        for h in hs:
            pr = {}
            if full:
                wb = load_block(HB0 + h)
                pr['z'] = proj_fm(wb, 0)
                pr['v'] = proj_v(wb)
            else:
                wb = wpre.sub(h * 2048, 2048)
                pr['z'] = proj_fm(wb, 0, 256)
                pr['v'] = proj_v(wb, 256)
            if full:
                pr['q'] = proj_fm(wb, 256)
                pr['g'] = proj_fm(wb, 384)
            prs.append(pr)
        return prs

    def ph_A(hs, SL, prs, full):
        if full:
            for g, h in enumerate(hs):
                B = SL[g]
                act(B['qs'].ap(), prs[g]['q'].ap(), AF.Silu, r=[prs[g]['q']], w=[B['qs']])
                act(B['sg'].ap(), prs[g]['g'].ap(), AF.Silu, r=[prs[g]['g']], w=[B['sg']])
                rel(prs[g]['q'].bank, prs[g]['g'].bank)
        for g, h in enumerate(hs):
            B = SL[g]
            act(B['ta'].ap(), prs[g]['z'].ap(), AF.Sigmoid, r=[prs[g]['z']], w=[B['ta']])
            S.op('dve', 'tensor_copy', (B['vh'].ap(), prs[g]['v'].ap()), r=[prs[g]['v']], w=[B['vh']])
            rel(prs[g]['z'].bank, prs[g]['v'].bank)
        for g, h in enumerate(hs):
            B = SL[g]
            lb, oml = col(lbv, h), col(lbv, 8 + h)
            act(B['tb'].ap(), B['ta'].ap(), AF.Ln, r=[B['ta'], lb, oml], w=[B['tb']], bias=lb.ap(), scale=oml.ap())

    def ph_B(hs, SL):
        scanm = cst.sub(CC_SCAN, T)
        for g, h in enumerate(hs):
            B = SL[g]
            S.op('dve', 'tensor_tensor_scan', (B['tc'].ap(), scanm.ap(), B['tb'].ap(), 0.0, ALU.mult, ALU.add),
                 r=[scanm, B['tb']], w=[B['tc']])
        for g, h in enumerate(hs):
            B = SL[g]
            oml, noml = col(lbv, 8 + h), col(lbv, 16 + h)
            S.op('dve', 'tensor_scalar', (B['ta'].ap(), B['ta'].ap(), noml.ap(), oml.ap(), ALU.mult, ALU.add),
                 r=[B['ta'], noml, oml], w=[B['ta']])

    def ph_C(hs, SL):
        for g, h in enumerate(hs):
            B = SL[g]
            act(B['tb'].ap(), B['tc'].ap(), AF.Exp, r=[B['tc']], w=[B['tb']])
            act(B['td'].ap(), B['tc'].ap(), AF.Exp, r=[B['tc']], w=[B['td']], scale=-1.0)

    def ph_D(hs, SL, full):
        for g, h in enumerate(hs):
            B = SL[g]
            tt('pool', B['td'].ap(), B['ta'].ap(), B['td'].ap(), ALU.mult, r=[B['ta'], B['td']], w=[B['td']])
        for g, h in enumerate(hs):
            B = SL[g]
            tt('dve', B['kdT'].ap([[64, 8], [1, 64]]), B['td'].ap([[64, 8], [1, 64]]),
               B['tb'].ap([[64, 8], [0, 64]], off=63), ALU.mult, r=[B['td'], B['tb']], w=[B['kdT']])
        if full:
            for g, h in enumerate(hs):
                B = SL[g]
                tt('pool', B['qb'].ap(), B['qs'].ap(), B['tb'].ap(), ALU.mult, r=[B['qs'], B['tb']], w=[B['qb']])
                act(B['kb'].ap(), B['td'].ap(), AF.Copy, r=[B['td']], w=[B['kb']])

    def ph_E(hs, SL):
        bs = []
        for g, h in enumerate(hs):
            B = SL[g]
            b = alloc()
            bs.append(b)
            for s in range(4):
                o = PSB(b, s * 128, 128)
                i_ = B['kdT'].sub(s * 128, 128)
                S.op('pe', 'transpose', (o.ap(), i_.ap(), identb.ap()), r=[i_, identb], w=[o])
        for g, h in enumerate(hs):
            B = SL[g]
            pv2 = PSB(bs[g], 0, 512)
            S.op('dve', 'tensor_copy', (B['kdtm'].ap(), pv2.ap()), r=[pv2], w=[B['kdtm']])
            rel(bs[g])

    def ph_F(hs, SL, full, cur):
        pds = []
        for g, h in enumerate(hs):
            B = SL[g]
            banks = alloc(2)
            pd = []
            for c in range(8):
                s_, hf = c // 2, c % 2
                o = PS(banks[hf], s_ * 128, 128)
                l = B['kdtm'].sub(s_ * 128, 128)
                rr = B['vh'].sub(s_ * 128, 128)
                mm(o.ap(), l.ap(p0=hf * 64, np_=64), rr.ap(p0=hf * 64, np_=64), True, True, r=[l, rr], w=[o])
                pd.append(o)
            pds.append((pd, banks))
        for c in range(8):
            for g, h in enumerate(hs):
                B = SL[g]
                pd, banks = pds[g]
                Sin_ = (S32 if c % 2 == 0 else S32b).sub(h * 128, 128)
                Sh = (S32b if c % 2 == 0 else S32).sub(h * 128, 128)
                ebl = col(B['tb'], c * 64 + 63)
                S.op('dve', 'scalar_tensor_tensor', (Sh.ap(), Sin_.ap(), ebl.ap(), pd[c].ap(), ALU.mult, ALU.add),
                     r=[Sin_, ebl, pd[c]], w=[Sh])
                if c < 7:
                    if not full:
                        continue
                    dst = B['Sall'].sub(c * 128, 128)
                else:
                    dst = Sc[1 - cur].sub(h * 128, 128)
                if c % 2 == 0:
                    act(dst.ap(), Sh.ap(), AF.Copy, r=[Sh], w=[dst])
                else:
                    S.op('pool', 'tensor_copy', (dst.ap(), Sh.ap()), r=[Sh], w=[dst])
        for g in range(len(hs)):
            rel(*pds[g][1])

    def ph_G1(hs, SL, cur, st):
        maskv = cst.sub(CC_MASK, 128)
        bss = []
        for g, h in enumerate(hs):
            B = SL[g]
            bs = alloc()
            bss.append(bs)
            for s in range(4):
                o = PS(bs, s * 128, 128)
                l = B['kb'].sub(s * 128, 128)
                rr = B['qb'].sub(s * 128, 128)
                mm(o.ap(), l.ap(), rr.ap(), True, True, r=[l, rr], w=[o])
        for g, h in enumerate(hs):
            psS = PS(bss[g])
            tt('dve', scm[g].ap([[128, 4], [1, 128]]), psS.ap([[128, 4], [1, 128]]), maskv.ap([[0, 4], [1, 128]]),
               ALU.mult, r=[psS, maskv], w=[scm[g]])
            rel(bss[g])

    def ph_G2(hs, SL, cur, st):
        bos = []
        for g, h in enumerate(hs):
            B = SL[g]
            bo = alloc()
            bos.append(bo)
            for s in range(4):
                o = PS(bo, s * 128, 128)
                l = B['vh'].sub(s * 128, 128)
                rr = scm[g].sub(s * 128, 128)
                mm(o.ap(), l.ap(), rr.ap(), True, False, r=[l, rr], w=[o])
                for hf in range(2):
                    c = 2 * s + hf
                    Sin = Sc[cur].sub(h * 128, 128) if c == 0 else B['Sall'].sub((c - 1) * 128, 128)
                    o2 = PS(bo, s * 128 + hf * 64, 64)
                    rr2 = B['qb'].sub(c * 64, 64)
                    mm(o2.ap(), Sin.ap(), rr2.ap(), False, hf == 1, r=[Sin, rr2], w=[o2])
        for g, h in enumerate(hs):
            psO = PS(bos[g])
            act(osq[g].ap(), psO.ap(), AF.Square, r=[psO], w=[osq[g]])
        st['bos'] = bos

    def ph_G3(hs, SL, cur, st):
        bms = []
        for g, h in enumerate(hs):
            bm = alloc()
            bms.append(bm)
            psM = PS(bm)
            mm(psM.ap(), onesb.ap(), osq[g].ap(), True, True, r=[onesb, osq[g]], w=[psM])
        for g, h in enumerate(hs):
            psM = PS(bms[g])
            act(rsv[g].ap(), psM.ap(), AF.Ln, r=[psM, epsc], w=[rsv[g]], bias=epsc.ap(), scale=1.0 / 128)
            rel(bms[g])
            act(rsv[g].ap(), rsv[g].ap(), AF.Exp, r=[rsv[g]], w=[rsv[g]], scale=-0.5)

    def ph_G4(hs, SL, cur, st):
        hg = col(par, PC_HG)
        bos = st['bos']
        for g, h in enumerate(hs):
            B = SL[g]
            psO = PS(bos[g])
            tt('dve', rsv[g].ap(), psO.ap(), rsv[g].ap(), ALU.mult, r=[psO, rsv[g]], w=[rsv[g]])
            rel(bos[g])
            oh = o_a.sub(h * T, T)
            S.op('dve', 'scalar_tensor_tensor', (oh.ap(), rsv[g].ap(), hg.ap(), B['sg'].ap(), ALU.mult, ALU.mult),
                 r=[rsv[g], hg, B['sg']], w=[oh])

    onec = col(stat, 13)
    eblv = sb("eblv", 8, F32)

    def pre_P(hs):
        return ph_P(hs, False)

    def pre_A(hs, SL, prs):
        for g, h in enumerate(hs):
            B = SL[g]
            act(B['ta'].ap(), prs[g]['z'].ap(), AF.Sigmoid, r=[prs[g]['z']], w=[B['ta']])
            S.op('dve', 'tensor_copy', (B['vh'].ap(), prs[g]['v'].ap()), r=[prs[g]['v']], w=[B['vh']])
            rel(prs[g]['z'].bank, prs[g]['v'].bank)
        for g, h in enumerate(hs):
            B = SL[g]
            lb, oml = col(lbv, h), col(lbv, 8 + h)
            act(B['tb'].ap(), B['ta'].ap(), AF.Ln, r=[B['ta'], lb, oml], w=[B['tb']], bias=lb.ap(), scale=oml.ap())

    def pre_BCD(hs, SL):
        for g, h in enumerate(hs):
            B = SL[g]
            S.op('dve', 'tensor_tensor_scan', (B['tc'].ap(), onec.ap([[0, T]]), B['tb'].ap(), 0.0, ALU.mult, ALU.add),
                 r=[onec, B['tb']], w=[B['tc']])
            bl = col(B['tc'], T - 1)
            S.op('dve', 'tensor_scalar', (B['tb'].ap(), B['tc'].ap(), -1.0, bl.ap(), ALU.mult, ALU.add),
                 r=[B['tc']], w=[B['tb']])
            oml, noml = col(lbv, 8 + h), col(lbv, 16 + h)
            S.op('dve', 'tensor_scalar', (B['ta'].ap(), B['ta'].ap(), noml.ap(), oml.ap(), ALU.mult, ALU.add),
                 r=[B['ta'], noml, oml], w=[B['ta']])
        for g, h in enumerate(hs):
            B = SL[g]
            act(B['tb'].ap(), B['tb'].ap(), AF.Exp, r=[B['tb']], w=[B['tb']])
            bl = col(B['tc'], T - 1)
            act(col(eblv, h).ap(), bl.ap(), AF.Exp, r=[bl], w=[col(eblv, h)])
        for g, h in enumerate(hs):
            B = SL[g]
            tt('pool', B['kdT'].ap(), B['ta'].ap(), B['tb'].ap(), ALU.mult, r=[B['ta'], B['tb']], w=[B['kdT']])

    def pre_back(hs, SL):
        ph_E(hs, SL)
        bds = []
        for g, h in enumerate(hs):
            B = SL[g]
            b = alloc()
            bds.append(b)
            o = PS(b, 0, 128)
            for s in range(4):
                l = B['kdtm'].sub(s * 128, 128)
                rr = B['vh'].sub(s * 128, 128)
                mm(o.ap(), l.ap(), rr.ap(), s == 0, s == 3, r=[l, rr], w=[o])
        for g, h in enumerate(hs):
            o = PS(bds[g], 0, 128)
            Sh = S32.sub(h * 128, 128)
            S.op('dve', 'scalar_tensor_tensor', (Sh.ap(), Sh.ap(), col(eblv, h).ap(), o.ap(), ALU.mult, ALU.add),
                 r=[Sh, col(eblv, h), o], w=[Sh])
            rel(bds[g])

    def pre_finish(cur):
        for h in range(H):
            Sh = S32.sub(h * 128, 128)
            dst = Sc[cur].sub(h * 128, 128)
            S.op('pool', 'tensor_copy', (dst.ap(), Sh.ap()), r=[Sh], w=[dst])

    def hgrn_stage(full, cur):
        assert full
        groups = [[0, 1], [2, 3], [4, 5], [6, 7]]

        def SLs(p):
            return SLF4[2 * (p % 2):2 * (p % 2) + 2]

        prs = ph_P(groups[0], True)
        ph_A(groups[0], SLs(0), prs, True)
        ph_B(groups[0], SLs(0))
        ph_C(groups[0], SLs(0))
        ph_D(groups[0], SLs(0), True)
        ph_E(groups[0], SLs(0))
        for p, hs in enumerate(groups):
            nxt = groups[p + 1] if p + 1 < len(groups) else None
            st = {}
            if nxt:
                prs = ph_P(nxt, True)
                ph_A(nxt, SLs(p + 1), prs, True)
            ph_F(hs, SLs(p), True, cur)
            ph_G1(hs, SLs(p), cur, st)
            if nxt:
                ph_B(nxt, SLs(p + 1))
            ph_G2(hs, SLs(p), cur, st)
            if nxt:
                ph_C(nxt, SLs(p + 1))
            ph_G3(hs, SLs(p), cur, st)
            if nxt:
                ph_D(nxt, SLs(p + 1), True)
            ph_G4(hs, SLs(p), cur, st)
            if nxt:
                ph_E(nxt, SLs(p + 1))

    def glu_stage():
        for j in range(4):
            wb = load_block(GLU0 + j)
            for mi in range(2):
                m = 2 * j + mi
                pa = proj_fm(wb, mi * 256)
                pb = proj_fm(wb, mi * 256 + 128)
                sg_ = sgate[m % 2]
                act(sg_.ap(), pb.ap(), AF.Sigmoid, r=[pb], w=[sg_])
                um = u.sub(m * UW + CW - 1, T)
                tt('dve', um.ap(), pa.ap(), sg_.ap(), ALU.mult, r=[pa, sg_], w=[um])
                rel(pa.bank, pb.bank)

    def halo_shift():
        S.op('pool', 'tensor_copy', (u.ap([[UW, 8], [1, CW - 1]]), u.ap([[UW, 8], [1, CW - 1]], off=T)),
             r=[u.sub(m * UW + T, CW - 1) for m in range(8)], w=[u.sub(m * UW, CW - 1) for m in range(8)])

    def conv_stage():
        bmean, bex2 = alloc(2)
        pmean, pex2 = PS(bmean), PS(bex2)
        build_diag(0)
        for m in range(8):
            if m + 1 < 8:
                build_diag(m + 1)
            b = alloc()
            py = PS(b)
            for j in range(CW):
                l = diag.sub(((m % 2) * CW + j) * 128, 128)
                rr = u.sub(m * UW + j, T)
                mm(py.ap(), l.ap(), rr.ap(), j == 0, j == CW - 1, r=[l, rr], w=[py])
            cbm = col(par, PC_CB + m)
            ym = y32.sub(m * T, T)
            act(ym.ap(), py.ap(), AF.Identity, r=[py, cbm], w=[ym], bias=cbm.ap())
            yb, yq = ybf[m % 2], ysq[m % 2]
            S.op('dve', 'tensor_copy', (yb.ap(), ym.ap()), r=[ym], w=[yb])
            act(yq.ap(), py.ap(), AF.Square, r=[py, cbm], w=[yq], bias=cbm.ap())
            rel(b)
            mm(pmean.ap(), onesb.ap(), yb.ap(), m == 0, m == 7, r=[onesb, yb], w=[pmean])
            mm(pex2.ap(), onesb.ap(), yq.ap(), m == 0, m == 7, r=[onesb, yq], w=[pex2])
        halo_shift()
        return bmean, bex2, pmean, pex2

    def conv_ln(bmean, bex2, pmean, pex2):
        act(meanv.ap(), pmean.ap(), AF.Copy, r=[pmean], w=[meanv], scale=1.0 / D)
        tt('dve', msqv.ap(), meanv.ap(), meanv.ap(), ALU.mult, r=[meanv], w=[msqv])
        S.op('dve', 'scalar_tensor_tensor', (msqv.ap(), pex2.ap(), 1.0 / D, msqv.ap(), ALU.mult, ALU.subtract),
             r=[pex2, msqv], w=[msqv])
        rel(bmean, bex2)
        act(rs2v.ap(), msqv.ap(), AF.Ln, r=[msqv, epsc], w=[rs2v], bias=epsc.ap())
        act(rs2v.ap(), rs2v.ap(), AF.Exp, r=[rs2v], w=[rs2v], scale=-0.5)
        for m in range(8):
            ym = y32.sub(m * T, T)
            tc_ = tcv[m % 2]
            tt('dve', tc_.ap(), ym.ap(), meanv.ap(), ALU.subtract, r=[ym, meanv], w=[tc_])
            tt('pool', tc_.ap(), tc_.ap(), rs2v.ap(), ALU.mult, r=[tc_, rs2v], w=[tc_])
            lg, lbb = col(par, PC_LG + m), col(par, PC_LB + m)
            yn = ynorm.sub(m * T, T)
            act(yn.ap(), tc_.ap(), AF.Silu, r=[tc_, lg, lbb], w=[yn], bias=lbb.ap(), scale=lg.ap())

    def branch_stage(gate0, w0, src, is_b, js=(0, 1)):
        for j in js:
            wg = load_block(gate0 + j)
            ww = load_block(w0 + j)
            for mi in range(4):
                m = 4 * j + mi
                pg = proj_fm(wg, mi * 128)
                b = alloc()
                po = PS(b)
                for kc in range(8):
                    l = ww.sub(kc * 512 + mi * 128, 128)
                    rr = src.sub(kc * T, T)
                    mm(po.ap(), l.ap(), rr.ap(), kc == 0, kc == 7, r=[l, rr], w=[po])
                sg_ = sgate[m % 2]
                act(sg_.ap(), pg.ap(), AF.Sigmoid, r=[pg], w=[sg_])
                rel(pg.bank)
                ma = mergedA.sub(m * T, T)
                if not is_b:
                    tt('dve', ma.ap(), po.ap(), sg_.ap(), ALU.mult, r=[po, sg_], w=[ma])
                else:
                    tb_ = tbv[m % 2]
                    tt('dve', tb_.ap(), po.ap(), sg_.ap(), ALU.mult, r=[po, sg_], w=[tb_])
                    mg = merged.sub(m * T, T)
                    tt('pool', mg.ap(), tb_.ap(), ma.ap(), ALU.add, r=[tb_, ma], w=[mg])
                rel(b)

    def wout_stage():
        wbs = [load_block(WO0 + 0), load_block(WO0 + 1)]
        for s in range(4):
            for hf in range(2):
                wb = wbs[hf]
                b = alloc()
                po = PS(b)
                for kc in range(8):
                    l = merged.sub(kc * T + s * 128, 128)
                    rr = wb.sub(kc * 512, 512)
                    mm(po.ap(), l.ap(), rr.ap(), kc == 0, kc == 7, r=[l, rr], w=[po])
                xs = x.sub(s * D + hf * 512, 512)
                tt('dve', xs.ap(), po.ap(), xs.ap(), ALU.add, r=[po, xs], w=[xs])
                rel(b)
            if s >= 1:
                rmsnorm_sub(s - 1, gmlp, xnF)
        rmsnorm_sub(3, gmlp, xnF)

    def mlp_stage(nxt=None):
        for j in range(8):
            wb = load_block(M10 + j)
            for fi in range(4):
                f = 4 * j + fi
                pz = proj_fm(wb, fi * 128)
                rr_ = r32[f % 2]
                act(rr_.ap(), pz.ap(), AF.Relu, r=[pz], w=[rr_])
                rel(pz.bank)
                zf = z2T.sub(f * T, T)
                tt('pool' if f % 2 == 0 else 'dve', zf.ap(), rr_.ap(), rr_.ap(), ALU.mult, r=[rr_], w=[zf])
        if nxt is not None:
            pf_elem(nxt)
        for hf in range(2):
            if hf == 1 and nxt is not None:
                pf_pe()
            banks = alloc(4)
            for kg in range(4):
                wb = load_block(M20 + hf * 4 + kg)
                for s in range(4):
                    po = PS(banks[s])
                    for kc in range(8):
                        l = z2T.sub((kg * 8 + kc) * T + s * 128, 128)
                        rr = wb.sub(kc * 512, 512)
                        mm(po.ap(), l.ap(), rr.ap(), kg == 0 and kc == 0, kg == 3 and kc == 7, r=[l, rr], w=[po])
            for s in range(4):
                po = PS(banks[s])
                xs = x.sub(s * D + hf * 512, 512)
                tt('dve', xs.ap(), po.ap(), xs.ap(), ALU.add, r=[po, xs], w=[xs])
            rel(*banks)
        if nxt is not None:
            preload(HB0 + 0)
            preload(HB0 + 1)

    xstage = [A((36 + 4 * i) * K, D, F32) for i in range(4)]
    xn4 = [A((52 + 2 * i) * K, D, BF16) for i in range(4)]

    def pf_elem(t):
        for s in range(4):
            xs = xstage[s]
            dma(xs, dram(x_d[t * 4 + s], "xin", 0, 1), "xs%d" % s)
            rs = rms_stats(xs, s, xn4[s])
            S.op('dve', 'scalar_tensor_tensor', (xn4[s].ap(), xs.ap(), rs.ap(), gmix.ap(), ALU.mult, ALU.mult), r=[xs, rs, gmix], w=[xn4[s]])

    def pf_pe():
        for s in range(4):
            b = alloc()
            xb = xn4[s]
            for kc in range(8):
                o = PSB(b, kc * 128, 128)
                i_ = xb.sub(kc * 128, 128)
                S.op('pe', 'transpose', (o.ap(), i_.ap(), identb.ap()), r=[i_, identb], w=[o])
            pv = PSB(b)
            dst_ap = hT.ap([[512, 8], [1, 128]], off=s * 128)
            src_ap = pv.ap([[128, 8], [1, 128]])
            wl = [hT.sub(kc * 512 + s * 128, 128) for kc in range(8)]
            if s % 2:
                S.op('dve', 'tensor_copy', (dst_ap, src_ap), r=[pv], w=wl)
            else:
                act(dst_ap, src_ap, AF.Copy, r=[pv], w=wl)
            rel(b)

    def final_stage(t):
        toks = []
        for s in range(4):
            xs = x.sub(s * D, D)
            rs = rms_stats(xs, s, xnF[s % 2])
            S.op('dve', 'scalar_tensor_tensor', (xs.ap(), xs.ap(), rs.ap(), gfin.ap(), ALU.mult, ALU.mult),
                 r=[xs, rs, gfin], w=[xs])
            toks.append(dma(dram(y_d[t * 4 + s], "y_d", t * 4 + s, t * 4 + s + 1), xs, "yo%d" % s))
        return toks

    def load_x(src, t):
        for s in range(4):
            dma(x.sub(s * D, D), dram(src[t * 4 + s], "xin", 0, 1), "x%d" % s)

    last_tok = []
    S.op('dve', 'memset', (onec.ap(), 1.0), w=[onec])
    for h in range(8):
        cast_group([HB0 + h], "cb%d" % h)
    later = [([b], "cq%d" % b) for b in [8, 9, 10, 11, 12, 13, 16, 17, 14, 15, 18, 19, 20, 21] + list(range(22, 38))]
    load_wpre()
    load_x(xp_d, 0)
    NG = 2 * NT

    def grp(k):
        half = k % 2
        return [4 * half + i for i in range(4)], SLP[4 * (k % 2):4 * (k % 2) + 4]

    def tile_start(k):
        if k % 2 == 0:
            t = k // 2
            rmsnorm_T(gmix, xnP)
            if t + 1 < NT:
                load_x(xp_d, t + 1)
            else:
                load_x(x_d, 0)

    tile_start(0)
    prs_k = pre_P(grp(0)[0])
    pre_A(grp(0)[0], grp(0)[1], prs_k)
    for k in range(NG):
        hs, SLg = grp(k)
        if k + 1 < NG:
            tile_start(k + 1)
            prs_n = pre_P(grp(k + 1)[0])
        pre_BCD(hs, SLg)
        for _ in range(2):
            if later and (k >= 1 or NT == 1):
                cast_group(*later.pop(0))
        if k + 1 < NG:
            pre_A(grp(k + 1)[0], grp(k + 1)[1], prs_n)
        pre_back(hs, SLg)
    while later:
        cast_group(*later.pop(0))
    glu_stage()
    halo_shift()
    cur = 0
    pre_finish(cur)
    for t in range(NT):
        if t == 0:
            rmsnorm_T(gmix, xnF)
        hgrn_stage(True, cur)
        cur = 1 - cur
        glu_stage()
        cs_ = conv_stage()
        branch_stage(GA0, WA0, o_a, False, js=(0,))
        conv_ln(*cs_)
        branch_stage(GA0, WA0, o_a, False, js=(1,))
        branch_stage(GB0, WB0, ynorm, True)
        wout_stage()
        mlp_stage(t + 1 if t + 1 < NT else None)
        last_tok = final_stage(t)
        if t + 1 < NT:
            load_x(x_d, t + 1)
    assert len(free_banks) == 8
    S.emit(nc, es, list(last_tok))
    es.close()
    S.stats = {e: len(S.ops[e]) for e in ENGS}
    print('ops per engine:', S.stats, flush=True)
    return nc


_CACHE = {}


def _prep_weights(inp):
    w_in = np.asarray(inp["w_in"][0], dtype=np.float32)
    blocks = []

    def blk(w, cols):
        kt = w.shape[0] // 128
        b = w[:, cols].reshape(kt, 128, len(cols)).transpose(1, 0, 2)
        return np.ascontiguousarray(b).reshape(128, kt * len(cols))

    ar = np.arange
    for h in range(8):
        cols = np.concatenate([1024 + h * 128 + ar(128), 2048 + h * 128 + ar(128), h * 128 + ar(128), 3072 + h * 128 + ar(128)])
        blocks.append(blk(w_in, cols))
    for j in range(4):
        m0, m1 = 2 * j, 2 * j + 1
        cols = np.concatenate([4096 + m0 * 128 + ar(128), 5120 + m0 * 128 + ar(128),
                               4096 + m1 * 128 + ar(128), 5120 + m1 * 128 + ar(128)])
        blocks.append(blk(w_in, cols))
    for j in range(2):
        blocks.append(blk(w_in, 6144 + j * 512 + ar(512)))
    for j in range(2):
        blocks.append(blk(w_in, 7168 + j * 512 + ar(512)))
    for name in ("w_a_out", "w_b_out", "w_out"):
        w = np.asarray(inp[name][0], dtype=np.float32)
        for j in range(2):
            blocks.append(blk(w, j * 512 + ar(512)))
    w1 = np.asarray(inp["w_mlp_in"][0], dtype=np.float32)
    for j in range(8):
        blocks.append(blk(w1, j * 512 + ar(512)))
    w2 = np.asarray(inp["w_mlp_out"][0], dtype=np.float32)
    for hf in range(2):
        for kg in range(4):
            blocks.append(blk(w2[kg * 1024:(kg + 1) * 1024], hf * 512 + ar(512)))
    wall = np.stack(blocks).astype(np.float32)
    assert wall.shape == (NBLK, 128, 4096)

    def fm(v):
        return np.asarray(v, dtype=np.float32).reshape(8, 128).T

    par = np.zeros((128, NPAR), np.float32)
    par[:, PC_GM:PC_GM + 8] = fm(inp["norm_mix_g"][0])
    par[:, PC_GML:PC_GML + 8] = fm(inp["norm_mlp_g"][0])
    par[:, PC_P0:PC_P0 + 8] = fm(inp["lb_param"][0])
    par[:, PC_P1:PC_P1 + 8] = fm(inp["lb_param"][1])
    par[:, PC_HG] = np.asarray(inp["hgrn_norm_g"][0], dtype=np.float32)
    par[:, PC_CB:PC_CB + 8] = fm(inp["conv_b"][0])
    par[:, PC_LG:PC_LG + 8] = fm(inp["conv_ln_g"][0])
    par[:, PC_LB:PC_LB + 8] = fm(inp["conv_ln_b"][0])
    cw = np.asarray(inp["conv_w"][0], dtype=np.float32)
    par[:, PC_CW:] = cw.reshape(CW, 8, 128).transpose(2, 1, 0).reshape(128, 8 * CW)
    gfin = np.ascontiguousarray(np.broadcast_to(np.asarray(inp["norm_final_g"], dtype=np.float32)[None, :], (128, D)))
    gmix = np.ascontiguousarray(np.broadcast_to(np.asarray(inp["norm_mix_g"][0], dtype=np.float32)[None, :], (128, D)))
    gmlp = np.ascontiguousarray(np.broadcast_to(np.asarray(inp["norm_mlp_g"][0], dtype=np.float32)[None, :], (128, D)))
    cst = np.zeros((128, NCST), np.float32)
    cst[:, CC_ID:CC_ID + 128] = np.eye(128, dtype=np.float32)
    sp, tt = np.meshgrid(ar(128), ar(128), indexing="ij")
    cst[:, CC_MASK:CC_MASK + 128] = ((sp // 64 == tt // 64) & (sp <= tt)).astype(np.float32)
    cst[:, CC_SCAN:] = (ar(T) % 64 != 0).astype(np.float32)[None, :]
    return wall, par, gfin, cst, gmix, gmlp


def kernel(**inp):
    xfull = np.asarray(inp["x"], dtype=np.float32)
    B, Sq, _ = xfull.shape
    half = Sq // 2
    NT = half // T
    assert half % T == 0 and B * 2 == 8
    if NT not in _CACHE:
        _CACHE[NT] = build(NT)
    nc = _CACHE[NT]
    wall, par, gfin, cst, gmix, gmlp = _prep_weights(inp)
    in_maps = []
    zeros = np.zeros((NT * 4, 128, D), np.float32)
    for c in range(8):
        b, hf = c // 2, c % 2
        xc = np.ascontiguousarray(xfull[b, hf * half:(hf + 1) * half]).reshape(NT * 4, 128, D)
        xp = zeros if hf == 0 else np.ascontiguousarray(xfull[b, 0:half]).reshape(NT * 4, 128, D)
        in_maps.append({"x": xc, "xprev": xp, "wall": wall, "par": par, "cst": cst, "gfin": gfin, "gmix": gmix, "gmlp": gmlp})
    res = run_bass_kernel_spmd(nc, in_maps, core_ids=list(range(8)))
    out = np.empty((B, Sq, D), np.float32)
    for c in range(8):
        b, hf = c // 2, c % 2
        out[b, hf * half:(hf + 1) * half] = np.asarray(res.results[c]["y"]).reshape(half, D)
    return out
```

```python
import contextlib
import numpy as np
import concourse.bass as bass
import concourse.mybir as mybir
from concourse.bass_utils import run_bass_kernel_spmd

F32 = mybir.dt.float32
BF16 = mybir.dt.bfloat16
AF = mybir.ActivationFunctionType
ALU = mybir.AluOpType

P = 128
D = 1024
H = 8
T = 512
EPS = 1e-6
CW = 31
UW = T + CW - 1
NBLK = 38
HB0, GLU0, GA0, GB0, WA0, WB0, WO0, M10, M20 = 0, 8, 12, 14, 16, 18, 20, 22, 30
PC_GM, PC_GML, PC_P0, PC_P1, PC_HG, PC_CB, PC_LG, PC_LB, PC_CW = 0, 8, 16, 24, 32, 33, 41, 49, 57
NPAR = 57 + 8 * CW
CC_ID, CC_MASK, CC_SCAN = 0, 128, 256
NCST = 256 + T

ENGS = ['pe', 'act', 'dve', 'pool', 'sp']


class V:
    def __init__(s, t, W, name, off, n, esz, ap=None, whole=None):
        s.t, s.W, s.name, s.off, s.n, s.esz = t, W, name, off, n, esz
        s._ap = ap
        s.whole = whole

    def res(s):
        if s.whole is not None:
            return (s.name, 0, s.whole)
        return (s.name, s.off * s.esz, (s.off + s.n) * s.esz)

    def sub(s, a, n):
        assert a + n <= s.n
        return V(s.t, s.W, s.name, s.off + a, n, s.esz)

    def ap(s, dims=None, off=0, p0=0, np_=P):
        if s._ap is not None:
            return s._ap
        if dims is None:
            dims = [[1, s.n - off]]
        return bass.AP(s.t, p0 * s.W + s.off + off, [[s.W, np_]] + [list(d) for d in dims])


class Sched:
    def __init__(s):
        s.ops = {e: [] for e in ENGS}
        s.res = {}
        s.seen = {e: {} for e in ENGS}
        s.dma_cnt = {}
        s.nbank = 0
        s.held = set()

    def bank(s):
        while True:
            b = s.nbank % 8
            s.nbank += 1
            if b not in s.held:
                return b

    def op(s, eng, iname, args, kw=None, r=(), w=(), dma=None):
        idx = len(s.ops[eng])
        deps = {}

        def add(d):
            if deps.get(d[0], -1) < d[1]:
                deps[d[0]] = d[1]

        rres = [v.res() for v in r]
        wres = [v.res() for v in w]
        for (name, lo, hi) in rres:
            R = s.res.setdefault(name, {'w': [], 'r': []})
            for (l, h, d) in R['w']:
                if l < hi and lo < h:
                    add(d)
        for (name, lo, hi) in wres:
            R = s.res.setdefault(name, {'w': [], 'r': []})
            for (l, h, d) in R['w']:
                if l < hi and lo < h:
                    add(d)
            for (l, h, d) in R['r']:
                if l < hi and lo < h:
                    add(d)
        if dma is not None:
            s.dma_cnt[dma] = s.dma_cnt.get(dma, 0) + 1
            tok = (('dma', dma), s.dma_cnt[dma] * 16)
        else:
            tok = (eng, idx)
        for (name, lo, hi) in rres:
            R = s.res[name]
            R['r'] = [x for x in R['r'] if not (x[2][0] == tok[0] and lo <= x[0] and x[1] <= hi)]
            R['r'].append((lo, hi, tok))
        for (name, lo, hi) in wres:
            R = s.res[name]
            R['w'] = [x for x in R['w'] if not (lo <= x[0] and x[1] <= hi)]
            R['r'] = [x for x in R['r'] if not (lo <= x[0] and x[1] <= hi)]
            R['w'].append((lo, hi, tok))
        waits = []
        for src, i in deps.items():
            if src == 'pe' and eng == 'pe':
                continue
            if s.seen[eng].get(src, -1) >= i:
                continue
            s.seen[eng][src] = i
            waits.append((src, i))
        s.ops[eng].append({'name': iname, 'args': args, 'kw': kw or {}, 'waits': waits, 'dma': dma})
        return tok

    def emit(s, nc, es, final_waits):
        sig = {e: set() for e in ENGS}
        for e in ENGS:
            for o in s.ops[e]:
                for (src, i) in o['waits']:
                    if isinstance(src, str):
                        sig[src].add(i)
        cnt = {}
        for e in ENGS:
            c = 0
            m = {}
            for i in range(len(s.ops[e])):
                if i in sig[e]:
                    c += 1
                    m[i] = c
            cnt[e] = m
        sems = {}
        for e in ENGS:
            if cnt[e]:
                sems[e] = es.enter_context(nc.semaphore('s_' + e))
        for k in s.dma_cnt:
            sems[('dma', k)] = es.enter_context(nc.semaphore('d_' + k))
        block = es.enter_context(nc.Block())

        def run(ename, eobj):
            for i, o in enumerate(s.ops[ename]):
                for (src, v) in o['waits']:
                    if isinstance(src, str):
                        eobj.wait_ge(sems[src], cnt[src][v])
                    else:
                        eobj.wait_ge(sems[src], v)
                ins = getattr(eobj, o['name'])(*o['args'], **o['kw'])
                if o['dma'] is not None:
                    ins.then_inc(sems[('dma', o['dma'])], 16)
                elif i in cnt[ename]:
                    ins.then_inc(sems[ename], 1)
            if ename == 'sp':
                for (src, v) in final_waits:
                    eobj.wait_ge(sems[src], v)

        @block.tensor
        def _(e):
            run('pe', e)

        @block.scalar
        def _(e):
            run('act', e)

        @block.vector
        def _(e):
            run('dve', e)

        @block.gpsimd
        def _(e):
            run('pool', e)

        @block.sync
        def _(e):
            run('sp', e)


def build(NT):
    import os as _os
    nc = bass.Bass("TRN2", target_bir_lowering=False)
    x_d = nc.dram_tensor("x", [NT * 4, P, D], F32, kind="ExternalInput").ap()
    xp_d = nc.dram_tensor("xprev", [NT * 4, P, D], F32, kind="ExternalInput").ap()
    w_d = nc.dram_tensor("wall", [NBLK, P, 4096], F32, kind="ExternalInput").ap()
    par_d = nc.dram_tensor("par", [P, NPAR], F32, kind="ExternalInput").ap()
    cst_d = nc.dram_tensor("cst", [P, NCST], F32, kind="ExternalInput").ap()
    gfin_d = nc.dram_tensor("gfin", [P, D], F32, kind="ExternalInput").ap()
    gmix_d = nc.dram_tensor("gmix", [P, D], F32, kind="ExternalInput").ap()
    gmlp_d = nc.dram_tensor("gmlp", [P, D], F32, kind="ExternalInput").ap()
    y_d = nc.dram_tensor("y", [NT * 4, P, D], F32, kind="ExternalOutput").ap()
    wbf_d = nc.dram_tensor("wbf", [NBLK, P, 4096], BF16).ap()

    es = contextlib.ExitStack()
    S = Sched()

    def sb(name, n, dt):
        t = es.enter_context(nc.sbuf_tensor("sb_" + name, [P, n], dt))
        return V(t, n, name, 0, n, 4 if dt == F32 else 2)

    par = sb("par", NPAR, F32)
    lbv = sb("lbv", 32, F32)
    cst = sb("cst", NCST, F32)
    identb = sb("identb", 128, BF16)
    onesb = sb("onesb", 128, BF16)
    gfin = sb("gfin", D, F32)
    gmix = sb("gmix", D, F32)
    gmlp = sb("gmlp", D, F32)
    diag = sb("diag", 2 * CW * 128, BF16)
    S32 = sb("S32", H * 128, F32)
    S32b = sb("S32b", H * 128, F32)
    Sc = [sb("Sc0", H * 128, BF16), sb("Sc1", H * 128, BF16)]
    u = sb("u", 8 * UW, BF16)
    x = sb("x", 4 * D, F32)
    hT = sb("hT", 8 * T, BF16)
    wbuf = [sb("wb%d" % i, 4096, BF16) for i in range(3)]
    stat = sb("stat", 16, F32)
    ARENA = 106 * 1024
    ar_t = es.enter_context(nc.sbuf_tensor("arena", [P, ARENA // 2], BF16))
    ar_f = ar_t.bitcast(F32)

    def A(byte_off, n, dt):
        byte_off = int(byte_off)
        if dt == F32:
            assert byte_off % 4 == 0 and byte_off + 4 * n <= ARENA
            return V(ar_f, ARENA // 4, "arena", byte_off // 4, n, 4)
        assert byte_off % 2 == 0 and byte_off + 2 * n <= ARENA
        return V(ar_t, ARENA // 2, "arena", byte_off // 2, n, 2)

    K = 1024
    o_a = A(0, 8 * T, BF16)
    mergedA = A(8 * K, 8 * T, F32)

    def slot_full(g):
        b = 24 * K + g * 19 * K
        return dict(ta=A(b, T, F32), tb=A(b + 2 * K, T, F32), tc=A(b + 4 * K, T, F32), td=A(b + 6 * K, T, F32),
                    qs=A(b + 8 * K, T, F32), sg=A(b + 10 * K, T, F32), qb=A(b + 12 * K, T, BF16),
                    kb=A(b + 13 * K, T, BF16), kdT=A(b + 14 * K, T, BF16), kdtm=A(b + 15 * K, T, BF16),
                    vh=A(b + 16 * K, T, BF16), Sall=A(b + 17 * K, 7 * 128, BF16))

    def slot_pre(g):
        b = g * 8 * K
        return dict(ta=A(b, T, F32), tb=A(b + 2 * K, T, F32), tc=A(b + 4 * K, T, F32),
                    kdT=A(b + 6 * K, T, BF16), kdtm=A(b + 6 * K, T, BF16), vh=A(b + 7 * K, T, BF16))

    SLF4 = [slot_full(g) for g in range(4)]
    wpre = A(64 * K, 8 * 2048, BF16)
    SLP = [slot_pre(g) for g in range(8)]
    osq = [A(8 * K, T, BF16), A(9 * K, T, BF16)]
    scm = [A(10 * K, T, BF16), A(11 * K, T, BF16)]
    rsv = [A(100 * K, T, F32), A(102 * K, T, F32)]
    xnF = [A(12 * K, D, BF16), A(14 * K, D, BF16)]
    xnP = [A(96 * K, D, BF16), A(98 * K, D, BF16)]
    ynorm = A(70 * K, 8 * T, BF16)
    y32 = A(24 * K, 8 * T, F32)
    ybf = [A(40 * K, T, BF16), A(41 * K, T, BF16)]
    ysq = [A(42 * K, T, BF16), A(43 * K, T, BF16)]
    meanv = A(44 * K, T, F32)
    msqv = A(46 * K, T, F32)
    rs2v = A(48 * K, T, F32)
    tcv = [A(50 * K, T, F32), A(52 * K, T, F32)]
    sgate = [A(54 * K, T, F32), A(56 * K, T, F32)]
    tbv = [A(58 * K, T, F32), A(60 * K, T, F32)]
    merged = A(62 * K, 8 * T, BF16)
    z2T = A(0, 32 * T, BF16)
    r32 = [A(32 * K, T, F32), A(34 * K, T, F32)]
    ps_t = [es.enter_context(nc.psum_tensor("ps%d" % i, [P, 512], F32)) for i in range(8)]
    ps_b = [t.bitcast(BF16) for t in ps_t]
    free_banks = list(range(8))

    def alloc(n=1):
        assert len(free_banks) >= n, "out of PSUM banks"
        out = [free_banks.pop(0) for _ in range(n)]
        return out if n > 1 else out[0]

    def rel(*bs):
        for b in bs:
            assert b not in free_banks
            free_banks.append(b)

    def PS(b, off=0, n=512):
        return V(ps_t[b], 512, "ps%d" % b, off, n, 4, whole=2048)

    def PSB(b, off=0, n=1024):
        return V(ps_b[b], 1024, "ps%d" % b, off, n, 2, whole=2048)

    def dram(apx, name, lo, hi):
        return V(None, 0, name, lo, hi - lo, 1, ap=apx)

    def mm(out, lhsT, rhs, start, stop, r, w):
        S.op('pe', 'matmul', (out, lhsT, rhs), dict(start=start, stop=stop), r=r, w=w)

    def act(out_ap, in_ap, func, r, w, bias=None, scale=None, accum=None):
        kw = {}
        if bias is not None:
            kw['bias'] = bias
        if scale is not None:
            kw['scale'] = scale
        if accum is not None:
            kw['accum_out'] = accum
        S.op('act', 'activation', (out_ap, in_ap, func), kw, r=r, w=w)

    def tt(eng, out, in0, in1, op, r, w):
        S.op(eng, 'tensor_tensor', (out, in0, in1, op), r=r, w=w)

    def dma(out_v, in_v, key, eng='sp'):
        return S.op(eng, 'dma_start', (), dict(out=out_v.ap(), in_=in_v.ap()), r=[in_v], w=[out_v], dma=key)

    wslot = [0]

    preloaded = {}

    def preload(b):
        preloaded[b] = load_block(b)

    def load_block(b):
        if b in preloaded:
            return preloaded.pop(b)
        k = wslot[0] % 3
        wslot[0] += 1
        dma(wbuf[k], dram(wbf_d[b], "wbf", b, b + 1), "w%d" % k)
        return wbuf[k]

    def col(v, c):
        return v.sub(c, 1)

    epsc = col(stat, 12)

    dma(par, dram(par_d, "par_d", 0, 1), "c0")
    dma(cst, dram(cst_d, "cst_d", 0, 1), "c1")
    dma(gfin, dram(gfin_d, "gfin_d", 0, 1), "c2")
    dma(gmix, dram(gmix_d, "gmix_d", 0, 1), "c3")
    dma(gmlp, dram(gmlp_d, "gmlp_d", 0, 1), "c4")
    S.op('dve', 'memset', (epsc.ap(), EPS), w=[epsc])
    dv = lbv.sub(24, 8)
    tt('dve', dv.ap(), par.sub(PC_P0, 8).ap(), par.sub(PC_P1, 8).ap(), ALU.subtract, r=[par], w=[dv])
    act(lbv.sub(0, 8).ap(), dv.ap(), AF.Sigmoid, r=[dv], w=[lbv.sub(0, 8)])
    act(lbv.sub(8, 8).ap(), dv.ap(), AF.Sigmoid, r=[dv], w=[lbv.sub(8, 8)], scale=-1.0)
    S.op('dve', 'tensor_scalar', (lbv.sub(16, 8).ap(), lbv.sub(8, 8).ap(), -1.0, None, ALU.mult),
         r=[lbv.sub(8, 8)], w=[lbv.sub(16, 8)])
    S.op('dve', 'tensor_copy', (identb.ap(), cst.sub(CC_ID, 128).ap()), r=[cst], w=[identb])
    S.op('dve', 'memset', (onesb.ap(), 1.0), w=[onesb])
    S.op('dve', 'memset', (S32.ap(), 0.0), w=[S32])
    S.op('pool', 'memset', (u.ap(), 0.0), w=[u])
    identf = cst.sub(CC_ID, 128)

    def build_diag(m):
        for j in range(CW):
            dst = diag.sub(((m % 2) * CW + j) * 128, 128)
            sc = col(par, PC_CW + m * CW + j)
            e3 = j % 3
            if e3 == 0:
                S.op('pool', 'tensor_scalar', (dst.ap(), identf.ap(), sc.ap(), 0.0, ALU.mult, ALU.add), r=[identf, sc], w=[dst])
            elif e3 == 1:
                act(dst.ap(), identf.ap(), AF.Copy, r=[identf, sc], w=[dst], scale=sc.ap())
            else:
                S.op('dve', 'tensor_scalar', (dst.ap(), identf.ap(), sc.ap(), 0.0, ALU.mult, ALU.add), r=[identf, sc], w=[dst])

    def cast_group(blocks, key):
        for b in blocks:
            srcv = dram(w_d[b], "w_d", b, b + 1)
            dstv = dram(wbf_d[b], "wbf", b, b + 1)
            S.op('pool', 'dma_start', (), dict(out=dstv.ap(), in_=srcv.ap()), r=[srcv], w=[dstv], dma=key)
        final = (('dma', key), S.dma_cnt[key] * 16)
        R = S.res["wbf"]
        R['w'] = [(lo, hi, final if tk[0] == ('dma', key) else tk) for (lo, hi, tk) in R['w']]

    def load_wpre():
        for h in range(8):
            srcap = wbf_d[HB0 + h].rearrange("p (k c) -> p k c", c=512)[:, :, 0:256]
            dstv = wpre.sub(h * 2048, 2048)
            S.op('sp', 'dma_start', (), dict(out=dstv.ap([[256, 8], [1, 256]]), in_=srcap),
                 r=[dram(srcap, "wbf", HB0 + h, HB0 + h + 1)], w=[dstv], dma="wp%d" % h)

    def rms_stats(xs, s, junkv):
        ss, ln_, rs = col(stat, s), col(stat, 4 + s), col(stat, 8 + s)
        act(junkv.ap(), xs.ap(), AF.Square, r=[xs], w=[junkv, ss], accum=ss.ap())
        act(ln_.ap(), ss.ap(), AF.Ln, r=[ss, epsc], w=[ln_], bias=epsc.ap(), scale=1.0 / D)
        act(rs.ap(), ln_.ap(), AF.Exp, r=[ln_], w=[rs], scale=-0.5)
        return rs

    def rmsnorm_sub(s, gv, xn):
        xs = x.sub(s * D, D)
        b = alloc()
        xb = xn[s % 2]
        rs = rms_stats(xs, s, xb)
        S.op('dve', 'scalar_tensor_tensor', (xb.ap(), xs.ap(), rs.ap(), gv.ap(), ALU.mult, ALU.mult), r=[xs, rs, gv], w=[xb])
        for kc in range(8):
            o = PSB(b, kc * 128, 128)
            i_ = xb.sub(kc * 128, 128)
            S.op('pe', 'transpose', (o.ap(), i_.ap(), identb.ap()), r=[i_, identb], w=[o])
        pv = PSB(b)
        dst_ap = hT.ap([[512, 8], [1, 128]], off=s * 128)
        src_ap = pv.ap([[128, 8], [1, 128]])
        wl = [hT.sub(kc * 512 + s * 128, 128) for kc in range(8)]
        if s % 2:
            S.op('dve', 'tensor_copy', (dst_ap, src_ap), r=[pv], w=wl)
        else:
            act(dst_ap, src_ap, AF.Copy, r=[pv], w=wl)
        rel(b)

    def rmsnorm_T(gv, xn):
        for s in range(4):
            rmsnorm_sub(s, gv, xn)

    def proj_fm(wb, coff, WST=512):
        b = alloc()
        o = PS(b)
        for kc in range(8):
            l = wb.sub(kc * WST + coff, 128)
            rr = hT.sub(kc * 512, 512)
            mm(o.ap(), l.ap(), rr.ap(), kc == 0, kc == 7, r=[l, rr], w=[o])
        o.bank = b
        return o

    def proj_v(wb, WST=512):
        b = alloc()
        for s in range(4):
            o = PS(b, s * 128, 128)
            for kc in range(8):
                l = hT.sub(kc * 512 + s * 128, 128)
                rr = wb.sub(kc * WST + 128, 128)
                mm(o.ap(), l.ap(), rr.ap(), kc == 0, kc == 7, r=[l, rr], w=[o])
        o = PS(b)
        o.bank = b
        return o

    def ph_P(hs, full):
        prs = []
        for h in hs:
            pr = {}
            if full:
                wb = load_block(HB0 + h)
                pr['z'] = proj_fm(wb, 0)
                pr['v'] = proj_v(wb)
            else:
                wb = wpre.sub(h * 2048, 2048)
                pr['z'] = proj_fm(wb, 0, 256)
                pr['v'] = proj_v(wb, 256)
            if full:
                pr['q'] = proj_fm(wb, 256)
                pr['g'] = proj_fm(wb, 384)
            prs.append(pr)
        return prs

    def ph_A(hs, SL, prs, full):
        if full:
            for g, h in enumerate(hs):
                B = SL[g]
                act(B['qs'].ap(), prs[g]['q'].ap(), AF.Silu, r=[prs[g]['q']], w=[B['qs']])
                act(B['sg'].ap(), prs[g]['g'].ap(), AF.Silu, r=[prs[g]['g']], w=[B['sg']])
                rel(prs[g]['q'].bank, prs[g]['g'].bank)
        for g, h in enumerate(hs):
            B = SL[g]
            act(B['ta'].ap(), prs[g]['z'].ap(), AF.Sigmoid, r=[prs[g]['z']], w=[B['ta']])
            S.op('dve', 'tensor_copy', (B['vh'].ap(), prs[g]['v'].ap()), r=[prs[g]['v']], w=[B['vh']])
            rel(prs[g]['z'].bank, prs[g]['v'].bank)
        for g, h in enumerate(hs):
            B = SL[g]
            lb, oml = col(lbv, h), col(lbv, 8 + h)
            act(B['tb'].ap(), B['ta'].ap(), AF.Ln, r=[B['ta'], lb, oml], w=[B['tb']], bias=lb.ap(), scale=oml.ap())

    def ph_B(hs, SL):
        scanm = cst.sub(CC_SCAN, T)
        for g, h in enumerate(hs):
            B = SL[g]
            S.op('dve', 'tensor_tensor_scan', (B['tc'].ap(), scanm.ap(), B['tb'].ap(), 0.0, ALU.mult, ALU.add),
                 r=[scanm, B['tb']], w=[B['tc']])
        for g, h in enumerate(hs):
            B = SL[g]
            oml, noml = col(lbv, 8 + h), col(lbv, 16 + h)
            S.op('dve', 'tensor_scalar', (B['ta'].ap(), B['ta'].ap(), noml.ap(), oml.ap(), ALU.mult, ALU.add),
                 r=[B['ta'], noml, oml], w=[B['ta']])

    def ph_C(hs, SL):
        for g, h in enumerate(hs):
            B = SL[g]
            act(B['tb'].ap(), B['tc'].ap(), AF.Exp, r=[B['tc']], w=[B['tb']])
            act(B['td'].ap(), B['tc'].ap(), AF.Exp, r=[B['tc']], w=[B['td']], scale=-1.0)

    def ph_D(hs, SL, full):
        for g, h in enumerate(hs):
            B = SL[g]
            tt('pool', B['td'].ap(), B['ta'].ap(), B['td'].ap(), ALU.mult, r=[B['ta'], B['td']], w=[B['td']])
        for g, h in enumerate(hs):
            B = SL[g]
            tt('dve', B['kdT'].ap([[64, 8], [1, 64]]), B['td'].ap([[64, 8], [1, 64]]),
               B['tb'].ap([[64, 8], [0, 64]], off=63), ALU.mult, r=[B['td'], B['tb']], w=[B['kdT']])
        if full:
            for g, h in enumerate(hs):
                B = SL[g]
                tt('pool', B['qb'].ap(), B['qs'].ap(), B['tb'].ap(), ALU.mult, r=[B['qs'], B['tb']], w=[B['qb']])
                act(B['kb'].ap(), B['td'].ap(), AF.Copy, r=[B['td']], w=[B['kb']])

    def ph_E(hs, SL):
        bs = []
        for g, h in enumerate(hs):
            B = SL[g]
            b = alloc()
            bs.append(b)
            for s in range(4):
                o = PSB(b, s * 128, 128)
                i_ = B['kdT'].sub(s * 128, 128)
                S.op('pe', 'transpose', (o.ap(), i_.ap(), identb.ap()), r=[i_, identb], w=[o])
        for g, h in enumerate(hs):
            B = SL[g]
            pv2 = PSB(bs[g], 0, 512)
            S.op('dve', 'tensor_copy', (B['kdtm'].ap(), pv2.ap()), r=[pv2], w=[B['kdtm']])
            rel(bs[g])

    def ph_F(hs, SL, full, cur):
        pds = []
        for g, h in enumerate(hs):
            B = SL[g]
            banks = alloc(2)
            pd = []
            for c in range(8):
                s_, hf = c // 2, c % 2
                o = PS(banks[hf], s_ * 128, 128)
                l = B['kdtm'].sub(s_ * 128, 128)
                rr = B['vh'].sub(s_ * 128, 128)
                mm(o.ap(), l.ap(p0=hf * 64, np_=64), rr.ap(p0=hf * 64, np_=64), True, True, r=[l, rr], w=[o])
                pd.append(o)
            pds.append((pd, banks))
        for c in range(8):
            for g, h in enumerate(hs):
                B = SL[g]
                pd, banks = pds[g]
                Sin_ = (S32 if c % 2 == 0 else S32b).sub(h * 128, 128)
                Sh = (S32b if c % 2 == 0 else S32).sub(h * 128, 128)
                ebl = col(B['tb'], c * 64 + 63)
                S.op('dve', 'scalar_tensor_tensor', (Sh.ap(), Sin_.ap(), ebl.ap(), pd[c].ap(), ALU.mult, ALU.add),
                     r=[Sin_, ebl, pd[c]], w=[Sh])
                if c < 7:
                    if not full:
                        continue
                    dst = B['Sall'].sub(c * 128, 128)
                else:
                    dst = Sc[1 - cur].sub(h * 128, 128)
                if c % 2 == 0:
                    act(dst.ap(), Sh.ap(), AF.Copy, r=[Sh], w=[dst])
                else:
                    S.op('pool', 'tensor_copy', (dst.ap(), Sh.ap()), r=[Sh], w=[dst])
        for g in range(len(hs)):
            rel(*pds[g][1])

    def ph_G1(hs, SL, cur, st):
        maskv = cst.sub(CC_MASK, 128)
        bss = []
        for g, h in enumerate(hs):
            B = SL[g]
            bs = alloc()
            bss.append(bs)
            for s in range(4):
                o = PS(bs, s * 128, 128)
                l = B['kb'].sub(s * 128, 128)
                rr = B['qb'].sub(s * 128, 128)
                mm(o.ap(), l.ap(), rr.ap(), True, True, r=[l, rr], w=[o])
        for g, h in enumerate(hs):
            psS = PS(bss[g])
            tt('dve', scm[g].ap([[128, 4], [1, 128]]), psS.ap([[128, 4], [1, 128]]), maskv.ap([[0, 4], [1, 128]]),
               ALU.mult, r=[psS, maskv], w=[scm[g]])
            rel(bss[g])

    def ph_G2(hs, SL, cur, st):
        bos = []
        for g, h in enumerate(hs):
            B = SL[g]
            bo = alloc()
            bos.append(bo)
            for s in range(4):
                o = PS(bo, s * 128, 128)
                l = B['vh'].sub(s * 128, 128)
                rr = scm[g].sub(s * 128, 128)
                mm(o.ap(), l.ap(), rr.ap(), True, False, r=[l, rr], w=[o])
                for hf in range(2):
                    c = 2 * s + hf
                    Sin = Sc[cur].sub(h * 128, 128) if c == 0 else B['Sall'].sub((c - 1) * 128, 128)
                    o2 = PS(bo, s * 128 + hf * 64, 64)
                    rr2 = B['qb'].sub(c * 64, 64)
                    mm(o2.ap(), Sin.ap(), rr2.ap(), False, hf == 1, r=[Sin, rr2], w=[o2])
        for g, h in enumerate(hs):
            psO = PS(bos[g])
            act(osq[g].ap(), psO.ap(), AF.Square, r=[psO], w=[osq[g]])
        st['bos'] = bos

    def ph_G3(hs, SL, cur, st):
        bms = []
        for g, h in enumerate(hs):
            bm = alloc()
            bms.append(bm)
            psM = PS(bm)
            mm(psM.ap(), onesb.ap(), osq[g].ap(), True, True, r=[onesb, osq[g]], w=[psM])
        for g, h in enumerate(hs):
            psM = PS(bms[g])
            act(rsv[g].ap(), psM.ap(), AF.Ln, r=[psM, epsc], w=[rsv[g]], bias=epsc.ap(), scale=1.0 / 128)
            rel(bms[g])
            act(rsv[g].ap(), rsv[g].ap(), AF.Exp, r=[rsv[g]], w=[rsv[g]], scale=-0.5)

    def ph_G4(hs, SL, cur, st):
        hg = col(par, PC_HG)
        bos = st['bos']
        for g, h in enumerate(hs):
            B = SL[g]
            psO = PS(bos[g])
            tt('dve', rsv[g].ap(), psO.ap(), rsv[g].ap(), ALU.mult, r=[psO, rsv[g]], w=[rsv[g]])
            rel(bos[g])
            oh = o_a.sub(h * T, T)
            S.op('dve', 'scalar_tensor_tensor', (oh.ap(), rsv[g].ap(), hg.ap(), B['sg'].ap(), ALU.mult, ALU.mult),
                 r=[rsv[g], hg, B['sg']], w=[oh])

    onec = col(stat, 13)
    eblv = sb("eblv", 8, F32)

    def pre_P(hs):
        return ph_P(hs, False)

    def pre_A(hs, SL, prs):
        for g, h in enumerate(hs):
            B = SL[g]
            act(B['ta'].ap(), prs[g]['z'].ap(), AF.Sigmoid, r=[prs[g]['z']], w=[B['ta']])
            S.op('dve', 'tensor_copy', (B['vh'].ap(), prs[g]['v'].ap()), r=[prs[g]['v']], w=[B['vh']])
            rel(prs[g]['z'].bank, prs[g]['v'].bank)
        for g, h in enumerate(hs):
            B = SL[g]
            lb, oml = col(lbv, h), col(lbv, 8 + h)
            act(B['tb'].ap(), B['ta'].ap(), AF.Ln, r=[B['ta'], lb, oml], w=[B['tb']], bias=lb.ap(), scale=oml.ap())

    def pre_BCD(hs, SL):
        for g, h in enumerate(hs):
            B = SL[g]
            S.op('dve', 'tensor_tensor_scan', (B['tc'].ap(), onec.ap([[0, T]]), B['tb'].ap(), 0.0, ALU.mult, ALU.add),
                 r=[onec, B['tb']], w=[B['tc']])
            bl = col(B['tc'], T - 1)
            S.op('dve', 'tensor_scalar', (B['tb'].ap(), B['tc'].ap(), -1.0, bl.ap(), ALU.mult, ALU.add),
                 r=[B['tc']], w=[B['tb']])
            oml, noml = col(lbv, 8 + h), col(lbv, 16 + h)
            S.op('dve', 'tensor_scalar', (B['ta'].ap(), B['ta'].ap(), noml.ap(), oml.ap(), ALU.mult, ALU.add),
                 r=[B['ta'], noml, oml], w=[B['ta']])
        for g, h in enumerate(hs):
            B = SL[g]
            act(B['tb'].ap(), B['tb'].ap(), AF.Exp, r=[B['tb']], w=[B['tb']])
            bl = col(B['tc'], T - 1)
            act(col(eblv, h).ap(), bl.ap(), AF.Exp, r=[bl], w=[col(eblv, h)])
        for g, h in enumerate(hs):
            B = SL[g]
            tt('pool', B['kdT'].ap(), B['ta'].ap(), B['tb'].ap(), ALU.mult, r=[B['ta'], B['tb']], w=[B['kdT']])

    def pre_back(hs, SL):
        ph_E(hs, SL)
        bds = []
        for g, h in enumerate(hs):
            B = SL[g]
            b = alloc()
            bds.append(b)
            o = PS(b, 0, 128)
            for s in range(4):
                l = B['kdtm'].sub(s * 128, 128)
                rr = B['vh'].sub(s * 128, 128)
                mm(o.ap(), l.ap(), rr.ap(), s == 0, s == 3, r=[l, rr], w=[o])
        for g, h in enumerate(hs):
            o = PS(bds[g], 0, 128)
            Sh = S32.sub(h * 128, 128)
            S.op('dve', 'scalar_tensor_tensor', (Sh.ap(), Sh.ap(), col(eblv, h).ap(), o.ap(), ALU.mult, ALU.add),
                 r=[Sh, col(eblv, h), o], w=[Sh])
            rel(bds[g])

    def pre_finish(cur):
        for h in range(H):
            Sh = S32.sub(h * 128, 128)
            dst = Sc[cur].sub(h * 128, 128)
            S.op('pool', 'tensor_copy', (dst.ap(), Sh.ap()), r=[Sh], w=[dst])

    def hgrn_stage(full, cur):
        assert full
        groups = [[0, 1], [2, 3], [4, 5], [6, 7]]

        def SLs(p):
            return SLF4[2 * (p % 2):2 * (p % 2) + 2]

        prs = ph_P(groups[0], True)
        ph_A(groups[0], SLs(0), prs, True)
        ph_B(groups[0], SLs(0))
        ph_C(groups[0], SLs(0))
        ph_D(groups[0], SLs(0), True)
        ph_E(groups[0], SLs(0))
        for p, hs in enumerate(groups):
            nxt = groups[p + 1] if p + 1 < len(groups) else None
            st = {}
            ph_F(hs, SLs(p), True, cur)
            if nxt:
                prs = ph_P(nxt, True)
                ph_A(nxt, SLs(p + 1), prs, True)
            ph_G1(hs, SLs(p), cur, st)
            if nxt:
                ph_B(nxt, SLs(p + 1))
            ph_G2(hs, SLs(p), cur, st)
            if nxt:
                ph_C(nxt, SLs(p + 1))
            ph_G3(hs, SLs(p), cur, st)
            if nxt:
                ph_D(nxt, SLs(p + 1), True)
            ph_G4(hs, SLs(p), cur, st)
            if nxt:
                ph_E(nxt, SLs(p + 1))

    def glu_stage():
        for j in range(4):
            wb = load_block(GLU0 + j)
            for mi in range(2):
                m = 2 * j + mi
                pa = proj_fm(wb, mi * 256)
                pb = proj_fm(wb, mi * 256 + 128)
                sg_ = sgate[m % 2]
                act(sg_.ap(), pb.ap(), AF.Sigmoid, r=[pb], w=[sg_])
                um = u.sub(m * UW + CW - 1, T)
                tt('dve', um.ap(), pa.ap(), sg_.ap(), ALU.mult, r=[pa, sg_], w=[um])
                rel(pa.bank, pb.bank)

    def halo_shift():
        S.op('pool', 'tensor_copy', (u.ap([[UW, 8], [1, CW - 1]]), u.ap([[UW, 8], [1, CW - 1]], off=T)),
             r=[u.sub(m * UW + T, CW - 1) for m in range(8)], w=[u.sub(m * UW, CW - 1) for m in range(8)])

    def conv_stage():
        bmean, bex2 = alloc(2)
        pmean, pex2 = PS(bmean), PS(bex2)
        build_diag(0)
        for m in range(8):
            if m + 1 < 8:
                build_diag(m + 1)
            b = alloc()
            py = PS(b)
            for j in range(CW):
                l = diag.sub(((m % 2) * CW + j) * 128, 128)
                rr = u.sub(m * UW + j, T)
                mm(py.ap(), l.ap(), rr.ap(), j == 0, j == CW - 1, r=[l, rr], w=[py])
            cbm = col(par, PC_CB + m)
            ym = y32.sub(m * T, T)
            act(ym.ap(), py.ap(), AF.Identity, r=[py, cbm], w=[ym], bias=cbm.ap())
            yb, yq = ybf[m % 2], ysq[m % 2]
            S.op('dve', 'tensor_copy', (yb.ap(), ym.ap()), r=[ym], w=[yb])
            act(yq.ap(), py.ap(), AF.Square, r=[py, cbm], w=[yq], bias=cbm.ap())
            rel(b)
            mm(pmean.ap(), onesb.ap(), yb.ap(), m == 0, m == 7, r=[onesb, yb], w=[pmean])
            mm(pex2.ap(), onesb.ap(), yq.ap(), m == 0, m == 7, r=[onesb, yq], w=[pex2])
        halo_shift()
        return bmean, bex2, pmean, pex2

    def conv_ln(bmean, bex2, pmean, pex2):
        act(meanv.ap(), pmean.ap(), AF.Copy, r=[pmean], w=[meanv], scale=1.0 / D)
        tt('dve', msqv.ap(), meanv.ap(), meanv.ap(), ALU.mult, r=[meanv], w=[msqv])
        S.op('dve', 'scalar_tensor_tensor', (msqv.ap(), pex2.ap(), 1.0 / D, msqv.ap(), ALU.mult, ALU.subtract),
             r=[pex2, msqv], w=[msqv])
        rel(bmean, bex2)
        act(rs2v.ap(), msqv.ap(), AF.Ln, r=[msqv, epsc], w=[rs2v], bias=epsc.ap())
        act(rs2v.ap(), rs2v.ap(), AF.Exp, r=[rs2v], w=[rs2v], scale=-0.5)

    def conv_norm(m):
        ym = y32.sub(m * T, T)
        tc_ = tcv[m % 2]
        tt('dve', tc_.ap(), ym.ap(), meanv.ap(), ALU.subtract, r=[ym, meanv], w=[tc_])
        tt('pool', tc_.ap(), tc_.ap(), rs2v.ap(), ALU.mult, r=[tc_, rs2v], w=[tc_])
        lg, lbb = col(par, PC_LG + m), col(par, PC_LB + m)
        yn = ynorm.sub(m * T, T)
        act(yn.ap(), tc_.ap(), AF.Silu, r=[tc_, lg, lbb], w=[yn], bias=lbb.ap(), scale=lg.ap())

    def branch_stage(gate0, w0, src, is_b, js=(0, 1), hook=None):
        for j in js:
            wg = load_block(gate0 + j)
            ww = load_block(w0 + j)
            for mi in range(4):
                m = 4 * j + mi
                pg = proj_fm(wg, mi * 128)
                b = alloc()
                po = PS(b)
                for kc in range(8):
                    l = ww.sub(kc * 512 + mi * 128, 128)
                    rr = src.sub(kc * T, T)
                    mm(po.ap(), l.ap(), rr.ap(), kc == 0, kc == 7, r=[l, rr], w=[po])
                sg_ = sgate[m % 2]
                act(sg_.ap(), pg.ap(), AF.Sigmoid, r=[pg], w=[sg_])
                rel(pg.bank)
                ma = mergedA.sub(m * T, T)
                if not is_b:
                    tt('dve', ma.ap(), po.ap(), sg_.ap(), ALU.mult, r=[po, sg_], w=[ma])
                else:
                    tb_ = tbv[m % 2]
                    tt('dve', tb_.ap(), po.ap(), sg_.ap(), ALU.mult, r=[po, sg_], w=[tb_])
                    mg = merged.sub(m * T, T)
                    tt('pool', mg.ap(), tb_.ap(), ma.ap(), ALU.add, r=[tb_, ma], w=[mg])
                rel(b)
                if hook is not None:
                    hook(mi)

    def wout_stage():
        wbs = [load_block(WO0 + 0), load_block(WO0 + 1)]
        for s in range(4):
            for hf in range(2):
                wb = wbs[hf]
                b = alloc()
                po = PS(b)
                for kc in range(8):
                    l = merged.sub(kc * T + s * 128, 128)
                    rr = wb.sub(kc * 512, 512)
                    mm(po.ap(), l.ap(), rr.ap(), kc == 0, kc == 7, r=[l, rr], w=[po])
                xs = x.sub(s * D + hf * 512, 512)
                tt('dve', xs.ap(), po.ap(), xs.ap(), ALU.add, r=[po, xs], w=[xs])
                rel(b)
            if s >= 1:
                rmsnorm_sub(s - 1, gmlp, xnF)
        rmsnorm_sub(3, gmlp, xnF)

    def mlp_stage(nxt=None):
        for j in range(8):
            wb = load_block(M10 + j)
            for fi in range(4):
                f = 4 * j + fi
                pz = proj_fm(wb, fi * 128)
                rr_ = r32[f % 2]
                act(rr_.ap(), pz.ap(), AF.Relu, r=[pz], w=[rr_])
                rel(pz.bank)
                zf = z2T.sub(f * T, T)
                tt('pool' if f % 2 == 0 else 'dve', zf.ap(), rr_.ap(), rr_.ap(), ALU.mult, r=[rr_], w=[zf])
        if nxt is not None:
            pf_elem(nxt)
        for hf in range(2):
            if hf == 1 and nxt is not None:
                pf_pe()
            banks = alloc(4)
            for kg in range(4):
                wb = load_block(M20 + hf * 4 + kg)
                for s in range(4):
                    po = PS(banks[s])
                    for kc in range(8):
                        l = z2T.sub((kg * 8 + kc) * T + s * 128, 128)
                        rr = wb.sub(kc * 512, 512)
                        mm(po.ap(), l.ap(), rr.ap(), kg == 0 and kc == 0, kg == 3 and kc == 7, r=[l, rr], w=[po])
            for s in range(4):
                po = PS(banks[s])
                xs = x.sub(s * D + hf * 512, 512)
                tt('dve', xs.ap(), po.ap(), xs.ap(), ALU.add, r=[po, xs], w=[xs])
            rel(*banks)
        if nxt is not None:
            preload(HB0 + 0)
            preload(HB0 + 1)

    xstage = [A((36 + 4 * i) * K, D, F32) for i in range(4)]
    xn4 = [A((52 + 2 * i) * K, D, BF16) for i in range(4)]

    def pf_elem(t):
        for s in range(4):
            xs = xstage[s]
            dma(xs, dram(x_d[t * 4 + s], "xin", 0, 1), "xs%d" % s)
            rs = rms_stats(xs, s, xn4[s])
            S.op('dve', 'scalar_tensor_tensor', (xn4[s].ap(), xs.ap(), rs.ap(), gmix.ap(), ALU.mult, ALU.mult), r=[xs, rs, gmix], w=[xn4[s]])

    def pf_pe():
        for s in range(4):
            b = alloc()
            xb = xn4[s]
            for kc in range(8):
                o = PSB(b, kc * 128, 128)
                i_ = xb.sub(kc * 128, 128)
                S.op('pe', 'transpose', (o.ap(), i_.ap(), identb.ap()), r=[i_, identb], w=[o])
            pv = PSB(b)
            dst_ap = hT.ap([[512, 8], [1, 128]], off=s * 128)
            src_ap = pv.ap([[128, 8], [1, 128]])
            wl = [hT.sub(kc * 512 + s * 128, 128) for kc in range(8)]
            if s % 2:
                S.op('dve', 'tensor_copy', (dst_ap, src_ap), r=[pv], w=wl)
            else:
                act(dst_ap, src_ap, AF.Copy, r=[pv], w=wl)
            rel(b)

    def final_stage(t):
        toks = []
        for s in range(4):
            xs = x.sub(s * D, D)
            rs = rms_stats(xs, s, xnF[s % 2])
            S.op('dve', 'scalar_tensor_tensor', (xs.ap(), xs.ap(), rs.ap(), gfin.ap(), ALU.mult, ALU.mult),
                 r=[xs, rs, gfin], w=[xs])
            toks.append(dma(dram(y_d[t * 4 + s], "y_d", t * 4 + s, t * 4 + s + 1), xs, "yo%d" % s))
        return toks

    def load_x(src, t):
        for s in range(4):
            dma(x.sub(s * D, D), dram(src[t * 4 + s], "xin", 0, 1), "x%d" % s)

    last_tok = []
    S.op('dve', 'memset', (onec.ap(), 1.0), w=[onec])
    for h in range(8):
        cast_group([HB0 + h], "cb%d" % h)
    later = [([b], "cq%d" % b) for b in [8, 9, 10, 11, 12, 13, 16, 17, 14, 15, 18, 19, 20, 21] + list(range(22, 38))]
    load_wpre()
    load_x(xp_d, 0)
    NG = 2 * NT

    def grp(k):
        half = k % 2
        return [4 * half + i for i in range(4)], SLP[4 * (k % 2):4 * (k % 2) + 4]

    def tile_start(k):
        if k % 2 == 0:
            t = k // 2
            rmsnorm_T(gmix, xnP)
            if t + 1 < NT:
                load_x(xp_d, t + 1)
            else:
                load_x(x_d, 0)

    tile_start(0)
    prs_k = pre_P(grp(0)[0])
    pre_A(grp(0)[0], grp(0)[1], prs_k)
    for k in range(NG):
        hs, SLg = grp(k)
        if k + 1 < NG:
            tile_start(k + 1)
            prs_n = pre_P(grp(k + 1)[0])
        pre_BCD(hs, SLg)
        for _ in range(2):
            if later and (k >= 1 or NT == 1):
                cast_group(*later.pop(0))
        if k + 1 < NG:
            pre_A(grp(k + 1)[0], grp(k + 1)[1], prs_n)
        pre_back(hs, SLg)
    while later:
        cast_group(*later.pop(0))
    glu_stage()
    halo_shift()
    cur = 0
    pre_finish(cur)
    for t in range(NT):
        if t == 0:
            rmsnorm_T(gmix, xnF)
        hgrn_stage(True, cur)
        cur = 1 - cur
        glu_stage()
        cs_ = conv_stage()
        branch_stage(GA0, WA0, o_a, False, js=(0,))
        conv_ln(*cs_)
        branch_stage(GA0, WA0, o_a, False, js=(1,), hook=lambda mi: (conv_norm(2 * mi), conv_norm(2 * mi + 1)))
        branch_stage(GB0, WB0, ynorm, True)
        wout_stage()
        mlp_stage(t + 1 if t + 1 < NT else None)
        last_tok = final_stage(t)
        if t + 1 < NT:
            load_x(x_d, t + 1)
    assert len(free_banks) == 8
    S.emit(nc, es, list(last_tok))
    es.close()
    S.stats = {e: len(S.ops[e]) for e in ENGS}
    print('ops per engine:', S.stats, flush=True)
    return nc


_CACHE = {}


def _prep_weights(inp):
    w_in = np.asarray(inp["w_in"][0], dtype=np.float32)
    blocks = []

    def blk(w, cols):
        kt = w.shape[0] // 128
        b = w[:, cols].reshape(kt, 128, len(cols)).transpose(1, 0, 2)
        return np.ascontiguousarray(b).reshape(128, kt * len(cols))

    ar = np.arange
    for h in range(8):
        cols = np.concatenate([1024 + h * 128 + ar(128), 2048 + h * 128 + ar(128), h * 128 + ar(128), 3072 + h * 128 + ar(128)])
        blocks.append(blk(w_in, cols))
    for j in range(4):
        m0, m1 = 2 * j, 2 * j + 1
        cols = np.concatenate([4096 + m0 * 128 + ar(128), 5120 + m0 * 128 + ar(128),
                               4096 + m1 * 128 + ar(128), 5120 + m1 * 128 + ar(128)])
        blocks.append(blk(w_in, cols))
    for j in range(2):
        blocks.append(blk(w_in, 6144 + j * 512 + ar(512)))
    for j in range(2):
        blocks.append(blk(w_in, 7168 + j * 512 + ar(512)))
    for name in ("w_a_out", "w_b_out", "w_out"):
        w = np.asarray(inp[name][0], dtype=np.float32)
        for j in range(2):
            blocks.append(blk(w, j * 512 + ar(512)))
    w1 = np.asarray(inp["w_mlp_in"][0], dtype=np.float32)
    for j in range(8):
        blocks.append(blk(w1, j * 512 + ar(512)))
    w2 = np.asarray(inp["w_mlp_out"][0], dtype=np.float32)
    for hf in range(2):
        for kg in range(4):
            blocks.append(blk(w2[kg * 1024:(kg + 1) * 1024], hf * 512 + ar(512)))
    wall = np.stack(blocks).astype(np.float32)
    assert wall.shape == (NBLK, 128, 4096)

    def fm(v):
        return np.asarray(v, dtype=np.float32).reshape(8, 128).T

    par = np.zeros((128, NPAR), np.float32)
    par[:, PC_GM:PC_GM + 8] = fm(inp["norm_mix_g"][0])
    par[:, PC_GML:PC_GML + 8] = fm(inp["norm_mlp_g"][0])
    par[:, PC_P0:PC_P0 + 8] = fm(inp["lb_param"][0])
    par[:, PC_P1:PC_P1 + 8] = fm(inp["lb_param"][1])
    par[:, PC_HG] = np.asarray(inp["hgrn_norm_g"][0], dtype=np.float32)
    par[:, PC_CB:PC_CB + 8] = fm(inp["conv_b"][0])
    par[:, PC_LG:PC_LG + 8] = fm(inp["conv_ln_g"][0])
    par[:, PC_LB:PC_LB + 8] = fm(inp["conv_ln_b"][0])
    cw = np.asarray(inp["conv_w"][0], dtype=np.float32)
    par[:, PC_CW:] = cw.reshape(CW, 8, 128).transpose(2, 1, 0).reshape(128, 8 * CW)
    gfin = np.ascontiguousarray(np.broadcast_to(np.asarray(inp["norm_final_g"], dtype=np.float32)[None, :], (128, D)))
    gmix = np.ascontiguousarray(np.broadcast_to(np.asarray(inp["norm_mix_g"][0], dtype=np.float32)[None, :], (128, D)))
    gmlp = np.ascontiguousarray(np.broadcast_to(np.asarray(inp["norm_mlp_g"][0], dtype=np.float32)[None, :], (128, D)))
    cst = np.zeros((128, NCST), np.float32)
    cst[:, CC_ID:CC_ID + 128] = np.eye(128, dtype=np.float32)
    sp, tt = np.meshgrid(ar(128), ar(128), indexing="ij")
    cst[:, CC_MASK:CC_MASK + 128] = ((sp // 64 == tt // 64) & (sp <= tt)).astype(np.float32)
    cst[:, CC_SCAN:] = (ar(T) % 64 != 0).astype(np.float32)[None, :]
    return wall, par, gfin, cst, gmix, gmlp


def kernel(**inp):
    xfull = np.asarray(inp["x"], dtype=np.float32)
    B, Sq, _ = xfull.shape
    half = Sq // 2
    NT = half // T
    assert half % T == 0 and B * 2 == 8
    if NT not in _CACHE:
        _CACHE[NT] = build(NT)
    nc = _CACHE[NT]
    wall, par, gfin, cst, gmix, gmlp = _prep_weights(inp)
    in_maps = []
    zeros = np.zeros((NT * 4, 128, D), np.float32)
    for c in range(8):
        b, hf = c // 2, c % 2
        xc = np.ascontiguousarray(xfull[b, hf * half:(hf + 1) * half]).reshape(NT * 4, 128, D)
        xp = zeros if hf == 0 else np.ascontiguousarray(xfull[b, 0:half]).reshape(NT * 4, 128, D)
        in_maps.append({"x": xc, "xprev": xp, "wall": wall, "par": par, "cst": cst, "gfin": gfin, "gmix": gmix, "gmlp": gmlp})
    res = run_bass_kernel_spmd(nc, in_maps, core_ids=list(range(8)))
    out = np.empty((B, Sq, D), np.float32)
    for c in range(8):
        b, hf = c // 2, c % 2
        out[b, hf * half:(hf + 1) * half] = np.asarray(res.results[c]["y"]).reshape(half, D)
    return out
```

```python
import contextlib
import numpy as np
import concourse.bass as bass
import concourse.mybir as mybir
from concourse.bass_utils import run_bass_kernel_spmd

F32 = mybir.dt.float32
BF16 = mybir.dt.bfloat16
AF = mybir.ActivationFunctionType
ALU = mybir.AluOpType

P = 128
D = 1024
H = 8
T = 512
EPS = 1e-6
CW = 31
UW = T + CW - 1
NBLK = 38
HB0, GLU0, GA0, GB0, WA0, WB0, WO0, M10, M20 = 0, 8, 12, 14, 16, 18, 20, 22, 30
PC_GM, PC_GML, PC_P0, PC_P1, PC_HG, PC_CB, PC_LG, PC_LB, PC_CW = 0, 8, 16, 24, 32, 33, 41, 49, 57
NPAR = 57 + 8 * CW
CC_ID, CC_MASK, CC_SCAN = 0, 128, 256
NCST = 256 + T

ENGS = ['pe', 'act', 'dve', 'pool', 'sp']


class V:
    def __init__(s, t, W, name, off, n, esz, ap=None, whole=None):
        s.t, s.W, s.name, s.off, s.n, s.esz = t, W, name, off, n, esz
        s._ap = ap
        s.whole = whole

    def res(s):
        if s.whole is not None:
            return (s.name, 0, s.whole)
        return (s.name, s.off * s.esz, (s.off + s.n) * s.esz)

    def sub(s, a, n):
        assert a + n <= s.n
        return V(s.t, s.W, s.name, s.off + a, n, s.esz)

    def ap(s, dims=None, off=0, p0=0, np_=P):
        if s._ap is not None:
            return s._ap
        if dims is None:
            dims = [[1, s.n - off]]
        return bass.AP(s.t, p0 * s.W + s.off + off, [[s.W, np_]] + [list(d) for d in dims])


class Sched:
    def __init__(s):
        s.ops = {e: [] for e in ENGS}
        s.res = {}
        s.seen = {e: {} for e in ENGS}
        s.dma_cnt = {}
        s.nbank = 0
        s.held = set()

    def bank(s):
        while True:
            b = s.nbank % 8
            s.nbank += 1
            if b not in s.held:
                return b

    def op(s, eng, iname, args, kw=None, r=(), w=(), dma=None):
        idx = len(s.ops[eng])
        deps = {}

        def add(d):
            if deps.get(d[0], -1) < d[1]:
                deps[d[0]] = d[1]

        rres = [v.res() for v in r]
        wres = [v.res() for v in w]
        for (name, lo, hi) in rres:
            R = s.res.setdefault(name, {'w': [], 'r': []})
            for (l, h, d) in R['w']:
                if l < hi and lo < h:
                    add(d)
        for (name, lo, hi) in wres:
            R = s.res.setdefault(name, {'w': [], 'r': []})
            for (l, h, d) in R['w']:
                if l < hi and lo < h:
                    add(d)
            for (l, h, d) in R['r']:
                if l < hi and lo < h:
                    add(d)
        if dma is not None:
            s.dma_cnt[dma] = s.dma_cnt.get(dma, 0) + 1
            tok = (('dma', dma), s.dma_cnt[dma] * 16)
        else:
            tok = (eng, idx)
        for (name, lo, hi) in rres:
            R = s.res[name]
            R['r'] = [x for x in R['r'] if not (x[2][0] == tok[0] and lo <= x[0] and x[1] <= hi)]
            R['r'].append((lo, hi, tok))
        for (name, lo, hi) in wres:
            R = s.res[name]
            R['w'] = [x for x in R['w'] if not (lo <= x[0] and x[1] <= hi)]
            R['r'] = [x for x in R['r'] if not (lo <= x[0] and x[1] <= hi)]
            R['w'].append((lo, hi, tok))
        waits = []
        for src, i in deps.items():
            if src == 'pe' and eng == 'pe':
                continue
            if s.seen[eng].get(src, -1) >= i:
                continue
            s.seen[eng][src] = i
            waits.append((src, i))
        s.ops[eng].append({'name': iname, 'args': args, 'kw': kw or {}, 'waits': waits, 'dma': dma})
        return tok

    def emit(s, nc, es, final_waits):
        sig = {e: set() for e in ENGS}
        for e in ENGS:
            for o in s.ops[e]:
                for (src, i) in o['waits']:
                    if isinstance(src, str):
                        sig[src].add(i)
        cnt = {}
        for e in ENGS:
            c = 0
            m = {}
            for i in range(len(s.ops[e])):
                if i in sig[e]:
                    c += 1
                    m[i] = c
            cnt[e] = m
        sems = {}
        for e in ENGS:
            if cnt[e]:
                sems[e] = es.enter_context(nc.semaphore('s_' + e))
        for k in s.dma_cnt:
            sems[('dma', k)] = es.enter_context(nc.semaphore('d_' + k))
        block = es.enter_context(nc.Block())

        def run(ename, eobj):
            for i, o in enumerate(s.ops[ename]):
                for (src, v) in o['waits']:
                    if isinstance(src, str):
                        eobj.wait_ge(sems[src], cnt[src][v])
                    else:
                        eobj.wait_ge(sems[src], v)
                ins = getattr(eobj, o['name'])(*o['args'], **o['kw'])
                if o['dma'] is not None:
                    ins.then_inc(sems[('dma', o['dma'])], 16)
                elif i in cnt[ename]:
                    ins.then_inc(sems[ename], 1)
            if ename == 'sp':
                for (src, v) in final_waits:
                    eobj.wait_ge(sems[src], v)

        @block.tensor
        def _(e):
            run('pe', e)

        @block.scalar
        def _(e):
            run('act', e)

        @block.vector
        def _(e):
            run('dve', e)

        @block.gpsimd
        def _(e):
            run('pool', e)

        @block.sync
        def _(e):
            run('sp', e)


def build(NT):
    import os as _os
    nc = bass.Bass("TRN2", target_bir_lowering=False)
    x_d = nc.dram_tensor("x", [NT * 4, P, D], F32, kind="ExternalInput").ap()
    xp_d = nc.dram_tensor("xprev", [NT * 4, P, D], F32, kind="ExternalInput").ap()
    w_d = nc.dram_tensor("wall", [NBLK, P, 4096], F32, kind="ExternalInput").ap()
    par_d = nc.dram_tensor("par", [P, NPAR], F32, kind="ExternalInput").ap()
    cst_d = nc.dram_tensor("cst", [P, NCST], F32, kind="ExternalInput").ap()
    gfin_d = nc.dram_tensor("gfin", [P, D], F32, kind="ExternalInput").ap()
    gmix_d = nc.dram_tensor("gmix", [P, D], F32, kind="ExternalInput").ap()
    gmlp_d = nc.dram_tensor("gmlp", [P, D], F32, kind="ExternalInput").ap()
    y_d = nc.dram_tensor("y", [NT * 4, P, D], F32, kind="ExternalOutput").ap()
    wbf_d = nc.dram_tensor("wbf", [NBLK, P, 4096], BF16).ap()

    es = contextlib.ExitStack()
    S = Sched()

    def sb(name, n, dt):
        t = es.enter_context(nc.sbuf_tensor("sb_" + name, [P, n], dt))
        return V(t, n, name, 0, n, 4 if dt == F32 else 2)

    par = sb("par", NPAR, F32)
    lbv = sb("lbv", 32, F32)
    cst = sb("cst", NCST, F32)
    identb = sb("identb", 128, BF16)
    onesb = sb("onesb", 128, BF16)
    gfin = sb("gfin", D, F32)
    gmix = sb("gmix", D, F32)
    gmlp = sb("gmlp", D, F32)
    diag = sb("diag", 2 * CW * 128, BF16)
    S32 = sb("S32", H * 128, F32)
    S32b = sb("S32b", H * 128, F32)
    Sc = [sb("Sc0", H * 128, BF16), sb("Sc1", H * 128, BF16)]
    u = sb("u", 8 * UW, BF16)
    x = sb("x", 4 * D, F32)
    hT = sb("hT", 8 * T, BF16)
    wbuf = [sb("wb%d" % i, 4096, BF16) for i in range(3)]
    stat = sb("stat", 16, F32)
    ARENA = 106 * 1024
    ar_t = es.enter_context(nc.sbuf_tensor("arena", [P, ARENA // 2], BF16))
    ar_f = ar_t.bitcast(F32)

    def A(byte_off, n, dt):
        byte_off = int(byte_off)
        if dt == F32:
            assert byte_off % 4 == 0 and byte_off + 4 * n <= ARENA
            return V(ar_f, ARENA // 4, "arena", byte_off // 4, n, 4)
        assert byte_off % 2 == 0 and byte_off + 2 * n <= ARENA
        return V(ar_t, ARENA // 2, "arena", byte_off // 2, n, 2)

    K = 1024
    o_a = A(0, 8 * T, BF16)
    mergedA = A(8 * K, 8 * T, F32)

    def slot_full(g):
        b = 24 * K + g * 19 * K
        return dict(ta=A(b, T, F32), tb=A(b + 2 * K, T, F32), tc=A(b + 4 * K, T, F32), td=A(b + 6 * K, T, F32),
                    qs=A(b + 8 * K, T, F32), sg=A(b + 10 * K, T, F32), qb=A(b + 12 * K, T, BF16),
                    kb=A(b + 13 * K, T, BF16), kdT=A(b + 14 * K, T, BF16), kdtm=A(b + 15 * K, T, BF16),
                    vh=A(b + 16 * K, T, BF16), Sall=A(b + 17 * K, 7 * 128, BF16))

    def slot_pre(g):
        b = g * 8 * K
        return dict(ta=A(b, T, F32), tb=A(b + 2 * K, T, F32), tc=A(b + 4 * K, T, F32),
                    kdT=A(b + 6 * K, T, BF16), kdtm=A(b + 6 * K, T, BF16), vh=A(b + 7 * K, T, BF16))

    SLF4 = [slot_full(g) for g in range(4)]
    wpre = A(64 * K, 8 * 2048, BF16)
    SLP = [slot_pre(g) for g in range(8)]
    osq = [A(8 * K, T, BF16), A(9 * K, T, BF16)]
    scm = [A(10 * K, T, BF16), A(11 * K, T, BF16)]
    rsv = [A(100 * K, T, F32), A(102 * K, T, F32)]
    xnF = [A(12 * K, D, BF16), A(14 * K, D, BF16)]
    xnP = [A(96 * K, D, BF16), A(98 * K, D, BF16)]
    ynorm = A(70 * K, 8 * T, BF16)
    y32 = A(24 * K, 8 * T, F32)
    ybf = [A(40 * K, T, BF16), A(41 * K, T, BF16)]
    ysq = [A(42 * K, T, BF16), A(43 * K, T, BF16)]
    meanv = A(44 * K, T, F32)
    msqv = A(46 * K, T, F32)
    rs2v = A(48 * K, T, F32)
    tcv = [A(50 * K, T, F32), A(52 * K, T, F32)]
    sgate = [A(54 * K, T, F32), A(56 * K, T, F32)]
    tbv = [A(58 * K, T, F32), A(60 * K, T, F32)]
    merged = A(62 * K, 8 * T, BF16)
    z2T = A(0, 32 * T, BF16)
    r32 = [A(32 * K, T, F32), A(34 * K, T, F32)]
    ps_t = [es.enter_context(nc.psum_tensor("ps%d" % i, [P, 512], F32)) for i in range(8)]
    ps_b = [t.bitcast(BF16) for t in ps_t]
    free_banks = list(range(8))

    def alloc(n=1):
        assert len(free_banks) >= n, "out of PSUM banks"
        out = [free_banks.pop(0) for _ in range(n)]
        return out if n > 1 else out[0]

    def rel(*bs):
        for b in bs:
            assert b not in free_banks
            free_banks.append(b)

    def PS(b, off=0, n=512):
        return V(ps_t[b], 512, "ps%d" % b, off, n, 4, whole=2048)

    def PSB(b, off=0, n=1024):
        return V(ps_b[b], 1024, "ps%d" % b, off, n, 2, whole=2048)

    def dram(apx, name, lo, hi):
        return V(None, 0, name, lo, hi - lo, 1, ap=apx)

    def mm(out, lhsT, rhs, start, stop, r, w):
        S.op('pe', 'matmul', (out, lhsT, rhs), dict(start=start, stop=stop), r=r, w=w)

    def act(out_ap, in_ap, func, r, w, bias=None, scale=None, accum=None):
        kw = {}
        if bias is not None:
            kw['bias'] = bias
        if scale is not None:
            kw['scale'] = scale
        if accum is not None:
            kw['accum_out'] = accum
        S.op('act', 'activation', (out_ap, in_ap, func), kw, r=r, w=w)

    def tt(eng, out, in0, in1, op, r, w):
        S.op(eng, 'tensor_tensor', (out, in0, in1, op), r=r, w=w)

    def dma(out_v, in_v, key, eng='sp'):
        return S.op(eng, 'dma_start', (), dict(out=out_v.ap(), in_=in_v.ap()), r=[in_v], w=[out_v], dma=key)

    wslot = [0]

    preloaded = {}

    def preload(b):
        preloaded[b] = load_block(b)

    def load_block(b):
        if b in preloaded:
            return preloaded.pop(b)
        k = wslot[0] % 3
        wslot[0] += 1
        dma(wbuf[k], dram(wbf_d[b], "wbf", b, b + 1), "w%d" % k)
        return wbuf[k]

    def col(v, c):
        return v.sub(c, 1)

    epsc = col(stat, 12)

    dma(par, dram(par_d, "par_d", 0, 1), "c0")
    dma(cst, dram(cst_d, "cst_d", 0, 1), "c1")
    dma(gfin, dram(gfin_d, "gfin_d", 0, 1), "c2")
    dma(gmix, dram(gmix_d, "gmix_d", 0, 1), "c3")
    dma(gmlp, dram(gmlp_d, "gmlp_d", 0, 1), "c4")
    S.op('dve', 'memset', (epsc.ap(), EPS), w=[epsc])
    dv = lbv.sub(24, 8)
    tt('dve', dv.ap(), par.sub(PC_P0, 8).ap(), par.sub(PC_P1, 8).ap(), ALU.subtract, r=[par], w=[dv])
    act(lbv.sub(0, 8).ap(), dv.ap(), AF.Sigmoid, r=[dv], w=[lbv.sub(0, 8)])
    act(lbv.sub(8, 8).ap(), dv.ap(), AF.Sigmoid, r=[dv], w=[lbv.sub(8, 8)], scale=-1.0)
    S.op('dve', 'tensor_scalar', (lbv.sub(16, 8).ap(), lbv.sub(8, 8).ap(), -1.0, None, ALU.mult),
         r=[lbv.sub(8, 8)], w=[lbv.sub(16, 8)])
    S.op('dve', 'tensor_copy', (identb.ap(), cst.sub(CC_ID, 128).ap()), r=[cst], w=[identb])
    S.op('dve', 'memset', (onesb.ap(), 1.0), w=[onesb])
    S.op('dve', 'memset', (S32.ap(), 0.0), w=[S32])
    S.op('pool', 'memset', (u.ap(), 0.0), w=[u])
    identf = cst.sub(CC_ID, 128)

    def build_diag(m):
        for j in range(CW):
            dst = diag.sub(((m % 2) * CW + j) * 128, 128)
            sc = col(par, PC_CW + m * CW + j)
            e3 = j % 3
            if e3 == 0:
                S.op('pool', 'tensor_scalar', (dst.ap(), identf.ap(), sc.ap(), 0.0, ALU.mult, ALU.add), r=[identf, sc], w=[dst])
            elif e3 == 1:
                act(dst.ap(), identf.ap(), AF.Copy, r=[identf, sc], w=[dst], scale=sc.ap())
            else:
                S.op('dve', 'tensor_scalar', (dst.ap(), identf.ap(), sc.ap(), 0.0, ALU.mult, ALU.add), r=[identf, sc], w=[dst])

    def cast_group(blocks, key):
        for b in blocks:
            srcv = dram(w_d[b], "w_d", b, b + 1)
            dstv = dram(wbf_d[b], "wbf", b, b + 1)
            S.op('pool', 'dma_start', (), dict(out=dstv.ap(), in_=srcv.ap()), r=[srcv], w=[dstv], dma=key)
        final = (('dma', key), S.dma_cnt[key] * 16)
        R = S.res["wbf"]
        R['w'] = [(lo, hi, final if tk[0] == ('dma', key) else tk) for (lo, hi, tk) in R['w']]

    def load_wpre():
        for h in range(8):
            srcap = wbf_d[HB0 + h].rearrange("p (k c) -> p k c", c=512)[:, :, 0:256]
            dstv = wpre.sub(h * 2048, 2048)
            S.op('sp', 'dma_start', (), dict(out=dstv.ap([[256, 8], [1, 256]]), in_=srcap),
                 r=[dram(srcap, "wbf", HB0 + h, HB0 + h + 1)], w=[dstv], dma="wp%d" % h)

    def rms_stats(xs, s, junkv):
        ss, ln_, rs = col(stat, s), col(stat, 4 + s), col(stat, 8 + s)
        act(junkv.ap(), xs.ap(), AF.Square, r=[xs], w=[junkv, ss], accum=ss.ap())
        act(ln_.ap(), ss.ap(), AF.Ln, r=[ss, epsc], w=[ln_], bias=epsc.ap(), scale=1.0 / D)
        act(rs.ap(), ln_.ap(), AF.Exp, r=[ln_], w=[rs], scale=-0.5)
        return rs

    def rmsnorm_sub(s, gv, xn):
        xs = x.sub(s * D, D)
        b = alloc()
        xb = xn[s % 2]
        rs = rms_stats(xs, s, xb)
        S.op('dve', 'scalar_tensor_tensor', (xb.ap(), xs.ap(), rs.ap(), gv.ap(), ALU.mult, ALU.mult), r=[xs, rs, gv], w=[xb])
        for kc in range(8):
            o = PSB(b, kc * 128, 128)
            i_ = xb.sub(kc * 128, 128)
            S.op('pe', 'transpose', (o.ap(), i_.ap(), identb.ap()), r=[i_, identb], w=[o])
        pv = PSB(b)
        dst_ap = hT.ap([[512, 8], [1, 128]], off=s * 128)
        src_ap = pv.ap([[128, 8], [1, 128]])
        wl = [hT.sub(kc * 512 + s * 128, 128) for kc in range(8)]
        if s % 2:
            S.op('dve', 'tensor_copy', (dst_ap, src_ap), r=[pv], w=wl)
        else:
            act(dst_ap, src_ap, AF.Copy, r=[pv], w=wl)
        rel(b)

    def rmsnorm_T(gv, xn):
        for s in range(4):
            rmsnorm_sub(s, gv, xn)

    def proj_fm(wb, coff, WST=512):
        b = alloc()
        o = PS(b)
        for kc in range(8):
            l = wb.sub(kc * WST + coff, 128)
            rr = hT.sub(kc * 512, 512)
            mm(o.ap(), l.ap(), rr.ap(), kc == 0, kc == 7, r=[l, rr], w=[o])
        o.bank = b
        return o

    def proj_v(wb, WST=512):
        b = alloc()
        for s in range(4):
            o = PS(b, s * 128, 128)
            for kc in range(8):
                l = hT.sub(kc * 512 + s * 128, 128)
                rr = wb.sub(kc * WST + 128, 128)
                mm(o.ap(), l.ap(), rr.ap(), kc == 0, kc == 7, r=[l, rr], w=[o])
        o = PS(b)
        o.bank = b
        return o

    def ph_P(hs, full):
        prs = []
        for h in hs:
            pr = {}
            if full:
                wb = load_block(HB0 + h)
                pr['z'] = proj_fm(wb, 0)
                pr['v'] = proj_v(wb)
            else:
                wb = wpre.sub(h * 2048, 2048)
                pr['z'] = proj_fm(wb, 0, 256)
                pr['v'] = proj_v(wb, 256)
            if full:
                pr['q'] = proj_fm(wb, 256)
                pr['g'] = proj_fm(wb, 384)
            prs.append(pr)
        return prs

    def ph_A(hs, SL, prs, full):
        if full:
            for g, h in enumerate(hs):
                B = SL[g]
                act(B['qs'].ap(), prs[g]['q'].ap(), AF.Silu, r=[prs[g]['q']], w=[B['qs']])
                act(B['sg'].ap(), prs[g]['g'].ap(), AF.Silu, r=[prs[g]['g']], w=[B['sg']])
                rel(prs[g]['q'].bank, prs[g]['g'].bank)
        for g, h in enumerate(hs):
            B = SL[g]
            act(B['ta'].ap(), prs[g]['z'].ap(), AF.Sigmoid, r=[prs[g]['z']], w=[B['ta']])
            S.op('dve', 'tensor_copy', (B['vh'].ap(), prs[g]['v'].ap()), r=[prs[g]['v']], w=[B['vh']])
            rel(prs[g]['z'].bank, prs[g]['v'].bank)
        for g, h in enumerate(hs):
            B = SL[g]
            lb, oml = col(lbv, h), col(lbv, 8 + h)
            act(B['tb'].ap(), B['ta'].ap(), AF.Ln, r=[B['ta'], lb, oml], w=[B['tb']], bias=lb.ap(), scale=oml.ap())

    def ph_B(hs, SL):
        scanm = cst.sub(CC_SCAN, T)
        for g, h in enumerate(hs):
            B = SL[g]
            S.op('dve', 'tensor_tensor_scan', (B['tc'].ap(), scanm.ap(), B['tb'].ap(), 0.0, ALU.mult, ALU.add),
                 r=[scanm, B['tb']], w=[B['tc']])
        for g, h in enumerate(hs):
            B = SL[g]
            oml, noml = col(lbv, 8 + h), col(lbv, 16 + h)
            S.op('dve', 'tensor_scalar', (B['ta'].ap(), B['ta'].ap(), noml.ap(), oml.ap(), ALU.mult, ALU.add),
                 r=[B['ta'], noml, oml], w=[B['ta']])

    def ph_C(hs, SL):
        for g, h in enumerate(hs):
            B = SL[g]
            act(B['tb'].ap(), B['tc'].ap(), AF.Exp, r=[B['tc']], w=[B['tb']])
            act(B['td'].ap(), B['tc'].ap(), AF.Exp, r=[B['tc']], w=[B['td']], scale=-1.0)

    def ph_D(hs, SL, full):
        for g, h in enumerate(hs):
            B = SL[g]
            tt('pool', B['td'].ap(), B['ta'].ap(), B['td'].ap(), ALU.mult, r=[B['ta'], B['td']], w=[B['td']])
        for g, h in enumerate(hs):
            B = SL[g]
            tt('dve', B['kdT'].ap([[64, 8], [1, 64]]), B['td'].ap([[64, 8], [1, 64]]),
               B['tb'].ap([[64, 8], [0, 64]], off=63), ALU.mult, r=[B['td'], B['tb']], w=[B['kdT']])
        if full:
            for g, h in enumerate(hs):
                B = SL[g]
                tt('pool', B['qb'].ap(), B['qs'].ap(), B['tb'].ap(), ALU.mult, r=[B['qs'], B['tb']], w=[B['qb']])
                act(B['kb'].ap(), B['td'].ap(), AF.Copy, r=[B['td']], w=[B['kb']])

    def ph_E(hs, SL):
        bs = []
        for g, h in enumerate(hs):
            B = SL[g]
            b = alloc()
            bs.append(b)
            for s in range(4):
                o = PSB(b, s * 128, 128)
                i_ = B['kdT'].sub(s * 128, 128)
                S.op('pe', 'transpose', (o.ap(), i_.ap(), identb.ap()), r=[i_, identb], w=[o])
        for g, h in enumerate(hs):
            B = SL[g]
            pv2 = PSB(bs[g], 0, 512)
            S.op('dve', 'tensor_copy', (B['kdtm'].ap(), pv2.ap()), r=[pv2], w=[B['kdtm']])
            rel(bs[g])

    def ph_F(hs, SL, full, cur):
        pds = []
        for g, h in enumerate(hs):
            B = SL[g]
            banks = alloc(2)
            pd = []
            for c in range(8):
                s_, hf = c // 2, c % 2
                o = PS(banks[hf], s_ * 128, 128)
                l = B['kdtm'].sub(s_ * 128, 128)
                rr = B['vh'].sub(s_ * 128, 128)
                mm(o.ap(), l.ap(p0=hf * 64, np_=64), rr.ap(p0=hf * 64, np_=64), True, True, r=[l, rr], w=[o])
                pd.append(o)
            pds.append((pd, banks))
        for c in range(8):
            for g, h in enumerate(hs):
                B = SL[g]
                pd, banks = pds[g]
                Sin_ = (S32 if c % 2 == 0 else S32b).sub(h * 128, 128)
                Sh = (S32b if c % 2 == 0 else S32).sub(h * 128, 128)
                ebl = col(B['tb'], c * 64 + 63)
                S.op('dve', 'scalar_tensor_tensor', (Sh.ap(), Sin_.ap(), ebl.ap(), pd[c].ap(), ALU.mult, ALU.add),
                     r=[Sin_, ebl, pd[c]], w=[Sh])
                if c < 7:
                    if not full:
                        continue
                    dst = B['Sall'].sub(c * 128, 128)
                else:
                    dst = Sc[1 - cur].sub(h * 128, 128)
                if c % 2 == 0:
                    act(dst.ap(), Sh.ap(), AF.Copy, r=[Sh], w=[dst])
                else:
                    S.op('pool', 'tensor_copy', (dst.ap(), Sh.ap()), r=[Sh], w=[dst])
        for g in range(len(hs)):
            rel(*pds[g][1])

    def ph_G1(hs, SL, cur, st):
        maskv = cst.sub(CC_MASK, 128)
        bss = []
        for g, h in enumerate(hs):
            B = SL[g]
            bs = alloc()
            bss.append(bs)
            for s in range(4):
                o = PS(bs, s * 128, 128)
                l = B['kb'].sub(s * 128, 128)
                rr = B['qb'].sub(s * 128, 128)
                mm(o.ap(), l.ap(), rr.ap(), True, True, r=[l, rr], w=[o])
        for g, h in enumerate(hs):
            psS = PS(bss[g])
            tt('dve', scm[g].ap([[128, 4], [1, 128]]), psS.ap([[128, 4], [1, 128]]), maskv.ap([[0, 4], [1, 128]]),
               ALU.mult, r=[psS, maskv], w=[scm[g]])
            rel(bss[g])

    def ph_G2(hs, SL, cur, st):
        bos = []
        for g, h in enumerate(hs):
            B = SL[g]
            bo = alloc()
            bos.append(bo)
            for s in range(4):
                o = PS(bo, s * 128, 128)
                l = B['vh'].sub(s * 128, 128)
                rr = scm[g].sub(s * 128, 128)
                mm(o.ap(), l.ap(), rr.ap(), True, False, r=[l, rr], w=[o])
                for hf in range(2):
                    c = 2 * s + hf
                    Sin = Sc[cur].sub(h * 128, 128) if c == 0 else B['Sall'].sub((c - 1) * 128, 128)
                    o2 = PS(bo, s * 128 + hf * 64, 64)
                    rr2 = B['qb'].sub(c * 64, 64)
                    mm(o2.ap(), Sin.ap(), rr2.ap(), False, hf == 1, r=[Sin, rr2], w=[o2])
        for g, h in enumerate(hs):
            psO = PS(bos[g])
            act(osq[g].ap(), psO.ap(), AF.Square, r=[psO], w=[osq[g]])
        st['bos'] = bos

    def ph_G3(hs, SL, cur, st):
        bms = []
        for g, h in enumerate(hs):
            bm = alloc()
            bms.append(bm)
            psM = PS(bm)
            mm(psM.ap(), onesb.ap(), osq[g].ap(), True, True, r=[onesb, osq[g]], w=[psM])
        for g, h in enumerate(hs):
            psM = PS(bms[g])
            act(rsv[g].ap(), psM.ap(), AF.Ln, r=[psM, epsc], w=[rsv[g]], bias=epsc.ap(), scale=1.0 / 128)
            rel(bms[g])
            act(rsv[g].ap(), rsv[g].ap(), AF.Exp, r=[rsv[g]], w=[rsv[g]], scale=-0.5)

    def ph_G4(hs, SL, cur, st):
        hg = col(par, PC_HG)
        bos = st['bos']
        for g, h in enumerate(hs):
            B = SL[g]
            psO = PS(bos[g])
            tt('dve', rsv[g].ap(), psO.ap(), rsv[g].ap(), ALU.mult, r=[psO, rsv[g]], w=[rsv[g]])
            rel(bos[g])
            oh = o_a.sub(h * T, T)
            S.op('dve', 'scalar_tensor_tensor', (oh.ap(), rsv[g].ap(), hg.ap(), B['sg'].ap(), ALU.mult, ALU.mult),
                 r=[rsv[g], hg, B['sg']], w=[oh])

    onec = col(stat, 13)
    eblv = sb("eblv", 8, F32)

    def pre_P(hs):
        return ph_P(hs, False)

    def pre_A(hs, SL, prs):
        for g, h in enumerate(hs):
            B = SL[g]
            act(B['ta'].ap(), prs[g]['z'].ap(), AF.Sigmoid, r=[prs[g]['z']], w=[B['ta']])
            S.op('dve', 'tensor_copy', (B['vh'].ap(), prs[g]['v'].ap()), r=[prs[g]['v']], w=[B['vh']])
            rel(prs[g]['z'].bank, prs[g]['v'].bank)
        for g, h in enumerate(hs):
            B = SL[g]
            lb, oml = col(lbv, h), col(lbv, 8 + h)
            act(B['tb'].ap(), B['ta'].ap(), AF.Ln, r=[B['ta'], lb, oml], w=[B['tb']], bias=lb.ap(), scale=oml.ap())

    def pre_BCD(hs, SL):
        for g, h in enumerate(hs):
            B = SL[g]
            S.op('dve', 'tensor_tensor_scan', (B['tc'].ap(), onec.ap([[0, T]]), B['tb'].ap(), 0.0, ALU.mult, ALU.add),
                 r=[onec, B['tb']], w=[B['tc']])
            bl = col(B['tc'], T - 1)
            S.op('dve', 'tensor_scalar', (B['tb'].ap(), B['tc'].ap(), -1.0, bl.ap(), ALU.mult, ALU.add),
                 r=[B['tc']], w=[B['tb']])
            oml, noml = col(lbv, 8 + h), col(lbv, 16 + h)
            S.op('dve', 'tensor_scalar', (B['ta'].ap(), B['ta'].ap(), noml.ap(), oml.ap(), ALU.mult, ALU.add),
                 r=[B['ta'], noml, oml], w=[B['ta']])
        for g, h in enumerate(hs):
            B = SL[g]
            act(B['tb'].ap(), B['tb'].ap(), AF.Exp, r=[B['tb']], w=[B['tb']])
            bl = col(B['tc'], T - 1)
            act(col(eblv, h).ap(), bl.ap(), AF.Exp, r=[bl], w=[col(eblv, h)])
        for g, h in enumerate(hs):
            B = SL[g]
            tt('pool', B['kdT'].ap(), B['ta'].ap(), B['tb'].ap(), ALU.mult, r=[B['ta'], B['tb']], w=[B['kdT']])

    def pre_back(hs, SL):
        ph_E(hs, SL)
        bds = []
        for g, h in enumerate(hs):
            B = SL[g]
            b = alloc()
            bds.append(b)
            o = PS(b, 0, 128)
            for s in range(4):
                l = B['kdtm'].sub(s * 128, 128)
                rr = B['vh'].sub(s * 128, 128)
                mm(o.ap(), l.ap(), rr.ap(), s == 0, s == 3, r=[l, rr], w=[o])
        for g, h in enumerate(hs):
            o = PS(bds[g], 0, 128)
            Sh = S32.sub(h * 128, 128)
            S.op('dve', 'scalar_tensor_tensor', (Sh.ap(), Sh.ap(), col(eblv, h).ap(), o.ap(), ALU.mult, ALU.add),
                 r=[Sh, col(eblv, h), o], w=[Sh])
            rel(bds[g])

    def pre_finish(cur):
        for h in range(H):
            Sh = S32.sub(h * 128, 128)
            dst = Sc[cur].sub(h * 128, 128)
            S.op('pool', 'tensor_copy', (dst.ap(), Sh.ap()), r=[Sh], w=[dst])

    def hgrn_stage(full, cur):
        assert full
        groups = [[0, 1], [2, 3], [4, 5], [6, 7]]

        def SLs(p):
            return SLF4[2 * (p % 2):2 * (p % 2) + 2]

        prs = ph_P(groups[0], True)
        ph_A(groups[0], SLs(0), prs, True)
        ph_B(groups[0], SLs(0))
        ph_C(groups[0], SLs(0))
        ph_D(groups[0], SLs(0), True)
        ph_E(groups[0], SLs(0))
        for p, hs in enumerate(groups):
            nxt = groups[p + 1] if p + 1 < len(groups) else None
            st = {}
            ph_F(hs, SLs(p), True, cur)
            if nxt:
                prs = ph_P(nxt, True)
                ph_A(nxt, SLs(p + 1), prs, True)
            ph_G1(hs, SLs(p), cur, st)
            if nxt:
                ph_B(nxt, SLs(p + 1))
            ph_G2(hs, SLs(p), cur, st)
            if nxt:
                ph_C(nxt, SLs(p + 1))
            ph_G3(hs, SLs(p), cur, st)
            if nxt:
                ph_D(nxt, SLs(p + 1), True)
            ph_G4(hs, SLs(p), cur, st)
            if nxt:
                ph_E(nxt, SLs(p + 1))

    def glu_stage():
        for j in range(4):
            wb = load_block(GLU0 + j)
            for mi in range(2):
                m = 2 * j + mi
                pa = proj_fm(wb, mi * 256)
                pb = proj_fm(wb, mi * 256 + 128)
                sg_ = sgate[m % 2]
                act(sg_.ap(), pb.ap(), AF.Sigmoid, r=[pb], w=[sg_])
                um = u.sub(m * UW + CW - 1, T)
                tt('dve', um.ap(), pa.ap(), sg_.ap(), ALU.mult, r=[pa, sg_], w=[um])
                rel(pa.bank, pb.bank)

    def halo_shift():
        S.op('pool', 'tensor_copy', (u.ap([[UW, 8], [1, CW - 1]]), u.ap([[UW, 8], [1, CW - 1]], off=T)),
             r=[u.sub(m * UW + T, CW - 1) for m in range(8)], w=[u.sub(m * UW, CW - 1) for m in range(8)])

    def conv_stage():
        bmean, bex2 = alloc(2)
        pmean, pex2 = PS(bmean), PS(bex2)
        build_diag(0)
        for m in range(8):
            if m + 1 < 8:
                build_diag(m + 1)
            b = alloc()
            py = PS(b)
            for j in range(CW):
                l = diag.sub(((m % 2) * CW + j) * 128, 128)
                rr = u.sub(m * UW + j, T)
                mm(py.ap(), l.ap(), rr.ap(), j == 0, j == CW - 1, r=[l, rr], w=[py])
            cbm = col(par, PC_CB + m)
            ym = y32.sub(m * T, T)
            act(ym.ap(), py.ap(), AF.Identity, r=[py, cbm], w=[ym], bias=cbm.ap())
            yb, yq = ybf[m % 2], ysq[m % 2]
            S.op('dve', 'tensor_copy', (yb.ap(), ym.ap()), r=[ym], w=[yb])
            act(yq.ap(), py.ap(), AF.Square, r=[py, cbm], w=[yq], bias=cbm.ap())
            rel(b)
            mm(pmean.ap(), onesb.ap(), yb.ap(), m == 0, m == 7, r=[onesb, yb], w=[pmean])
            mm(pex2.ap(), onesb.ap(), yq.ap(), m == 0, m == 7, r=[onesb, yq], w=[pex2])
        halo_shift()
        return bmean, bex2, pmean, pex2

    def conv_ln(bmean, bex2, pmean, pex2):
        act(meanv.ap(), pmean.ap(), AF.Copy, r=[pmean], w=[meanv], scale=1.0 / D)
        tt('dve', msqv.ap(), meanv.ap(), meanv.ap(), ALU.mult, r=[meanv], w=[msqv])
        S.op('dve', 'scalar_tensor_tensor', (msqv.ap(), pex2.ap(), 1.0 / D, msqv.ap(), ALU.mult, ALU.subtract),
             r=[pex2, msqv], w=[msqv])
        rel(bmean, bex2)
        act(rs2v.ap(), msqv.ap(), AF.Ln, r=[msqv, epsc], w=[rs2v], bias=epsc.ap())
        act(rs2v.ap(), rs2v.ap(), AF.Exp, r=[rs2v], w=[rs2v], scale=-0.5)

    def conv_norm(m):
        ym = y32.sub(m * T, T)
        tt('dve', ym.ap(), ym.ap(), meanv.ap(), ALU.subtract, r=[ym, meanv], w=[ym])
        tt('pool', ym.ap(), ym.ap(), rs2v.ap(), ALU.mult, r=[ym, rs2v], w=[ym])

    def conv_silu_all():
        for m in range(8):
            ym = y32.sub(m * T, T)
            lg, lbb = col(par, PC_LG + m), col(par, PC_LB + m)
            yn = ynorm.sub(m * T, T)
            act(yn.ap(), ym.ap(), AF.Silu, r=[ym, lg, lbb], w=[yn], bias=lbb.ap(), scale=lg.ap())

    def branch_stage(gate0, w0, src, is_b, js=(0, 1), hook=None):
        for j in js:
            wg = load_block(gate0 + j)
            ww = load_block(w0 + j)
            for mi in range(4):
                m = 4 * j + mi
                pg = proj_fm(wg, mi * 128)
                b = alloc()
                po = PS(b)
                for kc in range(8):
                    l = ww.sub(kc * 512 + mi * 128, 128)
                    rr = src.sub(kc * T, T)
                    mm(po.ap(), l.ap(), rr.ap(), kc == 0, kc == 7, r=[l, rr], w=[po])
                sg_ = sgate[m % 2]
                act(sg_.ap(), pg.ap(), AF.Sigmoid, r=[pg], w=[sg_])
                rel(pg.bank)
                ma = mergedA.sub(m * T, T)
                if not is_b:
                    tt('dve', ma.ap(), po.ap(), sg_.ap(), ALU.mult, r=[po, sg_], w=[ma])
                else:
                    tb_ = tbv[m % 2]
                    tt('dve', tb_.ap(), po.ap(), sg_.ap(), ALU.mult, r=[po, sg_], w=[tb_])
                    mg = merged.sub(m * T, T)
                    tt('pool', mg.ap(), tb_.ap(), ma.ap(), ALU.add, r=[tb_, ma], w=[mg])
                rel(b)
                if hook is not None:
                    hook(mi)

    def wout_stage():
        wbs = [load_block(WO0 + 0), load_block(WO0 + 1)]
        for s in range(4):
            for hf in range(2):
                wb = wbs[hf]
                b = alloc()
                po = PS(b)
                for kc in range(8):
                    l = merged.sub(kc * T + s * 128, 128)
                    rr = wb.sub(kc * 512, 512)
                    mm(po.ap(), l.ap(), rr.ap(), kc == 0, kc == 7, r=[l, rr], w=[po])
                xs = x.sub(s * D + hf * 512, 512)
                tt('dve', xs.ap(), po.ap(), xs.ap(), ALU.add, r=[po, xs], w=[xs])
                rel(b)
            if s >= 1:
                rmsnorm_sub(s - 1, gmlp, xnF)
        rmsnorm_sub(3, gmlp, xnF)

    def mlp_stage(nxt=None):
        for j in range(8):
            wb = load_block(M10 + j)
            for fi in range(4):
                f = 4 * j + fi
                pz = proj_fm(wb, fi * 128)
                rr_ = r32[f % 2]
                act(rr_.ap(), pz.ap(), AF.Relu, r=[pz], w=[rr_])
                rel(pz.bank)
                zf = z2T.sub(f * T, T)
                tt('pool' if f % 2 == 0 else 'dve', zf.ap(), rr_.ap(), rr_.ap(), ALU.mult, r=[rr_], w=[zf])
        if nxt is not None:
            pf_elem(nxt)
        for hf in range(2):
            if hf == 1 and nxt is not None:
                pf_pe()
            banks = alloc(4)
            for kg in range(4):
                wb = load_block(M20 + hf * 4 + kg)
                for s in range(4):
                    po = PS(banks[s])
                    for kc in range(8):
                        l = z2T.sub((kg * 8 + kc) * T + s * 128, 128)
                        rr = wb.sub(kc * 512, 512)
                        mm(po.ap(), l.ap(), rr.ap(), kg == 0 and kc == 0, kg == 3 and kc == 7, r=[l, rr], w=[po])
            for s in range(4):
                po = PS(banks[s])
                xs = x.sub(s * D + hf * 512, 512)
                tt('dve', xs.ap(), po.ap(), xs.ap(), ALU.add, r=[po, xs], w=[xs])
            rel(*banks)
        if nxt is not None:
            preload(HB0 + 0)
            preload(HB0 + 1)

    xstage = [A((36 + 4 * i) * K, D, F32) for i in range(4)]
    xn4 = [A((52 + 2 * i) * K, D, BF16) for i in range(4)]

    def pf_elem(t):
        for s in range(4):
            xs = xstage[s]
            dma(xs, dram(x_d[t * 4 + s], "xin", 0, 1), "xs%d" % s)
            rs = rms_stats(xs, s, xn4[s])
            S.op('dve', 'scalar_tensor_tensor', (xn4[s].ap(), xs.ap(), rs.ap(), gmix.ap(), ALU.mult, ALU.mult), r=[xs, rs, gmix], w=[xn4[s]])

    def pf_pe():
        for s in range(4):
            b = alloc()
            xb = xn4[s]
            for kc in range(8):
                o = PSB(b, kc * 128, 128)
                i_ = xb.sub(kc * 128, 128)
                S.op('pe', 'transpose', (o.ap(), i_.ap(), identb.ap()), r=[i_, identb], w=[o])
            pv = PSB(b)
            dst_ap = hT.ap([[512, 8], [1, 128]], off=s * 128)
            src_ap = pv.ap([[128, 8], [1, 128]])
            wl = [hT.sub(kc * 512 + s * 128, 128) for kc in range(8)]
            if s % 2:
                S.op('dve', 'tensor_copy', (dst_ap, src_ap), r=[pv], w=wl)
            else:
                act(dst_ap, src_ap, AF.Copy, r=[pv], w=wl)
            rel(b)

    def final_stage(t):
        toks = []
        for s in range(4):
            xs = x.sub(s * D, D)
            rs = rms_stats(xs, s, xnF[s % 2])
            S.op('dve', 'scalar_tensor_tensor', (xs.ap(), xs.ap(), rs.ap(), gfin.ap(), ALU.mult, ALU.mult),
                 r=[xs, rs, gfin], w=[xs])
            toks.append(dma(dram(y_d[t * 4 + s], "y_d", t * 4 + s, t * 4 + s + 1), xs, "yo%d" % s))
        return toks

    def load_x(src, t):
        for s in range(4):
            dma(x.sub(s * D, D), dram(src[t * 4 + s], "xin", 0, 1), "x%d" % s)

    last_tok = []
    S.op('dve', 'memset', (onec.ap(), 1.0), w=[onec])
    for h in range(8):
        cast_group([HB0 + h], "cb%d" % h)
    later = [([b], "cq%d" % b) for b in [8, 9, 10, 11, 12, 13, 16, 17, 14, 15, 18, 19, 20, 21] + list(range(22, 38))]
    load_wpre()
    load_x(xp_d, 0)
    NG = 2 * NT

    def grp(k):
        half = k % 2
        return [4 * half + i for i in range(4)], SLP[4 * (k % 2):4 * (k % 2) + 4]

    def tile_start(k):
        if k % 2 == 0:
            t = k // 2
            rmsnorm_T(gmix, xnP)
            if t + 1 < NT:
                load_x(xp_d, t + 1)
            else:
                load_x(x_d, 0)

    tile_start(0)
    prs_k = pre_P(grp(0)[0])
    pre_A(grp(0)[0], grp(0)[1], prs_k)
    for k in range(NG):
        hs, SLg = grp(k)
        if k + 1 < NG:
            tile_start(k + 1)
            prs_n = pre_P(grp(k + 1)[0])
        pre_BCD(hs, SLg)
        for _ in range(2):
            if later and (k >= 1 or NT == 1):
                cast_group(*later.pop(0))
        if k + 1 < NG:
            pre_A(grp(k + 1)[0], grp(k + 1)[1], prs_n)
        pre_back(hs, SLg)
    while later:
        cast_group(*later.pop(0))
    glu_stage()
    halo_shift()
    cur = 0
    pre_finish(cur)
    for t in range(NT):
        if t == 0:
            rmsnorm_T(gmix, xnF)
        hgrn_stage(True, cur)
        cur = 1 - cur
        glu_stage()
        cs_ = conv_stage()
        branch_stage(GA0, WA0, o_a, False, js=(0,))
        conv_ln(*cs_)
        branch_stage(GA0, WA0, o_a, False, js=(1,), hook=lambda mi: (conv_norm(2 * mi), conv_norm(2 * mi + 1)))
        conv_silu_all()
        branch_stage(GB0, WB0, ynorm, True)
        wout_stage()
        mlp_stage(t + 1 if t + 1 < NT else None)
        last_tok = final_stage(t)
        if t + 1 < NT:
            load_x(x_d, t + 1)
    assert len(free_banks) == 8
    S.emit(nc, es, list(last_tok))
    es.close()
    S.stats = {e: len(S.ops[e]) for e in ENGS}
    print('ops per engine:', S.stats, flush=True)
    return nc


_CACHE = {}


def _prep_weights(inp):
    w_in = np.asarray(inp["w_in"][0], dtype=np.float32)
    blocks = []

    def blk(w, cols):
        kt = w.shape[0] // 128
        b = w[:, cols].reshape(kt, 128, len(cols)).transpose(1, 0, 2)
        return np.ascontiguousarray(b).reshape(128, kt * len(cols))

    ar = np.arange
    for h in range(8):
        cols = np.concatenate([1024 + h * 128 + ar(128), 2048 + h * 128 + ar(128), h * 128 + ar(128), 3072 + h * 128 + ar(128)])
        blocks.append(blk(w_in, cols))
    for j in range(4):
        m0, m1 = 2 * j, 2 * j + 1
        cols = np.concatenate([4096 + m0 * 128 + ar(128), 5120 + m0 * 128 + ar(128),
                               4096 + m1 * 128 + ar(128), 5120 + m1 * 128 + ar(128)])
        blocks.append(blk(w_in, cols))
    for j in range(2):
        blocks.append(blk(w_in, 6144 + j * 512 + ar(512)))
    for j in range(2):
        blocks.append(blk(w_in, 7168 + j * 512 + ar(512)))
    for name in ("w_a_out", "w_b_out", "w_out"):
        w = np.asarray(inp[name][0], dtype=np.float32)
        for j in range(2):
            blocks.append(blk(w, j * 512 + ar(512)))
    w1 = np.asarray(inp["w_mlp_in"][0], dtype=np.float32)
    for j in range(8):
        blocks.append(blk(w1, j * 512 + ar(512)))
    w2 = np.asarray(inp["w_mlp_out"][0], dtype=np.float32)
    for hf in range(2):
        for kg in range(4):
            blocks.append(blk(w2[kg * 1024:(kg + 1) * 1024], hf * 512 + ar(512)))
    wall = np.stack(blocks).astype(np.float32)
    assert wall.shape == (NBLK, 128, 4096)

    def fm(v):
        return np.asarray(v, dtype=np.float32).reshape(8, 128).T

    par = np.zeros((128, NPAR), np.float32)
    par[:, PC_GM:PC_GM + 8] = fm(inp["norm_mix_g"][0])
    par[:, PC_GML:PC_GML + 8] = fm(inp["norm_mlp_g"][0])
    par[:, PC_P0:PC_P0 + 8] = fm(inp["lb_param"][0])
    par[:, PC_P1:PC_P1 + 8] = fm(inp["lb_param"][1])
    par[:, PC_HG] = np.asarray(inp["hgrn_norm_g"][0], dtype=np.float32)
    par[:, PC_CB:PC_CB + 8] = fm(inp["conv_b"][0])
    par[:, PC_LG:PC_LG + 8] = fm(inp["conv_ln_g"][0])
    par[:, PC_LB:PC_LB + 8] = fm(inp["conv_ln_b"][0])
    cw = np.asarray(inp["conv_w"][0], dtype=np.float32)
    par[:, PC_CW:] = cw.reshape(CW, 8, 128).transpose(2, 1, 0).reshape(128, 8 * CW)
    gfin = np.ascontiguousarray(np.broadcast_to(np.asarray(inp["norm_final_g"], dtype=np.float32)[None, :], (128, D)))
    gmix = np.ascontiguousarray(np.broadcast_to(np.asarray(inp["norm_mix_g"][0], dtype=np.float32)[None, :], (128, D)))
    gmlp = np.ascontiguousarray(np.broadcast_to(np.asarray(inp["norm_mlp_g"][0], dtype=np.float32)[None, :], (128, D)))
    cst = np.zeros((128, NCST), np.float32)
    cst[:, CC_ID:CC_ID + 128] = np.eye(128, dtype=np.float32)
    sp, tt = np.meshgrid(ar(128), ar(128), indexing="ij")
    cst[:, CC_MASK:CC_MASK + 128] = ((sp // 64 == tt // 64) & (sp <= tt)).astype(np.float32)
    cst[:, CC_SCAN:] = (ar(T) % 64 != 0).astype(np.float32)[None, :]
    return wall, par, gfin, cst, gmix, gmlp


def kernel(**inp):
    xfull = np.asarray(inp["x"], dtype=np.float32)
    B, Sq, _ = xfull.shape
    half = Sq // 2
    NT = half // T
    assert half % T == 0 and B * 2 == 8
    if NT not in _CACHE:
        _CACHE[NT] = build(NT)
    nc = _CACHE[NT]
    wall, par, gfin, cst, gmix, gmlp = _prep_weights(inp)
    in_maps = []
    zeros = np.zeros((NT * 4, 128, D), np.float32)
    for c in range(8):
        b, hf = c // 2, c % 2
        xc = np.ascontiguousarray(xfull[b, hf * half:(hf + 1) * half]).reshape(NT * 4, 128, D)
        xp = zeros if hf == 0 else np.ascontiguousarray(xfull[b, 0:half]).reshape(NT * 4, 128, D)
        in_maps.append({"x": xc, "xprev": xp, "wall": wall, "par": par, "cst": cst, "gfin": gfin, "gmix": gmix, "gmlp": gmlp})
    res = run_bass_kernel_spmd(nc, in_maps, core_ids=list(range(8)))
    out = np.empty((B, Sq, D), np.float32)
    for c in range(8):
        b, hf = c // 2, c % 2
        out[b, hf * half:(hf + 1) * half] = np.asarray(res.results[c]["y"]).reshape(half, D)
    return out
```

```python
import contextlib
import numpy as np
import concourse.bass as bass
import concourse.mybir as mybir
from concourse.bass_utils import run_bass_kernel_spmd

F32 = mybir.dt.float32
BF16 = mybir.dt.bfloat16
AF = mybir.ActivationFunctionType
ALU = mybir.AluOpType

P = 128
D = 1024
H = 8
T = 512
EPS = 1e-6
CW = 31
UW = T + CW - 1
NBLK = 38
HB0, GLU0, GA0, GB0, WA0, WB0, WO0, M10, M20 = 0, 8, 12, 14, 16, 18, 20, 22, 30
PC_GM, PC_GML, PC_P0, PC_P1, PC_HG, PC_CB, PC_LG, PC_LB, PC_CW = 0, 8, 16, 24, 32, 33, 41, 49, 57
NPAR = 57 + 8 * CW
CC_ID, CC_MASK, CC_SCAN = 0, 128, 256
NCST = 256 + T

ENGS = ['pe', 'act', 'dve', 'pool', 'sp']


class V:
    def __init__(s, t, W, name, off, n, esz, ap=None, whole=None):
        s.t, s.W, s.name, s.off, s.n, s.esz = t, W, name, off, n, esz
        s._ap = ap
        s.whole = whole

    def res(s):
        if s.whole is not None:
            return (s.name, 0, s.whole)
        return (s.name, s.off * s.esz, (s.off + s.n) * s.esz)

    def sub(s, a, n):
        assert a + n <= s.n
        return V(s.t, s.W, s.name, s.off + a, n, s.esz)

    def ap(s, dims=None, off=0, p0=0, np_=P):
        if s._ap is not None:
            return s._ap
        if dims is None:
            dims = [[1, s.n - off]]
        return bass.AP(s.t, p0 * s.W + s.off + off, [[s.W, np_]] + [list(d) for d in dims])


class Sched:
    def __init__(s):
        s.ops = {e: [] for e in ENGS}
        s.res = {}
        s.seen = {e: {} for e in ENGS}
        s.dma_cnt = {}
        s.nbank = 0
        s.held = set()

    def bank(s):
        while True:
            b = s.nbank % 8
            s.nbank += 1
            if b not in s.held:
                return b

    def op(s, eng, iname, args, kw=None, r=(), w=(), dma=None):
        idx = len(s.ops[eng])
        deps = {}

        def add(d):
            if deps.get(d[0], -1) < d[1]:
                deps[d[0]] = d[1]

        rres = [v.res() for v in r]
        wres = [v.res() for v in w]
        for (name, lo, hi) in rres:
            R = s.res.setdefault(name, {'w': [], 'r': []})
            for (l, h, d) in R['w']:
                if l < hi and lo < h:
                    add(d)
        for (name, lo, hi) in wres:
            R = s.res.setdefault(name, {'w': [], 'r': []})
            for (l, h, d) in R['w']:
                if l < hi and lo < h:
                    add(d)
            for (l, h, d) in R['r']:
                if l < hi and lo < h:
                    add(d)
        if dma is not None:
            s.dma_cnt[dma] = s.dma_cnt.get(dma, 0) + 1
            tok = (('dma', dma), s.dma_cnt[dma] * 16)
        else:
            tok = (eng, idx)
        for (name, lo, hi) in rres:
            R = s.res[name]
            R['r'] = [x for x in R['r'] if not (x[2][0] == tok[0] and lo <= x[0] and x[1] <= hi)]
            R['r'].append((lo, hi, tok))
        for (name, lo, hi) in wres:
            R = s.res[name]
            R['w'] = [x for x in R['w'] if not (lo <= x[0] and x[1] <= hi)]
            R['r'] = [x for x in R['r'] if not (lo <= x[0] and x[1] <= hi)]
            R['w'].append((lo, hi, tok))
        waits = []
        for src, i in deps.items():
            if src == 'pe' and eng == 'pe':
                continue
            if s.seen[eng].get(src, -1) >= i:
                continue
            s.seen[eng][src] = i
            waits.append((src, i))
        s.ops[eng].append({'name': iname, 'args': args, 'kw': kw or {}, 'waits': waits, 'dma': dma})
        return tok

    def emit(s, nc, es, final_waits):
        sig = {e: set() for e in ENGS}
        for e in ENGS:
            for o in s.ops[e]:
                for (src, i) in o['waits']:
                    if isinstance(src, str):
                        sig[src].add(i)
        cnt = {}
        for e in ENGS:
            c = 0
            m = {}
            for i in range(len(s.ops[e])):
                if i in sig[e]:
                    c += 1
                    m[i] = c
            cnt[e] = m
        sems = {}
        for e in ENGS:
            if cnt[e]:
                sems[e] = es.enter_context(nc.semaphore('s_' + e))
        for k in s.dma_cnt:
            sems[('dma', k)] = es.enter_context(nc.semaphore('d_' + k))
        block = es.enter_context(nc.Block())

        def run(ename, eobj):
            for i, o in enumerate(s.ops[ename]):
                for (src, v) in o['waits']:
                    if isinstance(src, str):
                        eobj.wait_ge(sems[src], cnt[src][v])
                    else:
                        eobj.wait_ge(sems[src], v)
                ins = getattr(eobj, o['name'])(*o['args'], **o['kw'])
                if o['dma'] is not None:
                    ins.then_inc(sems[('dma', o['dma'])], 16)
                elif i in cnt[ename]:
                    ins.then_inc(sems[ename], 1)
            if ename == 'sp':
                for (src, v) in final_waits:
                    eobj.wait_ge(sems[src], v)

        @block.tensor
        def _(e):
            run('pe', e)

        @block.scalar
        def _(e):
            run('act', e)

        @block.vector
        def _(e):
            run('dve', e)

        @block.gpsimd
        def _(e):
            run('pool', e)

        @block.sync
        def _(e):
            run('sp', e)


def build(NT):
    import os as _os
    nc = bass.Bass("TRN2", target_bir_lowering=False)
    x_d = nc.dram_tensor("x", [NT * 4, P, D], F32, kind="ExternalInput").ap()
    xp_d = nc.dram_tensor("xprev", [NT * 4, P, D], F32, kind="ExternalInput").ap()
    w_d = nc.dram_tensor("wall", [NBLK, P, 4096], F32, kind="ExternalInput").ap()
    par_d = nc.dram_tensor("par", [P, NPAR], F32, kind="ExternalInput").ap()
    cst_d = nc.dram_tensor("cst", [P, NCST], F32, kind="ExternalInput").ap()
    gfin_d = nc.dram_tensor("gfin", [P, D], F32, kind="ExternalInput").ap()
    gmix_d = nc.dram_tensor("gmix", [P, D], F32, kind="ExternalInput").ap()
    gmlp_d = nc.dram_tensor("gmlp", [P, D], F32, kind="ExternalInput").ap()
    y_d = nc.dram_tensor("y", [NT * 4, P, D], F32, kind="ExternalOutput").ap()
    wbf_d = nc.dram_tensor("wbf", [NBLK, P, 4096], BF16).ap()

    es = contextlib.ExitStack()
    S = Sched()

    def sb(name, n, dt):
        t = es.enter_context(nc.sbuf_tensor("sb_" + name, [P, n], dt))
        return V(t, n, name, 0, n, 4 if dt == F32 else 2)

    par = sb("par", NPAR, F32)
    lbv = sb("lbv", 32, F32)
    cst = sb("cst", NCST, F32)
    identb = sb("identb", 128, BF16)
    onesb = sb("onesb", 128, BF16)
    gfin = sb("gfin", D, F32)
    gmix = sb("gmix", D, F32)
    gmlp = sb("gmlp", D, F32)
    diag = sb("diag", 2 * CW * 128, BF16)
    S32 = sb("S32", H * 128, F32)
    S32b = sb("S32b", H * 128, F32)
    Sc = [sb("Sc0", H * 128, BF16), sb("Sc1", H * 128, BF16)]
    u = sb("u", 8 * UW, BF16)
    x = sb("x", 4 * D, F32)
    hT = sb("hT", 8 * T, BF16)
    wbuf = [sb("wb%d" % i, 4096, BF16) for i in range(3)]
    stat = sb("stat", 16, F32)
    ARENA = 106 * 1024
    ar_t = es.enter_context(nc.sbuf_tensor("arena", [P, ARENA // 2], BF16))
    ar_f = ar_t.bitcast(F32)

    def A(byte_off, n, dt):
        byte_off = int(byte_off)
        if dt == F32:
            assert byte_off % 4 == 0 and byte_off + 4 * n <= ARENA
            return V(ar_f, ARENA // 4, "arena", byte_off // 4, n, 4)
        assert byte_off % 2 == 0 and byte_off + 2 * n <= ARENA
        return V(ar_t, ARENA // 2, "arena", byte_off // 2, n, 2)

    K = 1024
    o_a = A(0, 8 * T, BF16)
    mergedA = A(8 * K, 8 * T, F32)

    def slot_full(g):
        b = 24 * K + g * 19 * K
        return dict(ta=A(b, T, F32), tb=A(b + 2 * K, T, F32), tc=A(b + 4 * K, T, F32), td=A(b + 6 * K, T, F32),
                    qs=A(b + 8 * K, T, F32), sg=A(b + 10 * K, T, F32), qb=A(b + 12 * K, T, BF16),
                    kb=A(b + 13 * K, T, BF16), kdT=A(b + 14 * K, T, BF16), kdtm=A(b + 15 * K, T, BF16),
                    vh=A(b + 16 * K, T, BF16), Sall=A(b + 17 * K, 7 * 128, BF16))

    def slot_pre(g):
        b = g * 8 * K
        return dict(ta=A(b, T, F32), tb=A(b + 2 * K, T, F32), tc=A(b + 4 * K, T, F32),
                    kdT=A(b + 6 * K, T, BF16), kdtm=A(b + 6 * K, T, BF16), vh=A(b + 7 * K, T, BF16))

    SLF4 = [slot_full(g) for g in range(4)]
    wpre = A(64 * K, 8 * 2048, BF16)
    SLP = [slot_pre(g) for g in range(8)]
    osq = [A(8 * K, T, BF16), A(9 * K, T, BF16)]
    scm = [A(10 * K, T, BF16), A(11 * K, T, BF16)]
    rsv = [A(100 * K, T, F32), A(102 * K, T, F32)]
    xnF = [A(12 * K, D, BF16), A(14 * K, D, BF16)]
    xnP = [A(96 * K, D, BF16), A(98 * K, D, BF16)]
    ynorm = A(70 * K, 8 * T, BF16)
    y32 = A(24 * K, 8 * T, F32)
    ybf = [A(40 * K, T, BF16), A(41 * K, T, BF16)]
    ysq = [A(42 * K, T, BF16), A(43 * K, T, BF16)]
    meanv = A(44 * K, T, F32)
    msqv = A(46 * K, T, F32)
    rs2v = A(48 * K, T, F32)
    tcv = [A(50 * K, T, F32), A(52 * K, T, F32)]
    sgate = [A(54 * K, T, F32), A(56 * K, T, F32)]
    tbv = [A(58 * K, T, F32), A(60 * K, T, F32)]
    merged = A(62 * K, 8 * T, BF16)
    z2T = A(0, 32 * T, BF16)
    r32 = [A(32 * K, T, F32), A(34 * K, T, F32)]
    ps_t = [es.enter_context(nc.psum_tensor("ps%d" % i, [P, 512], F32)) for i in range(8)]
    ps_b = [t.bitcast(BF16) for t in ps_t]
    free_banks = list(range(8))

    def alloc(n=1):
        assert len(free_banks) >= n, "out of PSUM banks"
        out = [free_banks.pop(0) for _ in range(n)]
        return out if n > 1 else out[0]

    def rel(*bs):
        for b in bs:
            assert b not in free_banks
            free_banks.append(b)

    def PS(b, off=0, n=512):
        return V(ps_t[b], 512, "ps%d" % b, off, n, 4, whole=2048)

    def PSB(b, off=0, n=1024):
        return V(ps_b[b], 1024, "ps%d" % b, off, n, 2, whole=2048)

    def dram(apx, name, lo, hi):
        return V(None, 0, name, lo, hi - lo, 1, ap=apx)

    def mm(out, lhsT, rhs, start, stop, r, w):
        S.op('pe', 'matmul', (out, lhsT, rhs), dict(start=start, stop=stop), r=r, w=w)

    def act(out_ap, in_ap, func, r, w, bias=None, scale=None, accum=None):
        kw = {}
        if bias is not None:
            kw['bias'] = bias
        if scale is not None:
            kw['scale'] = scale
        if accum is not None:
            kw['accum_out'] = accum
        S.op('act', 'activation', (out_ap, in_ap, func), kw, r=r, w=w)

    def tt(eng, out, in0, in1, op, r, w):
        S.op(eng, 'tensor_tensor', (out, in0, in1, op), r=r, w=w)

    def dma(out_v, in_v, key, eng='sp'):
        return S.op(eng, 'dma_start', (), dict(out=out_v.ap(), in_=in_v.ap()), r=[in_v], w=[out_v], dma=key)

    wslot = [0]

    preloaded = {}

    def preload(b):
        preloaded[b] = load_block(b)

    def load_block(b):
        if b in preloaded:
            return preloaded.pop(b)
        k = wslot[0] % 3
        wslot[0] += 1
        dma(wbuf[k], dram(wbf_d[b], "wbf", b, b + 1), "w%d" % k)
        return wbuf[k]

    def col(v, c):
        return v.sub(c, 1)

    epsc = col(stat, 12)

    dma(par, dram(par_d, "par_d", 0, 1), "c0")
    dma(cst, dram(cst_d, "cst_d", 0, 1), "c1")
    dma(gfin, dram(gfin_d, "gfin_d", 0, 1), "c2")
    dma(gmix, dram(gmix_d, "gmix_d", 0, 1), "c3")
    dma(gmlp, dram(gmlp_d, "gmlp_d", 0, 1), "c4")
    S.op('dve', 'memset', (epsc.ap(), EPS), w=[epsc])
    dv = lbv.sub(24, 8)
    tt('dve', dv.ap(), par.sub(PC_P0, 8).ap(), par.sub(PC_P1, 8).ap(), ALU.subtract, r=[par], w=[dv])
    act(lbv.sub(0, 8).ap(), dv.ap(), AF.Sigmoid, r=[dv], w=[lbv.sub(0, 8)])
    act(lbv.sub(8, 8).ap(), dv.ap(), AF.Sigmoid, r=[dv], w=[lbv.sub(8, 8)], scale=-1.0)
    S.op('dve', 'tensor_scalar', (lbv.sub(16, 8).ap(), lbv.sub(8, 8).ap(), -1.0, None, ALU.mult),
         r=[lbv.sub(8, 8)], w=[lbv.sub(16, 8)])
    S.op('dve', 'tensor_copy', (identb.ap(), cst.sub(CC_ID, 128).ap()), r=[cst], w=[identb])
    S.op('dve', 'memset', (onesb.ap(), 1.0), w=[onesb])
    S.op('dve', 'memset', (S32.ap(), 0.0), w=[S32])
    S.op('pool', 'memset', (u.ap(), 0.0), w=[u])
    identf = cst.sub(CC_ID, 128)

    def build_diag(m):
        for j in range(CW):
            dst = diag.sub(((m % 2) * CW + j) * 128, 128)
            sc = col(par, PC_CW + m * CW + j)
            e3 = j % 3
            if e3 == 0:
                S.op('pool', 'tensor_scalar', (dst.ap(), identf.ap(), sc.ap(), 0.0, ALU.mult, ALU.add), r=[identf, sc], w=[dst])
            elif e3 == 1:
                act(dst.ap(), identf.ap(), AF.Copy, r=[identf, sc], w=[dst], scale=sc.ap())
            else:
                S.op('dve', 'tensor_scalar', (dst.ap(), identf.ap(), sc.ap(), 0.0, ALU.mult, ALU.add), r=[identf, sc], w=[dst])

    def cast_group(blocks, key):
        for b in blocks:
            srcv = dram(w_d[b], "w_d", b, b + 1)
            dstv = dram(wbf_d[b], "wbf", b, b + 1)
            S.op('pool', 'dma_start', (), dict(out=dstv.ap(), in_=srcv.ap()), r=[srcv], w=[dstv], dma=key)
        final = (('dma', key), S.dma_cnt[key] * 16)
        R = S.res["wbf"]
        R['w'] = [(lo, hi, final if tk[0] == ('dma', key) else tk) for (lo, hi, tk) in R['w']]

    def load_wpre():
        for h in range(8):
            srcap = wbf_d[HB0 + h].rearrange("p (k c) -> p k c", c=512)[:, :, 0:256]
            dstv = wpre.sub(h * 2048, 2048)
            S.op('sp', 'dma_start', (), dict(out=dstv.ap([[256, 8], [1, 256]]), in_=srcap),
                 r=[dram(srcap, "wbf", HB0 + h, HB0 + h + 1)], w=[dstv], dma="wp%d" % h)

    def rms_stats(xs, s, junkv):
        ss, ln_, rs = col(stat, s), col(stat, 4 + s), col(stat, 8 + s)
        act(junkv.ap(), xs.ap(), AF.Square, r=[xs], w=[junkv, ss], accum=ss.ap())
        act(ln_.ap(), ss.ap(), AF.Ln, r=[ss, epsc], w=[ln_], bias=epsc.ap(), scale=1.0 / D)
        act(rs.ap(), ln_.ap(), AF.Exp, r=[ln_], w=[rs], scale=-0.5)
        return rs

    def rmsnorm_sub(s, gv, xn):
        xs = x.sub(s * D, D)
        b = alloc()
        xb = xn[s % 2]
        rs = rms_stats(xs, s, xb)
        S.op('dve', 'scalar_tensor_tensor', (xb.ap(), xs.ap(), rs.ap(), gv.ap(), ALU.mult, ALU.mult), r=[xs, rs, gv], w=[xb])
        for kc in range(8):
            o = PSB(b, kc * 128, 128)
            i_ = xb.sub(kc * 128, 128)
            S.op('pe', 'transpose', (o.ap(), i_.ap(), identb.ap()), r=[i_, identb], w=[o])
        pv = PSB(b)
        dst_ap = hT.ap([[512, 8], [1, 128]], off=s * 128)
        src_ap = pv.ap([[128, 8], [1, 128]])
        wl = [hT.sub(kc * 512 + s * 128, 128) for kc in range(8)]
        if s % 2:
            S.op('dve', 'tensor_copy', (dst_ap, src_ap), r=[pv], w=wl)
        else:
            act(dst_ap, src_ap, AF.Copy, r=[pv], w=wl)
        rel(b)

    def rmsnorm_T(gv, xn):
        for s in range(4):
            rmsnorm_sub(s, gv, xn)

    def proj_fm(wb, coff, WST=512):
        b = alloc()
        o = PS(b)
        for kc in range(8):
            l = wb.sub(kc * WST + coff, 128)
            rr = hT.sub(kc * 512, 512)
            mm(o.ap(), l.ap(), rr.ap(), kc == 0, kc == 7, r=[l, rr], w=[o])
        o.bank = b
        return o

    def proj_v(wb, WST=512):
        b = alloc()
        for s in range(4):
            o = PS(b, s * 128, 128)
            for kc in range(8):
                l = hT.sub(kc * 512 + s * 128, 128)
                rr = wb.sub(kc * WST + 128, 128)
                mm(o.ap(), l.ap(), rr.ap(), kc == 0, kc == 7, r=[l, rr], w=[o])
        o = PS(b)
        o.bank = b
        return o

    def ph_P(hs, full):
        prs = []
        for h in hs:
            pr = {}
            if full:
                wb = load_block(HB0 + h)
                pr['z'] = proj_fm(wb, 0)
                pr['v'] = proj_v(wb)
            else:
                wb = wpre.sub(h * 2048, 2048)
                pr['z'] = proj_fm(wb, 0, 256)
                pr['v'] = proj_v(wb, 256)
            if full:
                pr['q'] = proj_fm(wb, 256)
                pr['g'] = proj_fm(wb, 384)
            prs.append(pr)
        return prs

    def ph_A(hs, SL, prs, full):
        for g, h in enumerate(hs):
            B = SL[g]
            act(B['ta'].ap(), prs[g]['z'].ap(), AF.Sigmoid, r=[prs[g]['z']], w=[B['ta']])
            S.op('dve', 'tensor_copy', (B['vh'].ap(), prs[g]['v'].ap()), r=[prs[g]['v']], w=[B['vh']])
            rel(prs[g]['z'].bank, prs[g]['v'].bank)
        for g, h in enumerate(hs):
            B = SL[g]
            lb, oml = col(lbv, h), col(lbv, 8 + h)
            act(B['tb'].ap(), B['ta'].ap(), AF.Ln, r=[B['ta'], lb, oml], w=[B['tb']], bias=lb.ap(), scale=oml.ap())

    def ph_A2(hs, SL, prs):
        for g, h in enumerate(hs):
            B = SL[g]
            act(B['qs'].ap(), prs[g]['q'].ap(), AF.Silu, r=[prs[g]['q']], w=[B['qs']])
            act(B['sg'].ap(), prs[g]['g'].ap(), AF.Silu, r=[prs[g]['g']], w=[B['sg']])
            rel(prs[g]['q'].bank, prs[g]['g'].bank)

    def ph_B(hs, SL):
        scanm = cst.sub(CC_SCAN, T)
        for g, h in enumerate(hs):
            B = SL[g]
            S.op('dve', 'tensor_tensor_scan', (B['tc'].ap(), scanm.ap(), B['tb'].ap(), 0.0, ALU.mult, ALU.add),
                 r=[scanm, B['tb']], w=[B['tc']])
        for g, h in enumerate(hs):
            B = SL[g]
            oml, noml = col(lbv, 8 + h), col(lbv, 16 + h)
            S.op('dve', 'tensor_scalar', (B['ta'].ap(), B['ta'].ap(), noml.ap(), oml.ap(), ALU.mult, ALU.add),
                 r=[B['ta'], noml, oml], w=[B['ta']])

    def ph_C(hs, SL):
        for g, h in enumerate(hs):
            B = SL[g]
            act(B['tb'].ap(), B['tc'].ap(), AF.Exp, r=[B['tc']], w=[B['tb']])
            act(B['td'].ap(), B['tc'].ap(), AF.Exp, r=[B['tc']], w=[B['td']], scale=-1.0)

    def ph_D(hs, SL, full):
        for g, h in enumerate(hs):
            B = SL[g]
            tt('pool', B['td'].ap(), B['ta'].ap(), B['td'].ap(), ALU.mult, r=[B['ta'], B['td']], w=[B['td']])
        for g, h in enumerate(hs):
            B = SL[g]
            tt('dve', B['kdT'].ap([[64, 8], [1, 64]]), B['td'].ap([[64, 8], [1, 64]]),
               B['tb'].ap([[64, 8], [0, 64]], off=63), ALU.mult, r=[B['td'], B['tb']], w=[B['kdT']])
        if full:
            for g, h in enumerate(hs):
                B = SL[g]
                tt('pool', B['qb'].ap(), B['qs'].ap(), B['tb'].ap(), ALU.mult, r=[B['qs'], B['tb']], w=[B['qb']])
                act(B['kb'].ap(), B['td'].ap(), AF.Copy, r=[B['td']], w=[B['kb']])

    def ph_E(hs, SL):
        bs = []
        for g, h in enumerate(hs):
            B = SL[g]
            b = alloc()
            bs.append(b)
            for s in range(4):
                o = PSB(b, s * 128, 128)
                i_ = B['kdT'].sub(s * 128, 128)
                S.op('pe', 'transpose', (o.ap(), i_.ap(), identb.ap()), r=[i_, identb], w=[o])
        for g, h in enumerate(hs):
            B = SL[g]
            pv2 = PSB(bs[g], 0, 512)
            S.op('dve', 'tensor_copy', (B['kdtm'].ap(), pv2.ap()), r=[pv2], w=[B['kdtm']])
            rel(bs[g])

    def ph_F(hs, SL, full, cur):
        pds = []
        for g, h in enumerate(hs):
            B = SL[g]
            banks = alloc(2)
            pd = []
            for c in range(8):
                s_, hf = c // 2, c % 2
                o = PS(banks[hf], s_ * 128, 128)
                l = B['kdtm'].sub(s_ * 128, 128)
                rr = B['vh'].sub(s_ * 128, 128)
                mm(o.ap(), l.ap(p0=hf * 64, np_=64), rr.ap(p0=hf * 64, np_=64), True, True, r=[l, rr], w=[o])
                pd.append(o)
            pds.append((pd, banks))
        for c in range(8):
            for g, h in enumerate(hs):
                B = SL[g]
                pd, banks = pds[g]
                Sin_ = (S32 if c % 2 == 0 else S32b).sub(h * 128, 128)
                Sh = (S32b if c % 2 == 0 else S32).sub(h * 128, 128)
                ebl = col(B['tb'], c * 64 + 63)
                S.op('dve', 'scalar_tensor_tensor', (Sh.ap(), Sin_.ap(), ebl.ap(), pd[c].ap(), ALU.mult, ALU.add),
                     r=[Sin_, ebl, pd[c]], w=[Sh])
                if c < 7:
                    if not full:
                        continue
                    dst = B['Sall'].sub(c * 128, 128)
                else:
                    dst = Sc[1 - cur].sub(h * 128, 128)
                if c % 2 == 0:
                    act(dst.ap(), Sh.ap(), AF.Copy, r=[Sh], w=[dst])
                else:
                    S.op('pool', 'tensor_copy', (dst.ap(), Sh.ap()), r=[Sh], w=[dst])
        for g in range(len(hs)):
            rel(*pds[g][1])

    def ph_G1(hs, SL, cur, st):
        maskv = cst.sub(CC_MASK, 128)
        bss = []
        for g, h in enumerate(hs):
            B = SL[g]
            bs = alloc()
            bss.append(bs)
            for s in range(4):
                o = PS(bs, s * 128, 128)
                l = B['kb'].sub(s * 128, 128)
                rr = B['qb'].sub(s * 128, 128)
                mm(o.ap(), l.ap(), rr.ap(), True, True, r=[l, rr], w=[o])
        for g, h in enumerate(hs):
            psS = PS(bss[g])
            tt('dve', scm[g].ap([[128, 4], [1, 128]]), psS.ap([[128, 4], [1, 128]]), maskv.ap([[0, 4], [1, 128]]),
               ALU.mult, r=[psS, maskv], w=[scm[g]])
            rel(bss[g])

    def ph_G2(hs, SL, cur, st):
        bos = []
        for g, h in enumerate(hs):
            B = SL[g]
            bo = alloc()
            bos.append(bo)
            for s in range(4):
                o = PS(bo, s * 128, 128)
                l = B['vh'].sub(s * 128, 128)
                rr = scm[g].sub(s * 128, 128)
                mm(o.ap(), l.ap(), rr.ap(), True, False, r=[l, rr], w=[o])
                for hf in range(2):
                    c = 2 * s + hf
                    Sin = Sc[cur].sub(h * 128, 128) if c == 0 else B['Sall'].sub((c - 1) * 128, 128)
                    o2 = PS(bo, s * 128 + hf * 64, 64)
                    rr2 = B['qb'].sub(c * 64, 64)
                    mm(o2.ap(), Sin.ap(), rr2.ap(), False, hf == 1, r=[Sin, rr2], w=[o2])
        for g, h in enumerate(hs):
            psO = PS(bos[g])
            act(osq[g].ap(), psO.ap(), AF.Square, r=[psO], w=[osq[g]])
        st['bos'] = bos

    def ph_G3(hs, SL, cur, st):
        bms = []
        for g, h in enumerate(hs):
            bm = alloc()
            bms.append(bm)
            psM = PS(bm)
            mm(psM.ap(), onesb.ap(), osq[g].ap(), True, True, r=[onesb, osq[g]], w=[psM])
        for g, h in enumerate(hs):
            psM = PS(bms[g])
            act(rsv[g].ap(), psM.ap(), AF.Ln, r=[psM, epsc], w=[rsv[g]], bias=epsc.ap(), scale=1.0 / 128)
            rel(bms[g])
            act(rsv[g].ap(), rsv[g].ap(), AF.Exp, r=[rsv[g]], w=[rsv[g]], scale=-0.5)

    def ph_G4(hs, SL, cur, st):
        hg = col(par, PC_HG)
        bos = st['bos']
        for g, h in enumerate(hs):
            B = SL[g]
            psO = PS(bos[g])
            tt('dve', rsv[g].ap(), psO.ap(), rsv[g].ap(), ALU.mult, r=[psO, rsv[g]], w=[rsv[g]])
            rel(bos[g])
            oh = o_a.sub(h * T, T)
            S.op('dve', 'scalar_tensor_tensor', (oh.ap(), rsv[g].ap(), hg.ap(), B['sg'].ap(), ALU.mult, ALU.mult),
                 r=[rsv[g], hg, B['sg']], w=[oh])

    onec = col(stat, 13)
    eblv = sb("eblv", 8, F32)

    def pre_P(hs):
        return ph_P(hs, False)

    def pre_A(hs, SL, prs):
        for g, h in enumerate(hs):
            B = SL[g]
            act(B['ta'].ap(), prs[g]['z'].ap(), AF.Sigmoid, r=[prs[g]['z']], w=[B['ta']])
            S.op('dve', 'tensor_copy', (B['vh'].ap(), prs[g]['v'].ap()), r=[prs[g]['v']], w=[B['vh']])
            rel(prs[g]['z'].bank, prs[g]['v'].bank)
        for g, h in enumerate(hs):
            B = SL[g]
            lb, oml = col(lbv, h), col(lbv, 8 + h)
            act(B['tb'].ap(), B['ta'].ap(), AF.Ln, r=[B['ta'], lb, oml], w=[B['tb']], bias=lb.ap(), scale=oml.ap())

    def pre_BCD(hs, SL):
        for g, h in enumerate(hs):
            B = SL[g]
            S.op('dve', 'tensor_tensor_scan', (B['tc'].ap(), onec.ap([[0, T]]), B['tb'].ap(), 0.0, ALU.mult, ALU.add),
                 r=[onec, B['tb']], w=[B['tc']])
            bl = col(B['tc'], T - 1)
            S.op('dve', 'tensor_scalar', (B['tb'].ap(), B['tc'].ap(), -1.0, bl.ap(), ALU.mult, ALU.add),
                 r=[B['tc']], w=[B['tb']])
            oml, noml = col(lbv, 8 + h), col(lbv, 16 + h)
            S.op('dve', 'tensor_scalar', (B['ta'].ap(), B['ta'].ap(), noml.ap(), oml.ap(), ALU.mult, ALU.add),
                 r=[B['ta'], noml, oml], w=[B['ta']])
        for g, h in enumerate(hs):
            B = SL[g]
            act(B['tb'].ap(), B['tb'].ap(), AF.Exp, r=[B['tb']], w=[B['tb']])
            bl = col(B['tc'], T - 1)
            act(col(eblv, h).ap(), bl.ap(), AF.Exp, r=[bl], w=[col(eblv, h)])
        for g, h in enumerate(hs):
            B = SL[g]
            tt('pool', B['kdT'].ap(), B['ta'].ap(), B['tb'].ap(), ALU.mult, r=[B['ta'], B['tb']], w=[B['kdT']])

    def pre_back(hs, SL):
        ph_E(hs, SL)
        bds = []
        for g, h in enumerate(hs):
            B = SL[g]
            b = alloc()
            bds.append(b)
            o = PS(b, 0, 128)
            for s in range(4):
                l = B['kdtm'].sub(s * 128, 128)
                rr = B['vh'].sub(s * 128, 128)
                mm(o.ap(), l.ap(), rr.ap(), s == 0, s == 3, r=[l, rr], w=[o])
        for g, h in enumerate(hs):
            o = PS(bds[g], 0, 128)
            Sh = S32.sub(h * 128, 128)
            S.op('dve', 'scalar_tensor_tensor', (Sh.ap(), Sh.ap(), col(eblv, h).ap(), o.ap(), ALU.mult, ALU.add),
                 r=[Sh, col(eblv, h), o], w=[Sh])
            rel(bds[g])

    def pre_finish(cur):
        for h in range(H):
            Sh = S32.sub(h * 128, 128)
            dst = Sc[cur].sub(h * 128, 128)
            S.op('pool', 'tensor_copy', (dst.ap(), Sh.ap()), r=[Sh], w=[dst])

    def hgrn_stage(full, cur):
        assert full
        groups = [[0, 1], [2, 3], [4, 5], [6, 7]]

        def SLs(p):
            return SLF4[2 * (p % 2):2 * (p % 2) + 2]

        prs = ph_P(groups[0], True)
        ph_A(groups[0], SLs(0), prs, True)
        ph_B(groups[0], SLs(0))
        ph_C(groups[0], SLs(0))
        ph_A2(groups[0], SLs(0), prs)
        ph_D(groups[0], SLs(0), True)
        ph_E(groups[0], SLs(0))
        for p, hs in enumerate(groups):
            nxt = groups[p + 1] if p + 1 < len(groups) else None
            st = {}
            ph_F(hs, SLs(p), True, cur)
            if nxt:
                prs = ph_P(nxt, True)
                ph_A(nxt, SLs(p + 1), prs, True)
            ph_G1(hs, SLs(p), cur, st)
            if nxt:
                ph_B(nxt, SLs(p + 1))
            ph_G2(hs, SLs(p), cur, st)
            if nxt:
                ph_C(nxt, SLs(p + 1))
            ph_G3(hs, SLs(p), cur, st)
            if nxt:
                ph_A2(nxt, SLs(p + 1), prs)
                ph_D(nxt, SLs(p + 1), True)
            ph_G4(hs, SLs(p), cur, st)
            if nxt:
                ph_E(nxt, SLs(p + 1))

    def glu_stage():
        for j in range(4):
            wb = load_block(GLU0 + j)
            for mi in range(2):
                m = 2 * j + mi
                pa = proj_fm(wb, mi * 256)
                pb = proj_fm(wb, mi * 256 + 128)
                sg_ = sgate[m % 2]
                act(sg_.ap(), pb.ap(), AF.Sigmoid, r=[pb], w=[sg_])
                um = u.sub(m * UW + CW - 1, T)
                tt('dve', um.ap(), pa.ap(), sg_.ap(), ALU.mult, r=[pa, sg_], w=[um])
                rel(pa.bank, pb.bank)

    def halo_shift():
        S.op('pool', 'tensor_copy', (u.ap([[UW, 8], [1, CW - 1]]), u.ap([[UW, 8], [1, CW - 1]], off=T)),
             r=[u.sub(m * UW + T, CW - 1) for m in range(8)], w=[u.sub(m * UW, CW - 1) for m in range(8)])

    def conv_stage():
        bmean, bex2 = alloc(2)
        pmean, pex2 = PS(bmean), PS(bex2)
        build_diag(0)
        for m in range(8):
            if m + 1 < 8:
                build_diag(m + 1)
            b = alloc()
            py = PS(b)
            for j in range(CW):
                l = diag.sub(((m % 2) * CW + j) * 128, 128)
                rr = u.sub(m * UW + j, T)
                mm(py.ap(), l.ap(), rr.ap(), j == 0, j == CW - 1, r=[l, rr], w=[py])
            cbm = col(par, PC_CB + m)
            ym = y32.sub(m * T, T)
            act(ym.ap(), py.ap(), AF.Identity, r=[py, cbm], w=[ym], bias=cbm.ap())
            yb, yq = ybf[m % 2], ysq[m % 2]
            S.op('dve', 'tensor_copy', (yb.ap(), ym.ap()), r=[ym], w=[yb])
            act(yq.ap(), py.ap(), AF.Square, r=[py, cbm], w=[yq], bias=cbm.ap())
            rel(b)
            mm(pmean.ap(), onesb.ap(), yb.ap(), m == 0, m == 7, r=[onesb, yb], w=[pmean])
            mm(pex2.ap(), onesb.ap(), yq.ap(), m == 0, m == 7, r=[onesb, yq], w=[pex2])
        halo_shift()
        return bmean, bex2, pmean, pex2

    def conv_ln(bmean, bex2, pmean, pex2):
        act(meanv.ap(), pmean.ap(), AF.Copy, r=[pmean], w=[meanv], scale=1.0 / D)
        tt('dve', msqv.ap(), meanv.ap(), meanv.ap(), ALU.mult, r=[meanv], w=[msqv])
        S.op('dve', 'scalar_tensor_tensor', (msqv.ap(), pex2.ap(), 1.0 / D, msqv.ap(), ALU.mult, ALU.subtract),
             r=[pex2, msqv], w=[msqv])
        rel(bmean, bex2)
        act(rs2v.ap(), msqv.ap(), AF.Ln, r=[msqv, epsc], w=[rs2v], bias=epsc.ap())
        act(rs2v.ap(), rs2v.ap(), AF.Exp, r=[rs2v], w=[rs2v], scale=-0.5)

    def conv_norm(m):
        ym = y32.sub(m * T, T)
        tt('dve', ym.ap(), ym.ap(), meanv.ap(), ALU.subtract, r=[ym, meanv], w=[ym])
        tt('pool', ym.ap(), ym.ap(), rs2v.ap(), ALU.mult, r=[ym, rs2v], w=[ym])

    def conv_silu_all():
        for m in range(8):
            ym = y32.sub(m * T, T)
            lg, lbb = col(par, PC_LG + m), col(par, PC_LB + m)
            yn = ynorm.sub(m * T, T)
            act(yn.ap(), ym.ap(), AF.Silu, r=[ym, lg, lbb], w=[yn], bias=lbb.ap(), scale=lg.ap())

    def branch_stage(gate0, w0, src, is_b, js=(0, 1), hook=None):
        for j in js:
            wg = load_block(gate0 + j)
            ww = load_block(w0 + j)
            for mi in range(4):
                m = 4 * j + mi
                pg = proj_fm(wg, mi * 128)
                b = alloc()
                po = PS(b)
                for kc in range(8):
                    l = ww.sub(kc * 512 + mi * 128, 128)
                    rr = src.sub(kc * T, T)
                    mm(po.ap(), l.ap(), rr.ap(), kc == 0, kc == 7, r=[l, rr], w=[po])
                sg_ = sgate[m % 2]
                act(sg_.ap(), pg.ap(), AF.Sigmoid, r=[pg], w=[sg_])
                rel(pg.bank)
                ma = mergedA.sub(m * T, T)
                if not is_b:
                    tt('dve', ma.ap(), po.ap(), sg_.ap(), ALU.mult, r=[po, sg_], w=[ma])
                else:
                    tb_ = tbv[m % 2]
                    tt('dve', tb_.ap(), po.ap(), sg_.ap(), ALU.mult, r=[po, sg_], w=[tb_])
                    mg = merged.sub(m * T, T)
                    tt('pool', mg.ap(), tb_.ap(), ma.ap(), ALU.add, r=[tb_, ma], w=[mg])
                rel(b)
                if hook is not None:
                    hook(mi)

    def wout_stage():
        wbs = [load_block(WO0 + 0), load_block(WO0 + 1)]
        for s in range(4):
            for hf in range(2):
                wb = wbs[hf]
                b = alloc()
                po = PS(b)
                for kc in range(8):
                    l = merged.sub(kc * T + s * 128, 128)
                    rr = wb.sub(kc * 512, 512)
                    mm(po.ap(), l.ap(), rr.ap(), kc == 0, kc == 7, r=[l, rr], w=[po])
                xs = x.sub(s * D + hf * 512, 512)
                tt('dve', xs.ap(), po.ap(), xs.ap(), ALU.add, r=[po, xs], w=[xs])
                rel(b)
            if s >= 1:
                rmsnorm_sub(s - 1, gmlp, xnF)
        rmsnorm_sub(3, gmlp, xnF)

    def mlp_stage(nxt=None):
        for j in range(8):
            wb = load_block(M10 + j)
            for fi in range(4):
                f = 4 * j + fi
                pz = proj_fm(wb, fi * 128)
                rr_ = r32[f % 2]
                act(rr_.ap(), pz.ap(), AF.Relu, r=[pz], w=[rr_])
                rel(pz.bank)
                zf = z2T.sub(f * T, T)
                tt('pool' if f % 2 == 0 else 'dve', zf.ap(), rr_.ap(), rr_.ap(), ALU.mult, r=[rr_], w=[zf])
        if nxt is not None:
            pf_elem(nxt)
        for hf in range(2):
            if hf == 1 and nxt is not None:
                pf_pe()
            banks = alloc(4)
            for kg in range(4):
                wb = load_block(M20 + hf * 4 + kg)
                for s in range(4):
                    po = PS(banks[s])
                    for kc in range(8):
                        l = z2T.sub((kg * 8 + kc) * T + s * 128, 128)
                        rr = wb.sub(kc * 512, 512)
                        mm(po.ap(), l.ap(), rr.ap(), kg == 0 and kc == 0, kg == 3 and kc == 7, r=[l, rr], w=[po])
            for s in range(4):
                po = PS(banks[s])
                xs = x.sub(s * D + hf * 512, 512)
                tt('dve', xs.ap(), po.ap(), xs.ap(), ALU.add, r=[po, xs], w=[xs])
            rel(*banks)
        if nxt is not None:
            preload(HB0 + 0)
            preload(HB0 + 1)

    xstage = [A((36 + 4 * i) * K, D, F32) for i in range(4)]
    xn4 = [A((52 + 2 * i) * K, D, BF16) for i in range(4)]

    def pf_elem(t):
        for s in range(4):
            xs = xstage[s]
            dma(xs, dram(x_d[t * 4 + s], "xin", 0, 1), "xs%d" % s)
            rs = rms_stats(xs, s, xn4[s])
            S.op('dve', 'scalar_tensor_tensor', (xn4[s].ap(), xs.ap(), rs.ap(), gmix.ap(), ALU.mult, ALU.mult), r=[xs, rs, gmix], w=[xn4[s]])

    def pf_pe():
        for s in range(4):
            b = alloc()
            xb = xn4[s]
            for kc in range(8):
                o = PSB(b, kc * 128, 128)
                i_ = xb.sub(kc * 128, 128)
                S.op('pe', 'transpose', (o.ap(), i_.ap(), identb.ap()), r=[i_, identb], w=[o])
            pv = PSB(b)
            dst_ap = hT.ap([[512, 8], [1, 128]], off=s * 128)
            src_ap = pv.ap([[128, 8], [1, 128]])
            wl = [hT.sub(kc * 512 + s * 128, 128) for kc in range(8)]
            if s % 2:
                S.op('dve', 'tensor_copy', (dst_ap, src_ap), r=[pv], w=wl)
            else:
                act(dst_ap, src_ap, AF.Copy, r=[pv], w=wl)
            rel(b)

    def final_stage(t):
        toks = []
        for s in range(4):
            xs = x.sub(s * D, D)
            rs = rms_stats(xs, s, xnF[s % 2])
            S.op('dve', 'scalar_tensor_tensor', (xs.ap(), xs.ap(), rs.ap(), gfin.ap(), ALU.mult, ALU.mult),
                 r=[xs, rs, gfin], w=[xs])
            toks.append(dma(dram(y_d[t * 4 + s], "y_d", t * 4 + s, t * 4 + s + 1), xs, "yo%d" % s))
        return toks

    def load_x(src, t):
        for s in range(4):
            dma(x.sub(s * D, D), dram(src[t * 4 + s], "xin", 0, 1), "x%d" % s)

    last_tok = []
    S.op('dve', 'memset', (onec.ap(), 1.0), w=[onec])
    for h in range(8):
        cast_group([HB0 + h], "cb%d" % h)
    later = [([b], "cq%d" % b) for b in [8, 9, 10, 11, 12, 13, 16, 17, 14, 15, 18, 19, 20, 21] + list(range(22, 38))]
    load_wpre()
    load_x(xp_d, 0)
    NG = 2 * NT

    def grp(k):
        half = k % 2
        return [4 * half + i for i in range(4)], SLP[4 * (k % 2):4 * (k % 2) + 4]

    def tile_start(k):
        if k % 2 == 0:
            t = k // 2
            rmsnorm_T(gmix, xnP)
            if t + 1 < NT:
                load_x(xp_d, t + 1)
            else:
                load_x(x_d, 0)

    tile_start(0)
    prs_k = pre_P(grp(0)[0])
    pre_A(grp(0)[0], grp(0)[1], prs_k)
    for k in range(NG):
        hs, SLg = grp(k)
        if k + 1 < NG:
            tile_start(k + 1)
            prs_n = pre_P(grp(k + 1)[0])
        pre_BCD(hs, SLg)
        for _ in range(2):
            if later and (k >= 1 or NT == 1):
                cast_group(*later.pop(0))
        if k + 1 < NG:
            pre_A(grp(k + 1)[0], grp(k + 1)[1], prs_n)
        pre_back(hs, SLg)
    while later:
        cast_group(*later.pop(0))
    glu_stage()
    halo_shift()
    cur = 0
    pre_finish(cur)
    for t in range(NT):
        if t == 0:
            rmsnorm_T(gmix, xnF)
        hgrn_stage(True, cur)
        cur = 1 - cur
        glu_stage()
        cs_ = conv_stage()
        branch_stage(GA0, WA0, o_a, False, js=(0,))
        conv_ln(*cs_)
        branch_stage(GA0, WA0, o_a, False, js=(1,), hook=lambda mi: (conv_norm(2 * mi), conv_norm(2 * mi + 1)))
        conv_silu_all()
        branch_stage(GB0, WB0, ynorm, True)
        wout_stage()
        mlp_stage(t + 1 if t + 1 < NT else None)
        last_tok = final_stage(t)
        if t + 1 < NT:
            load_x(x_d, t + 1)
    assert len(free_banks) == 8
    S.emit(nc, es, list(last_tok))
    es.close()
    S.stats = {e: len(S.ops[e]) for e in ENGS}
    print('ops per engine:', S.stats, flush=True)
    return nc


_CACHE = {}


def _prep_weights(inp):
    w_in = np.asarray(inp["w_in"][0], dtype=np.float32)
    blocks = []

    def blk(w, cols):
        kt = w.shape[0] // 128
        b = w[:, cols].reshape(kt, 128, len(cols)).transpose(1, 0, 2)
        return np.ascontiguousarray(b).reshape(128, kt * len(cols))

    ar = np.arange
    for h in range(8):
        cols = np.concatenate([1024 + h * 128 + ar(128), 2048 + h * 128 + ar(128), h * 128 + ar(128), 3072 + h * 128 + ar(128)])
        blocks.append(blk(w_in, cols))
    for j in range(4):
        m0, m1 = 2 * j, 2 * j + 1
        cols = np.concatenate([4096 + m0 * 128 + ar(128), 5120 + m0 * 128 + ar(128),
                               4096 + m1 * 128 + ar(128), 5120 + m1 * 128 + ar(128)])
        blocks.append(blk(w_in, cols))
    for j in range(2):
        blocks.append(blk(w_in, 6144 + j * 512 + ar(512)))
    for j in range(2):
        blocks.append(blk(w_in, 7168 + j * 512 + ar(512)))
    for name in ("w_a_out", "w_b_out", "w_out"):
        w = np.asarray(inp[name][0], dtype=np.float32)
        for j in range(2):
            blocks.append(blk(w, j * 512 + ar(512)))
    w1 = np.asarray(inp["w_mlp_in"][0], dtype=np.float32)
    for j in range(8):
        blocks.append(blk(w1, j * 512 + ar(512)))
    w2 = np.asarray(inp["w_mlp_out"][0], dtype=np.float32)
    for hf in range(2):
        for kg in range(4):
            blocks.append(blk(w2[kg * 1024:(kg + 1) * 1024], hf * 512 + ar(512)))
    wall = np.stack(blocks).astype(np.float32)
    assert wall.shape == (NBLK, 128, 4096)

    def fm(v):
        return np.asarray(v, dtype=np.float32).reshape(8, 128).T

    par = np.zeros((128, NPAR), np.float32)
    par[:, PC_GM:PC_GM + 8] = fm(inp["norm_mix_g"][0])
    par[:, PC_GML:PC_GML + 8] = fm(inp["norm_mlp_g"][0])
    par[:, PC_P0:PC_P0 + 8] = fm(inp["lb_param"][0])
    par[:, PC_P1:PC_P1 + 8] = fm(inp["lb_param"][1])
    par[:, PC_HG] = np.asarray(inp["hgrn_norm_g"][0], dtype=np.float32)
    par[:, PC_CB:PC_CB + 8] = fm(inp["conv_b"][0])
    par[:, PC_LG:PC_LG + 8] = fm(inp["conv_ln_g"][0])
    par[:, PC_LB:PC_LB + 8] = fm(inp["conv_ln_b"][0])
    cw = np.asarray(inp["conv_w"][0], dtype=np.float32)
    par[:, PC_CW:] = cw.reshape(CW, 8, 128).transpose(2, 1, 0).reshape(128, 8 * CW)
    gfin = np.ascontiguousarray(np.broadcast_to(np.asarray(inp["norm_final_g"], dtype=np.float32)[None, :], (128, D)))
    gmix = np.ascontiguousarray(np.broadcast_to(np.asarray(inp["norm_mix_g"][0], dtype=np.float32)[None, :], (128, D)))
    gmlp = np.ascontiguousarray(np.broadcast_to(np.asarray(inp["norm_mlp_g"][0], dtype=np.float32)[None, :], (128, D)))
    cst = np.zeros((128, NCST), np.float32)
    cst[:, CC_ID:CC_ID + 128] = np.eye(128, dtype=np.float32)
    sp, tt = np.meshgrid(ar(128), ar(128), indexing="ij")
    cst[:, CC_MASK:CC_MASK + 128] = ((sp // 64 == tt // 64) & (sp <= tt)).astype(np.float32)
    cst[:, CC_SCAN:] = (ar(T) % 64 != 0).astype(np.float32)[None, :]
    return wall, par, gfin, cst, gmix, gmlp


def kernel(**inp):
    xfull = np.asarray(inp["x"], dtype=np.float32)
    B, Sq, _ = xfull.shape
    half = Sq // 2
    NT = half // T
    assert half % T == 0 and B * 2 == 8
    if NT not in _CACHE:
        _CACHE[NT] = build(NT)
    nc = _CACHE[NT]
    wall, par, gfin, cst, gmix, gmlp = _prep_weights(inp)
    in_maps = []
    zeros = np.zeros((NT * 4, 128, D), np.float32)
    for c in range(8):
        b, hf = c // 2, c % 2
        xc = np.ascontiguousarray(xfull[b, hf * half:(hf + 1) * half]).reshape(NT * 4, 128, D)
        xp = zeros if hf == 0 else np.ascontiguousarray(xfull[b, 0:half]).reshape(NT * 4, 128, D)
        in_maps.append({"x": xc, "xprev": xp, "wall": wall, "par": par, "cst": cst, "gfin": gfin, "gmix": gmix, "gmlp": gmlp})
    res = run_bass_kernel_spmd(nc, in_maps, core_ids=list(range(8)))
    out = np.empty((B, Sq, D), np.float32)
    for c in range(8):
        b, hf = c // 2, c % 2
        out[b, hf * half:(hf + 1) * half] = np.asarray(res.results[c]["y"]).reshape(half, D)
    return out
```
